# Optimizing a Trainium2 kernel written in Bass

```python
import math
import jax
import jax.numpy as jnp
from jax import lax
import numpy as np

D_MODEL = 1024
BATCH = 16
SEQ = 256
DEPTH = 4
DEC_BATCH = 2
DEC_SEQ = 2048
PAST_LEN = 512

GRID_W = 64
N_EVEN = (DEPTH + 1) // 2
N_ODD = DEPTH // 2
EPS = 1e-6
Q_BLOCK = 128

RET_HEADS = 4
RET_W = D_MODEL // 2
RET_DIM = RET_W // RET_HEADS
RET_CHUNK = 64

MLA_HEADS = 8
MLA_NOPE = 64
MLA_ROPE = 32
MLA_V = (D_MODEL - RET_W) // MLA_HEADS
MLA_Q_RANK = 256
MLA_KV_RANK = 128
MLA_SCALE = (MLA_NOPE + MLA_ROPE) ** -0.5
ROPE_BASE = 10000.0

S5_W = D_MODEL // 2
S5_GROUP = 16
S5_GROUPS = S5_W // S5_GROUP
S5_P = 64

NA_HEADS = 8
NA_W = D_MODEL - S5_W
NA_DIM = NA_W // NA_HEADS
NA_WIN_R = 8
NA_WIN_C = 16
NA_SCALE = NA_DIM ** -0.5

D_FF = 2816
CONV_W = 3

EVEN_IN = 4 * RET_W + MLA_Q_RANK + MLA_KV_RANK + MLA_ROPE
ODD_IN = S5_W + 3 * NA_W
EVEN_SPLITS = [RET_W, 2 * RET_W, 3 * RET_W, 4 * RET_W, 4 * RET_W + MLA_Q_RANK, 4 * RET_W + MLA_Q_RANK + MLA_KV_RANK]
ODD_SPLITS = [S5_W, S5_W + NA_W, S5_W + 2 * NA_W]

kernel_name = 'hybrid_diffusion_prefix_trunk_step'


def rmsnorm(x, g):
    xf = x.astype(jnp.float32)
    y = xf * lax.rsqrt(jnp.mean(xf * xf, axis=-1, keepdims=True) + EPS)
    return (y * g.astype(jnp.float32)).astype(x.dtype)


def ada_modulation(cvec, w, b):
    m = jnp.dot(jax.nn.silu(cvec), w) + b
    return jnp.split(m[..., None, :], 6, axis=-1)


def axial_rope(n_tok, dim):
    n_freq = dim // 4
    inv = ROPE_BASE ** (-jnp.arange(n_freq, dtype=jnp.float32) / n_freq)
    t = jnp.arange(n_tok)
    row = (t // GRID_W).astype(jnp.float32)
    col = (t % GRID_W).astype(jnp.float32)
    ang = jnp.concatenate([row[:, None] * inv, col[:, None] * inv], axis=-1)
    return jnp.cos(ang), jnp.sin(ang)


def apply_rope(x, cos, sin):
    x1, x2 = jnp.split(x.astype(jnp.float32), 2, axis=-1)
    return jnp.concatenate([x1 * cos - x2 * sin, x1 * sin + x2 * cos], axis=-1).astype(x.dtype)


def retention_scan(q, k, v, log_gamma, s0):
    bsz, seq, heads, dim = q.shape
    nc = seq // RET_CHUNK
    shp = (bsz, nc, RET_CHUNK, heads, dim)
    qc, kc, vc = q.reshape(shp), k.reshape(shp), v.reshape(shp)
    pos = jnp.arange(RET_CHUNK, dtype=jnp.float32)
    diff = pos[:, None] - pos[None, :]
    decay = jnp.where(diff >= 0, jnp.exp(log_gamma[:, None, None] * jnp.maximum(diff, 0.0)), 0.0)
    scores = jnp.einsum('bnihd,bnjhd->bnhij', qc, kc) * decay
    o_inner = jnp.einsum('bnhij,bnjhe->bnihe', scores, vc)
    k_w = jnp.exp(log_gamma[None, :] * (RET_CHUNK - 1.0 - pos)[:, None])
    q_w = jnp.exp(log_gamma[None, :] * (pos + 1.0)[:, None])
    kv = jnp.einsum('bnjhd,bnjhe->nbhde', kc * k_w[:, :, None], vc)
    chunk_decay = jnp.exp(log_gamma * RET_CHUNK)[:, None, None]

    def step(s, kv_n):
        return chunk_decay * s + kv_n, s

    s_final, s_prev = lax.scan(step, s0, kv)
    o_cross = jnp.einsum('bnihd,nbhde->bnihe', qc * q_w[:, :, None], s_prev)
    return (o_inner + o_cross).reshape(bsz, seq, heads, dim), s_final


def retention_mixer(q, k, v, g, ret_logit, ret_gn, s0_fb):
    bsz, seq, _ = q.shape
    heads = lambda a: a.astype(jnp.float32).reshape(bsz, seq, RET_HEADS, RET_DIM)
    flip = lambda a: jnp.flip(a, axis=1)
    log_gamma = jax.nn.log_sigmoid(ret_logit.astype(jnp.float32))
    s0 = s0_fb.astype(jnp.float32)
    qh, kh, vh = heads(q) * RET_DIM ** -0.5, heads(k), heads(v)
    o_f, s_f = retention_scan(qh, kh, vh, log_gamma[0], s0[:, 0])
    o_b, s_b = retention_scan(flip(qh), flip(kh), flip(vh), log_gamma[1], s0[:, 1])
    o = o_f + flip(o_b)
    mu = jnp.mean(o, axis=-1, keepdims=True)
    var = jnp.mean(jnp.square(o - mu), axis=-1, keepdims=True)
    o = ((o - mu) * lax.rsqrt(var + EPS)).reshape(bsz, seq, RET_W) * ret_gn.astype(jnp.float32)
    y = jax.nn.silu(g.astype(jnp.float32)) * o
    return y.astype(q.dtype), jnp.stack([s_f, s_b], axis=1)


def mla_queries(cq, q_norm, w_uq):
    bsz, seq = cq.shape[:2]
    q = jnp.dot(rmsnorm(cq, q_norm), w_uq).reshape(bsz, seq, MLA_HEADS, MLA_NOPE + MLA_ROPE)
    return q[..., :MLA_NOPE], q[..., MLA_NOPE:]


def mla_keys_values(ckv, w_ukv):
    bsz, seq = ckv.shape[:2]
    kv = jnp.dot(ckv, w_ukv).reshape(bsz, seq, MLA_HEADS, MLA_NOPE + MLA_V)
    return kv[..., :MLA_NOPE], kv[..., MLA_NOPE:]


def mla_attend(q_nope, q_rope, k_nope, k_rope, v):
    s = (jnp.einsum('bqhd,bkhd->bhqk', q_nope, k_nope, preferred_element_type=jnp.float32)
         + jnp.einsum('bqhr,bkr->bhqk', q_rope, k_rope, preferred_element_type=jnp.float32))
    p = jax.nn.softmax(s * MLA_SCALE, axis=-1).astype(v.dtype)
    return jnp.einsum('bhqk,bkhd->bqhd', p, v)


def mla_blocked(q_nope, q_rope, k_nope, k_rope, v):
    bsz, seq = q_nope.shape[:2]
    nb = seq // Q_BLOCK
    blk = lambda a: jnp.swapaxes(a.reshape((bsz, nb, Q_BLOCK) + a.shape[2:]), 0, 1)
    out = lax.map(lambda qs: mla_attend(qs[0], qs[1], k_nope, k_rope, v), (blk(q_nope), blk(q_rope)))
    return jnp.swapaxes(out, 0, 1).reshape(bsz, seq, MLA_HEADS * MLA_V)


def cmul(ar, ai, br, bi):
    return ar * br - ai * bi, ar * bi + ai * br


def s5_scan(u, lam_re, lam_im, log_step, b_re, b_im, c_re, c_im, h0_re, h0_im):
    step = jnp.exp(log_step)[:, None]
    mag = jnp.exp(lam_re * step)
    a_re, a_im = mag * jnp.cos(lam_im * step), mag * jnp.sin(lam_im * step)
    den = lam_re * lam_re + lam_im * lam_im
    z_re, z_im = cmul(a_re - 1.0, a_im, lam_re / den, -lam_im / den)
    bb_re, bb_im = cmul(z_re[..., None], z_im[..., None], b_re, b_im)
    bu_re = jnp.einsum('blgs,gps->blgp', u, bb_re)
    bu_im = jnp.einsum('blgs,gps->blgp', u, bb_im)
    ar = jnp.broadcast_to(a_re, bu_re.shape)
    ai = jnp.broadcast_to(a_im, bu_re.shape)

    def combine(e1, e2):
        a1r, a1i, b1r, b1i = e1
        a2r, a2i, b2r, b2i = e2
        nar, nai = cmul(a2r, a2i, a1r, a1i)
        nbr, nbi = cmul(a2r, a2i, b1r, b1i)
        return nar, nai, nbr + b2r, nbi + b2i

    pr, pi, hr, hi = lax.associative_scan(combine, (ar, ai, bu_re, bu_im), axis=1)
    ir, ii = cmul(pr, pi, h0_re[:, None], h0_im[:, None])
    hr, hi = hr + ir, hi + ii
    y = jnp.einsum('gsp,blgp->blgs', c_re, hr) - jnp.einsum('gsp,blgp->blgs', c_im, hi)
    return y, hr[:, -1], hi[:, -1]


def s5_mixer(u, s5p, h0_re, h0_im):
    lam_re, lam_im, log_step, b_re, b_im, c_re, c_im, d_skip, glu_w, glu_b = [a.astype(jnp.float32) for a in s5p]
    bsz, seq, _ = u.shape
    uf = u.astype(jnp.float32)
    ug = uf.reshape(bsz, seq, S5_GROUPS, S5_GROUP)
    h0_re = h0_re.astype(jnp.float32)
    h0_im = h0_im.astype(jnp.float32)
    y_f, f_re, f_im = s5_scan(ug, lam_re[0], lam_im[0], log_step[0], b_re[0], b_im[0], c_re[0], c_im[0], h0_re[:, 0], h0_im[:, 0])
    y_b, r_re, r_im = s5_scan(jnp.flip(ug, axis=1), lam_re[1], lam_im[1], log_step[1], b_re[1], b_im[1], c_re[1], c_im[1], h0_re[:, 1], h0_im[:, 1])
    y = (y_f + jnp.flip(y_b, axis=1)).reshape(bsz, seq, S5_W) + d_skip * uf
    y = jax.nn.gelu(y)
    y = y * jax.nn.sigmoid(jnp.dot(y, glu_w) + glu_b)
    return y.astype(u.dtype), jnp.stack([f_re, r_re], axis=1), jnp.stack([f_im, r_im], axis=1)


def na_context(q, k, v):
    bsz, seq = q.shape[:2]
    nb = seq // Q_BLOCK

    def attend(qb):
        s = jnp.einsum('bqhd,bkhd->bhqk', qb, k, preferred_element_type=jnp.float32) * NA_SCALE
        p = jax.nn.softmax(s, axis=-1).astype(v.dtype)
        return jnp.einsum('bhqk,bkhd->bqhd', p, v)

    out = lax.map(attend, jnp.swapaxes(q.reshape(bsz, nb, Q_BLOCK, NA_HEADS, NA_DIM), 0, 1))
    return jnp.swapaxes(out, 0, 1).reshape(bsz, seq, NA_W)


def na_latent(q, k, v, k_ctx, v_ctx, rpb):
    bsz, seq = q.shape[:2]
    rows = seq // GRID_W
    wr = min(NA_WIN_R, rows)
    r = jnp.arange(rows)
    rs = jnp.clip(r - wr // 2, 0, rows - wr)
    row_idx = rs[:, None] + jnp.arange(wr)[None, :]
    col = jnp.arange(GRID_W)
    cs = jnp.clip(col - NA_WIN_C // 2, 0, GRID_W - NA_WIN_C)
    in_band = (col[None, :] >= cs[:, None]) & (col[None, :] < cs[:, None] + NA_WIN_C)
    dr = row_idx - r[:, None] + NA_WIN_R - 1
    dc = jnp.clip(col[None, :] - col[:, None] + NA_WIN_C - 1, 0, 2 * NA_WIN_C - 2)
    n_loc = wr * GRID_W
    bias = rpb.astype(jnp.float32)[:, dr[:, None, :, None], dc[None, :, None, :]]
    bias = bias.reshape(NA_HEADS, rows, GRID_W, n_loc)
    mask = jnp.broadcast_to(in_band[:, None, :], (GRID_W, wr, GRID_W)).reshape(GRID_W, n_loc)
    grid = lambda a: a.reshape(bsz, rows, GRID_W, NA_HEADS, NA_DIM)
    qg = grid(q)
    kb = grid(k)[:, row_idx].reshape(bsz, rows, n_loc, NA_HEADS, NA_DIM)
    vb = grid(v)[:, row_idx].reshape(bsz, rows, n_loc, NA_HEADS, NA_DIM)
    s_loc = jnp.einsum('brqhd,brkhd->bhrqk', qg, kb, preferred_element_type=jnp.float32) * NA_SCALE + bias
    s_loc = jnp.where(mask, s_loc, -jnp.inf)
    s_ctx = jnp.einsum('brqhd,bkhd->bhrqk', qg, k_ctx, preferred_element_type=jnp.float32) * NA_SCALE
    p = jax.nn.softmax(jnp.concatenate([s_loc, s_ctx], axis=-1), axis=-1).astype(v.dtype)
    o = (jnp.einsum('bhrqk,brkhd->brqhd', p[..., :n_loc], vb)
         + jnp.einsum('bhrqk,bkhd->brqhd', p[..., n_loc:], v_ctx))
    return o.reshape(bsz, seq, NA_W)


def even_layer_context(h, w_in, w_out, ret_logit, ret_gn, q_norm, w_uq, kv_norm, w_ukv):
    bsz = h.shape[0]
    q, k, v, g, cq, ckv_raw, k_rope = jnp.split(jnp.dot(h, w_in), EVEN_SPLITS, axis=-1)
    s0 = jnp.zeros((bsz, 2, RET_HEADS, RET_DIM, RET_DIM), jnp.float32)
    y_ret, s_ret = retention_mixer(q, k, v, g, ret_logit, ret_gn, s0)
    ckv = rmsnorm(ckv_raw, kv_norm)
    k_nope, v_m = mla_keys_values(ckv, w_ukv)
    q_nope, q_rope = mla_queries(cq, q_norm, w_uq)
    y_mla = mla_blocked(q_nope, q_rope, k_nope, k_rope, v_m)
    y = jnp.dot(jnp.concatenate([y_ret, y_mla], axis=-1), w_out)
    return y, s_ret, ckv, k_rope


def even_layer_latent(h, s_ret_ctx, ckv_ctx, kr_ctx, w_in, w_out, ret_logit, ret_gn, q_norm, w_uq, kv_norm, w_ukv):
    seq = h.shape[1]
    q, k, v, g, cq, ckv_raw, k_rope = jnp.split(jnp.dot(h, w_in), EVEN_SPLITS, axis=-1)
    y_ret, _ = retention_mixer(q, k, v, g, ret_logit, ret_gn, s_ret_ctx)
    cos, sin = axial_rope(seq, MLA_ROPE)
    ckv = jnp.concatenate([rmsnorm(ckv_raw, kv_norm), ckv_ctx.astype(h.dtype)], axis=1)
    k_rope = jnp.concatenate([apply_rope(k_rope, cos, sin), kr_ctx.astype(h.dtype)], axis=1)
    k_nope, v_m = mla_keys_values(ckv, w_ukv)
    q_nope, q_rope = mla_queries(cq, q_norm, w_uq)
    q_rope = apply_rope(q_rope, cos[:, None], sin[:, None])
    y_mla = mla_blocked(q_nope, q_rope, k_nope, k_rope, v_m)
    return jnp.dot(jnp.concatenate([y_ret, y_mla], axis=-1), w_out)


def odd_layer_context(h, w_in, w_out, s5p):
    bsz, seq = h.shape[:2]
    u, q, k, v = jnp.split(jnp.dot(h, w_in), ODD_SPLITS, axis=-1)
    heads = lambda a: a.reshape(bsz, seq, NA_HEADS, NA_DIM)
    h0 = jnp.zeros((bsz, 2, S5_GROUPS, S5_P), jnp.float32)
    y_s5, st_re, st_im = s5_mixer(u, s5p, h0, h0)
    k, v = heads(k), heads(v)
    y_na = na_context(heads(q), k, v)
    y = jnp.dot(jnp.concatenate([y_s5, y_na], axis=-1), w_out)
    return y, st_re, st_im, k, v


def odd_layer_latent(h, st_re_ctx, st_im_ctx, k_ctx, v_ctx, w_in, w_out, s5p, rpb):
    bsz, seq = h.shape[:2]
    u, q, k, v = jnp.split(jnp.dot(h, w_in), ODD_SPLITS, axis=-1)
    heads = lambda a: a.reshape(bsz, seq, NA_HEADS, NA_DIM)
    y_s5, _, _ = s5_mixer(u, s5p, st_re_ctx, st_im_ctx)
    y_na = na_latent(heads(q), heads(k), heads(v), k_ctx.astype(h.dtype), v_ctx.astype(h.dtype), rpb)
    return jnp.dot(jnp.concatenate([y_s5, y_na], axis=-1), w_out)


def conv_ffn(h, w_up, conv_w, conv_b, w_down):
    u = jnp.dot(h, w_up)
    ch = u.shape[-1]
    u = lax.conv_general_dilated(u, conv_w[:, None, :].astype(u.dtype), window_strides=(1,),
                                 padding=((CONV_W // 2, CONV_W // 2),),
                                 dimension_numbers=('NWC', 'WIO', 'NWC'), feature_group_count=ch) + conv_b
    a, g = jnp.split(u, 2, axis=-1)
    return jnp.dot(jax.nn.silu(g) * a, w_down)


def setup_inputs(seed: int = 0) -> dict:
    key = jax.random.key(seed)
    ks = iter(jax.random.split(key, 64))
    nrm = lambda shape, scale: jax.random.normal(next(ks), shape, jnp.float32) * scale
    gain = lambda shape: 1.0 + nrm(shape, 0.05)
    D = D_MODEL
    ret_base = jnp.log(2.0 ** (5.0 + jnp.arange(RET_HEADS, dtype=jnp.float32)) - 1.0)
    lam_im_base = math.pi * jnp.arange(S5_P, dtype=jnp.float32)
    return {
        'x_prompt': nrm((BATCH, SEQ, D), 1.0),
        'x_sample': nrm((DEC_BATCH, DEC_SEQ, D), 1.0),
        'c': nrm((DEC_BATCH, D), 1.0),
        'state_ret': nrm((DEC_BATCH, N_EVEN, 2, RET_HEADS, RET_DIM, RET_DIM), 1.0),
        'cache_mla_ckv': nrm((DEC_BATCH, N_EVEN, PAST_LEN, MLA_KV_RANK), 1.0),
        'cache_mla_krope': nrm((DEC_BATCH, N_EVEN, PAST_LEN, MLA_ROPE), 1.0),
        'state_s5_re': nrm((DEC_BATCH, N_ODD, 2, S5_GROUPS, S5_P), 0.1),
        'state_s5_im': nrm((DEC_BATCH, N_ODD, 2, S5_GROUPS, S5_P), 0.1),
        'cache_na_k': nrm((DEC_BATCH, N_ODD, PAST_LEN, NA_HEADS, NA_DIM), 1.0),
        'cache_na_v': nrm((DEC_BATCH, N_ODD, PAST_LEN, NA_HEADS, NA_DIM), 1.0),
        'c_ctx': nrm((D,), 1.0),
        'ada_w': nrm((DEPTH, D, 6 * D), 0.5 * D ** -0.5),
        'ada_b': nrm((DEPTH, 6 * D), 0.01),
        'mix_pre_g': gain((DEPTH, D)),
        'mix_post_g': gain((DEPTH, D)),
        'ffn_pre_g': gain((DEPTH, D)),
        'ffn_post_g': gain((DEPTH, D)),
        'ffn_w_up': nrm((DEPTH, D, 2 * D_FF), D ** -0.5),
        'ffn_conv_w': nrm((DEPTH, CONV_W, 2 * D_FF), CONV_W ** -0.5),
        'ffn_conv_b': nrm((DEPTH, 2 * D_FF), 0.01),
        'ffn_w_down': nrm((DEPTH, D_FF, D), D_FF ** -0.5),
        'even_w_in': nrm((N_EVEN, D, EVEN_IN), D ** -0.5),
        'even_w_out': nrm((N_EVEN, D, D), D ** -0.5),
        'ret_logit': ret_base + nrm((N_EVEN, 2, RET_HEADS), 0.1),
        'ret_gn': gain((N_EVEN, RET_W)),
        'mla_q_norm': gain((N_EVEN, MLA_Q_RANK)),
        'mla_w_uq': nrm((N_EVEN, MLA_Q_RANK, MLA_HEADS * (MLA_NOPE + MLA_ROPE)), MLA_Q_RANK ** -0.5),
        'mla_kv_norm': gain((N_EVEN, MLA_KV_RANK)),
        'mla_w_ukv': nrm((N_EVEN, MLA_KV_RANK, MLA_HEADS * (MLA_NOPE + MLA_V)), MLA_KV_RANK ** -0.5),
        'odd_w_in': nrm((N_ODD, D, ODD_IN), D ** -0.5),
        'odd_w_out': nrm((N_ODD, D, D), D ** -0.5),
        's5_lambda_re': -0.5 + nrm((N_ODD, 2, S5_GROUPS, S5_P), 0.01),
        's5_lambda_im': lam_im_base + nrm((N_ODD, 2, S5_GROUPS, S5_P), 0.01),
        's5_log_step': jax.random.uniform(next(ks), (N_ODD, 2, S5_GROUPS), jnp.float32, math.log(0.001), math.log(0.1)),
        's5_b_re': nrm((N_ODD, 2, S5_GROUPS, S5_P, S5_GROUP), (2 * S5_GROUP) ** -0.5),
        's5_b_im': nrm((N_ODD, 2, S5_GROUPS, S5_P, S5_GROUP), (2 * S5_GROUP) ** -0.5),
        's5_c_re': nrm((N_ODD, 2, S5_GROUPS, S5_GROUP, S5_P), 0.5),
        's5_c_im': nrm((N_ODD, 2, S5_GROUPS, S5_GROUP, S5_P), 0.5),
        's5_d': nrm((N_ODD, S5_W), 1.0),
        's5_glu_w': nrm((N_ODD, S5_W, S5_W), S5_W ** -0.5),
        's5_glu_b': nrm((N_ODD, S5_W), 0.01),
        'na_rpb': nrm((N_ODD, NA_HEADS, 2 * NA_WIN_R - 1, 2 * NA_WIN_C - 1), 0.02),
    }


def reference(x_prompt, x_sample, c, state_ret, cache_mla_ckv, cache_mla_krope, state_s5_re, state_s5_im,
              cache_na_k, cache_na_v, c_ctx, ada_w, ada_b, mix_pre_g, mix_post_g, ffn_pre_g, ffn_post_g,
              ffn_w_up, ffn_conv_w, ffn_conv_b, ffn_w_down, even_w_in, even_w_out, ret_logit, ret_gn,
              mla_q_norm, mla_w_uq, mla_kv_norm, mla_w_ukv, odd_w_in, odd_w_out, s5_lambda_re, s5_lambda_im,
              s5_log_step, s5_b_re, s5_b_im, s5_c_re, s5_c_im, s5_d, s5_glu_w, s5_glu_b, na_rpb):
    xp, xs = x_prompt, x_sample
    new_ret, new_ckv, new_kr, new_s5_re, new_s5_im, new_nak, new_nav = [], [], [], [], [], [], []
    for layer in range(DEPTH):
        j = layer // 2
        mp = ada_modulation(c_ctx, ada_w[layer], ada_b[layer])
        ms = ada_modulation(c, ada_w[layer], ada_b[layer])
        hp = rmsnorm(xp, mix_pre_g[layer]) * (1.0 + mp[1]) + mp[0]
        hs = rmsnorm(xs, mix_pre_g[layer]) * (1.0 + ms[1]) + ms[0]
        if layer % 2 == 0:
            ep = (even_w_in[j], even_w_out[j], ret_logit[j], ret_gn[j],
                  mla_q_norm[j], mla_w_uq[j], mla_kv_norm[j], mla_w_ukv[j])
            yp, s_ret, ckv, kr = even_layer_context(hp, *ep)
            ys = even_layer_latent(hs, state_ret[:, j], cache_mla_ckv[:, j], cache_mla_krope[:, j], *ep)
            new_ret.append(s_ret)
            new_ckv.append(ckv)
            new_kr.append(kr)
        else:
            s5p = (s5_lambda_re[j], s5_lambda_im[j], s5_log_step[j], s5_b_re[j], s5_b_im[j],
                   s5_c_re[j], s5_c_im[j], s5_d[j], s5_glu_w[j], s5_glu_b[j])
            yp, st_re, st_im, nk, nv = odd_layer_context(hp, odd_w_in[j], odd_w_out[j], s5p)
            ys = odd_layer_latent(hs, state_s5_re[:, j], state_s5_im[:, j], cache_na_k[:, j], cache_na_v[:, j],
                                  odd_w_in[j], odd_w_out[j], s5p, na_rpb[j])
            new_s5_re.append(st_re)
            new_s5_im.append(st_im)
            new_nak.append(nk)
            new_nav.append(nv)
        xp = xp + mp[2] * rmsnorm(yp, mix_post_g[layer])
        xs = xs + ms[2] * rmsnorm(ys, mix_post_g[layer])
        fp = (ffn_w_up[layer], ffn_conv_w[layer], ffn_conv_b[layer], ffn_w_down[layer])
        hp = rmsnorm(xp, ffn_pre_g[layer]) * (1.0 + mp[4]) + mp[3]
        hs = rmsnorm(xs, ffn_pre_g[layer]) * (1.0 + ms[4]) + ms[3]
        xp = xp + mp[5] * rmsnorm(conv_ffn(hp, *fp), ffn_post_g[layer])
        xs = xs + ms[5] * rmsnorm(conv_ffn(hs, *fp), ffn_post_g[layer])
    st_ret = jnp.stack(new_ret, axis=1)
    ck_ckv = jnp.stack(new_ckv, axis=1)
    ck_kr = jnp.stack(new_kr, axis=1)
    st_s5_re = jnp.stack(new_s5_re, axis=1)
    st_s5_im = jnp.stack(new_s5_im, axis=1)
    ck_na_k = jnp.stack(new_nak, axis=1)
    ck_na_v = jnp.stack(new_nav, axis=1)
    return (xp, xs, st_ret, ck_ckv, ck_kr, st_s5_re, st_s5_im, ck_na_k, ck_na_v)
```

```python
import math
import numpy as np
import concourse.bass as bass
import concourse.mybir as mybir
from concourse.bass_utils import run_bass_kernel_spmd

F32 = mybir.dt.float32
BF16 = mybir.dt.bfloat16
I32 = mybir.dt.int32
ALU = mybir.AluOpType
AF = mybir.ActivationFunctionType
AX = mybir.AxisListType

ENGS = ("pe", "act", "dve", "pool", "sp")
STRICT_SAME_ENGINE = True
N_DMA_SEMS = 12


def _box(ap):
    t = ap.tensor
    es = 2 if ap.dtype == BF16 else 4
    pat = [(a * es, b) for a, b in ap.ap]
    off = int(ap.offset) * es
    sp = str(ap.space)
    if "DRAM" in sp.upper() or "Dram" in sp or "dram" in sp:
        lo = off
        hi = off
        for st, cnt in pat:
            if st >= 0:
                hi += st * (cnt - 1)
            else:
                lo += st * (cnt - 1)
        return (t.name, 0, 1, lo, hi + 1)
    pstride = pat[0][0]
    npart = pat[0][1]
    if pstride == 0:
        pstride = 1 << 30
    p0 = off // pstride
    f0 = off % pstride
    lo = f0
    hi = f0
    for st, cnt in pat[1:]:
        if st >= 0:
            hi += st * (cnt - 1)
        else:
            lo += st * (cnt - 1)
    if t.name.startswith("psb"):
        return (t.name, 0, 128, 0, 2048)
    return (t.name, p0, p0 + npart, lo, hi + 1)


def _ovl(a, b):
    return a[1] < b[2] and b[1] < a[2] and a[3] < b[4] and b[3] < a[4]


def _contains(a, b):
    return a[1] <= b[1] and b[2] <= a[2] and a[3] <= b[3] and b[4] <= a[4]


class Sched:
    def __init__(self, nc):
        self.nc = nc
        self.ops = {e: [] for e in ENGS}
        self.track = {}
        self.known = {e: {} for e in ENGS}
        self.ndma = {e: 0 for e in ENGS}
        self.ncomp = {e: 0 for e in ENGS}
        self.dma_last = {e: {} for e in ENGS}
        self.notrack = set()
        self.final_events = []
        self.tag = ""
        self.tags = {e: [] for e in ENGS}

    def _deps(self, reads, writes, eng=None):
        deps = set()
        for ap in reads:
            b = _box(ap)
            if b[0] in self.notrack:
                continue
            tr = self.track.get(b[0])
            if tr is None:
                continue
            for wb, ev in tr["w"]:
                if _ovl(wb, b):
                    deps.add(ev)
            if b[0].startswith("psb"):
                for rb, evs in tr["r"].items():
                    if _ovl(rb, b):
                        deps.update(ev for e_, ev in evs.items() if e_ != eng)
        for ap in writes:
            b = _box(ap)
            tr = self.track.get(b[0])
            if tr is None:
                continue
            for wb, ev in tr["w"]:
                if _ovl(wb, b):
                    deps.add(ev)
            for rb, evs in tr["r"].items():
                if _ovl(rb, b):
                    deps.update(evs.values())
        return deps

    def _record(self, reads, writes, ev, eng):
        for ap in writes:
            b = _box(ap)
            tr = self.track.setdefault(b[0], {"w": [], "r": {}})
            tr["w"] = [(wb, e) for wb, e in tr["w"] if not _contains(b, wb)]
            tr["w"].append((b, ev))
            tr["r"] = {rb: evs for rb, evs in tr["r"].items() if not _contains(b, rb)}
        for ap in reads:
            b = _box(ap)
            if b[0] in self.notrack:
                continue
            tr = self.track.setdefault(b[0], {"w": [], "r": {}})
            tr["r"].setdefault(b, {})[eng] = ev

    def _waits(self, eng, deps, idx, is_dma=False):
        waits = {}
        for semkey, val in deps:
            if semkey == ("c", eng) and not is_dma:
                if eng == "pe":
                    continue
                if (not STRICT_SAME_ENGINE) and val < idx - 1:
                    continue
            if self.known[eng].get(semkey, 0) >= val:
                continue
            if waits.get(semkey, 0) < val:
                waits[semkey] = val
        for k, v in waits.items():
            self.known[eng][k] = v
        return list(waits.items())

    def op(self, eng, fn, reads=(), writes=()):
        reads = [r for r in reads if r is not None and not isinstance(r, (int, float))]
        idx = self.ncomp[eng]
        self.ncomp[eng] = idx + 1
        deps = self._deps(reads, writes, eng)
        waits = self._waits(eng, deps, idx)
        ev = (("c", eng), idx + 1)
        self.tags[eng].append(self.tag)
        self.ops[eng].append((fn, waits, ev, "c"))
        self._record(reads, writes, ev, eng)
        return ev

    def dma(self, eng, out, in_, **kw):
        n = self.ndma[eng]
        self.ndma[eng] = n + 1
        k = n % N_DMA_SEMS
        val = (n // N_DMA_SEMS + 1) * 16
        semkey = ("d", eng, k)
        deps = self._deps([in_], [out])
        if val > 16:
            deps.add((semkey, val - 16))
        idx = self.ncomp[eng]
        waits = self._waits(eng, deps, idx, True)
        ev = (semkey, val)

        def fn(e, out=out, in_=in_, kw=kw):
            return e.dma_start(out=out, in_=in_, **kw)

        self.ops[eng].append((fn, waits, ev, "d"))
        self._record([in_], [out], ev, eng)
        return ev

    def finish(self, final_events):
        self.final_events = list(final_events)

    def emit(self):
        nc = self.nc
        import contextlib

        with contextlib.ExitStack() as st:
            sems = {}
            for e in ENGS:
                sems[("c", e)] = st.enter_context(nc.semaphore("c_" + e))
            for e in ENGS:
                for k in range(min(N_DMA_SEMS, self.ndma[e])):
                    sems[("d", e, k)] = st.enter_context(nc.semaphore("d_%s_%d" % (e, k)))
            block = st.enter_context(nc.Block())

            waited = {e: set() for e in ENGS}
            for e in ENGS:
                for fn, waits, ev, kind in self.ops[e]:
                    for semkey, val in waits:
                        if semkey[0] == "c":
                            waited[semkey[1]].add(val)
            for semkey, val in self.final_events:
                if semkey[0] == "c":
                    waited[semkey[1]].add(val)
            rank = {e: {v: i + 1 for i, v in enumerate(sorted(waited[e]))} for e in ENGS}

            def semval(semkey, val):
                return rank[semkey[1]][val] if semkey[0] == "c" else val

            def run(engname, eobj):
                for fn, waits, ev, kind in self.ops[engname]:
                    for semkey, val in waits:
                        eobj.wait_ge(sems[semkey], semval(semkey, val))
                    ins = fn(eobj)
                    if kind == "c":
                        if ev[1] in waited[engname]:
                            ins.then_inc(sems[ev[0]], 1)
                    else:
                        ins.then_inc(sems[ev[0]], 16)
                if engname == "sp":
                    mx = {}
                    for semkey, val in self.final_events:
                        v = semval(semkey, val)
                        mx[semkey] = max(mx.get(semkey, 0), v)
                    for semkey, val in mx.items():
                        eobj.wait_ge(sems[semkey], val)

            @block.tensor
            def _(e):
                run("pe", e)

            @block.scalar
            def _(e):
                run("act", e)

            @block.vector
            def _(e):
                run("dve", e)

            @block.gpsimd
            def _(e):
                run("pool", e)

            @block.sync
            def _(e):
                run("sp", e)

import contextlib

ENABLE_EVEN = True
ENABLE_ODD = True

D = 1024
NPT = 512
LS = 2048
T = NPT + LS
DFF = 2816
NJ = DFF // 128
EPS = 1e-6
DEPTH = 4


class KB:
    def __init__(self, nc, st):
        self.nc = nc
        self.st = st
        self.S = Sched(nc)
        self.psn = 0
        self.rot = {}

    def sb(self, name, shape, dt=F32):
        return self.st.enter_context(self.nc.sbuf_tensor(name, list(shape), dt))

    def dram(self, name, shape, dt=F32, kind="Internal"):
        return self.nc.dram_tensor(name, list(shape), dt, kind=kind).ap()

    def rotbuf(self, name, shape, dt, n=2):
        key = name
        if key not in self.rot:
            self.rot[key] = [[self.sb("%s_%d" % (name, i), shape, dt) for i in range(n)], 0]
        r = self.rot[key]
        b = r[0][r[1] % len(r[0])]
        r[1] += 1
        return b

    def arena_init(self, nf):
        self.arF = self.sb("arena", [128, nf], F32)
        self.arB = self.arF.bitcast(BF16)
        self.arI = self.arF.bitcast(I32)
        self.nbytes = nf * 4
        self.ptr = 0
        self.arot = {}

    def phase(self):
        self.ptr = 0
        self.arot = {}

    def mark(self):
        return self.ptr

    def reset_to(self, m):
        self.ptr = m
        self.arot = {}

    def af(self, shape, dt=F32):
        n = int(np.prod(shape))
        es = 4 if dt == F32 else 2
        self.ptr = (self.ptr + 3) // 4 * 4
        e0 = self.ptr // es
        a = (self.arF if dt == F32 else self.arB)[:, e0:e0 + n]
        self.ptr += n * es
        assert self.ptr <= self.nbytes, ("arena overflow", self.ptr, self.nbytes)
        if len(shape) == 2:
            a = a.rearrange("p (a b) -> p a b", a=shape[0])
        elif len(shape) == 3:
            a = a.rearrange("p (a b c) -> p a b c", a=shape[0], b=shape[1])
        elif len(shape) == 4:
            a = a.rearrange("p (a b c d) -> p a b c d", a=shape[0], b=shape[1], c=shape[2])
        return a

    def arotbuf(self, name, shape, dt, n=2):
        if name not in self.arot:
            self.arot[name] = [[self.af(shape, dt) for _ in range(n)], 0]
        r = self.arot[name]
        b = r[0][r[1] % len(r[0])]
        r[1] += 1
        return b

    def psum(self):
        b = self.psb[self.psn % len(self.psb)]
        self.psn += 1
        return b

    def mm(self, out, lhsT, rhs, start=True, stop=True):
        return self.S.op("pe", lambda e: e.matmul(out, lhsT=lhsT, rhs=rhs, start=start, stop=stop),
                         reads=[lhsT, rhs], writes=[out])

    def tr(self, out, in_, ident):
        return self.S.op("pe", lambda e: e.transpose(out, in_, ident), reads=[in_, ident], writes=[out])

    def act(self, out, in_, func, bias=None, scale=1.0, accum_out=None):
        kw = {}
        rd = [in_]
        wr = [out]
        if bias is not None:
            kw["bias"] = bias
            rd.append(bias)
        if accum_out is not None:
            kw["accum_out"] = accum_out
            wr.append(accum_out)
        if not isinstance(scale, (int, float)):
            rd.append(scale)
        return self.S.op("act", lambda e: e.activation(out=out, in_=in_, func=func, scale=scale, **kw),
                         reads=rd, writes=wr)

    def tt(self, out, in0, in1, op, eng="dve"):
        return self.S.op(eng, lambda e: e.tensor_tensor(out=out, in0=in0, in1=in1, op=op),
                         reads=[in0, in1], writes=[out])

    def ts(self, out, in0, s1, s2, op0, op1=None, eng="dve", accum_out=None):
        kw = {}
        wr = [out]
        if op1 is not None:
            kw["op1"] = op1
        if accum_out is not None:
            kw["accum_out"] = accum_out
            wr.append(accum_out)
        return self.S.op(eng, lambda e: e.tensor_scalar(out=out, in0=in0, scalar1=s1, scalar2=s2, op0=op0, **kw),
                         reads=[in0, s1, s2], writes=wr)

    def stt(self, out, in0, scalar, in1, op0, op1):
        return self.S.op("dve", lambda e: e.scalar_tensor_tensor(out=out, in0=in0, scalar=scalar, in1=in1, op0=op0, op1=op1),
                         reads=[in0, scalar, in1], writes=[out])

    def copy(self, out, in_, eng="dve"):
        if eng == "act":
            return self.S.op("act", lambda e: e.copy(out=out, in_=in_), reads=[in_], writes=[out])
        return self.S.op(eng, lambda e: e.tensor_copy(out=out, in_=in_), reads=[in_], writes=[out])

    def memset(self, ap, val, eng="dve"):
        return self.S.op(eng, lambda e: e.memset(ap, val), reads=[], writes=[ap])

    def red(self, out, in_, op, axis=AX.X):
        return self.S.op("dve", lambda e: e.tensor_reduce(out=out, in_=in_, axis=axis, op=op), reads=[in_], writes=[out])

    def recip(self, out, in_):
        return self.S.op("dve", lambda e: e.reciprocal(out=out, in_=in_), reads=[in_], writes=[out])

    def dma(self, out, in_, q="sp", **kw):
        return self.S.dma(q, out, in_, **kw)


def build_program(dbg_stop=None):
    nc = bass.Bass("TRN2", target_bir_lowering=False)
    st = contextlib.ExitStack()
    with st:
        K = KB(nc, st)
        S = K.S
        IN = {}

        def inp(name, shape, dt=F32):
            IN[name] = nc.dram_tensor(name, list(shape), dt, kind="ExternalInput").ap()
            S.notrack.add(name)
            return IN[name]

        def outp(name, shape):
            return nc.dram_tensor(name, list(shape), F32, kind="ExternalOutput").ap()

        xp = inp("xp", [NPT, D])
        xs = inp("xs", [LS, D])
        cv = inp("cv", [2, D])
        ident_d = inp("ident_in", [128, 128])
        ada_w = inp("ada_w", [DEPTH, D, 6 * D])
        ada_b = inp("ada_b", [DEPTH * 48, 128])
        gvec = {}
        for nm in ("mix_pre_g", "mix_post_g", "ffn_pre_g", "ffn_post_g"):
            gvec[nm] = inp(nm, [DEPTH * 8, 128])
        ffn_w_up = inp("ffn_w_up", [DEPTH, D, 2 * DFF])
        ffn_conv_w = inp("ffn_conv_w", [DEPTH * 3 * 44, 128])
        ffn_conv_b = inp("ffn_conv_b", [DEPTH * 44, 128])
        ffn_w_down = inp("ffn_w_down", [DEPTH, DFF, D])

        even_w_in = inp("even_w_in", [2, D, 2464])
        even_w_out = inp("even_w_out", [2, D, D])
        ret_logit_b = inp("ret_logit_b", [2, 128, 8])
        ret_gn = inp("ret_gn", [8, 128])
        mla_q_norm = inp("mla_q_norm", [4, 128])
        mla_w_uq = inp("mla_w_uq", [2, 256, 768])
        mla_kv_norm = inp("mla_kv_norm", [2, 128])
        kvn_bc_d = inp("kvn_bc", [128, 256])
        mla_w_ukv = inp("mla_w_ukv", [2, 128, 1024])
        state_ret_in = inp("state_ret_in", [2, 2, 4, 128, 128])
        ckv_in = inp("ckv_in", [2, 512, 128])
        kr_in = inp("kr_in", [2, 512, 32])
        rconst_d = inp("rconst_in", [128, 772])
        ropeC_d = inp("ropeC_in", [32, 2048])
        ropeS_d = inp("ropeS_in", [32, 2048])
        odd_w_in = inp("odd_w_in", [2, D, 2048])
        odd_w_out = inp("odd_w_out", [2, D, D])
        s5p = inp("s5p", [2, 2, 128, 1104])
        s5_d = inp("s5_d", [8, 128])
        s5_glu_b = inp("s5_glu_b", [8, 128])
        s5_glu_w = inp("s5_glu_w", [2, 512, 512])
        na_rpb = inp("na_rpb", [2, 8, 15, 31])
        nak_in = inp("nak_in", [2, 512, 512])
        nav_in = inp("nav_in", [2, 512, 512])
        oh_d = inp("oh_in", [32, 64, 128])
        iota_d = inp("iota_in", [128, 2048])
        o_s5re = outp("o_s5re", [2, 2, 2, 128, 16])
        o_s5im = outp("o_s5im", [2, 2, 2, 128, 16])
        o_nak = outp("o_nak", [2, 2, 256, 512])
        o_nav = outp("o_nav", [2, 2, 256, 512])
        o_ret = outp("o_ret", [2, 2, 2, 4, 128, 128])
        o_ckv = outp("o_ckv", [2, 2, 256, 128])
        o_kr = outp("o_kr", [2, 2, 256, 32])
        finals = []
        yp = outp("yp", [NPT, D])
        ys = outp("ys", [LS, D])

        xT_d = K.dram("xT_d", [D, T])
        xT_v = xT_d.rearrange("(c p) t -> p c t", p=128)

        K.psb = [st.enter_context(nc.psum_tensor("psb%d" % i, [128, 512], F32)) for i in range(8)]

        K.arena_init(50200)
        ident = K.sb("ident", [128, 128], F32)
        K.dma(ident[:], ident_d)
        identb = K.sb("identb", [128, 128], BF16)
        K.copy(identb[:], ident[:])
        epsT = K.sb("epsT", [128, 1], F32)
        K.memset(epsT[:], EPS)
        halo_stash = K.sb("halo_stash", [128, 8, 2], BF16)
        ones1024 = K.sb("ones1024", [128, 128], BF16)
        K.memset(ones1024[:], 1.0 / 1024.0)
        ones256 = K.sb("ones256", [128, 128], BF16)
        K.memset(ones256[:], 1.0 / 256.0)
        ones128b = K.sb("ones128b", [128, 128], BF16)
        K.memset(ones128b[:], 1.0 / 128.0)
        onesb = K.sb("onesb", [128, 128], BF16)
        K.memset(onesb[:], 1.0)
        oneT = K.sb("oneT", [128, 1], F32)
        K.memset(oneT[:], 1.0)
        halfpi = K.sb("halfpi", [128, 1], F32)
        K.memset(halfpi[:], math.pi / 2.0)


        def load_T(dram2d, R, name):
            dst = K.sb(name, [128, R], F32)
            r0 = 0
            while r0 < R:
                r = min(128, R - r0)
                stg = K.rotbuf("ldT_stg", [128, 128], F32)
                K.dma(stg[0:r, :], dram2d[r0:r0 + r, :])
                pb = K.psum()
                K.tr(pb[:, 0:r], stg[0:r, :], ident[0:r, 0:r])
                K.copy(dst[:, r0:r0 + r], pb[:, 0:r])
                r0 += r
            return dst

        gT = {nm: load_T(gvec[nm], DEPTH * 8, "gT_" + nm) for nm in gvec}
        adabT = load_T(ada_b, DEPTH * 48, "adabT")
        convwT = load_T(ffn_conv_w, DEPTH * 3 * 44, "convwT")
        convbT = load_T(ffn_conv_b, DEPTH * 44, "convbT")
        nconvwT = K.sb("nconvwT", [128, DEPTH * 3 * 44], F32)
        K.ts(nconvwT[:], convwT[:], -1.0, None, ALU.mult)
        s5dT = load_T(s5_d, 8, "s5dT")
        glubT = load_T(s5_glu_b, 8, "glubT")
        gnT = load_T(ret_gn, 8, "gnT")
        qnT = load_T(mla_q_norm, 4, "qnT")
        kvnT = load_T(mla_kv_norm, 2, "kvnT")
        cT = load_T(cv.rearrange("v (c p) -> (v c) p", p=128), 16, "cT")
        scT = K.sb("scT", [128, 8, 2], BF16)
        K.act(scT[:].rearrange("p c v -> p v c"), cT[:].rearrange("p (v c) -> p v c", v=2), AF.Silu)

        def x_to_xT(src, ntile, col0):
            for i in range(ntile):
                xt = K.arotbuf("xin", [D], F32)
                K.dma(xt[:], src[i * 128:(i + 1) * 128, :])
                xo = K.arotbuf("xinT", [8, 128], F32)
                for g in range(2):
                    pb = K.psum()
                    for c in range(4):
                        K.tr(pb[:, c * 128:(c + 1) * 128], xt[:, (g * 4 + c) * 128:(g * 4 + c + 1) * 128], ident[:])
                    K.copy(xo[:, g * 4:(g + 1) * 4, :], pb[:].rearrange("p (c t) -> p c t", c=4), eng=("act" if g else "dve"))
                K.dma(xT_v[:, :, col0 + i * 128: col0 + (i + 1) * 128], xo[:])

        K.phase()
        x_to_xT(xp, NPT // 128, 0)
        x_to_xT(xs, LS // 128, NPT)

        def ada(layer):
            K.phase()
            S.tag = "L%d:ada" % layer
            modp = K.psum()
            mview = modp[:, 0:96].rearrange("p (c v) -> p c v", v=2)
            for blk in range(12):
                wb = K.arotbuf("adaw", [8, 512], BF16)
                K.dma(wb[:], ada_w[layer].rearrange("(k p) o -> p k o", p=128)[:, :, blk * 512:(blk + 1) * 512], q="pool")
                for cc in range(4):
                    ch = blk * 4 + cc
                    for k in range(8):
                        K.mm(mview[:, ch, :], wb[:, k, cc * 128:(cc + 1) * 128], scT[:, k, :], start=(k == 0), stop=(k == 7))
            mod = K.rotbuf("mod", [128, 48, 2], F32)
            for v in range(2):
                K.tt(mod[:, :, v], mview[:, :, v], adabT[:, layer * 48:(layer + 1) * 48], ALU.add)
            return mod

        def mod_vectors(layer, mod, sub):
            base = sub * 24
            pre_g = gT["mix_pre_g" if sub == 0 else "ffn_pre_g"]
            post_g = gT["mix_post_g" if sub == 0 else "ffn_post_g"]
            gs = K.rotbuf("gs", [128, 8, 2], F32)
            gg = K.rotbuf("gg", [128, 8, 2], F32)
            for v in range(2):
                K.stt(gs[:, :, v], mod[:, base + 8:base + 16, v], 1.0, pre_g[:, layer * 8:(layer + 1) * 8], ALU.add, ALU.mult)
                K.tt(gg[:, :, v], mod[:, base + 16:base + 24, v], post_g[:, layer * 8:(layer + 1) * 8], ALU.mult)
            return gs, mod[:, base:base + 8, :], gg

        def rstd_from(src3, n, ones):
            C = src3.shape[1]
            sq = K.arotbuf("sq", [8, 512], BF16, n=1)
            K.act(sq[:, 0:C, 0:n], src3, AF.Square)
            pb = K.psum()
            for c in range(C):
                K.mm(pb[:, 0:n], ones[:], sq[:, c, 0:n], start=(c == 0), stop=(c == C - 1))
            rstd = K.arotbuf("rstd", [512], F32)
            K.act(rstd[:, 0:n], pb[:, 0:n], AF.Sqrt, bias=epsT[:, 0:1])
            K.recip(rstd[:, 0:n], rstd[:, 0:n])
            return rstd

        def pre_block(t0, n, mc, gs, shift, dst):
            xb = K.arotbuf("xb", [8, 512], F32)
            if n == 1:
                K.dma(xb[:, :, 0:n], xT_v[:, :, t0:t0 + n], allow_slow_non_contiguous=True)
            else:
                K.dma(xb[:, :, 0:n], xT_v[:, :, t0:t0 + n])
            rstd = rstd_from(xb[:, :, 0:n], n, ones1024)
            for c in range(8):
                tmp = K.arotbuf("tmpn", [512], F32)
                K.stt(tmp[:, 0:n], xb[:, c, 0:n], gs[:, c, mc:mc + 1], rstd[:, 0:n], ALU.mult, ALU.mult)
                K.act(dst[:, c, :], tmp[:, 0:n], AF.Identity, bias=shift[:, c, mc:mc + 1])

        def post_block(t0, n, mc, gg, yf):
            rstd = rstd_from(yf, n, ones1024)
            xb = K.arotbuf("xb", [8, 512], F32)
            K.dma(xb[:, :, 0:n], xT_v[:, :, t0:t0 + n])
            for c in range(8):
                tmp = K.arotbuf("tmpn", [512], F32)
                K.stt(tmp[:, 0:n], yf[:, c, :], gg[:, c, mc:mc + 1], rstd[:, 0:n], ALU.mult, ALU.mult)
                K.tt(xb[:, c, 0:n], xb[:, c, 0:n], tmp[:, 0:n], ALU.add, eng="pool")
            K.dma(xT_v[:, :, t0:t0 + n], xb[:, :, 0:n])

        FFN_GROUPS = [
            (0, 1024, 0, 1, [(0, 256), (256, 512), (512, 1025)]),
            (1024, 2048, 1, 1, [(1023, 2049)]),
            (2048, 2560, 1, 0, [(2047, 2560)]),
        ]

        def ffn(layer, gs, shift, gg):
            wup_v = ffn_w_up[layer].rearrange("(k p) o -> p k o", p=128)
            wdn_v = ffn_w_down[layer].rearrange("(j p) o -> p j o", p=128)
            for (c0, c1, hl, hr, segs) in FFN_GROUPS:
                K.phase()
                S.tag = "L%d:ffn_up" % layer
                b0 = c0 - hl
                W = c1 + hr - b0
                hTg = K.af([8, 1026], BF16)
                cols = []
                if hl:
                    cols.append((b0, 1))
                for t0 in range(c0, c1, 512):
                    cols.append((t0, 512))
                if hr:
                    cols.append((c1, 1))
                for (t0, n) in cols:
                    mc = 0 if t0 < NPT else 1
                    if hl and t0 == b0:
                        K.copy(hTg[:, :, 0:1], halo_stash[:, :, 0:1], eng="pool")
                        continue
                    pre_block(t0, n, mc, gs, shift, hTg[:, :, t0 - b0:t0 - b0 + n])
                if c1 < T:
                    K.copy(halo_stash[:, :, 0:1], hTg[:, :, c1 - 1 - b0:c1 - b0], eng="pool")
                npart = (W + 511) // 512
                psz = (W + npart - 1) // npart
                parts = [(i * psz, min(psz, W - i * psz)) for i in range(npart)]
                mT = K.af([NJ, 1024], BF16)

                def load_up(j):
                    wa = K.arotbuf("wupa", [8, 128], BF16, n=2)
                    wg = K.arotbuf("wupg", [8, 128], BF16, n=2)
                    K.dma(wa[:], wup_v[:, :, j * 128:(j + 1) * 128], q="pool")
                    K.dma(wg[:], wup_v[:, :, DFF + j * 128:DFF + (j + 1) * 128], q="pool")
                    return wa, wg

                bounds = [s0_ for (s0_, s1_) in segs[1:]]
                lo = 1 + hl
                n_own = c1 - c0

                def ffn_X(j, wts):
                    wa, wg = wts
                    us, os_ = [], []
                    for wi, wmat in enumerate((wa, wg)):
                        u = K.arotbuf("u%d" % wi, [1028], F32, n=2)
                        for (p0, pn) in parts:
                            pb = K.psum()
                            for k in range(8):
                                K.mm(pb[:, 0:pn], wmat[:, k, :], hTg[:, k, p0:p0 + pn], start=(k == 0), stop=(k == 7))
                            K.copy(u[:, 1 + p0:1 + p0 + pn], pb[:, 0:pn], eng="act")
                        blk = wi * NJ + j
                        w1 = convwT[:, (layer * 3 + 1) * 44 + blk:(layer * 3 + 1) * 44 + blk + 1]
                        bb = convbT[:, layer * 44 + blk:layer * 44 + blk + 1]
                        o = K.arotbuf("o%d" % wi, [1028], F32, n=2)
                        K.act(o[:, 1:1 + W], u[:, 1:1 + W], AF.Identity, bias=bb, scale=w1)
                        us.append(u)
                        os_.append(o)
                    return us, os_

                def ffn_Y(j, us, os_):
                    for wi in range(2):
                        u, o = us[wi], os_[wi]
                        blk = wi * NJ + j
                        c0w = (layer * 3 + 0) * 44 + blk
                        c2w = (layer * 3 + 2) * 44 + blk
                        w0, w2 = convwT[:, c0w:c0w + 1], convwT[:, c2w:c2w + 1]
                        nw0, nw2 = nconvwT[:, c0w:c0w + 1], nconvwT[:, c2w:c2w + 1]
                        K.stt(o[:, 2:1 + W], u[:, 1:W], w0, o[:, 2:1 + W], ALU.mult, ALU.add)
                        K.stt(o[:, 1:W], u[:, 2:1 + W], w2, o[:, 1:W], ALU.mult, ALU.add)
                        for B in bounds:
                            iB = B - b0 + 1
                            K.stt(o[:, iB:iB + 1], u[:, iB - 1:iB], nw0, o[:, iB:iB + 1], ALU.mult, ALU.add)
                            K.stt(o[:, iB - 1:iB], u[:, iB:iB + 1], nw2, o[:, iB - 1:iB], ALU.mult, ALU.add)
                    oa, og = os_
                    sg = K.arotbuf("sg", [1024], F32, n=1)
                    K.act(sg[:, 0:n_own], og[:, lo:lo + n_own], AF.Silu)
                    K.tt(mT[:, j, 0:n_own], oa[:, lo:lo + n_own], sg[:, 0:n_own], ALU.mult)

                wts = load_up(0)
                wts_next = load_up(1)
                stX = ffn_X(0, wts)
                for j in range(NJ):
                    if j + 1 < NJ:
                        wts = wts_next
                        nxX = ffn_X(j + 1, wts)
                        if j + 2 < NJ:
                            wts_next = load_up(j + 2)
                    ffn_Y(j, *stX)
                    if j + 1 < NJ:
                        stX = nxX
                S.tag = "L%d:ffn_down" % layer
                n_own = c1 - c0
                yfa = K.af([8, 1024], F32)

                def load_dn(c):
                    wd = K.arotbuf("wdn", [NJ, 128], BF16, n=2)
                    K.dma(wd[:], wdn_v[:, :, c * 128:(c + 1) * 128], q="pool")
                    return wd

                nxt = load_dn(0)
                for c in range(8):
                    wd = nxt
                    if c + 1 < 8:
                        nxt = load_dn(c + 1)
                    for t0 in range(0, n_own, 512):
                        pb = K.psum()
                        for j in range(NJ):
                            K.mm(pb[:, :], wd[:, j, :], mT[:, j, t0:t0 + 512], start=(j == 0), stop=(j == NJ - 1))
                        K.copy(yfa[:, c, t0:t0 + 512], pb[:, :], eng="act")
                for t0 in range(0, n_own, 512):
                    mc = 0 if (c0 + t0) < NPT else 1
                    post_block(c0 + t0, 512, mc, gg, yfa[:, :, t0:t0 + 512])

        RS = 128.0 ** -0.5
        MLA_SCALE = 96.0 ** -0.5
        SEQS = [(0, 256, 0), (256, 256, 0), (512, 2048, 1)]

        PW = {"m": 2560, "n": 1152}
        NTT = {"m": 20, "n": 9}

        ATT = {"prev": None, "tb": 0}

        def attn_A(qT_ap, kbanks):
            for b, (kT_ap, bias_fn) in enumerate(kbanks):
                w = kT_ap.shape[1]
                K.mm(K.psb[b][:, 0:w], qT_ap, kT_ap)

        def attn_C(kbanks, tag):
            nb = len(kbanks)
            mx = K.arotbuf("a_mx", [8], F32)
            srcs = []
            col = 0
            for b, (kT_ap, bias_fn) in enumerate(kbanks):
                w = kT_ap.shape[1]
                src = K.psb[b][:, 0:w]
                if bias_fn is not None:
                    src = bias_fn(K.psb[b], w)
                srcs.append((src, col, w))
                K.red(mx[:, b:b + 1], src, ALU.max)
                col += w
            nm = K.arotbuf("a_nm", [1], F32)
            K.red(nm[:, 0:1], mx[:, 0:nb], ALU.max)
            K.ts(nm[:, 0:1], nm[:, 0:1], -1.0, None, ALU.mult)
            P = K.arotbuf("a_P" + tag, [PW[tag]], BF16)
            sm = K.arotbuf("a_sm", [8], F32)
            K.memset(sm[:], 0.0, eng="pool")
            for i_, (src, c0_, w) in enumerate(srcs):
                K.act(P[:, c0_:c0_ + w], src, AF.Exp, bias=nm[:, 0:1], accum_out=sm[:, i_:i_ + 1])
            rs = K.arotbuf("a_rs", [1], F32)
            K.red(rs[:, 0:1], sm[:, 0:nb], ALU.add)
            K.recip(rs[:, 0:1], rs[:, 0:1])
            engs = ["act", "act", "pool", "pool", "dve"] if nb > 3 else ["act", "pool", "dve"]
            for i_, (src, c0_, w) in enumerate(srcs):
                e_ = engs[i_]
                if e_ == "act":
                    K.act(P[:, c0_:c0_ + w], P[:, c0_:c0_ + w], AF.Identity, scale=rs[:, 0:1])
                else:
                    K.ts(P[:, c0_:c0_ + w], P[:, c0_:c0_ + w], rs[:, 0:1], None, ALU.mult, eng=e_)
            return P

        def attn_B(st_):
            P, ktiles, hs, dst, tag = st_
            PT = K.arotbuf("a_PT" + tag, [NTT[tag], 128], BF16)
            nkt = len(ktiles)
            for g in range(0, nkt, 8):
                cnt = min(8, nkt - g)
                pbT = K.psb[5 + ATT["tb"] % 2].bitcast(BF16)
                ATT["tb"] += 1
                for i_ in range(cnt):
                    pc0, n_, _ = ktiles[g + i_]
                    K.tr(pbT[0:n_, i_ * 128:(i_ + 1) * 128], P[:, pc0:pc0 + n_], identb[:])
                full = all(ktiles[g + i_][1] == 128 for i_ in range(cnt))
                ce = "dve" if ATT["tb"] % 2 == 0 else "act"
                if full:
                    K.copy(PT[:, g:g + cnt, :], pbT[:, 0:cnt * 128].rearrange("p (a b) -> p a b", a=cnt), eng=ce)
                else:
                    for i_ in range(cnt):
                        n_ = ktiles[g + i_][1]
                        K.copy(PT[0:n_, g + i_, :], pbT[0:n_, i_ * 128:(i_ + 1) * 128], eng=ce)
            po = K.psb[7][:, 0:128]
            for i_, (pc0, n_, vl) in enumerate(ktiles):
                K.mm(po, vl, PT[0:n_, i_, :], start=(i_ == 0), stop=(i_ == nkt - 1))

        def attn_D(st_):
            P, ktiles, hs, dst, tag = st_
            K.copy(dst, K.psb[7][hs, 0:128], eng="dve")

        def attn_unit(qT_ap, kbanks, ktiles, hs, dst, tag):
            prev = ATT["prev"]
            attn_A(qT_ap, kbanks)
            if prev is not None:
                attn_B(prev)
            P = attn_C(kbanks, tag)
            if prev is not None:
                attn_D(prev)
            ATT["prev"] = (P, ktiles, hs, dst, tag)

        def attn_flush():
            if ATT["prev"] is not None:
                attn_B(ATT["prev"])
                attn_D(ATT["prev"])
                ATT["prev"] = None

        def out_proj_post(w_out_l, yT, gg):
            wo = K.af([8, 1024], BF16)
            K.dma(wo[:], w_out_l.rearrange("(k p) o -> p k o", p=128), q="pool")
            for t0 in range(0, T, 512):
                yf = K.arotbuf("yf", [8, 512], F32, n=1)
                for c in range(8):
                    pb = K.psum()
                    for k in range(8):
                        K.mm(pb[:], wo[:, k, c * 128:(c + 1) * 128], yT[:, k, t0:t0 + 512], start=(k == 0), stop=(k == 7))
                    K.copy(yf[:, c, :], pb[:], eng="act")
                post_block(t0, 512, 0 if t0 < NPT else 1, gg, yf[:])

        def even_layer(layer, j, mod):
            gs, shift, gg = mod_vectors(layer, mod, 0)
            K.phase()
            hT = K.af([8, T], BF16)
            yT = K.af([8, T], BF16)
            m0 = K.mark()
            S.tag = "L%d:e_pre" % layer
            for t0 in range(0, T, 512):
                pre_block(t0, 512, 0 if t0 < NPT else 1, gs, shift, hT[:, :, t0:t0 + 512])
            K.reset_to(m0)
            S.tag = "L%d:e_ret" % layer
            w_in_v = even_w_in[j].rearrange("(k p) o -> p k o", p=128)
            rconst = K.af([772], F32)
            K.dma(rconst[:], rconst_d)
            RC = lambda i: rconst[:, i * 128:(i + 1) * 128]
            lg = K.af([8], F32)
            K.dma(lg[:], ret_logit_b[j])
            K.act(lg[:], lg[:], AF.Exp, scale=-1.0)
            K.act(lg[:], lg[:], AF.Ln, bias=oneT[:, 0:1])
            K.ts(lg[:], lg[:], -1.0, None, ALU.mult)
            qTh = K.af([T], BF16)
            kTh = K.af([T], BF16)
            sgTh = K.af([T], BF16)
            kvtok = K.af([20, 256], BF16)
            kwfT = K.af([20, 128], BF16)
            kwbT = K.af([20, 128], BF16)
            Sfs = K.af([20, 128], BF16)
            Sbs = K.af([20, 128], BF16)
            o_sb = K.af([T], F32)
            Dm = K.af([128], F32)
            e2 = K.af([128], F32)
            Wqf = K.af([128], F32)
            Wqb = K.af([128], F32)
            kw = K.af([4], F32)
            for h in range(4):
                lf = lg[:, h:h + 1]
                lb = lg[:, 4 + h:5 + h]
                K.act(Dm[:], RC(0), AF.Exp, scale=lf)
                K.tt(Dm[:], Dm[:], RC(1), ALU.mult)
                K.act(e2[:], RC(2), AF.Exp, scale=lb)
                K.tt(e2[:], e2[:], RC(3), ALU.mult)
                K.tt(Dm[:], Dm[:], e2[:], ALU.add)
                K.act(Wqf[:], RC(4), AF.Exp, scale=lf)
                K.act(Wqb[:], RC(5), AF.Exp, scale=lb)
                K.act(kw[:, 0:1], rconst[:, 768:769], AF.Exp, scale=lf)
                K.act(kw[:, 1:2], rconst[:, 769:770], AF.Exp, scale=lb)
                K.act(kw[:, 2:3], rconst[:, 770:771], AF.Exp, scale=lf)
                K.act(kw[:, 3:4], rconst[:, 770:771], AF.Exp, scale=lb)
                wh = K.arotbuf("wh", [8, 4, 128], BF16, n=2)
                for si_ in range(4):
                    K.dma(wh[:, :, si_, :], w_in_v[:, :, si_ * 512 + h * 128:si_ * 512 + (h + 1) * 128], q="pool")
                for t0 in range(0, T, 512):
                    for si, dst in ((0, qTh), (1, kTh), (3, sgTh)):
                        pb = K.psum()
                        for k in range(8):
                            K.mm(pb[:], wh[:, k, si, :], hT[:, k, t0:t0 + 512], start=(k == 0), stop=(k == 7))
                        if si == 0:
                            K.act(dst[:, t0:t0 + 512], pb[:], AF.Identity, scale=RS)
                        elif si == 1:
                            K.copy(dst[:, t0:t0 + 512], pb[:], eng="dve")
                        else:
                            K.act(dst[:, t0:t0 + 512], pb[:], AF.Silu)
                for ti in range(20):
                    pb = K.psum()
                    for k in range(8):
                        K.mm(pb[:, 0:256].rearrange("p (a b) -> p a b", a=2), hT[:, k, ti * 128:(ti + 1) * 128], wh[:, k, 1:3, :], start=(k == 0), stop=(k == 7))
                    K.copy(kvtok[:, ti, :], pb[:, 0:256], eng="act")
                    K.ts(kwfT[:, ti, :], pb[:, 0:128], kw[:, 0:1], None, ALU.mult)
                    K.ts(kwbT[:, ti, :], pb[:, 0:128], kw[:, 1:2], None, ALU.mult)
                for (c0, L, smp) in SEQS:
                    nch = L // 128
                    tb = c0 // 128
                    for d in range(2):
                        Sx = K.arotbuf("Sx", [128], F32, n=2)
                        if smp:
                            K.dma(Sx[:], state_ret_in[j, d, h])
                        else:
                            K.memset(Sx[:], 0.0, eng="pool")
                        store = Sfs if d == 0 else Sbs
                        kwT = kwfT if d == 0 else kwbT
                        order = range(nch) if d == 0 else range(nch - 1, -1, -1)
                        for n in order:
                            ti = tb + n
                            K.copy(store[:, ti, :], Sx[:], eng="pool")
                            pb = K.psum()
                            K.mm(pb[:, 0:128], kwT[:, ti, :], kvtok[:, ti, 128:256])
                            K.stt(Sx[:], Sx[:], kw[:, 2 + d:3 + d], pb[:, 0:128], ALU.mult, ALU.add)
                        if not smp:
                            finals.append(K.dma(o_ret[c0 // 256, j, d, h], Sx[:]))
                    for n0 in range(0, nch, 4):
                        pbo = K.psum()
                        cnt = min(4, nch - n0)
                        for n in range(n0, n0 + cnt):
                            ti = tb + n
                            cs = slice(ti * 128, (ti + 1) * 128)
                            pbs = K.psum()
                            K.mm(pbs[:, 0:128], kTh[:, cs], qTh[:, cs])
                            MT = K.arotbuf("MT", [128], BF16, n=2)
                            K.tt(MT[:], pbs[:, 0:128], Dm[:], ALU.mult)
                            qf = K.arotbuf("qf", [128], BF16, n=2)
                            qb = K.arotbuf("qb", [128], BF16, n=2)
                            K.tt(qf[:], qTh[:, cs], Wqf[:], ALU.mult, eng="pool")
                            K.tt(qb[:], qTh[:, cs], Wqb[:], ALU.mult, eng="pool")
                            oc = pbo[:, (n - n0) * 128:(n - n0 + 1) * 128]
                            K.mm(oc, kvtok[:, ti, 128:256], MT[:], start=True, stop=False)
                            K.mm(oc, Sfs[:, ti, :], qf[:], start=False, stop=False)
                            K.mm(oc, Sbs[:, ti, :], qb[:], start=False, stop=True)
                        K.copy(o_sb[:, (tb + n0) * 128:(tb + n0 + cnt) * 128], pbo[:, 0:cnt * 128], eng="act")
                for t0 in range(0, T, 512):
                    ob = o_sb[:, t0:t0 + 512]
                    o2 = K.arotbuf("gn_o2", [512], F32, n=1)
                    K.tt(o2[:], ob, ob, ALU.mult)
                    pm = K.psum()
                    pq = K.psum()
                    for src_, pdst in ((ob, pm), (o2[:], pq)):
                        hi = K.arotbuf("gn_hi", [512], BF16, n=2)
                        lo = K.arotbuf("gn_lo", [512], BF16, n=2)
                        K.copy(hi[:], src_, eng="act")
                        K.tt(lo[:], src_, hi[:], ALU.subtract)
                        K.mm(pdst[:], ones128b[:], hi[:], start=True, stop=False)
                        K.mm(pdst[:], ones128b[:], lo[:], start=False, stop=True)
                    mean = K.arotbuf("gn_mean", [512], F32, n=1)
                    K.copy(mean[:], pm[:], eng="act")
                    msq = K.arotbuf("gn_msq", [512], F32, n=1)
                    K.tt(msq[:], mean[:], mean[:], ALU.mult, eng="pool")
                    K.tt(msq[:], pq[:], msq[:], ALU.subtract)
                    K.act(msq[:], msq[:], AF.Sqrt, bias=epsT[:, 0:1])
                    K.recip(msq[:], msq[:])
                    K.tt(o2[:], ob, mean[:], ALU.subtract, eng="pool")
                    K.tt(o2[:], o2[:], msq[:], ALU.mult)
                    K.stt(yT[:, h, t0:t0 + 512], o2[:], gnT[:, j * 4 + h:j * 4 + h + 1], sgTh[:, t0:t0 + 512], ALU.mult, ALU.mult)
            K.reset_to(m0)
            S.tag = "L%d:e_mlaproj" % layer
            cqn = K.af([2, T], BF16)
            ckvK = K.af([3072], BF16)
            krK = K.af([3072], BF16)
            ropeC = K.af([2048], F32)
            ropeS = K.af([2048], F32)
            K.dma(ropeC[64:96, :], ropeC_d)
            K.dma(ropeS[64:96, :], ropeS_d)
            m1 = K.mark()
            kvn_bc = K.af([256], F32)
            K.dma(kvn_bc[:], kvn_bc_d)
            wm = K.af([8, 416], BF16)
            K.dma(wm[:], w_in_v[:, :, 2048:2464], q="pool")
            wkr = K.af([8, 96], BF16)
            wkrs = K.af([8, 96], BF16)
            K.memset(wkr[:], 0.0, eng="pool")
            K.memset(wkrs[:], 0.0, eng="pool")
            K.dma(wkr[:, :, 64:96], w_in_v[:, :, 2432:2464], q="pool")
            K.dma(wkrs[:, :, 64:80], w_in_v[:, :, 2448:2464], q="pool")
            K.dma(wkrs[:, :, 80:96], w_in_v[:, :, 2432:2448], q="pool")
            for t0 in range(0, T, 512):
                bl = slice(t0, t0 + 512)
                pbs = [K.psum(), K.psum()]
                for cc in range(2):
                    for k in range(8):
                        K.mm(pbs[cc][:], wm[:, k, cc * 128:(cc + 1) * 128], hT[:, k, bl], start=(k == 0), stop=(k == 7))
                sq = K.arotbuf("m_sq", [2, 512], BF16, n=1)
                for cc in range(2):
                    K.act(sq[:, cc, :], pbs[cc][:], AF.Square)
                pr = K.psum()
                for cc in range(2):
                    K.mm(pr[:], ones256[:], sq[:, cc, :], start=(cc == 0), stop=(cc == 1))
                rstd = K.arotbuf("m_rstd", [512], F32, n=1)
                K.act(rstd[:], pr[:], AF.Sqrt, bias=epsT[:, 0:1])
                K.recip(rstd[:], rstd[:])
                for cc in range(2):
                    K.stt(cqn[:, cc, bl], pbs[cc][:], qnT[:, j * 2 + cc:j * 2 + cc + 1], rstd[:], ALU.mult, ALU.mult)
                pc = K.psum()
                for k in range(8):
                    K.mm(pc[:], wm[:, k, 256:384], hT[:, k, bl], start=(k == 0), stop=(k == 7))
                K.act(sq[:, 0, :], pc[:], AF.Square)
                pr2 = K.psum()
                K.mm(pr2[:], ones128b[:], sq[:, 0, :])
                rstd2 = K.arotbuf("m_rstd2", [512], F32, n=1)
                K.act(rstd2[:], pr2[:], AF.Sqrt, bias=epsT[:, 0:1])
                K.recip(rstd2[:], rstd2[:])
                K.stt(ckvK[:, bl], pc[:], kvnT[:, j:j + 1], rstd2[:], ALU.mult, ALU.mult)
                pk = K.psum()
                for k in range(8):
                    K.mm(pk[0:96, :], wkr[:, k, :], hT[:, k, bl], start=(k == 0), stop=(k == 7))
                if t0 < NPT:
                    K.copy(krK[64:96, bl], pk[64:96, :], eng="dve")
                else:
                    pks = K.psum()
                    for k in range(8):
                        K.mm(pks[0:96, :], wkrs[:, k, :], hT[:, k, bl], start=(k == 0), stop=(k == 7))
                    st0 = t0 - NPT
                    t1 = K.arotbuf("m_t1", [512], F32, n=1)
                    t2 = K.arotbuf("m_t2", [512], F32, n=1)
                    K.tt(t1[64:96, :], pk[64:96, :], ropeC[64:96, st0:st0 + 512], ALU.mult)
                    K.tt(t2[64:96, :], pks[64:96, :], ropeS[64:96, st0:st0 + 512], ALU.mult)
                    K.tt(krK[64:96, bl], t1[64:96, :], t2[64:96, :], ALU.add, eng="pool")
            for i in range(4):
                stg = K.arotbuf("m_stg", [128], F32, n=2)
                K.dma(stg[:], ckv_in[j, i * 128:(i + 1) * 128, :])
                pb = K.psum()
                K.tr(pb[:, 0:128], stg[:], ident[:])
                K.copy(ckvK[:, 2560 + i * 128:2560 + (i + 1) * 128], pb[:, 0:128], eng="dve")
                stg2 = K.arotbuf("m_stg2", [96], F32, n=2)
                K.memset(stg2[:, 0:64], 0.0, eng="pool")
                K.dma(stg2[:, 64:96], kr_in[j, i * 128:(i + 1) * 128, :])
                pb2 = K.psum()
                K.tr(pb2[0:96, 0:128], stg2[:], ident[:])
                K.copy(krK[64:96, 2560 + i * 128:2560 + (i + 1) * 128], pb2[64:96, 0:128], eng="dve")
            for ti in range(4):
                pt = K.psum()
                for k in range(8):
                    K.mm(pt[:, 0:160], hT[:, k, ti * 128:(ti + 1) * 128], wm[:, k, 256:416], start=(k == 0), stop=(k == 7))
                junk = K.arotbuf("m_junk", [128], F32, n=1)
                ssq = K.arotbuf("m_ssq", [1], F32, n=2)
                K.memset(ssq[:], 0.0, eng="pool")
                K.act(junk[:], pt[:, 0:128], AF.Square, accum_out=ssq[:, 0:1])
                K.act(ssq[:], ssq[:], AF.Sqrt, bias=epsT[:, 0:1], scale=1.0 / 128.0)
                K.recip(ssq[:], ssq[:])
                ock = K.arotbuf("m_ock", [128], F32, n=2)
                K.stt(ock[:], pt[:, 0:128], ssq[:, 0:1], kvn_bc[:, j * 128:(j + 1) * 128], ALU.mult, ALU.mult)
                sq_i, r0 = ti // 2, (ti % 2) * 128
                finals.append(K.dma(o_ckv[sq_i, j, r0:r0 + 128, :], ock[:]))
                okr = K.arotbuf("m_okr", [32], F32, n=2)
                K.copy(okr[:], pt[:, 128:160], eng="act")
                finals.append(K.dma(o_kr[sq_i, j, r0:r0 + 128, :], okr[:]))
            K.reset_to(m1)
            S.tag = "L%d:e_mlaattn" % layer
            wuq = K.af([2, 768], BF16)
            K.dma(wuq[:], mla_w_uq[j].rearrange("(k p) o -> p k o", p=128), q="pool")
            wuqs = K.af([2, 8, 96], BF16)
            K.memset(wuqs[:], 0.0, eng="pool")
            uqv = mla_w_uq[j].rearrange("(k p) (h x) -> p k h x", p=128, h=8)
            for k_ in range(2):
                K.dma(wuqs[:, k_, :, 64:80], uqv[:, k_, :, 80:96], q="pool")
                K.dma(wuqs[:, k_, :, 80:96], uqv[:, k_, :, 64:80], q="pool")
            wukv = K.af([1024], BF16)
            K.dma(wukv[:], mla_w_ukv[j], q="pool")
            wukv_v = wukv[:].rearrange("p (h x) -> p h x", h=8)
            v_all = K.af([20, 512], BF16)
            for (c0, L, smp) in SEQS:
                attn_flush()
                nk = L + (512 if smp else 0)
                nkt = nk // 128
                for kt in range(nkt):
                    pb = K.psum()
                    K.mm(pb[:].rearrange("p (h x) -> p h x", h=8), ckvK[:, c0 + kt * 128:c0 + (kt + 1) * 128], wukv_v[:, :, 64:128])
                    K.copy(v_all[:, kt, :], pb[:], eng="act")
                for h in range(8):
                    kh = K.arotbuf("kh", [2560], BF16, n=1)
                    for kb in range(0, nk, 512):
                        w = min(512, nk - kb)
                        pb = K.psum()
                        K.mm(pb[0:64, 0:w], wukv[:, h * 128:h * 128 + 64], ckvK[:, c0 + kb:c0 + kb + w])
                        K.copy(kh[0:64, kb:kb + w], pb[0:64, 0:w], eng="dve")
                    K.copy(kh[64:96, 0:nk], krK[64:96, c0:c0 + nk], eng="pool")
                    qh = K.arotbuf("qh", [2048], BF16, n=1)
                    for qb0 in range(0, L, 512):
                        w = min(512, L - qb0)
                        pb = K.psum()
                        for k in range(2):
                            K.mm(pb[0:96, 0:w], wuq[:, k, h * 96:(h + 1) * 96], cqn[:, k, c0 + qb0:c0 + qb0 + w], start=(k == 0), stop=(k == 1))
                        if smp:
                            pb2 = K.psum()
                            for k in range(2):
                                K.mm(pb2[0:96, 0:w], wuqs[:, k, h, :], cqn[:, k, c0 + qb0:c0 + qb0 + w], start=(k == 0), stop=(k == 1))
                            t1 = K.arotbuf("q_t1", [512], F32, n=1)
                            t2 = K.arotbuf("q_t2", [512], F32, n=1)
                            K.tt(t1[64:96, 0:w], pb[64:96, 0:w], ropeC[64:96, qb0:qb0 + w], ALU.mult)
                            K.tt(t2[64:96, 0:w], pb2[64:96, 0:w], ropeS[64:96, qb0:qb0 + w], ALU.mult)
                            K.tt(t1[64:96, 0:w], t1[64:96, 0:w], t2[64:96, 0:w], ALU.add, eng="pool")
                            K.act(qh[0:64, qb0:qb0 + w], pb[0:64, 0:w], AF.Identity, scale=MLA_SCALE)
                            K.act(qh[64:96, qb0:qb0 + w], t1[64:96, 0:w], AF.Identity, scale=MLA_SCALE)
                        else:
                            K.act(qh[0:96, qb0:qb0 + w], pb[0:96, 0:w], AF.Identity, scale=MLA_SCALE)
                    half = h % 2
                    hs = slice(half * 64, half * 64 + 64)
                    kbanks = [(kh[0:96, kb:min(kb + 512, nk)], None) for kb in range(0, nk, 512)]
                    ktiles = [(kt * 128, 128, v_all[:, kt, (h // 2) * 128:(h // 2) * 128 + 128]) for kt in range(nkt)]
                    for qt in range(L // 128):
                        attn_unit(qh[0:96, qt * 128:(qt + 1) * 128], kbanks, ktiles,
                                  hs, yT[hs, 4 + h // 2, c0 + qt * 128:c0 + (qt + 1) * 128], "m")
            attn_flush()
            K.reset_to(m0)
            S.tag = "L%d:e_out" % layer
            out_proj_post(even_w_out[j], yT, gg)

        NA_SCALE = 0.125
        NEGV = -30000.0
        TWO_PI = 2.0 * math.pi
        CW1 = 6.28125
        CW2 = TWO_PI - 6.28125
        SIN_S = 1.0 - 2e-6
        GELU_C = 1.5957691216057308

        def sincos(ang, kiI, kf, sin_out, cos_out):
            K.ts(kiI, ang, 1.0 / TWO_PI, None, ALU.mult)
            K.copy(kf, kiI)
            K.stt(ang, kf, -CW1, ang, ALU.mult, ALU.add)
            K.stt(ang, kf, -CW2, ang, ALU.mult, ALU.add)
            K.ts(ang, ang, -3.1415925, 3.1415925, ALU.max, ALU.min)
            K.act(sin_out, ang, AF.Sin, scale=SIN_S)
            K.act(kf, ang, AF.Abs)
            K.act(cos_out, kf, AF.Sin, scale=-SIN_S, bias=halfpi[:, 0:1])

        def scan(out, d0, d1, init):
            rd = [d0, d1] + ([] if isinstance(init, float) else [init])
            return S.op("dve", lambda e: e.tensor_tensor_scan(out=out, data0=d0, data1=d1, initial=init, op0=ALU.mult, op1=ALU.add),
                        reads=rd, writes=[out])

        def odd_layer(layer, j, mod):
            gs, shift, gg = mod_vectors(layer, mod, 0)
            K.phase()
            yT = K.af([8, T], BF16)
            m0 = K.mark()
            w_in_v = odd_w_in[j].rearrange("(k p) o -> p k o", p=128)

            def make_hT():
                hT = K.af([8, T], BF16)
                mk = K.mark()
                for t0 in range(0, T, 512):
                    pre_block(t0, 512, 0 if t0 < NPT else 1, gs, shift, hT[:, :, t0:t0 + 512])
                K.reset_to(mk)
                return hT

            S.tag = "L%d:o_s5" % layer
            uT = K.af([4, T], BF16)
            mS = K.mark()
            hT = make_hT()
            wu = K.af([8, 512], BF16)
            K.dma(wu[:], w_in_v[:, :, 0:512], q="pool")
            for t0 in range(0, T, 512):
                for c in range(4):
                    pb = K.psum()
                    for k in range(8):
                        K.mm(pb[:], wu[:, k, c * 128:(c + 1) * 128], hT[:, k, t0:t0 + 512], start=(k == 0), stop=(k == 7))
                    K.copy(uT[:, c, t0:t0 + 512], pb[:], eng=("act" if c % 2 else "dve"))
            K.reset_to(mS)
            ygT = K.af([4, T], BF16)
            iotaF = K.af([2048], F32)
            K.dma(iotaF[:], iota_d)
            mT = K.mark()
            y_acc = K.af([T], F32)
            e_tb = K.ptr // 4
            tb = K.af([3, 2048], F32)
            ang = tb[:, 0, :]
            kiI = K.arI[:, e_tb + 2048:e_tb + 4096]
            kf = tb[:, 2, :]
            gbuf = [tb[:, 1, :], tb[:, 2, :]]
            sinT = K.af([2048], F32)
            cosT = K.af([2048], F32)
            rtile = tb[:, 0, :]
            bp = K.af([2, 2048], F32)
            Hb = K.af([2, 2048], BF16)
            prm = [K.af([1104], F32) for _ in range(2)]
            der = K.af([2, 14, 16], F32)
            bbt = K.af([2, 2, 16, 16], F32)
            cng = K.af([2, 16, 16], F32)
            tmp3 = K.af([16, 16], F32)
            hfin = K.af([2, 2, 2, 16], F32)
            e_s = K.ptr // 4
            sml = K.af([3, 16], F32)
            smlI = K.arI[:, e_s + 16:e_s + 32]
            for d in range(2):
                P = prm[d]
                K.dma(P[:], s5p[j, d])
                lre, lim, lst = P[:, 0:16], P[:, 16:32], P[:, 32:48]
                bre3 = P[:, 80:336].rearrange("p (s c) -> p s c", s=16)
                bim3 = P[:, 336:592].rearrange("p (s c) -> p s c", s=16)
                cim3 = P[:, 848:1104].rearrange("p (s c) -> p s c", s=16)
                Dv = lambda i, d=d: der[:, d, i, :]
                K.act(Dv(0), lst, AF.Exp)
                K.tt(Dv(1), lre, Dv(0), ALU.mult)
                K.act(Dv(1), Dv(1), AF.Exp)
                K.tt(Dv(2), lim, Dv(0), ALU.mult)
                K.copy(sml[:, 0, :], Dv(2))
                sincos(sml[:, 0, :], smlI, sml[:, 2, :], Dv(3), Dv(4))
                K.tt(Dv(5), Dv(1), Dv(4), ALU.mult)
                K.tt(Dv(6), Dv(1), Dv(3), ALU.mult)
                K.ts(Dv(5), Dv(5), -1.0, None, ALU.add)
                K.tt(Dv(7), lre, lre, ALU.mult)
                K.tt(Dv(8), lim, lim, ALU.mult)
                K.tt(Dv(7), Dv(7), Dv(8), ALU.add)
                K.recip(Dv(7), Dv(7))
                K.tt(Dv(8), Dv(5), lre, ALU.mult)
                K.tt(Dv(9), Dv(6), lim, ALU.mult)
                K.tt(Dv(8), Dv(8), Dv(9), ALU.add)
                K.tt(Dv(8), Dv(8), Dv(7), ALU.mult)
                K.tt(Dv(9), Dv(6), lre, ALU.mult)
                K.tt(Dv(10), Dv(5), lim, ALU.mult)
                K.tt(Dv(9), Dv(9), Dv(10), ALU.subtract)
                K.tt(Dv(9), Dv(9), Dv(7), ALU.mult)
                K.ts(Dv(13), Dv(2), 1.0 / TWO_PI, None, ALU.mult)
                K.ts(Dv(11), Dv(13), -1.0, None, ALU.mult)
                K.ts(Dv(12), Dv(13), 2049.0, None, ALU.mult)
                zre_b = Dv(8).unsqueeze(2).to_broadcast([128, 16, 16])
                zim_b = Dv(9).unsqueeze(2).to_broadcast([128, 16, 16])
                K.tt(bbt[:, d, 0], bre3, zre_b, ALU.mult)
                K.tt(tmp3[:], bim3, zim_b, ALU.mult)
                K.tt(bbt[:, d, 0], bbt[:, d, 0], tmp3[:], ALU.subtract)
                K.tt(bbt[:, d, 1], bim3, zre_b, ALU.mult)
                K.tt(tmp3[:], bre3, zim_b, ALU.mult)
                K.tt(bbt[:, d, 1], bbt[:, d, 1], tmp3[:], ALU.add)
                K.ts(cng[:, d], cim3, -1.0, None, ALU.mult)
            for c in range(4):
                first = True
                for d in range(2):
                    P = prm[d]
                    cre3 = P[:, 592:848].rearrange("p (s c) -> p s c", s=16)
                    for s_ in range(4 * c, 4 * c + 4):
                        base = (s_ % 4) * 32
                        if d == 0:
                            K.ts(ang, iotaF[:], der[:, d, 13, s_:s_ + 1], None, ALU.mult, eng="pool")
                        else:
                            K.ts(ang, iotaF[:], der[:, d, 11, s_:s_ + 1], der[:, d, 12, s_:s_ + 1], ALU.mult, ALU.add, eng="pool")
                        K.ts(kiI, ang, 1.0, None, ALU.mult)
                        K.copy(kf, kiI)
                        K.tt(ang, ang, kf, ALU.subtract)
                        K.act(sinT[:], ang, AF.Sin, scale=TWO_PI * SIN_S)
                        K.act(kf, ang, AF.Abs)
                        K.act(cosT[:], kf, AF.Sin, scale=-TWO_PI * SIN_S, bias=halfpi[:, 0:1])
                        K.ts(rtile[:], iotaF[:], 0.0, der[:, d, 1, s_:s_ + 1], ALU.mult, ALU.add, eng="pool")
                        BBT = []
                        for ri in range(2):
                            Wp = K.arotbuf("s5_Wp", [128], F32, n=2)
                            K.memset(Wp[:], 0.0, eng="pool")
                            K.copy(Wp[0:64, base:base + 16], bbt[0:64, d, ri, s_, :], eng="pool")
                            K.copy(Wp[64:128, base + 16:base + 32], bbt[64:128, d, ri, s_, :], eng="pool")
                            pb = K.psum()
                            K.tr(pb[:, 0:128], Wp[:], ident[:])
                            bt = K.arotbuf("s5_BBT%d" % ri, [128], BF16, n=2)
                            K.copy(bt[:], pb[:, 0:128], eng="act")
                            BBT.append(bt)
                        CP = []
                        for ri, csrc in ((0, cre3), (1, cng[:, d])):
                            cp = K.arotbuf("s5_Cp%d" % ri, [128], BF16, n=2)
                            K.memset(cp[:], 0.0, eng="pool")
                            K.copy(cp[0:64, base:base + 16], csrc[0:64, s_, :], eng="pool")
                            K.copy(cp[64:128, base + 16:base + 32], csrc[64:128, s_, :], eng="pool")
                            CP.append(cp)
                        for seg in range(2):
                            if seg == 0:
                                c0, L, nseq = 0, 256, 2
                            else:
                                c0, L, nseq = 512, 2048, 1
                            tab0 = 0 if d == 0 else 2048 - L
                            Wt = L * nseq
                            if seg == 0:
                                v3 = lambda ap: ap.rearrange("p (q t) -> p q t", q=2)
                                tabv = lambda tb_, b0, w: tb_[:, tab0:tab0 + 256].unsqueeze(1).to_broadcast([128, 2, 256])
                            else:
                                v3 = lambda ap: ap
                                tabv = lambda tb_, b0, w: tb_[:, tab0 + b0:tab0 + b0 + w]
                            for b0 in range(0, Wt, 512):
                                w = min(512, Wt - b0)
                                pr = K.psum()
                                pi_ = K.psum()
                                K.mm(pr[:, 0:w], BBT[0][:], uT[:, c, c0 + b0:c0 + b0 + w])
                                K.mm(pi_[:, 0:w], BBT[1][:], uT[:, c, c0 + b0:c0 + b0 + w])
                                cosv = tabv(cosT, b0, w)
                                sinv = tabv(sinT, b0, w)
                                t1 = K.arotbuf("s5_t1", [512], F32, n=1)
                                t2 = K.arotbuf("s5_t2", [512], F32, n=1)
                                t3 = K.arotbuf("s5_t3", [512], F32, n=1)
                                t4 = K.arotbuf("s5_t4", [512], F32, n=1)
                                K.tt(v3(t1[:, 0:w]), v3(pr[:, 0:w]), cosv, ALU.mult)
                                K.tt(v3(t2[:, 0:w]), v3(pi_[:, 0:w]), sinv, ALU.mult)
                                K.tt(bp[:, 0, b0:b0 + w], t1[:, 0:w], t2[:, 0:w], ALU.add, eng="pool")
                                K.tt(v3(t3[:, 0:w]), v3(pi_[:, 0:w]), cosv, ALU.mult)
                                K.tt(v3(t4[:, 0:w]), v3(pr[:, 0:w]), sinv, ALU.mult)
                                K.tt(bp[:, 1, b0:b0 + w], t3[:, 0:w], t4[:, 0:w], ALU.subtract, eng="pool")
                            for q_ in range(nseq):
                                qs = slice(q_ * L, (q_ + 1) * L)
                                for ri in range(2):
                                    init = P[:, 48 + ri * 16 + s_:48 + ri * 16 + s_ + 1] if seg == 1 else 0.0
                                    if d == 0:
                                        scan(gbuf[ri][:, qs], rtile[:, 0:L], bp[:, ri, qs], init)
                                    else:
                                        scan(gbuf[ri][:, qs][:, ::-1], rtile[:, 0:L][:, ::-1], bp[:, ri, qs][:, ::-1], init)
                            for b0 in range(0, Wt, 512):
                                w = min(512, Wt - b0)
                                cosv = tabv(cosT, b0, w)
                                sinv = tabv(sinT, b0, w)
                                t1 = K.arotbuf("s5_t1", [512], F32, n=1)
                                t2 = K.arotbuf("s5_t2", [512], F32, n=1)
                                t3 = K.arotbuf("s5_t3", [512], F32, n=1)
                                t4 = K.arotbuf("s5_t4", [512], F32, n=1)
                                K.tt(v3(t1[:, 0:w]), v3(gbuf[0][:, b0:b0 + w]), cosv, ALU.mult, eng="pool")
                                K.tt(v3(t2[:, 0:w]), v3(gbuf[1][:, b0:b0 + w]), sinv, ALU.mult, eng="pool")
                                K.tt(Hb[:, 0, b0:b0 + w], t1[:, 0:w], t2[:, 0:w], ALU.subtract)
                                K.tt(v3(t3[:, 0:w]), v3(gbuf[0][:, b0:b0 + w]), sinv, ALU.mult)
                                K.tt(v3(t4[:, 0:w]), v3(gbuf[1][:, b0:b0 + w]), cosv, ALU.mult)
                                K.tt(Hb[:, 1, b0:b0 + w], t3[:, 0:w], t4[:, 0:w], ALU.add)
                                py = K.psum()
                                K.mm(py[:, 0:w], CP[0][:], Hb[:, 0, b0:b0 + w], start=True, stop=False)
                                K.mm(py[:, 0:w], CP[1][:], Hb[:, 1, b0:b0 + w], start=False, stop=True)
                                ya = y_acc[:, c0 + b0:c0 + b0 + w]
                                if first:
                                    K.copy(ya, py[:, 0:w], eng="act")
                                else:
                                    K.tt(ya, ya, py[:, 0:w], ALU.add)
                            if seg == 0:
                                col = L - 1 if d == 0 else 0
                                tc_ = tab0 + col
                                g0c = gbuf[0][:, 0:512].rearrange("p (q t) -> p q t", q=2)[:, :, col]
                                g1c = gbuf[1][:, 0:512].rearrange("p (q t) -> p q t", q=2)[:, :, col]
                                cc_ = cosT[:, tc_:tc_ + 1].to_broadcast([128, 2])
                                sc_ = sinT[:, tc_:tc_ + 1].to_broadcast([128, 2])
                                f1 = K.arotbuf("s5_f1", [2], F32, n=2)
                                f2 = K.arotbuf("s5_f2", [2], F32, n=2)
                                K.tt(f1[:], g0c, cc_, ALU.mult, eng="pool")
                                K.tt(f2[:], g1c, sc_, ALU.mult, eng="pool")
                                K.tt(hfin[:, :, d, 0, s_], f1[:], f2[:], ALU.subtract, eng="pool")
                                f1 = K.arotbuf("s5_f1", [2], F32, n=2)
                                f2 = K.arotbuf("s5_f2", [2], F32, n=2)
                                K.tt(f1[:], g0c, sc_, ALU.mult, eng="pool")
                                K.tt(f2[:], g1c, cc_, ALU.mult, eng="pool")
                                K.tt(hfin[:, :, d, 1, s_], f1[:], f2[:], ALU.add, eng="pool")
                        first = False
                for t0 in range(0, T, 512):
                    yb = y_acc[:, t0:t0 + 512]
                    K.stt(yb, uT[:, c, t0:t0 + 512], s5dT[:, j * 4 + c:j * 4 + c + 1], yb, ALU.mult, ALU.add)
                    y2 = K.arotbuf("s5_y2", [512], F32, n=1)
                    K.tt(y2[:], yb, yb, ALU.mult, eng="pool")
                    K.ts(y2[:], y2[:], 0.044715, 1.0, ALU.mult, ALU.add)
                    K.tt(y2[:], y2[:], yb, ALU.mult, eng="pool")
                    K.act(y2[:], y2[:], AF.Sigmoid, scale=GELU_C)
                    K.tt(ygT[:, c, t0:t0 + 512], yb, y2[:], ALU.mult)
            for si in range(2):
                for d in range(2):
                    finals.append(K.dma(o_s5re[si, j, d], hfin[:, si, d, 0, :]))
                    finals.append(K.dma(o_s5im[si, j, d], hfin[:, si, d, 1, :]))
            S.tag = "L%d:o_glu" % layer
            K.reset_to(mT)
            wg = K.af([4, 512], BF16)
            K.dma(wg[:], s5_glu_w[j].rearrange("(k p) o -> p k o", p=128), q="pool")
            for t0 in range(0, T, 512):
                for c2 in range(4):
                    pb = K.psum()
                    for k in range(4):
                        K.mm(pb[:], wg[:, k, c2 * 128:(c2 + 1) * 128], ygT[:, k, t0:t0 + 512], start=(k == 0), stop=(k == 3))
                    sg = K.arotbuf("s5_sg", [512], F32, n=2)
                    K.act(sg[:], pb[:], AF.Sigmoid, bias=glubT[:, j * 4 + c2:j * 4 + c2 + 1])
                    K.tt(yT[:, c2, t0:t0 + 512], ygT[:, c2, t0:t0 + 512], sg[:], ALU.mult)

            K.reset_to(m0)
            S.tag = "L%d:o_naproj" % layer
            qT = K.af([4, T], BF16)
            kT = K.af([4, 3072], BF16)
            vtok = K.af([24, 512], BF16)
            mN = K.mark()
            hT = make_hT()
            wq = K.af([8, 512], BF16)
            wk = K.af([8, 512], BF16)
            wv = K.af([8, 512], BF16)
            K.dma(wq[:], w_in_v[:, :, 512:1024], q="pool")
            K.dma(wk[:], w_in_v[:, :, 1024:1536], q="pool")
            K.dma(wv[:], w_in_v[:, :, 1536:2048], q="pool")
            for t0 in range(0, T, 512):
                for c in range(4):
                    pb = K.psum()
                    for k in range(8):
                        K.mm(pb[:], wq[:, k, c * 128:(c + 1) * 128], hT[:, k, t0:t0 + 512], start=(k == 0), stop=(k == 7))
                    K.act(qT[:, c, t0:t0 + 512], pb[:], AF.Identity, scale=NA_SCALE)
                    pb = K.psum()
                    for k in range(8):
                        K.mm(pb[:], wk[:, k, c * 128:(c + 1) * 128], hT[:, k, t0:t0 + 512], start=(k == 0), stop=(k == 7))
                    K.copy(kT[:, c, t0:t0 + 512], pb[:], eng="dve")
            for ti in range(20):
                pb = K.psum()
                for k in range(8):
                    K.mm(pb[:], hT[:, k, ti * 128:(ti + 1) * 128], wv[:, k, :], start=(k == 0), stop=(k == 7))
                K.copy(vtok[:, ti, :], pb[:], eng="act")
                if ti < 4:
                    sq_i, r0 = ti // 2, (ti % 2) * 128
                    ov = K.arotbuf("na_ov", [512], F32, n=2)
                    K.copy(ov[:], pb[:], eng="dve")
                    finals.append(K.dma(o_nav[sq_i, j, r0:r0 + 128, :], ov[:]))
                    pb2 = K.psum()
                    for k in range(8):
                        K.mm(pb2[:], hT[:, k, ti * 128:(ti + 1) * 128], wk[:, k, :], start=(k == 0), stop=(k == 7))
                    ok_ = K.arotbuf("na_ok", [512], F32, n=2)
                    K.copy(ok_[:], pb2[:], eng="act")
                    finals.append(K.dma(o_nak[sq_i, j, r0:r0 + 128, :], ok_[:]))
            for i in range(4):
                stg = K.arotbuf("na_stg", [512], F32, n=2)
                K.dma(stg[:], nak_in[j, i * 128:(i + 1) * 128, :])
                pb = K.psum()
                for c in range(4):
                    K.tr(pb[:, c * 128:(c + 1) * 128], stg[:, c * 128:(c + 1) * 128], ident[:])
                K.copy(kT[:, :, 2560 + i * 128:2560 + (i + 1) * 128], pb[:].rearrange("p (c t) -> p c t", c=4), eng="dve")
                K.dma(vtok[:, 20 + i, :], nav_in[j, i * 128:(i + 1) * 128, :], q="pool")
            K.reset_to(mN)
            Ap = K.af([64, 8, 15], F32)
            rp = K.af([120], F32)
            K.memset(rp[0:32, :], NEGV)
            K.dma(rp[0:31, :], na_rpb[j].rearrange("h r c -> c (h r)"), allow_slow_non_contiguous=True)
            rhi = K.af([120], BF16)
            rlo = K.af([120], BF16)
            K.copy(rhi[0:32, :], rp[0:32, :], eng="act")
            K.tt(rlo[0:32, :], rp[0:32, :], rhi[0:32, :], ALU.subtract)
            for q4 in range(4):
                oh = K.arotbuf("na_oh", [16, 128], BF16, n=2)
                K.dma(oh[0:32, :, :], oh_d[:, q4 * 16:(q4 + 1) * 16, :], q="pool")
                for g4 in range(4):
                    pb = K.psum()
                    for i in range(4):
                        oc = pb[:, i * 120:(i + 1) * 120]
                        K.mm(oc, oh[0:32, g4 * 4 + i, :], rhi[0:32, :], start=True, stop=False)
                        K.mm(oc, oh[0:32, g4 * 4 + i, :], rlo[0:32, :], start=False, stop=True)
                    kc0 = q4 * 16 + g4 * 4
                    K.copy(Ap[:, kc0:kc0 + 4, :, :].rearrange("p k h r -> p k (h r)"), pb[:, 0:480].rearrange("p (k x) -> p k x", k=4),
                           eng=("act" if g4 % 2 else "dve"))
            S.tag = "L%d:o_naattn" % layer
            for (c0, L, smp) in SEQS[0:2]:
                for h in range(8):
                    ch, half = h // 2, h % 2
                    hsl = slice(half * 64, half * 64 + 64)
                    for qt in range(2):
                        qc = slice(c0 + qt * 128, c0 + (qt + 1) * 128)
                        attn_unit(qT[hsl, ch, qc], [(kT[hsl, ch, c0:c0 + 256], None)],
                                  [(t_ * 128, 128, vtok[:, c0 // 128 + t_, ch * 128:(ch + 1) * 128]) for t_ in range(2)],
                                  hsl, yT[hsl, 4 + ch, qc], "n")
            for i in range(16):
                if i < 2:
                    kr0, nr = 0, 8
                elif i > 13:
                    kr0, nr = 24, 8
                else:
                    kr0, nr = 2 * i - 4, 9
                kcol0 = 512 + kr0 * 64
                vt0 = 4 + kr0 // 2
                for h in range(8):
                    ch, half = h // 2, h % 2
                    hsl = slice(half * 64, half * 64 + 64)

                    def biasA(bank, w, i=i, h=h, kr0=kr0):
                        Ssb = K.arotbuf("na_S", [512], F32, n=2)
                        for ql in range(2):
                            qr = 2 * i + ql
                            rs_ = min(max(qr - 4, 0), 24)
                            mlo = max(0, rs_ - kr0)
                            mhi = min(8, rs_ - kr0 + 8)
                            ps_ = slice(ql * 64, ql * 64 + 64)
                            if mlo > 0:
                                K.memset(Ssb[ps_, 0:mlo * 64], NEGV, eng="pool")
                            if mhi < 8:
                                K.memset(Ssb[ps_, mhi * 64:512], NEGV, eng="pool")
                            dr0 = kr0 + mlo - qr + 7
                            nm_ = mhi - mlo
                            K.tt(Ssb[ps_, mlo * 64:mhi * 64].rearrange("p (m k) -> p m k", m=nm_),
                                 bank[ps_, mlo * 64:mhi * 64].rearrange("p (m k) -> p m k", m=nm_),
                                 Ap[ps_, :, h, dr0:dr0 + nm_].rearrange("p k r -> p r k"), ALU.add)
                        return Ssb[:, 0:512]

                    def biasB(bank, w, i=i, h=h, kr0=kr0):
                        Ssb = K.arotbuf("na_SB", [64], F32, n=2)
                        for ql in range(2):
                            qr = 2 * i + ql
                            rs_ = min(max(qr - 4, 0), 24)
                            ps_ = slice(ql * 64, ql * 64 + 64)
                            if rs_ <= kr0 + 8 < rs_ + 8:
                                dr = kr0 + 8 - qr + 7
                                K.tt(Ssb[ps_, 0:64], bank[ps_, 0:64], Ap[ps_, :, h, dr], ALU.add)
                            else:
                                K.memset(Ssb[ps_, 0:64], NEGV, eng="pool")
                        return Ssb[:, 0:64]

                    kbanks = [(kT[hsl, ch, 2560:3072], None), (kT[hsl, ch, kcol0:kcol0 + 512], biasA)]
                    ktiles = [(t_ * 128, 128, vtok[:, 20 + t_, ch * 128:(ch + 1) * 128]) for t_ in range(4)]
                    ktiles += [(512 + t_ * 128, 128, vtok[:, vt0 + t_, ch * 128:(ch + 1) * 128]) for t_ in range(4)]
                    if nr == 9:
                        kbanks.append((kT[hsl, ch, kcol0 + 512:kcol0 + 576], biasB))
                        ktiles.append((1024, 64, vtok[0:64, vt0 + 4, ch * 128:(ch + 1) * 128]))
                    qc = slice(512 + i * 128, 512 + (i + 1) * 128)
                    attn_unit(qT[hsl, ch, qc], kbanks, ktiles, hsl, yT[hsl, 4 + ch, qc], "n")
            attn_flush()
            K.reset_to(m0)
            S.tag = "L%d:o_out" % layer
            out_proj_post(odd_w_out[j], yT, gg)

        for layer in range(DEPTH):
            mod = ada(layer)
            if layer % 2 == 0 and ENABLE_EVEN:
                even_layer(layer, layer // 2, mod)
            if layer % 2 == 1 and ENABLE_ODD:
                odd_layer(layer, layer // 2, mod)
            gs, shift, gg = mod_vectors(layer, mod, 1)
            ffn(layer, gs, shift, gg)
            if dbg_stop == layer:
                break

        S.tag = "final"
        finals = []

        K.phase()

        def xT_to_out(dst, ntile, col0):
            for i in range(ntile):
                xb = K.arotbuf("xinT", [8, 128], F32)
                K.dma(xb[:], xT_v[:, :, col0 + i * 128: col0 + (i + 1) * 128])
                xo = K.arotbuf("xin", [D], F32)
                for g in range(2):
                    pb = K.psum()
                    for c in range(4):
                        K.tr(pb[:, c * 128:(c + 1) * 128], xb[:, g * 4 + c, :], ident[:])
                    K.copy(xo[:, g * 512:(g + 1) * 512], pb[:], eng=("act" if g else "dve"))
                finals.append(K.dma(dst[i * 128:(i + 1) * 128, :], xo[:]))

        xT_to_out(yp, NPT // 128, 0)
        xT_to_out(ys, LS // 128, NPT)
        S.finish(finals)
        global LAST_TAGS
        LAST_TAGS = S.tags
        S.emit()
    return nc


def _make_consts():
    i = np.arange(128, dtype=np.float32)
    jj = i[:, None]
    ii = i[None, :]
    rc = np.zeros((128, 772), np.float32)
    rc[:, 0:128] = np.maximum(ii - jj, 0)
    rc[:, 128:256] = (ii >= jj)
    rc[:, 256:384] = np.maximum(jj - ii, 0)
    rc[:, 384:512] = (jj >= ii)
    rc[:, 512:640] = np.broadcast_to(ii + 1.0, (128, 128))
    rc[:, 640:768] = np.broadcast_to(128.0 - ii, (128, 128))
    rc[:, 768] = 127.0 - i
    rc[:, 769] = i
    rc[:, 770] = 128.0
    inv = (10000.0 ** (-np.arange(8, dtype=np.float32) / np.float32(8))).astype(np.float32)
    t = np.arange(2048)
    row = (t // 64).astype(np.float32)
    col = (t % 64).astype(np.float32)
    ang = np.concatenate([row[:, None] * inv, col[:, None] * inv], axis=-1).astype(np.float32)
    cos, sin = np.cos(ang).T, np.sin(ang).T
    ropeC = np.concatenate([cos, cos], 0).astype(np.float32)
    ropeS = np.concatenate([-sin, sin], 0).astype(np.float32)
    oh = np.zeros((32, 64, 128), np.float32)
    qc = np.arange(64)
    cs = np.clip(qc - 8, 0, 48)
    for kc in range(64):
        jidx = np.clip(kc - qc + 15, 0, 30)
        for ql in range(2):
            oh[jidx, kc, ql * 64 + qc] = 1.0
            oh[31, kc, ql * 64 + qc] = ((kc < cs) | (kc >= cs + 16)).astype(np.float32)
    iota = np.broadcast_to(np.arange(1, 2049, dtype=np.float32)[None, :], (128, 2048))
    return {"rconst": rc, "ropeC": np.ascontiguousarray(ropeC), "ropeS": np.ascontiguousarray(ropeS),
            "oh": oh, "iota": np.ascontiguousarray(iota)}


def _s5_lay(a):
    a = np.asarray(a, np.float32)
    lead = a.shape[:-2] if a.ndim >= 2 else ()
    return a


def _s5_pack(inputs, s):
    def gp(a):
        a = np.asarray(a, np.float32).reshape(2, 2, 16, 2, 64)
        return a.transpose(0, 1, 3, 4, 2).reshape(2, 2, 128, 16)
    lre = gp(inputs["s5_lambda_re"])
    lim = gp(inputs["s5_lambda_im"])
    lst = gp(np.broadcast_to(np.asarray(inputs["s5_log_step"], np.float32)[..., None], (2, 2, 32, 64)))
    h0re = gp(np.asarray(inputs["state_s5_re"], np.float32)[s])
    h0im = gp(np.asarray(inputs["state_s5_im"], np.float32)[s])
    def gps(a):
        a = np.asarray(a, np.float32).reshape(2, 2, 16, 2, 64, 16)
        return a.transpose(0, 1, 3, 4, 2, 5).reshape(2, 2, 128, 256)
    def gsp(a):
        a = np.asarray(a, np.float32).reshape(2, 2, 16, 2, 16, 64)
        return a.transpose(0, 1, 3, 5, 2, 4).reshape(2, 2, 128, 256)
    pk = np.concatenate([lre, lim, lst, h0re, h0im, gps(inputs["s5_b_re"]), gps(inputs["s5_b_im"]),
                         gsp(inputs["s5_c_re"]), gsp(inputs["s5_c_im"])], axis=-1)
    return np.ascontiguousarray(pk.astype(np.float32))


CONSTS = _make_consts()


def make_in_maps(inputs):
    f = lambda a: np.ascontiguousarray(np.asarray(a, dtype=np.float32))
    maps = []
    ident = np.eye(128, dtype=np.float32)
    for core in range(8):
        s = core // 4
        m = {}
        m["xp"] = f(inputs["x_prompt"][2 * core:2 * core + 2]).reshape(NPT, D)
        m["xs"] = f(inputs["x_sample"][s])
        m["cv"] = f(np.stack([inputs["c_ctx"], inputs["c"][s]], 0))
        m["ident_in"] = ident
        m["ada_w"] = f(inputs["ada_w"])
        m["ada_b"] = f(inputs["ada_b"]).reshape(DEPTH * 48, 128)
        for nm in ("mix_pre_g", "mix_post_g", "ffn_pre_g", "ffn_post_g"):
            m[nm] = f(inputs[nm]).reshape(DEPTH * 8, 128)
        m["ffn_w_up"] = f(inputs["ffn_w_up"])
        m["ffn_conv_w"] = f(inputs["ffn_conv_w"]).reshape(DEPTH * 3 * 44, 128)
        m["ffn_conv_b"] = f(inputs["ffn_conv_b"]).reshape(DEPTH * 44, 128)
        m["ffn_w_down"] = f(inputs["ffn_w_down"])
        m["even_w_in"] = f(inputs["even_w_in"])
        m["even_w_out"] = f(inputs["even_w_out"])
        m["ret_logit_b"] = f(np.broadcast_to(np.asarray(inputs["ret_logit"], np.float32).reshape(2, 1, 8), (2, 128, 8)))
        m["ret_gn"] = f(inputs["ret_gn"]).reshape(8, 128)
        m["mla_q_norm"] = f(inputs["mla_q_norm"]).reshape(4, 128)
        m["mla_w_uq"] = f(inputs["mla_w_uq"])
        m["mla_kv_norm"] = f(inputs["mla_kv_norm"])
        m["kvn_bc"] = f(np.broadcast_to(np.asarray(inputs["mla_kv_norm"], np.float32).reshape(1, 256), (128, 256)))
        m["mla_w_ukv"] = f(inputs["mla_w_ukv"])
        m["state_ret_in"] = f(inputs["state_ret"][s])
        m["ckv_in"] = f(inputs["cache_mla_ckv"][s])
        m["kr_in"] = f(inputs["cache_mla_krope"][s])
        m["odd_w_in"] = f(inputs["odd_w_in"])
        m["odd_w_out"] = f(inputs["odd_w_out"])
        m["s5p"] = _s5_pack(inputs, s)
        m["s5_d"] = f(inputs["s5_d"]).reshape(8, 128)
        m["s5_glu_b"] = f(inputs["s5_glu_b"]).reshape(8, 128)
        m["s5_glu_w"] = f(inputs["s5_glu_w"])
        m["na_rpb"] = f(inputs["na_rpb"])
        m["nak_in"] = f(inputs["cache_na_k"][s]).reshape(2, 512, 512)
        m["nav_in"] = f(inputs["cache_na_v"][s]).reshape(2, 512, 512)
        m["oh_in"] = CONSTS["oh"]
        m["iota_in"] = CONSTS["iota"]
        m["rconst_in"] = CONSTS["rconst"]
        m["ropeC_in"] = CONSTS["ropeC"]
        m["ropeS_in"] = CONSTS["ropeS"]
        maps.append(m)
    return maps


def kernel(**inputs):
    nc = build_program()
    maps = make_in_maps(inputs)
    res = run_bass_kernel_spmd(nc, maps, core_ids=list(range(8)))
    R = res.results
    y_prompt = np.concatenate([R[c]["yp"].reshape(2, 256, D) for c in range(8)], 0)
    y_sample = np.stack([R[0]["ys"], R[4]["ys"]], 0)
    st_ret = np.concatenate([R[c]["o_ret"] for c in range(8)], 0)
    ck_ckv = np.concatenate([R[c]["o_ckv"] for c in range(8)], 0)
    ck_kr = np.concatenate([R[c]["o_kr"] for c in range(8)], 0)
    def s5o(name):
        o = np.concatenate([R[c][name] for c in range(8)], 0)
        o = o.reshape(16, 2, 2, 2, 64, 16).transpose(0, 1, 2, 5, 3, 4)
        return np.ascontiguousarray(o.reshape(16, 2, 2, 32, 64))
    st_re, st_im = s5o("o_s5re"), s5o("o_s5im")
    nak = np.concatenate([R[c]["o_nak"] for c in range(8)], 0).reshape(16, 2, 256, 8, 64)
    nav = np.concatenate([R[c]["o_nav"] for c in range(8)], 0).reshape(16, 2, 256, 8, 64)
    return (y_prompt, y_sample, st_ret, ck_ckv, ck_kr, st_re, st_im, nak, nav)
```

```python
import math
import numpy as np
import concourse.bass as bass
import concourse.mybir as mybir
from concourse.bass_utils import run_bass_kernel_spmd

F32 = mybir.dt.float32
BF16 = mybir.dt.bfloat16
I32 = mybir.dt.int32
ALU = mybir.AluOpType
AF = mybir.ActivationFunctionType
AX = mybir.AxisListType

ENGS = ("pe", "act", "dve", "pool", "sp")
STRICT_SAME_ENGINE = True
N_DMA_SEMS = 12


def _box(ap):
    t = ap.tensor
    es = 2 if ap.dtype == BF16 else 4
    pat = [(a * es, b) for a, b in ap.ap]
    off = int(ap.offset) * es
    sp = str(ap.space)
    if "DRAM" in sp.upper() or "Dram" in sp or "dram" in sp:
        lo = off
        hi = off
        for st, cnt in pat:
            if st >= 0:
                hi += st * (cnt - 1)
            else:
                lo += st * (cnt - 1)
        return (t.name, 0, 1, lo, hi + 1)
    pstride = pat[0][0]
    npart = pat[0][1]
    if pstride == 0:
        pstride = 1 << 30
    p0 = off // pstride
    f0 = off % pstride
    lo = f0
    hi = f0
    for st, cnt in pat[1:]:
        if st >= 0:
            hi += st * (cnt - 1)
        else:
            lo += st * (cnt - 1)
    if t.name.startswith("psb"):
        return (t.name, 0, 128, 0, 2048)
    return (t.name, p0, p0 + npart, lo, hi + 1)


def _ovl(a, b):
    return a[1] < b[2] and b[1] < a[2] and a[3] < b[4] and b[3] < a[4]


def _contains(a, b):
    return a[1] <= b[1] and b[2] <= a[2] and a[3] <= b[3] and b[4] <= a[4]


class Sched:
    def __init__(self, nc):
        self.nc = nc
        self.ops = {e: [] for e in ENGS}
        self.track = {}
        self.known = {e: {} for e in ENGS}
        self.ndma = {e: 0 for e in ENGS}
        self.ncomp = {e: 0 for e in ENGS}
        self.dma_last = {e: {} for e in ENGS}
        self.notrack = set()
        self.final_events = []
        self.tag = ""
        self.tags = {e: [] for e in ENGS}

    def _deps(self, reads, writes, eng=None):
        deps = set()
        for ap in reads:
            b = _box(ap)
            if b[0] in self.notrack:
                continue
            tr = self.track.get(b[0])
            if tr is None:
                continue
            for wb, ev in tr["w"]:
                if _ovl(wb, b):
                    deps.add(ev)
            if b[0].startswith("psb"):
                for rb, evs in tr["r"].items():
                    if _ovl(rb, b):
                        deps.update(ev for e_, ev in evs.items() if e_ != eng)
        for ap in writes:
            b = _box(ap)
            tr = self.track.get(b[0])
            if tr is None:
                continue
            for wb, ev in tr["w"]:
                if _ovl(wb, b):
                    deps.add(ev)
            for rb, evs in tr["r"].items():
                if _ovl(rb, b):
                    deps.update(evs.values())
        return deps

    def _record(self, reads, writes, ev, eng):
        for ap in writes:
            b = _box(ap)
            tr = self.track.setdefault(b[0], {"w": [], "r": {}})
            tr["w"] = [(wb, e) for wb, e in tr["w"] if not _contains(b, wb)]
            tr["w"].append((b, ev))
            tr["r"] = {rb: evs for rb, evs in tr["r"].items() if not _contains(b, rb)}
        for ap in reads:
            b = _box(ap)
            if b[0] in self.notrack:
                continue
            tr = self.track.setdefault(b[0], {"w": [], "r": {}})
            tr["r"].setdefault(b, {})[eng] = ev

    def _waits(self, eng, deps, idx, is_dma=False):
        waits = {}
        for semkey, val in deps:
            if semkey == ("c", eng) and not is_dma:
                if eng == "pe":
                    continue
                if (not STRICT_SAME_ENGINE) and val < idx - 1:
                    continue
            if self.known[eng].get(semkey, 0) >= val:
                continue
            if waits.get(semkey, 0) < val:
                waits[semkey] = val
        for k, v in waits.items():
            self.known[eng][k] = v
        return list(waits.items())

    def op(self, eng, fn, reads=(), writes=()):
        reads = [r for r in reads if r is not None and not isinstance(r, (int, float))]
        idx = self.ncomp[eng]
        self.ncomp[eng] = idx + 1
        deps = self._deps(reads, writes, eng)
        waits = self._waits(eng, deps, idx)
        ev = (("c", eng), idx + 1)
        self.tags[eng].append(self.tag)
        self.ops[eng].append((fn, waits, ev, "c"))
        self._record(reads, writes, ev, eng)
        return ev

    def dma(self, eng, out, in_, **kw):
        n = self.ndma[eng]
        self.ndma[eng] = n + 1
        k = n % N_DMA_SEMS
        val = (n // N_DMA_SEMS + 1) * 16
        semkey = ("d", eng, k)
        deps = self._deps([in_], [out])
        if val > 16:
            deps.add((semkey, val - 16))
        idx = self.ncomp[eng]
        waits = self._waits(eng, deps, idx, True)
        ev = (semkey, val)

        def fn(e, out=out, in_=in_, kw=kw):
            return e.dma_start(out=out, in_=in_, **kw)

        self.ops[eng].append((fn, waits, ev, "d"))
        self._record([in_], [out], ev, eng)
        return ev

    def finish(self, final_events):
        self.final_events = list(final_events)

    def emit(self):
        nc = self.nc
        import contextlib

        with contextlib.ExitStack() as st:
            sems = {}
            for e in ENGS:
                sems[("c", e)] = st.enter_context(nc.semaphore("c_" + e))
            for e in ENGS:
                for k in range(min(N_DMA_SEMS, self.ndma[e])):
                    sems[("d", e, k)] = st.enter_context(nc.semaphore("d_%s_%d" % (e, k)))
            block = st.enter_context(nc.Block())

            waited = {e: set() for e in ENGS}
            for e in ENGS:
                for fn, waits, ev, kind in self.ops[e]:
                    for semkey, val in waits:
                        if semkey[0] == "c":
                            waited[semkey[1]].add(val)
            for semkey, val in self.final_events:
                if semkey[0] == "c":
                    waited[semkey[1]].add(val)
            rank = {e: {v: i + 1 for i, v in enumerate(sorted(waited[e]))} for e in ENGS}

            def semval(semkey, val):
                return rank[semkey[1]][val] if semkey[0] == "c" else val

            def run(engname, eobj):
                for fn, waits, ev, kind in self.ops[engname]:
                    for semkey, val in waits:
                        eobj.wait_ge(sems[semkey], semval(semkey, val))
                    ins = fn(eobj)
                    if kind == "c":
                        if ev[1] in waited[engname]:
                            ins.then_inc(sems[ev[0]], 1)
                    else:
                        ins.then_inc(sems[ev[0]], 16)
                if engname == "sp":
                    mx = {}
                    for semkey, val in self.final_events:
                        v = semval(semkey, val)
                        mx[semkey] = max(mx.get(semkey, 0), v)
                    for semkey, val in mx.items():
                        eobj.wait_ge(sems[semkey], val)

            @block.tensor
            def _(e):
                run("pe", e)

            @block.scalar
            def _(e):
                run("act", e)

            @block.vector
            def _(e):
                run("dve", e)

            @block.gpsimd
            def _(e):
                run("pool", e)

            @block.sync
            def _(e):
                run("sp", e)

import contextlib

ENABLE_EVEN = True
ENABLE_ODD = True

D = 1024
NPT = 512
LS = 2048
T = NPT + LS
DFF = 2816
NJ = DFF // 128
EPS = 1e-6
DEPTH = 4


class KB:
    def __init__(self, nc, st):
        self.nc = nc
        self.st = st
        self.S = Sched(nc)
        self.psn = 0
        self.rot = {}

    def sb(self, name, shape, dt=F32):
        return self.st.enter_context(self.nc.sbuf_tensor(name, list(shape), dt))

    def dram(self, name, shape, dt=F32, kind="Internal"):
        return self.nc.dram_tensor(name, list(shape), dt, kind=kind).ap()

    def rotbuf(self, name, shape, dt, n=2):
        key = name
        if key not in self.rot:
            self.rot[key] = [[self.sb("%s_%d" % (name, i), shape, dt) for i in range(n)], 0]
        r = self.rot[key]
        b = r[0][r[1] % len(r[0])]
        r[1] += 1
        return b

    def arena_init(self, nf):
        self.arF = self.sb("arena", [128, nf], F32)
        self.arB = self.arF.bitcast(BF16)
        self.arI = self.arF.bitcast(I32)
        self.nbytes = nf * 4
        self.ptr = 0
        self.arot = {}

    def phase(self):
        self.ptr = 0
        self.arot = {}

    def mark(self):
        return self.ptr

    def reset_to(self, m):
        self.ptr = m
        self.arot = {}

    def af(self, shape, dt=F32):
        n = int(np.prod(shape))
        es = 4 if dt == F32 else 2
        self.ptr = (self.ptr + 3) // 4 * 4
        e0 = self.ptr // es
        a = (self.arF if dt == F32 else self.arB)[:, e0:e0 + n]
        self.ptr += n * es
        assert self.ptr <= self.nbytes, ("arena overflow", self.ptr, self.nbytes)
        if len(shape) == 2:
            a = a.rearrange("p (a b) -> p a b", a=shape[0])
        elif len(shape) == 3:
            a = a.rearrange("p (a b c) -> p a b c", a=shape[0], b=shape[1])
        elif len(shape) == 4:
            a = a.rearrange("p (a b c d) -> p a b c d", a=shape[0], b=shape[1], c=shape[2])
        return a

    def arotbuf(self, name, shape, dt, n=2):
        if name not in self.arot:
            self.arot[name] = [[self.af(shape, dt) for _ in range(n)], 0]
        r = self.arot[name]
        b = r[0][r[1] % len(r[0])]
        r[1] += 1
        return b

    def psum(self):
        b = self.psb[self.psn % len(self.psb)]
        self.psn += 1
        return b

    def mm(self, out, lhsT, rhs, start=True, stop=True):
        return self.S.op("pe", lambda e: e.matmul(out, lhsT=lhsT, rhs=rhs, start=start, stop=stop),
                         reads=[lhsT, rhs], writes=[out])

    def tr(self, out, in_, ident):
        return self.S.op("pe", lambda e: e.transpose(out, in_, ident), reads=[in_, ident], writes=[out])

    def act(self, out, in_, func, bias=None, scale=1.0, accum_out=None):
        kw = {}
        rd = [in_]
        wr = [out]
        if bias is not None:
            kw["bias"] = bias
            rd.append(bias)
        if accum_out is not None:
            kw["accum_out"] = accum_out
            wr.append(accum_out)
        if not isinstance(scale, (int, float)):
            rd.append(scale)
        return self.S.op("act", lambda e: e.activation(out=out, in_=in_, func=func, scale=scale, **kw),
                         reads=rd, writes=wr)

    def tt(self, out, in0, in1, op, eng="dve"):
        return self.S.op(eng, lambda e: e.tensor_tensor(out=out, in0=in0, in1=in1, op=op),
                         reads=[in0, in1], writes=[out])

    def ts(self, out, in0, s1, s2, op0, op1=None, eng="dve", accum_out=None):
        kw = {}
        wr = [out]
        if op1 is not None:
            kw["op1"] = op1
        if accum_out is not None:
            kw["accum_out"] = accum_out
            wr.append(accum_out)
        return self.S.op(eng, lambda e: e.tensor_scalar(out=out, in0=in0, scalar1=s1, scalar2=s2, op0=op0, **kw),
                         reads=[in0, s1, s2], writes=wr)

    def stt(self, out, in0, scalar, in1, op0, op1):
        return self.S.op("dve", lambda e: e.scalar_tensor_tensor(out=out, in0=in0, scalar=scalar, in1=in1, op0=op0, op1=op1),
                         reads=[in0, scalar, in1], writes=[out])

    def copy(self, out, in_, eng="dve"):
        if eng == "act":
            return self.S.op("act", lambda e: e.copy(out=out, in_=in_), reads=[in_], writes=[out])
        return self.S.op(eng, lambda e: e.tensor_copy(out=out, in_=in_), reads=[in_], writes=[out])

    def memset(self, ap, val, eng="dve"):
        return self.S.op(eng, lambda e: e.memset(ap, val), reads=[], writes=[ap])

    def red(self, out, in_, op, axis=AX.X):
        return self.S.op("dve", lambda e: e.tensor_reduce(out=out, in_=in_, axis=axis, op=op), reads=[in_], writes=[out])

    def recip(self, out, in_):
        return self.S.op("dve", lambda e: e.reciprocal(out=out, in_=in_), reads=[in_], writes=[out])

    def dma(self, out, in_, q="sp", **kw):
        return self.S.dma(q, out, in_, **kw)


def build_program(dbg_stop=None):
    nc = bass.Bass("TRN2", target_bir_lowering=False)
    st = contextlib.ExitStack()
    with st:
        K = KB(nc, st)
        S = K.S
        IN = {}

        def inp(name, shape, dt=F32):
            IN[name] = nc.dram_tensor(name, list(shape), dt, kind="ExternalInput").ap()
            S.notrack.add(name)
            return IN[name]

        def outp(name, shape):
            return nc.dram_tensor(name, list(shape), F32, kind="ExternalOutput").ap()

        xp = inp("xp", [NPT, D])
        xs = inp("xs", [LS, D])
        cv = inp("cv", [2, D])
        ident_d = inp("ident_in", [128, 128])
        ada_w = inp("ada_w", [DEPTH, D, 6 * D])
        ada_b = inp("ada_b", [DEPTH * 48, 128])
        gvec = {}
        for nm in ("mix_pre_g", "mix_post_g", "ffn_pre_g", "ffn_post_g"):
            gvec[nm] = inp(nm, [DEPTH * 8, 128])
        ffn_w_up = inp("ffn_w_up", [DEPTH, D, 2 * DFF])
        ffn_conv_w = inp("ffn_conv_w", [DEPTH * 3 * 44, 128])
        ffn_conv_b = inp("ffn_conv_b", [DEPTH * 44, 128])
        ffn_w_down = inp("ffn_w_down", [DEPTH, DFF, D])

        even_w_in = inp("even_w_in", [2, D, 2464])
        even_w_out = inp("even_w_out", [2, D, D])
        ret_logit_b = inp("ret_logit_b", [2, 128, 8])
        ret_gn = inp("ret_gn", [8, 128])
        mla_q_norm = inp("mla_q_norm", [4, 128])
        mla_w_uq = inp("mla_w_uq", [2, 256, 768])
        mla_kv_norm = inp("mla_kv_norm", [2, 128])
        kvn_bc_d = inp("kvn_bc", [128, 256])
        mla_w_ukv = inp("mla_w_ukv", [2, 128, 1024])
        state_ret_in = inp("state_ret_in", [2, 2, 4, 128, 128])
        ckv_in = inp("ckv_in", [2, 512, 128])
        kr_in = inp("kr_in", [2, 512, 32])
        rconst_d = inp("rconst_in", [128, 772])
        ropeC_d = inp("ropeC_in", [32, 2048])
        ropeS_d = inp("ropeS_in", [32, 2048])
        odd_w_in = inp("odd_w_in", [2, D, 2048])
        odd_w_out = inp("odd_w_out", [2, D, D])
        s5p = inp("s5p", [2, 2, 128, 1104])
        s5_d = inp("s5_d", [8, 128])
        s5_glu_b = inp("s5_glu_b", [8, 128])
        s5_glu_w = inp("s5_glu_w", [2, 512, 512])
        na_rpb = inp("na_rpb", [2, 8, 15, 31])
        nak_in = inp("nak_in", [2, 512, 512])
        nav_in = inp("nav_in", [2, 512, 512])
        oh_d = inp("oh_in", [32, 64, 128])
        iota_d = inp("iota_in", [128, 2048])
        o_s5re = outp("o_s5re", [2, 2, 2, 128, 16])
        o_s5im = outp("o_s5im", [2, 2, 2, 128, 16])
        o_nak = outp("o_nak", [2, 2, 256, 512])
        o_nav = outp("o_nav", [2, 2, 256, 512])
        o_ret = outp("o_ret", [2, 2, 2, 4, 128, 128])
        o_ckv = outp("o_ckv", [2, 2, 256, 128])
        o_kr = outp("o_kr", [2, 2, 256, 32])
        finals = []
        yp = outp("yp", [NPT, D])
        ys = outp("ys", [LS, D])

        xT_d = K.dram("xT_d", [D, T])
        xT_v = xT_d.rearrange("(c p) t -> p c t", p=128)

        K.psb = [st.enter_context(nc.psum_tensor("psb%d" % i, [128, 512], F32)) for i in range(8)]

        K.arena_init(50200)
        ident = K.sb("ident", [128, 128], F32)
        K.dma(ident[:], ident_d)
        identb = K.sb("identb", [128, 128], BF16)
        K.copy(identb[:], ident[:])
        epsT = K.sb("epsT", [128, 1], F32)
        K.memset(epsT[:], EPS)
        halo_stash = K.sb("halo_stash", [128, 8, 2], BF16)
        ones1024 = K.sb("ones1024", [128, 128], BF16)
        K.memset(ones1024[:], 1.0 / 1024.0)
        ones256 = K.sb("ones256", [128, 128], BF16)
        K.memset(ones256[:], 1.0 / 256.0)
        ones128b = K.sb("ones128b", [128, 128], BF16)
        K.memset(ones128b[:], 1.0 / 128.0)
        onesb = K.sb("onesb", [128, 128], BF16)
        K.memset(onesb[:], 1.0)
        oneT = K.sb("oneT", [128, 1], F32)
        K.memset(oneT[:], 1.0)
        halfpi = K.sb("halfpi", [128, 1], F32)
        K.memset(halfpi[:], math.pi / 2.0)


        def load_T(dram2d, R, name):
            dst = K.sb(name, [128, R], F32)
            r0 = 0
            while r0 < R:
                r = min(128, R - r0)
                stg = K.rotbuf("ldT_stg", [128, 128], F32)
                K.dma(stg[0:r, :], dram2d[r0:r0 + r, :])
                pb = K.psum()
                K.tr(pb[:, 0:r], stg[0:r, :], ident[0:r, 0:r])
                K.copy(dst[:, r0:r0 + r], pb[:, 0:r])
                r0 += r
            return dst

        gT = {nm: load_T(gvec[nm], DEPTH * 8, "gT_" + nm) for nm in gvec}
        adabT = load_T(ada_b, DEPTH * 48, "adabT")
        convwT = load_T(ffn_conv_w, DEPTH * 3 * 44, "convwT")
        convbT = load_T(ffn_conv_b, DEPTH * 44, "convbT")
        nconvwT = K.sb("nconvwT", [128, DEPTH * 3 * 44], F32)
        K.ts(nconvwT[:], convwT[:], -1.0, None, ALU.mult)
        s5dT = load_T(s5_d, 8, "s5dT")
        glubT = load_T(s5_glu_b, 8, "glubT")
        gnT = load_T(ret_gn, 8, "gnT")
        qnT = load_T(mla_q_norm, 4, "qnT")
        kvnT = load_T(mla_kv_norm, 2, "kvnT")
        cT = load_T(cv.rearrange("v (c p) -> (v c) p", p=128), 16, "cT")
        scT = K.sb("scT", [128, 8, 2], BF16)
        K.act(scT[:].rearrange("p c v -> p v c"), cT[:].rearrange("p (v c) -> p v c", v=2), AF.Silu)

        def x_to_xT(src, ntile, col0):
            for i in range(ntile):
                xt = K.arotbuf("xin", [D], F32)
                K.dma(xt[:], src[i * 128:(i + 1) * 128, :])
                xo = K.arotbuf("xinT", [8, 128], F32)
                for g in range(2):
                    pb = K.psum()
                    for c in range(4):
                        K.tr(pb[:, c * 128:(c + 1) * 128], xt[:, (g * 4 + c) * 128:(g * 4 + c + 1) * 128], ident[:])
                    K.copy(xo[:, g * 4:(g + 1) * 4, :], pb[:].rearrange("p (c t) -> p c t", c=4), eng=("act" if g else "dve"))
                K.dma(xT_v[:, :, col0 + i * 128: col0 + (i + 1) * 128], xo[:])

        K.phase()
        x_to_xT(xp, NPT // 128, 0)
        x_to_xT(xs, LS // 128, NPT)

        def ada(layer):
            K.phase()
            S.tag = "L%d:ada" % layer
            modp = K.psum()
            mview = modp[:, 0:96].rearrange("p (c v) -> p c v", v=2)
            for blk in range(12):
                wb = K.arotbuf("adaw", [8, 512], BF16)
                K.dma(wb[:], ada_w[layer].rearrange("(k p) o -> p k o", p=128)[:, :, blk * 512:(blk + 1) * 512], q="pool")
                for cc in range(4):
                    ch = blk * 4 + cc
                    for k in range(8):
                        K.mm(mview[:, ch, :], wb[:, k, cc * 128:(cc + 1) * 128], scT[:, k, :], start=(k == 0), stop=(k == 7))
            mod = K.rotbuf("mod", [128, 48, 2], F32)
            for v in range(2):
                K.tt(mod[:, :, v], mview[:, :, v], adabT[:, layer * 48:(layer + 1) * 48], ALU.add)
            return mod

        def mod_vectors(layer, mod, sub):
            base = sub * 24
            pre_g = gT["mix_pre_g" if sub == 0 else "ffn_pre_g"]
            post_g = gT["mix_post_g" if sub == 0 else "ffn_post_g"]
            gs = K.rotbuf("gs", [128, 8, 2], F32)
            gg = K.rotbuf("gg", [128, 8, 2], F32)
            for v in range(2):
                K.stt(gs[:, :, v], mod[:, base + 8:base + 16, v], 1.0, pre_g[:, layer * 8:(layer + 1) * 8], ALU.add, ALU.mult)
                K.tt(gg[:, :, v], mod[:, base + 16:base + 24, v], post_g[:, layer * 8:(layer + 1) * 8], ALU.mult)
            return gs, mod[:, base:base + 8, :], gg

        def rstd_from(src3, n, ones):
            C = src3.shape[1]
            sq = K.arotbuf("sq", [8, 512], BF16, n=1)
            K.act(sq[:, 0:C, 0:n], src3, AF.Square)
            pb = K.psum()
            for c in range(C):
                K.mm(pb[:, 0:n], ones[:], sq[:, c, 0:n], start=(c == 0), stop=(c == C - 1))
            rstd = K.arotbuf("rstd", [512], F32)
            K.act(rstd[:, 0:n], pb[:, 0:n], AF.Sqrt, bias=epsT[:, 0:1])
            K.recip(rstd[:, 0:n], rstd[:, 0:n])
            return rstd

        def pre_block(t0, n, mc, gs, shift, dst):
            xb = K.arotbuf("xb", [8, 512], F32)
            if n == 1:
                K.dma(xb[:, :, 0:n], xT_v[:, :, t0:t0 + n], allow_slow_non_contiguous=True)
            else:
                K.dma(xb[:, :, 0:n], xT_v[:, :, t0:t0 + n])
            rstd = rstd_from(xb[:, :, 0:n], n, ones1024)
            for c in range(8):
                tmp = K.arotbuf("tmpn", [512], F32)
                K.stt(tmp[:, 0:n], xb[:, c, 0:n], gs[:, c, mc:mc + 1], rstd[:, 0:n], ALU.mult, ALU.mult)
                K.act(dst[:, c, :], tmp[:, 0:n], AF.Identity, bias=shift[:, c, mc:mc + 1])

        def post_block(t0, n, mc, gg, yf):
            rstd = rstd_from(yf, n, ones1024)
            xb = K.arotbuf("xb", [8, 512], F32)
            K.dma(xb[:, :, 0:n], xT_v[:, :, t0:t0 + n])
            for c in range(8):
                tmp = K.arotbuf("tmpn", [512], F32)
                K.stt(tmp[:, 0:n], yf[:, c, :], gg[:, c, mc:mc + 1], rstd[:, 0:n], ALU.mult, ALU.mult)
                K.tt(xb[:, c, 0:n], xb[:, c, 0:n], tmp[:, 0:n], ALU.add, eng="pool")
            K.dma(xT_v[:, :, t0:t0 + n], xb[:, :, 0:n])

        FFN_GROUPS = [
            (0, 1024, 0, 1, [(0, 256), (256, 512), (512, 1025)]),
            (1024, 2048, 1, 1, [(1023, 2049)]),
            (2048, 2560, 1, 0, [(2047, 2560)]),
        ]

        def ffn(layer, gs, shift, gg):
            wup_v = ffn_w_up[layer].rearrange("(k p) o -> p k o", p=128)
            wdn_v = ffn_w_down[layer].rearrange("(j p) o -> p j o", p=128)
            for (c0, c1, hl, hr, segs) in FFN_GROUPS:
                K.phase()
                S.tag = "L%d:ffn_up" % layer
                b0 = c0 - hl
                W = c1 + hr - b0
                hTg = K.af([8, 1026], BF16)
                cols = []
                if hl:
                    cols.append((b0, 1))
                for t0 in range(c0, c1, 512):
                    cols.append((t0, 512))
                if hr:
                    cols.append((c1, 1))
                for (t0, n) in cols:
                    mc = 0 if t0 < NPT else 1
                    if hl and t0 == b0:
                        K.copy(hTg[:, :, 0:1], halo_stash[:, :, 0:1], eng="pool")
                        continue
                    pre_block(t0, n, mc, gs, shift, hTg[:, :, t0 - b0:t0 - b0 + n])
                if c1 < T:
                    K.copy(halo_stash[:, :, 0:1], hTg[:, :, c1 - 1 - b0:c1 - b0], eng="pool")
                npart = (W + 511) // 512
                psz = (W + npart - 1) // npart
                parts = [(i * psz, min(psz, W - i * psz)) for i in range(npart)]
                mT = K.af([NJ, 1024], BF16)

                def load_up(j):
                    wa = K.arotbuf("wupa", [8, 128], BF16, n=2)
                    wg = K.arotbuf("wupg", [8, 128], BF16, n=2)
                    K.dma(wa[:], wup_v[:, :, j * 128:(j + 1) * 128], q="pool")
                    K.dma(wg[:], wup_v[:, :, DFF + j * 128:DFF + (j + 1) * 128], q="pool")
                    return wa, wg

                bounds = [s0_ for (s0_, s1_) in segs[1:]]
                lo = 1 + hl
                n_own = c1 - c0

                def ffn_X(j, wts):
                    wa, wg = wts
                    us, os_ = [], []
                    for wi, wmat in enumerate((wa, wg)):
                        u = K.arotbuf("u%d" % wi, [1028], F32, n=2)
                        for (p0, pn) in parts:
                            pb = K.psum()
                            for k in range(8):
                                K.mm(pb[:, 0:pn], wmat[:, k, :], hTg[:, k, p0:p0 + pn], start=(k == 0), stop=(k == 7))
                            K.copy(u[:, 1 + p0:1 + p0 + pn], pb[:, 0:pn], eng="act")
                        blk = wi * NJ + j
                        w1 = convwT[:, (layer * 3 + 1) * 44 + blk:(layer * 3 + 1) * 44 + blk + 1]
                        bb = convbT[:, layer * 44 + blk:layer * 44 + blk + 1]
                        o = K.arotbuf("o%d" % wi, [1028], F32, n=2)
                        K.act(o[:, 1:1 + W], u[:, 1:1 + W], AF.Identity, bias=bb, scale=w1)
                        us.append(u)
                        os_.append(o)
                    return us, os_

                def ffn_Y(j, us, os_):
                    for wi in range(2):
                        u, o = us[wi], os_[wi]
                        blk = wi * NJ + j
                        c0w = (layer * 3 + 0) * 44 + blk
                        c2w = (layer * 3 + 2) * 44 + blk
                        w0, w2 = convwT[:, c0w:c0w + 1], convwT[:, c2w:c2w + 1]
                        nw0, nw2 = nconvwT[:, c0w:c0w + 1], nconvwT[:, c2w:c2w + 1]
                        K.stt(o[:, 2:1 + W], u[:, 1:W], w0, o[:, 2:1 + W], ALU.mult, ALU.add)
                        K.stt(o[:, 1:W], u[:, 2:1 + W], w2, o[:, 1:W], ALU.mult, ALU.add)
                        for B in bounds:
                            iB = B - b0 + 1
                            K.stt(o[:, iB:iB + 1], u[:, iB - 1:iB], nw0, o[:, iB:iB + 1], ALU.mult, ALU.add)
                            K.stt(o[:, iB - 1:iB], u[:, iB:iB + 1], nw2, o[:, iB - 1:iB], ALU.mult, ALU.add)
                    oa, og = os_
                    sg = K.arotbuf("sg", [1024], F32, n=1)
                    K.act(sg[:, 0:n_own], og[:, lo:lo + n_own], AF.Silu)
                    K.tt(mT[:, j, 0:n_own], oa[:, lo:lo + n_own], sg[:, 0:n_own], ALU.mult)

                wts = load_up(0)
                wts_next = load_up(1)
                stX = ffn_X(0, wts)
                for j in range(NJ):
                    if j + 1 < NJ:
                        wts = wts_next
                        nxX = ffn_X(j + 1, wts)
                        if j + 2 < NJ:
                            wts_next = load_up(j + 2)
                    ffn_Y(j, *stX)
                    if j + 1 < NJ:
                        stX = nxX
                S.tag = "L%d:ffn_down" % layer
                n_own = c1 - c0
                yfa = K.af([8, 1024], F32)

                def load_dn(c):
                    wd = K.arotbuf("wdn", [NJ, 128], BF16, n=2)
                    K.dma(wd[:], wdn_v[:, :, c * 128:(c + 1) * 128], q="pool")
                    return wd

                nxt = load_dn(0)
                for c in range(8):
                    wd = nxt
                    if c + 1 < 8:
                        nxt = load_dn(c + 1)
                    for t0 in range(0, n_own, 512):
                        pb = K.psum()
                        for j in range(NJ):
                            K.mm(pb[:, :], wd[:, j, :], mT[:, j, t0:t0 + 512], start=(j == 0), stop=(j == NJ - 1))
                        K.copy(yfa[:, c, t0:t0 + 512], pb[:, :], eng="act")
                for t0 in range(0, n_own, 512):
                    mc = 0 if (c0 + t0) < NPT else 1
                    post_block(c0 + t0, 512, mc, gg, yfa[:, :, t0:t0 + 512])

        RS = 128.0 ** -0.5
        MLA_SCALE = 96.0 ** -0.5
        SEQS = [(0, 256, 0), (256, 256, 0), (512, 2048, 1)]

        PW = {"m": 2560, "n": 1152}
        NTT = {"m": 20, "n": 9}

        ATT = {"prev": None, "tb": 0}

        def attn_A(qT_ap, kbanks):
            for b, (kT_ap, bias_fn) in enumerate(kbanks):
                w = kT_ap.shape[1]
                K.mm(K.psb[b][:, 0:w], qT_ap, kT_ap)

        def attn_C(kbanks, tag):
            nb = len(kbanks)
            mx = K.arotbuf("a_mx", [8], F32)
            srcs = []
            col = 0
            for b, (kT_ap, bias_fn) in enumerate(kbanks):
                w = kT_ap.shape[1]
                src = K.psb[b][:, 0:w]
                if bias_fn is not None:
                    src = bias_fn(K.psb[b], w)
                srcs.append((src, col, w))
                K.red(mx[:, b:b + 1], src, ALU.max)
                col += w
            nm = K.arotbuf("a_nm", [1], F32)
            K.red(nm[:, 0:1], mx[:, 0:nb], ALU.max)
            K.ts(nm[:, 0:1], nm[:, 0:1], -1.0, None, ALU.mult)
            P = K.arotbuf("a_P" + tag, [PW[tag]], BF16)
            sm = K.arotbuf("a_sm", [8], F32)
            K.memset(sm[:], 0.0, eng="pool")
            for i_, (src, c0_, w) in enumerate(srcs):
                K.act(P[:, c0_:c0_ + w], src, AF.Exp, bias=nm[:, 0:1], accum_out=sm[:, i_:i_ + 1])
            rs = K.arotbuf("a_rs", [1], F32)
            K.red(rs[:, 0:1], sm[:, 0:nb], ALU.add)
            K.recip(rs[:, 0:1], rs[:, 0:1])
            engs = ["dve", "act", "dve", "act", "dve"] if nb > 3 else ["dve", "act", "dve"]
            for i_, (src, c0_, w) in enumerate(srcs):
                e_ = engs[i_]
                if e_ == "act":
                    K.act(P[:, c0_:c0_ + w], P[:, c0_:c0_ + w], AF.Identity, scale=rs[:, 0:1])
                else:
                    K.ts(P[:, c0_:c0_ + w], P[:, c0_:c0_ + w], rs[:, 0:1], None, ALU.mult, eng=e_)
            return P

        def attn_B(st_):
            P, ktiles, hs, dst, tag = st_
            PT = K.arotbuf("a_PT" + tag, [NTT[tag], 128], BF16)
            nkt = len(ktiles)
            for g in range(0, nkt, 8):
                cnt = min(8, nkt - g)
                pbT = K.psb[5 + ATT["tb"] % 2].bitcast(BF16)
                ATT["tb"] += 1
                for i_ in range(cnt):
                    pc0, n_, _ = ktiles[g + i_]
                    K.tr(pbT[0:n_, i_ * 128:(i_ + 1) * 128], P[:, pc0:pc0 + n_], identb[:])
                full = all(ktiles[g + i_][1] == 128 for i_ in range(cnt))
                ce = "dve" if ATT["tb"] % 2 == 0 else "act"
                if full:
                    K.copy(PT[:, g:g + cnt, :], pbT[:, 0:cnt * 128].rearrange("p (a b) -> p a b", a=cnt), eng=ce)
                else:
                    for i_ in range(cnt):
                        n_ = ktiles[g + i_][1]
                        K.copy(PT[0:n_, g + i_, :], pbT[0:n_, i_ * 128:(i_ + 1) * 128], eng=ce)
            po = K.psb[7][:, 0:128]
            for i_, (pc0, n_, vl) in enumerate(ktiles):
                K.mm(po, vl, PT[0:n_, i_, :], start=(i_ == 0), stop=(i_ == nkt - 1))

        def attn_D(st_):
            P, ktiles, hs, dst, tag = st_
            K.copy(dst, K.psb[7][hs, 0:128], eng="dve")

        def attn_unit(qT_ap, kbanks, ktiles, hs, dst, tag):
            prev = ATT["prev"]
            attn_A(qT_ap, kbanks)
            if prev is not None:
                attn_B(prev)
            P = attn_C(kbanks, tag)
            if prev is not None:
                attn_D(prev)
            ATT["prev"] = (P, ktiles, hs, dst, tag)

        def attn_flush():
            if ATT["prev"] is not None:
                attn_B(ATT["prev"])
                attn_D(ATT["prev"])
                ATT["prev"] = None

        def out_proj_post(w_out_l, yT, gg):
            wo = K.af([8, 1024], BF16)
            K.dma(wo[:], w_out_l.rearrange("(k p) o -> p k o", p=128), q="pool")
            for t0 in range(0, T, 512):
                yf = K.arotbuf("yf", [8, 512], F32, n=1)
                for c in range(8):
                    pb = K.psum()
                    for k in range(8):
                        K.mm(pb[:], wo[:, k, c * 128:(c + 1) * 128], yT[:, k, t0:t0 + 512], start=(k == 0), stop=(k == 7))
                    K.copy(yf[:, c, :], pb[:], eng="act")
                post_block(t0, 512, 0 if t0 < NPT else 1, gg, yf[:])

        def even_layer(layer, j, mod):
            gs, shift, gg = mod_vectors(layer, mod, 0)
            K.phase()
            hT = K.af([8, T], BF16)
            yT = K.af([8, T], BF16)
            m0 = K.mark()
            S.tag = "L%d:e_pre" % layer
            for t0 in range(0, T, 512):
                pre_block(t0, 512, 0 if t0 < NPT else 1, gs, shift, hT[:, :, t0:t0 + 512])
            K.reset_to(m0)
            S.tag = "L%d:e_ret" % layer
            w_in_v = even_w_in[j].rearrange("(k p) o -> p k o", p=128)
            rconst = K.af([772], F32)
            K.dma(rconst[:], rconst_d)
            RC = lambda i: rconst[:, i * 128:(i + 1) * 128]
            lg = K.af([8], F32)
            K.dma(lg[:], ret_logit_b[j])
            K.act(lg[:], lg[:], AF.Exp, scale=-1.0)
            K.act(lg[:], lg[:], AF.Ln, bias=oneT[:, 0:1])
            K.ts(lg[:], lg[:], -1.0, None, ALU.mult)
            qTh = K.af([T], BF16)
            kTh = K.af([T], BF16)
            sgTh = K.af([T], BF16)
            kvtok = K.af([20, 256], BF16)
            kwfT = K.af([20, 128], BF16)
            kwbT = K.af([20, 128], BF16)
            Sfs = K.af([20, 128], BF16)
            Sbs = K.af([20, 128], BF16)
            o_sb = K.af([T], F32)
            Dm = K.af([128], F32)
            e2 = K.af([128], F32)
            Wqf = K.af([128], F32)
            Wqb = K.af([128], F32)
            kw = K.af([4], F32)
            for h in range(4):
                lf = lg[:, h:h + 1]
                lb = lg[:, 4 + h:5 + h]
                K.act(Dm[:], RC(0), AF.Exp, scale=lf)
                K.tt(Dm[:], Dm[:], RC(1), ALU.mult)
                K.act(e2[:], RC(2), AF.Exp, scale=lb)
                K.tt(e2[:], e2[:], RC(3), ALU.mult)
                K.tt(Dm[:], Dm[:], e2[:], ALU.add)
                K.act(Wqf[:], RC(4), AF.Exp, scale=lf)
                K.act(Wqb[:], RC(5), AF.Exp, scale=lb)
                K.act(kw[:, 0:1], rconst[:, 768:769], AF.Exp, scale=lf)
                K.act(kw[:, 1:2], rconst[:, 769:770], AF.Exp, scale=lb)
                K.act(kw[:, 2:3], rconst[:, 770:771], AF.Exp, scale=lf)
                K.act(kw[:, 3:4], rconst[:, 770:771], AF.Exp, scale=lb)
                wh = K.arotbuf("wh", [8, 4, 128], BF16, n=2)
                for si_ in range(4):
                    K.dma(wh[:, :, si_, :], w_in_v[:, :, si_ * 512 + h * 128:si_ * 512 + (h + 1) * 128], q="pool")
                for t0 in range(0, T, 512):
                    for si, dst in ((0, qTh), (1, kTh), (3, sgTh)):
                        pb = K.psum()
                        for k in range(8):
                            K.mm(pb[:], wh[:, k, si, :], hT[:, k, t0:t0 + 512], start=(k == 0), stop=(k == 7))
                        if si == 0:
                            K.act(dst[:, t0:t0 + 512], pb[:], AF.Identity, scale=RS)
                        elif si == 1:
                            K.copy(dst[:, t0:t0 + 512], pb[:], eng="dve")
                        else:
                            K.act(dst[:, t0:t0 + 512], pb[:], AF.Silu)
                for ti in range(20):
                    pb = K.psum()
                    for k in range(8):
                        K.mm(pb[:, 0:256].rearrange("p (a b) -> p a b", a=2), hT[:, k, ti * 128:(ti + 1) * 128], wh[:, k, 1:3, :], start=(k == 0), stop=(k == 7))
                    K.copy(kvtok[:, ti, :], pb[:, 0:256], eng="act")
                    K.ts(kwfT[:, ti, :], pb[:, 0:128], kw[:, 0:1], None, ALU.mult)
                    K.ts(kwbT[:, ti, :], pb[:, 0:128], kw[:, 1:2], None, ALU.mult)
                for (c0, L, smp) in SEQS:
                    nch = L // 128
                    tb = c0 // 128
                    for d in range(2):
                        Sx = K.arotbuf("Sx", [128], F32, n=2)
                        if smp:
                            K.dma(Sx[:], state_ret_in[j, d, h])
                        else:
                            K.memset(Sx[:], 0.0, eng="pool")
                        store = Sfs if d == 0 else Sbs
                        kwT = kwfT if d == 0 else kwbT
                        order = range(nch) if d == 0 else range(nch - 1, -1, -1)
                        for n in order:
                            ti = tb + n
                            K.copy(store[:, ti, :], Sx[:], eng="pool")
                            pb = K.psum()
                            K.mm(pb[:, 0:128], kwT[:, ti, :], kvtok[:, ti, 128:256])
                            K.stt(Sx[:], Sx[:], kw[:, 2 + d:3 + d], pb[:, 0:128], ALU.mult, ALU.add)
                        if not smp:
                            finals.append(K.dma(o_ret[c0 // 256, j, d, h], Sx[:]))
                    for n0 in range(0, nch, 4):
                        pbo = K.psum()
                        cnt = min(4, nch - n0)
                        for n in range(n0, n0 + cnt):
                            ti = tb + n
                            cs = slice(ti * 128, (ti + 1) * 128)
                            pbs = K.psum()
                            K.mm(pbs[:, 0:128], kTh[:, cs], qTh[:, cs])
                            MT = K.arotbuf("MT", [128], BF16, n=2)
                            K.tt(MT[:], pbs[:, 0:128], Dm[:], ALU.mult)
                            qf = K.arotbuf("qf", [128], BF16, n=2)
                            qb = K.arotbuf("qb", [128], BF16, n=2)
                            K.tt(qf[:], qTh[:, cs], Wqf[:], ALU.mult, eng="pool")
                            K.tt(qb[:], qTh[:, cs], Wqb[:], ALU.mult, eng="pool")
                            oc = pbo[:, (n - n0) * 128:(n - n0 + 1) * 128]
                            K.mm(oc, kvtok[:, ti, 128:256], MT[:], start=True, stop=False)
                            K.mm(oc, Sfs[:, ti, :], qf[:], start=False, stop=False)
                            K.mm(oc, Sbs[:, ti, :], qb[:], start=False, stop=True)
                        K.copy(o_sb[:, (tb + n0) * 128:(tb + n0 + cnt) * 128], pbo[:, 0:cnt * 128], eng="act")
                for t0 in range(0, T, 512):
                    ob = o_sb[:, t0:t0 + 512]
                    o2 = K.arotbuf("gn_o2", [512], F32, n=1)
                    K.tt(o2[:], ob, ob, ALU.mult)
                    pm = K.psum()
                    pq = K.psum()
                    for src_, pdst in ((ob, pm), (o2[:], pq)):
                        hi = K.arotbuf("gn_hi", [512], BF16, n=2)
                        lo = K.arotbuf("gn_lo", [512], BF16, n=2)
                        K.copy(hi[:], src_, eng="act")
                        K.tt(lo[:], src_, hi[:], ALU.subtract)
                        K.mm(pdst[:], ones128b[:], hi[:], start=True, stop=False)
                        K.mm(pdst[:], ones128b[:], lo[:], start=False, stop=True)
                    mean = K.arotbuf("gn_mean", [512], F32, n=1)
                    K.copy(mean[:], pm[:], eng="act")
                    msq = K.arotbuf("gn_msq", [512], F32, n=1)
                    K.tt(msq[:], mean[:], mean[:], ALU.mult, eng="pool")
                    K.tt(msq[:], pq[:], msq[:], ALU.subtract)
                    K.act(msq[:], msq[:], AF.Sqrt, bias=epsT[:, 0:1])
                    K.recip(msq[:], msq[:])
                    K.tt(o2[:], ob, mean[:], ALU.subtract, eng="pool")
                    K.tt(o2[:], o2[:], msq[:], ALU.mult)
                    K.stt(yT[:, h, t0:t0 + 512], o2[:], gnT[:, j * 4 + h:j * 4 + h + 1], sgTh[:, t0:t0 + 512], ALU.mult, ALU.mult)
            K.reset_to(m0)
            S.tag = "L%d:e_mlaproj" % layer
            cqn = K.af([2, T], BF16)
            ckvK = K.af([3072], BF16)
            krK = K.af([3072], BF16)
            ropeC = K.af([2048], F32)
            ropeS = K.af([2048], F32)
            K.dma(ropeC[64:96, :], ropeC_d)
            K.dma(ropeS[64:96, :], ropeS_d)
            m1 = K.mark()
            kvn_bc = K.af([256], F32)
            K.dma(kvn_bc[:], kvn_bc_d)
            wm = K.af([8, 416], BF16)
            K.dma(wm[:], w_in_v[:, :, 2048:2464], q="pool")
            wkr = K.af([8, 96], BF16)
            wkrs = K.af([8, 96], BF16)
            K.memset(wkr[:], 0.0, eng="pool")
            K.memset(wkrs[:], 0.0, eng="pool")
            K.dma(wkr[:, :, 64:96], w_in_v[:, :, 2432:2464], q="pool")
            K.dma(wkrs[:, :, 64:80], w_in_v[:, :, 2448:2464], q="pool")
            K.dma(wkrs[:, :, 80:96], w_in_v[:, :, 2432:2448], q="pool")
            for t0 in range(0, T, 512):
                bl = slice(t0, t0 + 512)
                pbs = [K.psum(), K.psum()]
                for cc in range(2):
                    for k in range(8):
                        K.mm(pbs[cc][:], wm[:, k, cc * 128:(cc + 1) * 128], hT[:, k, bl], start=(k == 0), stop=(k == 7))
                sq = K.arotbuf("m_sq", [2, 512], BF16, n=1)
                for cc in range(2):
                    K.act(sq[:, cc, :], pbs[cc][:], AF.Square)
                pr = K.psum()
                for cc in range(2):
                    K.mm(pr[:], ones256[:], sq[:, cc, :], start=(cc == 0), stop=(cc == 1))
                rstd = K.arotbuf("m_rstd", [512], F32, n=1)
                K.act(rstd[:], pr[:], AF.Sqrt, bias=epsT[:, 0:1])
                K.recip(rstd[:], rstd[:])
                for cc in range(2):
                    K.stt(cqn[:, cc, bl], pbs[cc][:], qnT[:, j * 2 + cc:j * 2 + cc + 1], rstd[:], ALU.mult, ALU.mult)
                pc = K.psum()
                for k in range(8):
                    K.mm(pc[:], wm[:, k, 256:384], hT[:, k, bl], start=(k == 0), stop=(k == 7))
                K.act(sq[:, 0, :], pc[:], AF.Square)
                pr2 = K.psum()
                K.mm(pr2[:], ones128b[:], sq[:, 0, :])
                rstd2 = K.arotbuf("m_rstd2", [512], F32, n=1)
                K.act(rstd2[:], pr2[:], AF.Sqrt, bias=epsT[:, 0:1])
                K.recip(rstd2[:], rstd2[:])
                K.stt(ckvK[:, bl], pc[:], kvnT[:, j:j + 1], rstd2[:], ALU.mult, ALU.mult)
                pk = K.psum()
                for k in range(8):
                    K.mm(pk[0:96, :], wkr[:, k, :], hT[:, k, bl], start=(k == 0), stop=(k == 7))
                if t0 < NPT:
                    K.copy(krK[64:96, bl], pk[64:96, :], eng="dve")
                else:
                    pks = K.psum()
                    for k in range(8):
                        K.mm(pks[0:96, :], wkrs[:, k, :], hT[:, k, bl], start=(k == 0), stop=(k == 7))
                    st0 = t0 - NPT
                    t1 = K.arotbuf("m_t1", [512], F32, n=1)
                    t2 = K.arotbuf("m_t2", [512], F32, n=1)
                    K.tt(t1[64:96, :], pk[64:96, :], ropeC[64:96, st0:st0 + 512], ALU.mult)
                    K.tt(t2[64:96, :], pks[64:96, :], ropeS[64:96, st0:st0 + 512], ALU.mult)
                    K.tt(krK[64:96, bl], t1[64:96, :], t2[64:96, :], ALU.add, eng="pool")
            for i in range(4):
                stg = K.arotbuf("m_stg", [128], F32, n=2)
                K.dma(stg[:], ckv_in[j, i * 128:(i + 1) * 128, :])
                pb = K.psum()
                K.tr(pb[:, 0:128], stg[:], ident[:])
                K.copy(ckvK[:, 2560 + i * 128:2560 + (i + 1) * 128], pb[:, 0:128], eng="dve")
                stg2 = K.arotbuf("m_stg2", [96], F32, n=2)
                K.memset(stg2[:, 0:64], 0.0, eng="pool")
                K.dma(stg2[:, 64:96], kr_in[j, i * 128:(i + 1) * 128, :])
                pb2 = K.psum()
                K.tr(pb2[0:96, 0:128], stg2[:], ident[:])
                K.copy(krK[64:96, 2560 + i * 128:2560 + (i + 1) * 128], pb2[64:96, 0:128], eng="dve")
            for ti in range(4):
                pt = K.psum()
                for k in range(8):
                    K.mm(pt[:, 0:160], hT[:, k, ti * 128:(ti + 1) * 128], wm[:, k, 256:416], start=(k == 0), stop=(k == 7))
                junk = K.arotbuf("m_junk", [128], F32, n=1)
                ssq = K.arotbuf("m_ssq", [1], F32, n=2)
                K.memset(ssq[:], 0.0, eng="pool")
                K.act(junk[:], pt[:, 0:128], AF.Square, accum_out=ssq[:, 0:1])
                K.act(ssq[:], ssq[:], AF.Sqrt, bias=epsT[:, 0:1], scale=1.0 / 128.0)
                K.recip(ssq[:], ssq[:])
                ock = K.arotbuf("m_ock", [128], F32, n=2)
                K.stt(ock[:], pt[:, 0:128], ssq[:, 0:1], kvn_bc[:, j * 128:(j + 1) * 128], ALU.mult, ALU.mult)
                sq_i, r0 = ti // 2, (ti % 2) * 128
                finals.append(K.dma(o_ckv[sq_i, j, r0:r0 + 128, :], ock[:]))
                okr = K.arotbuf("m_okr", [32], F32, n=2)
                K.copy(okr[:], pt[:, 128:160], eng="act")
                finals.append(K.dma(o_kr[sq_i, j, r0:r0 + 128, :], okr[:]))
            K.reset_to(m1)
            S.tag = "L%d:e_mlaattn" % layer
            wuq = K.af([2, 768], BF16)
            K.dma(wuq[:], mla_w_uq[j].rearrange("(k p) o -> p k o", p=128), q="pool")
            wuqs = K.af([2, 8, 96], BF16)
            K.memset(wuqs[:], 0.0, eng="pool")
            uqv = mla_w_uq[j].rearrange("(k p) (h x) -> p k h x", p=128, h=8)
            for k_ in range(2):
                K.dma(wuqs[:, k_, :, 64:80], uqv[:, k_, :, 80:96], q="pool")
                K.dma(wuqs[:, k_, :, 80:96], uqv[:, k_, :, 64:80], q="pool")
            wukv = K.af([1024], BF16)
            K.dma(wukv[:], mla_w_ukv[j], q="pool")
            wukv_v = wukv[:].rearrange("p (h x) -> p h x", h=8)
            v_all = K.af([20, 512], BF16)
            for (c0, L, smp) in SEQS:
                attn_flush()
                nk = L + (512 if smp else 0)
                nkt = nk // 128
                for kt in range(nkt):
                    pb = K.psum()
                    K.mm(pb[:].rearrange("p (h x) -> p h x", h=8), ckvK[:, c0 + kt * 128:c0 + (kt + 1) * 128], wukv_v[:, :, 64:128])
                    K.copy(v_all[:, kt, :], pb[:], eng="act")
                for h in range(8):
                    kh = K.arotbuf("kh", [2560], BF16, n=1)
                    for kb in range(0, nk, 512):
                        w = min(512, nk - kb)
                        pb = K.psum()
                        K.mm(pb[0:64, 0:w], wukv[:, h * 128:h * 128 + 64], ckvK[:, c0 + kb:c0 + kb + w])
                        K.copy(kh[0:64, kb:kb + w], pb[0:64, 0:w], eng="dve")
                    K.copy(kh[64:96, 0:nk], krK[64:96, c0:c0 + nk], eng="pool")
                    qh = K.arotbuf("qh", [2048], BF16, n=1)
                    for qb0 in range(0, L, 512):
                        w = min(512, L - qb0)
                        pb = K.psum()
                        for k in range(2):
                            K.mm(pb[0:96, 0:w], wuq[:, k, h * 96:(h + 1) * 96], cqn[:, k, c0 + qb0:c0 + qb0 + w], start=(k == 0), stop=(k == 1))
                        if smp:
                            pb2 = K.psum()
                            for k in range(2):
                                K.mm(pb2[0:96, 0:w], wuqs[:, k, h, :], cqn[:, k, c0 + qb0:c0 + qb0 + w], start=(k == 0), stop=(k == 1))
                            t1 = K.arotbuf("q_t1", [512], F32, n=1)
                            t2 = K.arotbuf("q_t2", [512], F32, n=1)
                            K.tt(t1[64:96, 0:w], pb[64:96, 0:w], ropeC[64:96, qb0:qb0 + w], ALU.mult)
                            K.tt(t2[64:96, 0:w], pb2[64:96, 0:w], ropeS[64:96, qb0:qb0 + w], ALU.mult)
                            K.tt(t1[64:96, 0:w], t1[64:96, 0:w], t2[64:96, 0:w], ALU.add, eng="pool")
                            K.act(qh[0:64, qb0:qb0 + w], pb[0:64, 0:w], AF.Identity, scale=MLA_SCALE)
                            K.act(qh[64:96, qb0:qb0 + w], t1[64:96, 0:w], AF.Identity, scale=MLA_SCALE)
                        else:
                            K.act(qh[0:96, qb0:qb0 + w], pb[0:96, 0:w], AF.Identity, scale=MLA_SCALE)
                    half = h % 2
                    hs = slice(half * 64, half * 64 + 64)
                    kbanks = [(kh[0:96, kb:min(kb + 512, nk)], None) for kb in range(0, nk, 512)]
                    ktiles = [(kt * 128, 128, v_all[:, kt, (h // 2) * 128:(h // 2) * 128 + 128]) for kt in range(nkt)]
                    for qt in range(L // 128):
                        attn_unit(qh[0:96, qt * 128:(qt + 1) * 128], kbanks, ktiles,
                                  hs, yT[hs, 4 + h // 2, c0 + qt * 128:c0 + (qt + 1) * 128], "m")
            attn_flush()
            K.reset_to(m0)
            S.tag = "L%d:e_out" % layer
            out_proj_post(even_w_out[j], yT, gg)

        NA_SCALE = 0.125
        NEGV = -30000.0
        TWO_PI = 2.0 * math.pi
        CW1 = 6.28125
        CW2 = TWO_PI - 6.28125
        SIN_S = 1.0 - 2e-6
        GELU_C = 1.5957691216057308

        def sincos(ang, kiI, kf, sin_out, cos_out):
            K.ts(kiI, ang, 1.0 / TWO_PI, None, ALU.mult)
            K.copy(kf, kiI)
            K.stt(ang, kf, -CW1, ang, ALU.mult, ALU.add)
            K.stt(ang, kf, -CW2, ang, ALU.mult, ALU.add)
            K.ts(ang, ang, -3.1415925, 3.1415925, ALU.max, ALU.min)
            K.act(sin_out, ang, AF.Sin, scale=SIN_S)
            K.act(kf, ang, AF.Abs)
            K.act(cos_out, kf, AF.Sin, scale=-SIN_S, bias=halfpi[:, 0:1])

        def scan(out, d0, d1, init):
            rd = [d0, d1] + ([] if isinstance(init, float) else [init])
            return S.op("dve", lambda e: e.tensor_tensor_scan(out=out, data0=d0, data1=d1, initial=init, op0=ALU.mult, op1=ALU.add),
                        reads=rd, writes=[out])

        def odd_layer(layer, j, mod):
            gs, shift, gg = mod_vectors(layer, mod, 0)
            K.phase()
            yT = K.af([8, T], BF16)
            m0 = K.mark()
            w_in_v = odd_w_in[j].rearrange("(k p) o -> p k o", p=128)

            def make_hT():
                hT = K.af([8, T], BF16)
                mk = K.mark()
                for t0 in range(0, T, 512):
                    pre_block(t0, 512, 0 if t0 < NPT else 1, gs, shift, hT[:, :, t0:t0 + 512])
                K.reset_to(mk)
                return hT

            S.tag = "L%d:o_s5" % layer
            uT = K.af([4, T], BF16)
            mS = K.mark()
            hT = make_hT()
            wu = K.af([8, 512], BF16)
            K.dma(wu[:], w_in_v[:, :, 0:512], q="pool")
            for t0 in range(0, T, 512):
                for c in range(4):
                    pb = K.psum()
                    for k in range(8):
                        K.mm(pb[:], wu[:, k, c * 128:(c + 1) * 128], hT[:, k, t0:t0 + 512], start=(k == 0), stop=(k == 7))
                    K.copy(uT[:, c, t0:t0 + 512], pb[:], eng=("act" if c % 2 else "dve"))
            K.reset_to(mS)
            ygT = K.af([4, T], BF16)
            iotaF = K.af([2048], F32)
            K.dma(iotaF[:], iota_d)
            mT = K.mark()
            y_acc = K.af([T], F32)
            e_tb = K.ptr // 4
            tb = K.af([3, 2048], F32)
            ang = tb[:, 0, :]
            kiI = K.arI[:, e_tb + 2048:e_tb + 4096]
            kf = tb[:, 2, :]
            gbuf = [tb[:, 1, :], tb[:, 2, :]]
            sinT = K.af([2048], F32)
            cosT = K.af([2048], F32)
            rtile = tb[:, 0, :]
            bp = K.af([2, 2048], F32)
            Hb = K.af([2, 2048], BF16)
            prm = [K.af([1104], F32) for _ in range(2)]
            der = K.af([2, 14, 16], F32)
            bbt = K.af([2, 2, 16, 16], F32)
            cng = K.af([2, 16, 16], F32)
            tmp3 = K.af([16, 16], F32)
            hfin = K.af([2, 2, 2, 16], F32)
            e_s = K.ptr // 4
            sml = K.af([3, 16], F32)
            smlI = K.arI[:, e_s + 16:e_s + 32]
            for d in range(2):
                P = prm[d]
                K.dma(P[:], s5p[j, d])
                lre, lim, lst = P[:, 0:16], P[:, 16:32], P[:, 32:48]
                bre3 = P[:, 80:336].rearrange("p (s c) -> p s c", s=16)
                bim3 = P[:, 336:592].rearrange("p (s c) -> p s c", s=16)
                cim3 = P[:, 848:1104].rearrange("p (s c) -> p s c", s=16)
                Dv = lambda i, d=d: der[:, d, i, :]
                K.act(Dv(0), lst, AF.Exp)
                K.tt(Dv(1), lre, Dv(0), ALU.mult)
                K.act(Dv(1), Dv(1), AF.Exp)
                K.tt(Dv(2), lim, Dv(0), ALU.mult)
                K.copy(sml[:, 0, :], Dv(2))
                sincos(sml[:, 0, :], smlI, sml[:, 2, :], Dv(3), Dv(4))
                K.tt(Dv(5), Dv(1), Dv(4), ALU.mult)
                K.tt(Dv(6), Dv(1), Dv(3), ALU.mult)
                K.ts(Dv(5), Dv(5), -1.0, None, ALU.add)
                K.tt(Dv(7), lre, lre, ALU.mult)
                K.tt(Dv(8), lim, lim, ALU.mult)
                K.tt(Dv(7), Dv(7), Dv(8), ALU.add)
                K.recip(Dv(7), Dv(7))
                K.tt(Dv(8), Dv(5), lre, ALU.mult)
                K.tt(Dv(9), Dv(6), lim, ALU.mult)
                K.tt(Dv(8), Dv(8), Dv(9), ALU.add)
                K.tt(Dv(8), Dv(8), Dv(7), ALU.mult)
                K.tt(Dv(9), Dv(6), lre, ALU.mult)
                K.tt(Dv(10), Dv(5), lim, ALU.mult)
                K.tt(Dv(9), Dv(9), Dv(10), ALU.subtract)
                K.tt(Dv(9), Dv(9), Dv(7), ALU.mult)
                K.ts(Dv(13), Dv(2), 1.0 / TWO_PI, None, ALU.mult)
                K.ts(Dv(11), Dv(13), -1.0, None, ALU.mult)
                K.ts(Dv(12), Dv(13), 2049.0, None, ALU.mult)
                zre_b = Dv(8).unsqueeze(2).to_broadcast([128, 16, 16])
                zim_b = Dv(9).unsqueeze(2).to_broadcast([128, 16, 16])
                K.tt(bbt[:, d, 0], bre3, zre_b, ALU.mult)
                K.tt(tmp3[:], bim3, zim_b, ALU.mult)
                K.tt(bbt[:, d, 0], bbt[:, d, 0], tmp3[:], ALU.subtract)
                K.tt(bbt[:, d, 1], bim3, zre_b, ALU.mult)
                K.tt(tmp3[:], bre3, zim_b, ALU.mult)
                K.tt(bbt[:, d, 1], bbt[:, d, 1], tmp3[:], ALU.add)
                K.ts(cng[:, d], cim3, -1.0, None, ALU.mult)
            for c in range(4):
                first = True
                for d in range(2):
                    P = prm[d]
                    cre3 = P[:, 592:848].rearrange("p (s c) -> p s c", s=16)
                    for s_ in range(4 * c, 4 * c + 4):
                        base = (s_ % 4) * 32
                        if d == 0:
                            K.ts(ang, iotaF[:], der[:, d, 13, s_:s_ + 1], None, ALU.mult, eng="pool")
                        else:
                            K.ts(ang, iotaF[:], der[:, d, 11, s_:s_ + 1], der[:, d, 12, s_:s_ + 1], ALU.mult, ALU.add, eng="pool")
                        K.ts(kiI, ang, 1.0, None, ALU.mult)
                        K.copy(kf, kiI)
                        K.tt(ang, ang, kf, ALU.subtract)
                        K.act(sinT[:], ang, AF.Sin, scale=TWO_PI * SIN_S)
                        K.act(kf, ang, AF.Abs)
                        K.act(cosT[:], kf, AF.Sin, scale=-TWO_PI * SIN_S, bias=halfpi[:, 0:1])
                        K.ts(rtile[:], iotaF[:], 0.0, der[:, d, 1, s_:s_ + 1], ALU.mult, ALU.add, eng="pool")
                        BBT = []
                        for ri in range(2):
                            Wp = K.arotbuf("s5_Wp", [128], F32, n=2)
                            K.memset(Wp[:], 0.0, eng="pool")
                            K.copy(Wp[0:64, base:base + 16], bbt[0:64, d, ri, s_, :], eng="pool")
                            K.copy(Wp[64:128, base + 16:base + 32], bbt[64:128, d, ri, s_, :], eng="pool")
                            pb = K.psum()
                            K.tr(pb[:, 0:128], Wp[:], ident[:])
                            bt = K.arotbuf("s5_BBT%d" % ri, [128], BF16, n=2)
                            K.copy(bt[:], pb[:, 0:128], eng="act")
                            BBT.append(bt)
                        CP = []
                        for ri, csrc in ((0, cre3), (1, cng[:, d])):
                            cp = K.arotbuf("s5_Cp%d" % ri, [128], BF16, n=2)
                            K.memset(cp[:], 0.0, eng="pool")
                            K.copy(cp[0:64, base:base + 16], csrc[0:64, s_, :], eng="pool")
                            K.copy(cp[64:128, base + 16:base + 32], csrc[64:128, s_, :], eng="pool")
                            CP.append(cp)
                        for seg in range(2):
                            if seg == 0:
                                c0, L, nseq = 0, 256, 2
                            else:
                                c0, L, nseq = 512, 2048, 1
                            tab0 = 0 if d == 0 else 2048 - L
                            Wt = L * nseq
                            if seg == 0:
                                v3 = lambda ap: ap.rearrange("p (q t) -> p q t", q=2)
                                tabv = lambda tb_, b0, w: tb_[:, tab0:tab0 + 256].unsqueeze(1).to_broadcast([128, 2, 256])
                            else:
                                v3 = lambda ap: ap
                                tabv = lambda tb_, b0, w: tb_[:, tab0 + b0:tab0 + b0 + w]
                            for b0 in range(0, Wt, 512):
                                w = min(512, Wt - b0)
                                pr = K.psum()
                                pi_ = K.psum()
                                K.mm(pr[:, 0:w], BBT[0][:], uT[:, c, c0 + b0:c0 + b0 + w])
                                K.mm(pi_[:, 0:w], BBT[1][:], uT[:, c, c0 + b0:c0 + b0 + w])
                                cosv = tabv(cosT, b0, w)
                                sinv = tabv(sinT, b0, w)
                                t1 = K.arotbuf("s5_t1", [512], F32, n=1)
                                t2 = K.arotbuf("s5_t2", [512], F32, n=1)
                                t3 = K.arotbuf("s5_t3", [512], F32, n=1)
                                t4 = K.arotbuf("s5_t4", [512], F32, n=1)
                                K.tt(v3(t1[:, 0:w]), v3(pr[:, 0:w]), cosv, ALU.mult)
                                K.tt(v3(t2[:, 0:w]), v3(pi_[:, 0:w]), sinv, ALU.mult)
                                K.tt(bp[:, 0, b0:b0 + w], t1[:, 0:w], t2[:, 0:w], ALU.add, eng="pool")
                                K.tt(v3(t3[:, 0:w]), v3(pi_[:, 0:w]), cosv, ALU.mult)
                                K.tt(v3(t4[:, 0:w]), v3(pr[:, 0:w]), sinv, ALU.mult)
                                K.tt(bp[:, 1, b0:b0 + w], t3[:, 0:w], t4[:, 0:w], ALU.subtract, eng="pool")
                            for q_ in range(nseq):
                                qs = slice(q_ * L, (q_ + 1) * L)
                                for ri in range(2):
                                    init = P[:, 48 + ri * 16 + s_:48 + ri * 16 + s_ + 1] if seg == 1 else 0.0
                                    if d == 0:
                                        scan(gbuf[ri][:, qs], rtile[:, 0:L], bp[:, ri, qs], init)
                                    else:
                                        scan(gbuf[ri][:, qs][:, ::-1], rtile[:, 0:L][:, ::-1], bp[:, ri, qs][:, ::-1], init)
                            for b0 in range(0, Wt, 512):
                                w = min(512, Wt - b0)
                                cosv = tabv(cosT, b0, w)
                                sinv = tabv(sinT, b0, w)
                                t1 = K.arotbuf("s5_t1", [512], F32, n=1)
                                t2 = K.arotbuf("s5_t2", [512], F32, n=1)
                                t3 = K.arotbuf("s5_t3", [512], F32, n=1)
                                t4 = K.arotbuf("s5_t4", [512], F32, n=1)
                                K.tt(v3(t1[:, 0:w]), v3(gbuf[0][:, b0:b0 + w]), cosv, ALU.mult, eng="pool")
                                K.tt(v3(t2[:, 0:w]), v3(gbuf[1][:, b0:b0 + w]), sinv, ALU.mult, eng="pool")
                                K.tt(Hb[:, 0, b0:b0 + w], t1[:, 0:w], t2[:, 0:w], ALU.subtract)
                                K.tt(v3(t3[:, 0:w]), v3(gbuf[0][:, b0:b0 + w]), sinv, ALU.mult)
                                K.tt(v3(t4[:, 0:w]), v3(gbuf[1][:, b0:b0 + w]), cosv, ALU.mult)
                                K.tt(Hb[:, 1, b0:b0 + w], t3[:, 0:w], t4[:, 0:w], ALU.add)
                                py = K.psum()
                                K.mm(py[:, 0:w], CP[0][:], Hb[:, 0, b0:b0 + w], start=True, stop=False)
                                K.mm(py[:, 0:w], CP[1][:], Hb[:, 1, b0:b0 + w], start=False, stop=True)
                                ya = y_acc[:, c0 + b0:c0 + b0 + w]
                                if first:
                                    K.copy(ya, py[:, 0:w], eng="act")
                                else:
                                    K.tt(ya, ya, py[:, 0:w], ALU.add)
                            if seg == 0:
                                col = L - 1 if d == 0 else 0
                                tc_ = tab0 + col
                                g0c = gbuf[0][:, 0:512].rearrange("p (q t) -> p q t", q=2)[:, :, col]
                                g1c = gbuf[1][:, 0:512].rearrange("p (q t) -> p q t", q=2)[:, :, col]
                                cc_ = cosT[:, tc_:tc_ + 1].to_broadcast([128, 2])
                                sc_ = sinT[:, tc_:tc_ + 1].to_broadcast([128, 2])
                                f1 = K.arotbuf("s5_f1", [2], F32, n=2)
                                f2 = K.arotbuf("s5_f2", [2], F32, n=2)
                                K.tt(f1[:], g0c, cc_, ALU.mult, eng="pool")
                                K.tt(f2[:], g1c, sc_, ALU.mult, eng="pool")
                                K.tt(hfin[:, :, d, 0, s_], f1[:], f2[:], ALU.subtract, eng="pool")
                                f1 = K.arotbuf("s5_f1", [2], F32, n=2)
                                f2 = K.arotbuf("s5_f2", [2], F32, n=2)
                                K.tt(f1[:], g0c, sc_, ALU.mult, eng="pool")
                                K.tt(f2[:], g1c, cc_, ALU.mult, eng="pool")
                                K.tt(hfin[:, :, d, 1, s_], f1[:], f2[:], ALU.add, eng="pool")
                        first = False
                for t0 in range(0, T, 512):
                    yb = y_acc[:, t0:t0 + 512]
                    K.stt(yb, uT[:, c, t0:t0 + 512], s5dT[:, j * 4 + c:j * 4 + c + 1], yb, ALU.mult, ALU.add)
                    y2 = K.arotbuf("s5_y2", [512], F32, n=1)
                    K.tt(y2[:], yb, yb, ALU.mult, eng="pool")
                    K.ts(y2[:], y2[:], 0.044715, 1.0, ALU.mult, ALU.add)
                    K.tt(y2[:], y2[:], yb, ALU.mult, eng="pool")
                    K.act(y2[:], y2[:], AF.Sigmoid, scale=GELU_C)
                    K.tt(ygT[:, c, t0:t0 + 512], yb, y2[:], ALU.mult)
            for si in range(2):
                for d in range(2):
                    finals.append(K.dma(o_s5re[si, j, d], hfin[:, si, d, 0, :]))
                    finals.append(K.dma(o_s5im[si, j, d], hfin[:, si, d, 1, :]))
            S.tag = "L%d:o_glu" % layer
            K.reset_to(mT)
            wg = K.af([4, 512], BF16)
            K.dma(wg[:], s5_glu_w[j].rearrange("(k p) o -> p k o", p=128), q="pool")
            for t0 in range(0, T, 512):
                for c2 in range(4):
                    pb = K.psum()
                    for k in range(4):
                        K.mm(pb[:], wg[:, k, c2 * 128:(c2 + 1) * 128], ygT[:, k, t0:t0 + 512], start=(k == 0), stop=(k == 3))
                    sg = K.arotbuf("s5_sg", [512], F32, n=2)
                    K.act(sg[:], pb[:], AF.Sigmoid, bias=glubT[:, j * 4 + c2:j * 4 + c2 + 1])
                    K.tt(yT[:, c2, t0:t0 + 512], ygT[:, c2, t0:t0 + 512], sg[:], ALU.mult)

            K.reset_to(m0)
            S.tag = "L%d:o_naproj" % layer
            qT = K.af([4, T], BF16)
            kT = K.af([4, 3072], BF16)
            vtok = K.af([24, 512], BF16)
            mN = K.mark()
            hT = make_hT()
            wq = K.af([8, 512], BF16)
            wk = K.af([8, 512], BF16)
            wv = K.af([8, 512], BF16)
            K.dma(wq[:], w_in_v[:, :, 512:1024], q="pool")
            K.dma(wk[:], w_in_v[:, :, 1024:1536], q="pool")
            K.dma(wv[:], w_in_v[:, :, 1536:2048], q="pool")
            for t0 in range(0, T, 512):
                for c in range(4):
                    pb = K.psum()
                    for k in range(8):
                        K.mm(pb[:], wq[:, k, c * 128:(c + 1) * 128], hT[:, k, t0:t0 + 512], start=(k == 0), stop=(k == 7))
                    K.act(qT[:, c, t0:t0 + 512], pb[:], AF.Identity, scale=NA_SCALE)
                    pb = K.psum()
                    for k in range(8):
                        K.mm(pb[:], wk[:, k, c * 128:(c + 1) * 128], hT[:, k, t0:t0 + 512], start=(k == 0), stop=(k == 7))
                    K.copy(kT[:, c, t0:t0 + 512], pb[:], eng="dve")
            for ti in range(20):
                pb = K.psum()
                for k in range(8):
                    K.mm(pb[:], hT[:, k, ti * 128:(ti + 1) * 128], wv[:, k, :], start=(k == 0), stop=(k == 7))
                K.copy(vtok[:, ti, :], pb[:], eng="act")
                if ti < 4:
                    sq_i, r0 = ti // 2, (ti % 2) * 128
                    ov = K.arotbuf("na_ov", [512], F32, n=2)
                    K.copy(ov[:], pb[:], eng="dve")
                    finals.append(K.dma(o_nav[sq_i, j, r0:r0 + 128, :], ov[:]))
                    pb2 = K.psum()
                    for k in range(8):
                        K.mm(pb2[:], hT[:, k, ti * 128:(ti + 1) * 128], wk[:, k, :], start=(k == 0), stop=(k == 7))
                    ok_ = K.arotbuf("na_ok", [512], F32, n=2)
                    K.copy(ok_[:], pb2[:], eng="act")
                    finals.append(K.dma(o_nak[sq_i, j, r0:r0 + 128, :], ok_[:]))
            for i in range(4):
                stg = K.arotbuf("na_stg", [512], F32, n=2)
                K.dma(stg[:], nak_in[j, i * 128:(i + 1) * 128, :])
                pb = K.psum()
                for c in range(4):
                    K.tr(pb[:, c * 128:(c + 1) * 128], stg[:, c * 128:(c + 1) * 128], ident[:])
                K.copy(kT[:, :, 2560 + i * 128:2560 + (i + 1) * 128], pb[:].rearrange("p (c t) -> p c t", c=4), eng="dve")
                K.dma(vtok[:, 20 + i, :], nav_in[j, i * 128:(i + 1) * 128, :], q="pool")
            K.reset_to(mN)
            Ap = K.af([64, 8, 15], F32)
            rp = K.af([120], F32)
            K.memset(rp[0:32, :], NEGV)
            K.dma(rp[0:31, :], na_rpb[j].rearrange("h r c -> c (h r)"), allow_slow_non_contiguous=True)
            rhi = K.af([120], BF16)
            rlo = K.af([120], BF16)
            K.copy(rhi[0:32, :], rp[0:32, :], eng="act")
            K.tt(rlo[0:32, :], rp[0:32, :], rhi[0:32, :], ALU.subtract)
            for q4 in range(4):
                oh = K.arotbuf("na_oh", [16, 128], BF16, n=2)
                K.dma(oh[0:32, :, :], oh_d[:, q4 * 16:(q4 + 1) * 16, :], q="pool")
                for g4 in range(4):
                    pb = K.psum()
                    for i in range(4):
                        oc = pb[:, i * 120:(i + 1) * 120]
                        K.mm(oc, oh[0:32, g4 * 4 + i, :], rhi[0:32, :], start=True, stop=False)
                        K.mm(oc, oh[0:32, g4 * 4 + i, :], rlo[0:32, :], start=False, stop=True)
                    kc0 = q4 * 16 + g4 * 4
                    K.copy(Ap[:, kc0:kc0 + 4, :, :].rearrange("p k h r -> p k (h r)"), pb[:, 0:480].rearrange("p (k x) -> p k x", k=4),
                           eng=("act" if g4 % 2 else "dve"))
            S.tag = "L%d:o_naattn" % layer
            for (c0, L, smp) in SEQS[0:2]:
                for h in range(8):
                    ch, half = h // 2, h % 2
                    hsl = slice(half * 64, half * 64 + 64)
                    for qt in range(2):
                        qc = slice(c0 + qt * 128, c0 + (qt + 1) * 128)
                        attn_unit(qT[hsl, ch, qc], [(kT[hsl, ch, c0:c0 + 256], None)],
                                  [(t_ * 128, 128, vtok[:, c0 // 128 + t_, ch * 128:(ch + 1) * 128]) for t_ in range(2)],
                                  hsl, yT[hsl, 4 + ch, qc], "n")
            for i in range(16):
                if i < 2:
                    kr0, nr = 0, 8
                elif i > 13:
                    kr0, nr = 24, 8
                else:
                    kr0, nr = 2 * i - 4, 9
                kcol0 = 512 + kr0 * 64
                vt0 = 4 + kr0 // 2
                for h in range(8):
                    ch, half = h // 2, h % 2
                    hsl = slice(half * 64, half * 64 + 64)

                    def biasA(bank, w, i=i, h=h, kr0=kr0):
                        Ssb = K.arotbuf("na_S", [512], F32, n=2)
                        for ql in range(2):
                            qr = 2 * i + ql
                            rs_ = min(max(qr - 4, 0), 24)
                            mlo = max(0, rs_ - kr0)
                            mhi = min(8, rs_ - kr0 + 8)
                            ps_ = slice(ql * 64, ql * 64 + 64)
                            if mlo > 0:
                                K.memset(Ssb[ps_, 0:mlo * 64], NEGV, eng="pool")
                            if mhi < 8:
                                K.memset(Ssb[ps_, mhi * 64:512], NEGV, eng="pool")
                            dr0 = kr0 + mlo - qr + 7
                            nm_ = mhi - mlo
                            K.tt(Ssb[ps_, mlo * 64:mhi * 64].rearrange("p (m k) -> p m k", m=nm_),
                                 bank[ps_, mlo * 64:mhi * 64].rearrange("p (m k) -> p m k", m=nm_),
                                 Ap[ps_, :, h, dr0:dr0 + nm_].rearrange("p k r -> p r k"), ALU.add)
                        return Ssb[:, 0:512]

                    def biasB(bank, w, i=i, h=h, kr0=kr0):
                        Ssb = K.arotbuf("na_SB", [64], F32, n=2)
                        for ql in range(2):
                            qr = 2 * i + ql
                            rs_ = min(max(qr - 4, 0), 24)
                            ps_ = slice(ql * 64, ql * 64 + 64)
                            if rs_ <= kr0 + 8 < rs_ + 8:
                                dr = kr0 + 8 - qr + 7
                                K.tt(Ssb[ps_, 0:64], bank[ps_, 0:64], Ap[ps_, :, h, dr], ALU.add)
                            else:
                                K.memset(Ssb[ps_, 0:64], NEGV, eng="pool")
                        return Ssb[:, 0:64]

                    kbanks = [(kT[hsl, ch, 2560:3072], None), (kT[hsl, ch, kcol0:kcol0 + 512], biasA)]
                    ktiles = [(t_ * 128, 128, vtok[:, 20 + t_, ch * 128:(ch + 1) * 128]) for t_ in range(4)]
                    ktiles += [(512 + t_ * 128, 128, vtok[:, vt0 + t_, ch * 128:(ch + 1) * 128]) for t_ in range(4)]
                    if nr == 9:
                        kbanks.append((kT[hsl, ch, kcol0 + 512:kcol0 + 576], biasB))
                        ktiles.append((1024, 64, vtok[0:64, vt0 + 4, ch * 128:(ch + 1) * 128]))
                    qc = slice(512 + i * 128, 512 + (i + 1) * 128)
                    attn_unit(qT[hsl, ch, qc], kbanks, ktiles, hsl, yT[hsl, 4 + ch, qc], "n")
            attn_flush()
            K.reset_to(m0)
            S.tag = "L%d:o_out" % layer
            out_proj_post(odd_w_out[j], yT, gg)

        for layer in range(DEPTH):
            mod = ada(layer)
            if layer % 2 == 0 and ENABLE_EVEN:
                even_layer(layer, layer // 2, mod)
            if layer % 2 == 1 and ENABLE_ODD:
                odd_layer(layer, layer // 2, mod)
            gs, shift, gg = mod_vectors(layer, mod, 1)
            ffn(layer, gs, shift, gg)
            if dbg_stop == layer:
                break

        S.tag = "final"
        finals = []

        K.phase()

        def xT_to_out(dst, ntile, col0):
            for i in range(ntile):
                xb = K.arotbuf("xinT", [8, 128], F32)
                K.dma(xb[:], xT_v[:, :, col0 + i * 128: col0 + (i + 1) * 128])
                xo = K.arotbuf("xin", [D], F32)
                for g in range(2):
                    pb = K.psum()
                    for c in range(4):
                        K.tr(pb[:, c * 128:(c + 1) * 128], xb[:, g * 4 + c, :], ident[:])
                    K.copy(xo[:, g * 512:(g + 1) * 512], pb[:], eng=("act" if g else "dve"))
                finals.append(K.dma(dst[i * 128:(i + 1) * 128, :], xo[:]))

        xT_to_out(yp, NPT // 128, 0)
        xT_to_out(ys, LS // 128, NPT)
        S.finish(finals)
        global LAST_TAGS
        LAST_TAGS = S.tags
        S.emit()
    return nc


def _make_consts():
    i = np.arange(128, dtype=np.float32)
    jj = i[:, None]
    ii = i[None, :]
    rc = np.zeros((128, 772), np.float32)
    rc[:, 0:128] = np.maximum(ii - jj, 0)
    rc[:, 128:256] = (ii >= jj)
    rc[:, 256:384] = np.maximum(jj - ii, 0)
    rc[:, 384:512] = (jj >= ii)
    rc[:, 512:640] = np.broadcast_to(ii + 1.0, (128, 128))
    rc[:, 640:768] = np.broadcast_to(128.0 - ii, (128, 128))
    rc[:, 768] = 127.0 - i
    rc[:, 769] = i
    rc[:, 770] = 128.0
    inv = (10000.0 ** (-np.arange(8, dtype=np.float32) / np.float32(8))).astype(np.float32)
    t = np.arange(2048)
    row = (t // 64).astype(np.float32)
    col = (t % 64).astype(np.float32)
    ang = np.concatenate([row[:, None] * inv, col[:, None] * inv], axis=-1).astype(np.float32)
    cos, sin = np.cos(ang).T, np.sin(ang).T
    ropeC = np.concatenate([cos, cos], 0).astype(np.float32)
    ropeS = np.concatenate([-sin, sin], 0).astype(np.float32)
    oh = np.zeros((32, 64, 128), np.float32)
    qc = np.arange(64)
    cs = np.clip(qc - 8, 0, 48)
    for kc in range(64):
        jidx = np.clip(kc - qc + 15, 0, 30)
        for ql in range(2):
            oh[jidx, kc, ql * 64 + qc] = 1.0
            oh[31, kc, ql * 64 + qc] = ((kc < cs) | (kc >= cs + 16)).astype(np.float32)
    iota = np.broadcast_to(np.arange(1, 2049, dtype=np.float32)[None, :], (128, 2048))
    return {"rconst": rc, "ropeC": np.ascontiguousarray(ropeC), "ropeS": np.ascontiguousarray(ropeS),
            "oh": oh, "iota": np.ascontiguousarray(iota)}


def _s5_lay(a):
    a = np.asarray(a, np.float32)
    lead = a.shape[:-2] if a.ndim >= 2 else ()
    return a


def _s5_pack(inputs, s):
    def gp(a):
        a = np.asarray(a, np.float32).reshape(2, 2, 16, 2, 64)
        return a.transpose(0, 1, 3, 4, 2).reshape(2, 2, 128, 16)
    lre = gp(inputs["s5_lambda_re"])
    lim = gp(inputs["s5_lambda_im"])
    lst = gp(np.broadcast_to(np.asarray(inputs["s5_log_step"], np.float32)[..., None], (2, 2, 32, 64)))
    h0re = gp(np.asarray(inputs["state_s5_re"], np.float32)[s])
    h0im = gp(np.asarray(inputs["state_s5_im"], np.float32)[s])
    def gps(a):
        a = np.asarray(a, np.float32).reshape(2, 2, 16, 2, 64, 16)
        return a.transpose(0, 1, 3, 4, 2, 5).reshape(2, 2, 128, 256)
    def gsp(a):
        a = np.asarray(a, np.float32).reshape(2, 2, 16, 2, 16, 64)
        return a.transpose(0, 1, 3, 5, 2, 4).reshape(2, 2, 128, 256)
    pk = np.concatenate([lre, lim, lst, h0re, h0im, gps(inputs["s5_b_re"]), gps(inputs["s5_b_im"]),
                         gsp(inputs["s5_c_re"]), gsp(inputs["s5_c_im"])], axis=-1)
    return np.ascontiguousarray(pk.astype(np.float32))


CONSTS = _make_consts()


def make_in_maps(inputs):
    f = lambda a: np.ascontiguousarray(np.asarray(a, dtype=np.float32))
    maps = []
    ident = np.eye(128, dtype=np.float32)
    for core in range(8):
        s = core // 4
        m = {}
        m["xp"] = f(inputs["x_prompt"][2 * core:2 * core + 2]).reshape(NPT, D)
        m["xs"] = f(inputs["x_sample"][s])
        m["cv"] = f(np.stack([inputs["c_ctx"], inputs["c"][s]], 0))
        m["ident_in"] = ident
        m["ada_w"] = f(inputs["ada_w"])
        m["ada_b"] = f(inputs["ada_b"]).reshape(DEPTH * 48, 128)
        for nm in ("mix_pre_g", "mix_post_g", "ffn_pre_g", "ffn_post_g"):
            m[nm] = f(inputs[nm]).reshape(DEPTH * 8, 128)
        m["ffn_w_up"] = f(inputs["ffn_w_up"])
        m["ffn_conv_w"] = f(inputs["ffn_conv_w"]).reshape(DEPTH * 3 * 44, 128)
        m["ffn_conv_b"] = f(inputs["ffn_conv_b"]).reshape(DEPTH * 44, 128)
        m["ffn_w_down"] = f(inputs["ffn_w_down"])
        m["even_w_in"] = f(inputs["even_w_in"])
        m["even_w_out"] = f(inputs["even_w_out"])
        m["ret_logit_b"] = f(np.broadcast_to(np.asarray(inputs["ret_logit"], np.float32).reshape(2, 1, 8), (2, 128, 8)))
        m["ret_gn"] = f(inputs["ret_gn"]).reshape(8, 128)
        m["mla_q_norm"] = f(inputs["mla_q_norm"]).reshape(4, 128)
        m["mla_w_uq"] = f(inputs["mla_w_uq"])
        m["mla_kv_norm"] = f(inputs["mla_kv_norm"])
        m["kvn_bc"] = f(np.broadcast_to(np.asarray(inputs["mla_kv_norm"], np.float32).reshape(1, 256), (128, 256)))
        m["mla_w_ukv"] = f(inputs["mla_w_ukv"])
        m["state_ret_in"] = f(inputs["state_ret"][s])
        m["ckv_in"] = f(inputs["cache_mla_ckv"][s])
        m["kr_in"] = f(inputs["cache_mla_krope"][s])
        m["odd_w_in"] = f(inputs["odd_w_in"])
        m["odd_w_out"] = f(inputs["odd_w_out"])
        m["s5p"] = _s5_pack(inputs, s)
        m["s5_d"] = f(inputs["s5_d"]).reshape(8, 128)
        m["s5_glu_b"] = f(inputs["s5_glu_b"]).reshape(8, 128)
        m["s5_glu_w"] = f(inputs["s5_glu_w"])
        m["na_rpb"] = f(inputs["na_rpb"])
        m["nak_in"] = f(inputs["cache_na_k"][s]).reshape(2, 512, 512)
        m["nav_in"] = f(inputs["cache_na_v"][s]).reshape(2, 512, 512)
        m["oh_in"] = CONSTS["oh"]
        m["iota_in"] = CONSTS["iota"]
        m["rconst_in"] = CONSTS["rconst"]
        m["ropeC_in"] = CONSTS["ropeC"]
        m["ropeS_in"] = CONSTS["ropeS"]
        maps.append(m)
    return maps


def kernel(**inputs):
    nc = build_program()
    maps = make_in_maps(inputs)
    res = run_bass_kernel_spmd(nc, maps, core_ids=list(range(8)))
    R = res.results
    y_prompt = np.concatenate([R[c]["yp"].reshape(2, 256, D) for c in range(8)], 0)
    y_sample = np.stack([R[0]["ys"], R[4]["ys"]], 0)
    st_ret = np.concatenate([R[c]["o_ret"] for c in range(8)], 0)
    ck_ckv = np.concatenate([R[c]["o_ckv"] for c in range(8)], 0)
    ck_kr = np.concatenate([R[c]["o_kr"] for c in range(8)], 0)
    def s5o(name):
        o = np.concatenate([R[c][name] for c in range(8)], 0)
        o = o.reshape(16, 2, 2, 2, 64, 16).transpose(0, 1, 2, 5, 3, 4)
        return np.ascontiguousarray(o.reshape(16, 2, 2, 32, 64))
    st_re, st_im = s5o("o_s5re"), s5o("o_s5im")
    nak = np.concatenate([R[c]["o_nak"] for c in range(8)], 0).reshape(16, 2, 256, 8, 64)
    nav = np.concatenate([R[c]["o_nav"] for c in range(8)], 0).reshape(16, 2, 256, 8, 64)
    return (y_prompt, y_sample, st_ret, ck_ckv, ck_kr, st_re, st_im, nak, nav)
```

```python
import math
import numpy as np
import concourse.bass as bass
import concourse.mybir as mybir
from concourse.bass_utils import run_bass_kernel_spmd

F32 = mybir.dt.float32
BF16 = mybir.dt.bfloat16
I32 = mybir.dt.int32
ALU = mybir.AluOpType
AF = mybir.ActivationFunctionType
AX = mybir.AxisListType

ENGS = ("pe", "act", "dve", "pool", "sp")
STRICT_SAME_ENGINE = True
N_DMA_SEMS = 12


def _box(ap):
    t = ap.tensor
    es = 2 if ap.dtype == BF16 else 4
    pat = [(a * es, b) for a, b in ap.ap]
    off = int(ap.offset) * es
    sp = str(ap.space)
    if "DRAM" in sp.upper() or "Dram" in sp or "dram" in sp:
        lo = off
        hi = off
        for st, cnt in pat:
            if st >= 0:
                hi += st * (cnt - 1)
            else:
                lo += st * (cnt - 1)
        return (t.name, 0, 1, lo, hi + 1)
    pstride = pat[0][0]
    npart = pat[0][1]
    if pstride == 0:
        pstride = 1 << 30
    p0 = off // pstride
    f0 = off % pstride
    lo = f0
    hi = f0
    for st, cnt in pat[1:]:
        if st >= 0:
            hi += st * (cnt - 1)
        else:
            lo += st * (cnt - 1)
    if t.name.startswith("psb"):
        return (t.name, 0, 128, 0, 2048)
    return (t.name, p0, p0 + npart, lo, hi + 1)


def _ovl(a, b):
    return a[1] < b[2] and b[1] < a[2] and a[3] < b[4] and b[3] < a[4]


def _contains(a, b):
    return a[1] <= b[1] and b[2] <= a[2] and a[3] <= b[3] and b[4] <= a[4]


class Sched:
    def __init__(self, nc):
        self.nc = nc
        self.ops = {e: [] for e in ENGS}
        self.track = {}
        self.known = {e: {} for e in ENGS}
        self.ndma = {e: 0 for e in ENGS}
        self.ncomp = {e: 0 for e in ENGS}
        self.dma_last = {e: {} for e in ENGS}
        self.notrack = set()
        self.final_events = []
        self.tag = ""
        self.tags = {e: [] for e in ENGS}

    def _deps(self, reads, writes, eng=None):
        deps = set()
        for ap in reads:
            b = _box(ap)
            if b[0] in self.notrack:
                continue
            tr = self.track.get(b[0])
            if tr is None:
                continue
            for wb, ev in tr["w"]:
                if _ovl(wb, b):
                    deps.add(ev)
            if b[0].startswith("psb"):
                for rb, evs in tr["r"].items():
                    if _ovl(rb, b):
                        deps.update(ev for e_, ev in evs.items() if e_ != eng)
        for ap in writes:
            b = _box(ap)
            tr = self.track.get(b[0])
            if tr is None:
                continue
            for wb, ev in tr["w"]:
                if _ovl(wb, b):
                    deps.add(ev)
            for rb, evs in tr["r"].items():
                if _ovl(rb, b):
                    deps.update(evs.values())
        return deps

    def _record(self, reads, writes, ev, eng):
        for ap in writes:
            b = _box(ap)
            tr = self.track.setdefault(b[0], {"w": [], "r": {}})
            tr["w"] = [(wb, e) for wb, e in tr["w"] if not _contains(b, wb)]
            tr["w"].append((b, ev))
            tr["r"] = {rb: evs for rb, evs in tr["r"].items() if not _contains(b, rb)}
        for ap in reads:
            b = _box(ap)
            if b[0] in self.notrack:
                continue
            tr = self.track.setdefault(b[0], {"w": [], "r": {}})
            tr["r"].setdefault(b, {})[eng] = ev

    def _waits(self, eng, deps, idx, is_dma=False):
        waits = {}
        for semkey, val in deps:
            if semkey == ("c", eng) and not is_dma:
                if eng == "pe":
                    continue
                if (not STRICT_SAME_ENGINE) and val < idx - 1:
                    continue
            if self.known[eng].get(semkey, 0) >= val:
                continue
            if waits.get(semkey, 0) < val:
                waits[semkey] = val
        for k, v in waits.items():
            self.known[eng][k] = v
        return list(waits.items())

    def op(self, eng, fn, reads=(), writes=()):
        reads = [r for r in reads if r is not None and not isinstance(r, (int, float))]
        idx = self.ncomp[eng]
        self.ncomp[eng] = idx + 1
        deps = self._deps(reads, writes, eng)
        waits = self._waits(eng, deps, idx)
        ev = (("c", eng), idx + 1)
        self.tags[eng].append(self.tag)
        self.ops[eng].append((fn, waits, ev, "c"))
        self._record(reads, writes, ev, eng)
        return ev

    def dma(self, eng, out, in_, **kw):
        n = self.ndma[eng]
        self.ndma[eng] = n + 1
        k = n % N_DMA_SEMS
        val = (n // N_DMA_SEMS + 1) * 16
        semkey = ("d", eng, k)
        deps = self._deps([in_], [out])
        if val > 16:
            deps.add((semkey, val - 16))
        idx = self.ncomp[eng]
        waits = self._waits(eng, deps, idx, True)
        ev = (semkey, val)

        def fn(e, out=out, in_=in_, kw=kw):
            return e.dma_start(out=out, in_=in_, **kw)

        self.ops[eng].append((fn, waits, ev, "d"))
        self._record([in_], [out], ev, eng)
        return ev

    def finish(self, final_events):
        self.final_events = list(final_events)

    def emit(self):
        nc = self.nc
        import contextlib

        with contextlib.ExitStack() as st:
            sems = {}
            for e in ENGS:
                sems[("c", e)] = st.enter_context(nc.semaphore("c_" + e))
            for e in ENGS:
                for k in range(min(N_DMA_SEMS, self.ndma[e])):
                    sems[("d", e, k)] = st.enter_context(nc.semaphore("d_%s_%d" % (e, k)))
            block = st.enter_context(nc.Block())

            waited = {e: set() for e in ENGS}
            for e in ENGS:
                for fn, waits, ev, kind in self.ops[e]:
                    for semkey, val in waits:
                        if semkey[0] == "c":
                            waited[semkey[1]].add(val)
            for semkey, val in self.final_events:
                if semkey[0] == "c":
                    waited[semkey[1]].add(val)
            rank = {e: {v: i + 1 for i, v in enumerate(sorted(waited[e]))} for e in ENGS}

            def semval(semkey, val):
                return rank[semkey[1]][val] if semkey[0] == "c" else val

            def run(engname, eobj):
                for fn, waits, ev, kind in self.ops[engname]:
                    for semkey, val in waits:
                        eobj.wait_ge(sems[semkey], semval(semkey, val))
                    ins = fn(eobj)
                    if kind == "c":
                        if ev[1] in waited[engname]:
                            ins.then_inc(sems[ev[0]], 1)
                    else:
                        ins.then_inc(sems[ev[0]], 16)
                if engname == "sp":
                    mx = {}
                    for semkey, val in self.final_events:
                        v = semval(semkey, val)
                        mx[semkey] = max(mx.get(semkey, 0), v)
                    for semkey, val in mx.items():
                        eobj.wait_ge(sems[semkey], val)

            @block.tensor
            def _(e):
                run("pe", e)

            @block.scalar
            def _(e):
                run("act", e)

            @block.vector
            def _(e):
                run("dve", e)

            @block.gpsimd
            def _(e):
                run("pool", e)

            @block.sync
            def _(e):
                run("sp", e)

import contextlib

ENABLE_EVEN = True
ENABLE_ODD = True

D = 1024
NPT = 512
LS = 2048
T = NPT + LS
DFF = 2816
NJ = DFF // 128
EPS = 1e-6
DEPTH = 4


class KB:
    def __init__(self, nc, st):
        self.nc = nc
        self.st = st
        self.S = Sched(nc)
        self.psn = 0
        self.rot = {}

    def sb(self, name, shape, dt=F32):
        return self.st.enter_context(self.nc.sbuf_tensor(name, list(shape), dt))

    def dram(self, name, shape, dt=F32, kind="Internal"):
        return self.nc.dram_tensor(name, list(shape), dt, kind=kind).ap()

    def rotbuf(self, name, shape, dt, n=2):
        key = name
        if key not in self.rot:
            self.rot[key] = [[self.sb("%s_%d" % (name, i), shape, dt) for i in range(n)], 0]
        r = self.rot[key]
        b = r[0][r[1] % len(r[0])]
        r[1] += 1
        return b

    def arena_init(self, nf):
        self.arF = self.sb("arena", [128, nf], F32)
        self.arB = self.arF.bitcast(BF16)
        self.arI = self.arF.bitcast(I32)
        self.nbytes = nf * 4
        self.ptr = 0
        self.arot = {}

    def phase(self):
        self.ptr = 0
        self.arot = {}

    def mark(self):
        return self.ptr

    def reset_to(self, m):
        self.ptr = m
        self.arot = {}

    def af(self, shape, dt=F32):
        n = int(np.prod(shape))
        es = 4 if dt == F32 else 2
        self.ptr = (self.ptr + 3) // 4 * 4
        e0 = self.ptr // es
        a = (self.arF if dt == F32 else self.arB)[:, e0:e0 + n]
        self.ptr += n * es
        assert self.ptr <= self.nbytes, ("arena overflow", self.ptr, self.nbytes)
        if len(shape) == 2:
            a = a.rearrange("p (a b) -> p a b", a=shape[0])
        elif len(shape) == 3:
            a = a.rearrange("p (a b c) -> p a b c", a=shape[0], b=shape[1])
        elif len(shape) == 4:
            a = a.rearrange("p (a b c d) -> p a b c d", a=shape[0], b=shape[1], c=shape[2])
        return a

    def arotbuf(self, name, shape, dt, n=2):
        if name not in self.arot:
            self.arot[name] = [[self.af(shape, dt) for _ in range(n)], 0]
        r = self.arot[name]
        b = r[0][r[1] % len(r[0])]
        r[1] += 1
        return b

    def psum(self):
        b = self.psb[self.psn % len(self.psb)]
        self.psn += 1
        return b

    def mm(self, out, lhsT, rhs, start=True, stop=True):
        return self.S.op("pe", lambda e: e.matmul(out, lhsT=lhsT, rhs=rhs, start=start, stop=stop),
                         reads=[lhsT, rhs], writes=[out])

    def tr(self, out, in_, ident):
        return self.S.op("pe", lambda e: e.transpose(out, in_, ident), reads=[in_, ident], writes=[out])

    def act(self, out, in_, func, bias=None, scale=1.0, accum_out=None):
        kw = {}
        rd = [in_]
        wr = [out]
        if bias is not None:
            kw["bias"] = bias
            rd.append(bias)
        if accum_out is not None:
            kw["accum_out"] = accum_out
            wr.append(accum_out)
        if not isinstance(scale, (int, float)):
            rd.append(scale)
        return self.S.op("act", lambda e: e.activation(out=out, in_=in_, func=func, scale=scale, **kw),
                         reads=rd, writes=wr)

    def tt(self, out, in0, in1, op, eng="dve"):
        return self.S.op(eng, lambda e: e.tensor_tensor(out=out, in0=in0, in1=in1, op=op),
                         reads=[in0, in1], writes=[out])

    def ts(self, out, in0, s1, s2, op0, op1=None, eng="dve", accum_out=None):
        kw = {}
        wr = [out]
        if op1 is not None:
            kw["op1"] = op1
        if accum_out is not None:
            kw["accum_out"] = accum_out
            wr.append(accum_out)
        return self.S.op(eng, lambda e: e.tensor_scalar(out=out, in0=in0, scalar1=s1, scalar2=s2, op0=op0, **kw),
                         reads=[in0, s1, s2], writes=wr)

    def stt(self, out, in0, scalar, in1, op0, op1):
        return self.S.op("dve", lambda e: e.scalar_tensor_tensor(out=out, in0=in0, scalar=scalar, in1=in1, op0=op0, op1=op1),
                         reads=[in0, scalar, in1], writes=[out])

    def copy(self, out, in_, eng="dve"):
        if eng == "act":
            return self.S.op("act", lambda e: e.copy(out=out, in_=in_), reads=[in_], writes=[out])
        return self.S.op(eng, lambda e: e.tensor_copy(out=out, in_=in_), reads=[in_], writes=[out])

    def memset(self, ap, val, eng="dve"):
        return self.S.op(eng, lambda e: e.memset(ap, val), reads=[], writes=[ap])

    def red(self, out, in_, op, axis=AX.X):
        return self.S.op("dve", lambda e: e.tensor_reduce(out=out, in_=in_, axis=axis, op=op), reads=[in_], writes=[out])

    def recip(self, out, in_):
        return self.S.op("dve", lambda e: e.reciprocal(out=out, in_=in_), reads=[in_], writes=[out])

    def dma(self, out, in_, q="sp", **kw):
        return self.S.dma(q, out, in_, **kw)


def build_program(dbg_stop=None):
    nc = bass.Bass("TRN2", target_bir_lowering=False)
    st = contextlib.ExitStack()
    with st:
        K = KB(nc, st)
        S = K.S
        IN = {}

        def inp(name, shape, dt=F32):
            IN[name] = nc.dram_tensor(name, list(shape), dt, kind="ExternalInput").ap()
            S.notrack.add(name)
            return IN[name]

        def outp(name, shape):
            return nc.dram_tensor(name, list(shape), F32, kind="ExternalOutput").ap()

        xp = inp("xp", [NPT, D])
        xs = inp("xs", [LS, D])
        cv = inp("cv", [2, D])
        ident_d = inp("ident_in", [128, 128])
        ada_w = inp("ada_w", [DEPTH, D, 6 * D])
        ada_b = inp("ada_b", [DEPTH * 48, 128])
        gvec = {}
        for nm in ("mix_pre_g", "mix_post_g", "ffn_pre_g", "ffn_post_g"):
            gvec[nm] = inp(nm, [DEPTH * 8, 128])
        ffn_w_up = inp("ffn_w_up", [DEPTH, D, 2 * DFF])
        ffn_conv_w = inp("ffn_conv_w", [DEPTH * 3 * 44, 128])
        ffn_conv_b = inp("ffn_conv_b", [DEPTH * 44, 128])
        ffn_w_down = inp("ffn_w_down", [DEPTH, DFF, D])

        even_w_in = inp("even_w_in", [2, D, 2464])
        even_w_out = inp("even_w_out", [2, D, D])
        ret_logit_b = inp("ret_logit_b", [2, 128, 8])
        ret_gn = inp("ret_gn", [8, 128])
        mla_q_norm = inp("mla_q_norm", [4, 128])
        mla_w_uq = inp("mla_w_uq", [2, 256, 768])
        mla_kv_norm = inp("mla_kv_norm", [2, 128])
        kvn_bc_d = inp("kvn_bc", [128, 256])
        mla_w_ukv = inp("mla_w_ukv", [2, 128, 1024])
        state_ret_in = inp("state_ret_in", [2, 2, 4, 128, 128])
        ckv_in = inp("ckv_in", [2, 512, 128])
        kr_in = inp("kr_in", [2, 512, 32])
        rconst_d = inp("rconst_in", [128, 772])
        ropeC_d = inp("ropeC_in", [32, 2048])
        ropeS_d = inp("ropeS_in", [32, 2048])
        odd_w_in = inp("odd_w_in", [2, D, 2048])
        odd_w_out = inp("odd_w_out", [2, D, D])
        s5p = inp("s5p", [2, 2, 128, 1104])
        s5_d = inp("s5_d", [8, 128])
        s5_glu_b = inp("s5_glu_b", [8, 128])
        s5_glu_w = inp("s5_glu_w", [2, 512, 512])
        na_rpb = inp("na_rpb", [2, 8, 15, 31])
        nak_in = inp("nak_in", [2, 512, 512])
        nav_in = inp("nav_in", [2, 512, 512])
        oh_d = inp("oh_in", [32, 64, 128])
        iota_d = inp("iota_in", [128, 2048])
        o_s5re = outp("o_s5re", [2, 2, 2, 128, 16])
        o_s5im = outp("o_s5im", [2, 2, 2, 128, 16])
        o_nak = outp("o_nak", [2, 2, 256, 512])
        o_nav = outp("o_nav", [2, 2, 256, 512])
        o_ret = outp("o_ret", [2, 2, 2, 4, 128, 128])
        o_ckv = outp("o_ckv", [2, 2, 256, 128])
        o_kr = outp("o_kr", [2, 2, 256, 32])
        finals = []
        yp = outp("yp", [NPT, D])
        ys = outp("ys", [LS, D])

        xT_d = K.dram("xT_d", [D, T])
        xT_v = xT_d.rearrange("(c p) t -> p c t", p=128)

        K.psb = [st.enter_context(nc.psum_tensor("psb%d" % i, [128, 512], F32)) for i in range(8)]

        K.arena_init(50200)
        ident = K.sb("ident", [128, 128], F32)
        K.dma(ident[:], ident_d)
        identb = K.sb("identb", [128, 128], BF16)
        K.copy(identb[:], ident[:])
        epsT = K.sb("epsT", [128, 1], F32)
        K.memset(epsT[:], EPS)
        halo_stash = K.sb("halo_stash", [128, 8, 2], BF16)
        ones1024 = K.sb("ones1024", [128, 128], BF16)
        K.memset(ones1024[:], 1.0 / 1024.0)
        ones256 = K.sb("ones256", [128, 128], BF16)
        K.memset(ones256[:], 1.0 / 256.0)
        ones128b = K.sb("ones128b", [128, 128], BF16)
        K.memset(ones128b[:], 1.0 / 128.0)
        onesb = K.sb("onesb", [128, 128], BF16)
        K.memset(onesb[:], 1.0)
        oneT = K.sb("oneT", [128, 1], F32)
        K.memset(oneT[:], 1.0)
        halfpi = K.sb("halfpi", [128, 1], F32)
        K.memset(halfpi[:], math.pi / 2.0)


        def load_T(dram2d, R, name):
            dst = K.sb(name, [128, R], F32)
            r0 = 0
            while r0 < R:
                r = min(128, R - r0)
                stg = K.rotbuf("ldT_stg", [128, 128], F32)
                K.dma(stg[0:r, :], dram2d[r0:r0 + r, :])
                pb = K.psum()
                K.tr(pb[:, 0:r], stg[0:r, :], ident[0:r, 0:r])
                K.copy(dst[:, r0:r0 + r], pb[:, 0:r])
                r0 += r
            return dst

        gT = {nm: load_T(gvec[nm], DEPTH * 8, "gT_" + nm) for nm in gvec}
        adabT = load_T(ada_b, DEPTH * 48, "adabT")
        convwT = load_T(ffn_conv_w, DEPTH * 3 * 44, "convwT")
        convbT = load_T(ffn_conv_b, DEPTH * 44, "convbT")
        nconvwT = K.sb("nconvwT", [128, DEPTH * 3 * 44], F32)
        K.ts(nconvwT[:], convwT[:], -1.0, None, ALU.mult)
        s5dT = load_T(s5_d, 8, "s5dT")
        glubT = load_T(s5_glu_b, 8, "glubT")
        gnT = load_T(ret_gn, 8, "gnT")
        qnT = load_T(mla_q_norm, 4, "qnT")
        kvnT = load_T(mla_kv_norm, 2, "kvnT")
        cT = load_T(cv.rearrange("v (c p) -> (v c) p", p=128), 16, "cT")
        scT = K.sb("scT", [128, 8, 2], BF16)
        K.act(scT[:].rearrange("p c v -> p v c"), cT[:].rearrange("p (v c) -> p v c", v=2), AF.Silu)

        def x_to_xT(src, ntile, col0):
            for i in range(ntile):
                xt = K.arotbuf("xin", [D], F32)
                K.dma(xt[:], src[i * 128:(i + 1) * 128, :])
                xo = K.arotbuf("xinT", [8, 128], F32)
                for g in range(2):
                    pb = K.psum()
                    for c in range(4):
                        K.tr(pb[:, c * 128:(c + 1) * 128], xt[:, (g * 4 + c) * 128:(g * 4 + c + 1) * 128], ident[:])
                    K.copy(xo[:, g * 4:(g + 1) * 4, :], pb[:].rearrange("p (c t) -> p c t", c=4), eng=("act" if g else "dve"))
                K.dma(xT_v[:, :, col0 + i * 128: col0 + (i + 1) * 128], xo[:])

        K.phase()
        x_to_xT(xp, NPT // 128, 0)
        x_to_xT(xs, LS // 128, NPT)

        def ada(layer):
            K.phase()
            S.tag = "L%d:ada" % layer
            modp = K.psum()
            mview = modp[:, 0:96].rearrange("p (c v) -> p c v", v=2)
            for blk in range(12):
                wb = K.arotbuf("adaw", [8, 512], BF16)
                K.dma(wb[:], ada_w[layer].rearrange("(k p) o -> p k o", p=128)[:, :, blk * 512:(blk + 1) * 512], q="pool")
                for cc in range(4):
                    ch = blk * 4 + cc
                    for k in range(8):
                        K.mm(mview[:, ch, :], wb[:, k, cc * 128:(cc + 1) * 128], scT[:, k, :], start=(k == 0), stop=(k == 7))
            mod = K.rotbuf("mod", [128, 48, 2], F32)
            for v in range(2):
                K.tt(mod[:, :, v], mview[:, :, v], adabT[:, layer * 48:(layer + 1) * 48], ALU.add)
            return mod

        def mod_vectors(layer, mod, sub):
            base = sub * 24
            pre_g = gT["mix_pre_g" if sub == 0 else "ffn_pre_g"]
            post_g = gT["mix_post_g" if sub == 0 else "ffn_post_g"]
            gs = K.rotbuf("gs", [128, 8, 2], F32)
            gg = K.rotbuf("gg", [128, 8, 2], F32)
            for v in range(2):
                K.stt(gs[:, :, v], mod[:, base + 8:base + 16, v], 1.0, pre_g[:, layer * 8:(layer + 1) * 8], ALU.add, ALU.mult)
                K.tt(gg[:, :, v], mod[:, base + 16:base + 24, v], post_g[:, layer * 8:(layer + 1) * 8], ALU.mult)
            return gs, mod[:, base:base + 8, :], gg

        def rstd_from(src3, n, ones):
            C = src3.shape[1]
            sq = K.arotbuf("sq", [8, 512], BF16, n=1)
            K.act(sq[:, 0:C, 0:n], src3, AF.Square)
            pb = K.psum()
            for c in range(C):
                K.mm(pb[:, 0:n], ones[:], sq[:, c, 0:n], start=(c == 0), stop=(c == C - 1))
            rstd = K.arotbuf("rstd", [512], F32)
            K.act(rstd[:, 0:n], pb[:, 0:n], AF.Sqrt, bias=epsT[:, 0:1])
            K.recip(rstd[:, 0:n], rstd[:, 0:n])
            return rstd

        def pre_block(t0, n, mc, gs, shift, dst):
            xb = K.arotbuf("xb", [8, 512], F32)
            if n == 1:
                K.dma(xb[:, :, 0:n], xT_v[:, :, t0:t0 + n], allow_slow_non_contiguous=True)
            else:
                K.dma(xb[:, :, 0:n], xT_v[:, :, t0:t0 + n])
            rstd = rstd_from(xb[:, :, 0:n], n, ones1024)
            for c in range(8):
                tmp = K.arotbuf("tmpn", [512], F32)
                K.stt(tmp[:, 0:n], xb[:, c, 0:n], gs[:, c, mc:mc + 1], rstd[:, 0:n], ALU.mult, ALU.mult)
                K.act(dst[:, c, :], tmp[:, 0:n], AF.Identity, bias=shift[:, c, mc:mc + 1])

        def post_block(t0, n, mc, gg, yf):
            rstd = rstd_from(yf, n, ones1024)
            xb = K.arotbuf("xb", [8, 512], F32)
            K.dma(xb[:, :, 0:n], xT_v[:, :, t0:t0 + n])
            for c in range(8):
                tmp = K.arotbuf("tmpn", [512], F32)
                K.stt(tmp[:, 0:n], yf[:, c, :], gg[:, c, mc:mc + 1], rstd[:, 0:n], ALU.mult, ALU.mult)
                K.tt(xb[:, c, 0:n], xb[:, c, 0:n], tmp[:, 0:n], ALU.add, eng="pool")
            K.dma(xT_v[:, :, t0:t0 + n], xb[:, :, 0:n])

        FFN_GROUPS = [
            (0, 1024, 0, 1, [(0, 256), (256, 512), (512, 1025)]),
            (1024, 2048, 1, 1, [(1023, 2049)]),
            (2048, 2560, 1, 0, [(2047, 2560)]),
        ]

        def ffn(layer, gs, shift, gg):
            wup_v = ffn_w_up[layer].rearrange("(k p) o -> p k o", p=128)
            wdn_v = ffn_w_down[layer].rearrange("(j p) o -> p j o", p=128)
            for (c0, c1, hl, hr, segs) in FFN_GROUPS:
                K.phase()
                S.tag = "L%d:ffn_up" % layer
                b0 = c0 - hl
                W = c1 + hr - b0
                hTg = K.af([8, 1026], BF16)
                cols = []
                if hl:
                    cols.append((b0, 1))
                for t0 in range(c0, c1, 512):
                    cols.append((t0, 512))
                if hr:
                    cols.append((c1, 1))
                for (t0, n) in cols:
                    mc = 0 if t0 < NPT else 1
                    if hl and t0 == b0:
                        K.copy(hTg[:, :, 0:1], halo_stash[:, :, 0:1], eng="pool")
                        continue
                    pre_block(t0, n, mc, gs, shift, hTg[:, :, t0 - b0:t0 - b0 + n])
                if c1 < T:
                    K.copy(halo_stash[:, :, 0:1], hTg[:, :, c1 - 1 - b0:c1 - b0], eng="pool")
                npart = (W + 511) // 512
                psz = (W + npart - 1) // npart
                parts = [(i * psz, min(psz, W - i * psz)) for i in range(npart)]
                mT = K.af([NJ, 1024], BF16)

                def load_up(j):
                    wa = K.arotbuf("wupa", [8, 128], BF16, n=2)
                    wg = K.arotbuf("wupg", [8, 128], BF16, n=2)
                    K.dma(wa[:], wup_v[:, :, j * 128:(j + 1) * 128], q="pool")
                    K.dma(wg[:], wup_v[:, :, DFF + j * 128:DFF + (j + 1) * 128], q="pool")
                    return wa, wg

                bounds = [s0_ for (s0_, s1_) in segs[1:]]
                lo = 1 + hl
                n_own = c1 - c0

                def ffn_X(j, wts):
                    wa, wg = wts
                    us, os_ = [], []
                    for wi, wmat in enumerate((wa, wg)):
                        u = K.arotbuf("u%d" % wi, [1028], F32, n=2)
                        for (p0, pn) in parts:
                            pb = K.psum()
                            for k in range(8):
                                K.mm(pb[:, 0:pn], wmat[:, k, :], hTg[:, k, p0:p0 + pn], start=(k == 0), stop=(k == 7))
                            K.copy(u[:, 1 + p0:1 + p0 + pn], pb[:, 0:pn], eng="act")
                        blk = wi * NJ + j
                        w1 = convwT[:, (layer * 3 + 1) * 44 + blk:(layer * 3 + 1) * 44 + blk + 1]
                        bb = convbT[:, layer * 44 + blk:layer * 44 + blk + 1]
                        o = K.arotbuf("o%d" % wi, [1028], F32, n=2)
                        K.act(o[:, 1:1 + W], u[:, 1:1 + W], AF.Identity, bias=bb, scale=w1)
                        us.append(u)
                        os_.append(o)
                    return us, os_

                def ffn_Y(j, us, os_):
                    for wi in range(2):
                        u, o = us[wi], os_[wi]
                        blk = wi * NJ + j
                        c0w = (layer * 3 + 0) * 44 + blk
                        c2w = (layer * 3 + 2) * 44 + blk
                        w0, w2 = convwT[:, c0w:c0w + 1], convwT[:, c2w:c2w + 1]
                        nw0, nw2 = nconvwT[:, c0w:c0w + 1], nconvwT[:, c2w:c2w + 1]
                        K.stt(o[:, 2:1 + W], u[:, 1:W], w0, o[:, 2:1 + W], ALU.mult, ALU.add)
                        K.stt(o[:, 1:W], u[:, 2:1 + W], w2, o[:, 1:W], ALU.mult, ALU.add)
                        for B in bounds:
                            iB = B - b0 + 1
                            K.stt(o[:, iB:iB + 1], u[:, iB - 1:iB], nw0, o[:, iB:iB + 1], ALU.mult, ALU.add)
                            K.stt(o[:, iB - 1:iB], u[:, iB:iB + 1], nw2, o[:, iB - 1:iB], ALU.mult, ALU.add)
                    oa, og = os_
                    sg = K.arotbuf("sg", [1024], F32, n=1)
                    K.act(sg[:, 0:n_own], og[:, lo:lo + n_own], AF.Silu)
                    K.tt(mT[:, j, 0:n_own], oa[:, lo:lo + n_own], sg[:, 0:n_own], ALU.mult)

                wts = load_up(0)
                wts_next = load_up(1)
                stX = ffn_X(0, wts)
                for j in range(NJ):
                    if j + 1 < NJ:
                        wts = wts_next
                        nxX = ffn_X(j + 1, wts)
                        if j + 2 < NJ:
                            wts_next = load_up(j + 2)
                    ffn_Y(j, *stX)
                    if j + 1 < NJ:
                        stX = nxX
                S.tag = "L%d:ffn_down" % layer
                n_own = c1 - c0
                yfa = K.af([8, 1024], F32)

                def load_dn(c):
                    wd = K.arotbuf("wdn", [NJ, 128], BF16, n=2)
                    K.dma(wd[:], wdn_v[:, :, c * 128:(c + 1) * 128], q="pool")
                    return wd

                nxt = load_dn(0)
                for c in range(8):
                    wd = nxt
                    if c + 1 < 8:
                        nxt = load_dn(c + 1)
                    for t0 in range(0, n_own, 512):
                        pb = K.psum()
                        for j in range(NJ):
                            K.mm(pb[:, :], wd[:, j, :], mT[:, j, t0:t0 + 512], start=(j == 0), stop=(j == NJ - 1))
                        K.copy(yfa[:, c, t0:t0 + 512], pb[:, :], eng="act")
                for t0 in range(0, n_own, 512):
                    mc = 0 if (c0 + t0) < NPT else 1
                    post_block(c0 + t0, 512, mc, gg, yfa[:, :, t0:t0 + 512])

        RS = 128.0 ** -0.5
        MLA_SCALE = 96.0 ** -0.5
        SEQS = [(0, 256, 0), (256, 256, 0), (512, 2048, 1)]

        PW = {"m": 2560, "n": 1152}
        NTT = {"m": 20, "n": 9}

        ATT = {"prev": None, "tb": 0}

        def attn_A(qT_ap, kbanks):
            for b, (kT_ap, bias_fn) in enumerate(kbanks):
                w = kT_ap.shape[1]
                K.mm(K.psb[b][:, 0:w], qT_ap, kT_ap)

        def attn_C(kbanks, tag):
            nb = len(kbanks)
            mx = K.arotbuf("a_mx", [8], F32)
            srcs = []
            col = 0
            for b, (kT_ap, bias_fn) in enumerate(kbanks):
                w = kT_ap.shape[1]
                src = K.psb[b][:, 0:w]
                if bias_fn is not None:
                    src = bias_fn(K.psb[b], w)
                srcs.append((src, col, w))
                K.red(mx[:, b:b + 1], src, ALU.max)
                col += w
            nm = K.arotbuf("a_nm", [1], F32)
            K.red(nm[:, 0:1], mx[:, 0:nb], ALU.max)
            K.ts(nm[:, 0:1], nm[:, 0:1], -1.0, None, ALU.mult)
            P = K.arotbuf("a_P" + tag, [PW[tag]], BF16)
            for (src, c0_, w) in srcs:
                K.act(P[:, c0_:c0_ + w], src, AF.Exp, bias=nm[:, 0:1])
            return P

        def attn_B(st_):
            P, ktiles, hs, dst, tag = st_
            PT = K.arotbuf("a_PT" + tag, [NTT[tag], 128], BF16)
            nkt = len(ktiles)
            for g in range(0, nkt, 8):
                cnt = min(8, nkt - g)
                pbT = K.psb[5 + ATT["tb"] % 2].bitcast(BF16)
                ATT["tb"] += 1
                for i_ in range(cnt):
                    pc0, n_, _ = ktiles[g + i_]
                    K.tr(pbT[0:n_, i_ * 128:(i_ + 1) * 128], P[:, pc0:pc0 + n_], identb[:])
                full = all(ktiles[g + i_][1] == 128 for i_ in range(cnt))
                ce = "dve" if ATT["tb"] % 2 == 0 else "act"
                if full:
                    K.copy(PT[:, g:g + cnt, :], pbT[:, 0:cnt * 128].rearrange("p (a b) -> p a b", a=cnt), eng=ce)
                else:
                    for i_ in range(cnt):
                        n_ = ktiles[g + i_][1]
                        K.copy(PT[0:n_, g + i_, :], pbT[0:n_, i_ * 128:(i_ + 1) * 128], eng=ce)
            po = K.psb[7][:, 0:128]
            pd = K.psb[7][:, 128:256]
            for i_, (pc0, n_, vl) in enumerate(ktiles):
                K.mm(po, vl, PT[0:n_, i_, :], start=(i_ == 0), stop=(i_ == nkt - 1))
            for i_, (pc0, n_, vl) in enumerate(ktiles):
                K.mm(pd, onesb[0:n_, :], PT[0:n_, i_, :], start=(i_ == 0), stop=(i_ == nkt - 1))

        def attn_D(st_):
            P, ktiles, hs, dst, tag = st_
            po = K.psb[7][:, 0:128]
            pd = K.psb[7][:, 128:256]
            rd = K.arotbuf("a_rd", [128], F32)
            K.recip(rd[hs, :], pd[hs, :])
            K.tt(dst, po[hs, :], rd[hs, :], ALU.mult)

        def attn_unit(qT_ap, kbanks, ktiles, hs, dst, tag):
            prev = ATT["prev"]
            attn_A(qT_ap, kbanks)
            if prev is not None:
                attn_B(prev)
            P = attn_C(kbanks, tag)
            if prev is not None:
                attn_D(prev)
            ATT["prev"] = (P, ktiles, hs, dst, tag)

        def attn_flush():
            if ATT["prev"] is not None:
                attn_B(ATT["prev"])
                attn_D(ATT["prev"])
                ATT["prev"] = None

        def out_proj_post(w_out_l, yT, gg):
            wo = K.af([8, 1024], BF16)
            K.dma(wo[:], w_out_l.rearrange("(k p) o -> p k o", p=128), q="pool")
            for t0 in range(0, T, 512):
                yf = K.arotbuf("yf", [8, 512], F32, n=1)
                for c in range(8):
                    pb = K.psum()
                    for k in range(8):
                        K.mm(pb[:], wo[:, k, c * 128:(c + 1) * 128], yT[:, k, t0:t0 + 512], start=(k == 0), stop=(k == 7))
                    K.copy(yf[:, c, :], pb[:], eng="act")
                post_block(t0, 512, 0 if t0 < NPT else 1, gg, yf[:])

        def even_layer(layer, j, mod):
            gs, shift, gg = mod_vectors(layer, mod, 0)
            K.phase()
            hT = K.af([8, T], BF16)
            yT = K.af([8, T], BF16)
            m0 = K.mark()
            S.tag = "L%d:e_pre" % layer
            for t0 in range(0, T, 512):
                pre_block(t0, 512, 0 if t0 < NPT else 1, gs, shift, hT[:, :, t0:t0 + 512])
            K.reset_to(m0)
            S.tag = "L%d:e_ret" % layer
            w_in_v = even_w_in[j].rearrange("(k p) o -> p k o", p=128)
            rconst = K.af([772], F32)
            K.dma(rconst[:], rconst_d)
            RC = lambda i: rconst[:, i * 128:(i + 1) * 128]
            lg = K.af([8], F32)
            K.dma(lg[:], ret_logit_b[j])
            K.act(lg[:], lg[:], AF.Exp, scale=-1.0)
            K.act(lg[:], lg[:], AF.Ln, bias=oneT[:, 0:1])
            K.ts(lg[:], lg[:], -1.0, None, ALU.mult)
            qTh = K.af([T], BF16)
            kTh = K.af([T], BF16)
            sgTh = K.af([T], BF16)
            kvtok = K.af([20, 256], BF16)
            kwfT = K.af([20, 128], BF16)
            kwbT = K.af([20, 128], BF16)
            Sfs = K.af([20, 128], BF16)
            Sbs = K.af([20, 128], BF16)
            o_sb = K.af([T], F32)
            Dm = K.af([128], F32)
            e2 = K.af([128], F32)
            Wqf = K.af([128], F32)
            Wqb = K.af([128], F32)
            kw = K.af([4], F32)
            for h in range(4):
                lf = lg[:, h:h + 1]
                lb = lg[:, 4 + h:5 + h]
                K.act(Dm[:], RC(0), AF.Exp, scale=lf)
                K.tt(Dm[:], Dm[:], RC(1), ALU.mult)
                K.act(e2[:], RC(2), AF.Exp, scale=lb)
                K.tt(e2[:], e2[:], RC(3), ALU.mult)
                K.tt(Dm[:], Dm[:], e2[:], ALU.add)
                K.act(Wqf[:], RC(4), AF.Exp, scale=lf)
                K.act(Wqb[:], RC(5), AF.Exp, scale=lb)
                K.act(kw[:, 0:1], rconst[:, 768:769], AF.Exp, scale=lf)
                K.act(kw[:, 1:2], rconst[:, 769:770], AF.Exp, scale=lb)
                K.act(kw[:, 2:3], rconst[:, 770:771], AF.Exp, scale=lf)
                K.act(kw[:, 3:4], rconst[:, 770:771], AF.Exp, scale=lb)
                wh = K.arotbuf("wh", [8, 4, 128], BF16, n=2)
                for si_ in range(4):
                    K.dma(wh[:, :, si_, :], w_in_v[:, :, si_ * 512 + h * 128:si_ * 512 + (h + 1) * 128], q="pool")
                for t0 in range(0, T, 512):
                    for si, dst in ((0, qTh), (1, kTh), (3, sgTh)):
                        pb = K.psum()
                        for k in range(8):
                            K.mm(pb[:], wh[:, k, si, :], hT[:, k, t0:t0 + 512], start=(k == 0), stop=(k == 7))
                        if si == 0:
                            K.act(dst[:, t0:t0 + 512], pb[:], AF.Identity, scale=RS)
                        elif si == 1:
                            K.copy(dst[:, t0:t0 + 512], pb[:], eng="dve")
                        else:
                            K.act(dst[:, t0:t0 + 512], pb[:], AF.Silu)
                for ti in range(20):
                    pb = K.psum()
                    for k in range(8):
                        K.mm(pb[:, 0:256].rearrange("p (a b) -> p a b", a=2), hT[:, k, ti * 128:(ti + 1) * 128], wh[:, k, 1:3, :], start=(k == 0), stop=(k == 7))
                    K.copy(kvtok[:, ti, :], pb[:, 0:256], eng="act")
                    K.ts(kwfT[:, ti, :], pb[:, 0:128], kw[:, 0:1], None, ALU.mult)
                    K.ts(kwbT[:, ti, :], pb[:, 0:128], kw[:, 1:2], None, ALU.mult)
                for (c0, L, smp) in SEQS:
                    nch = L // 128
                    tb = c0 // 128
                    for d in range(2):
                        Sx = K.arotbuf("Sx", [128], F32, n=2)
                        if smp:
                            K.dma(Sx[:], state_ret_in[j, d, h])
                        else:
                            K.memset(Sx[:], 0.0, eng="pool")
                        store = Sfs if d == 0 else Sbs
                        kwT = kwfT if d == 0 else kwbT
                        order = range(nch) if d == 0 else range(nch - 1, -1, -1)
                        for n in order:
                            ti = tb + n
                            K.copy(store[:, ti, :], Sx[:], eng="pool")
                            pb = K.psum()
                            K.mm(pb[:, 0:128], kwT[:, ti, :], kvtok[:, ti, 128:256])
                            K.stt(Sx[:], Sx[:], kw[:, 2 + d:3 + d], pb[:, 0:128], ALU.mult, ALU.add)
                        if not smp:
                            finals.append(K.dma(o_ret[c0 // 256, j, d, h], Sx[:]))
                    for n0 in range(0, nch, 4):
                        pbo = K.psum()
                        cnt = min(4, nch - n0)
                        for n in range(n0, n0 + cnt):
                            ti = tb + n
                            cs = slice(ti * 128, (ti + 1) * 128)
                            pbs = K.psum()
                            K.mm(pbs[:, 0:128], kTh[:, cs], qTh[:, cs])
                            MT = K.arotbuf("MT", [128], BF16, n=2)
                            K.tt(MT[:], pbs[:, 0:128], Dm[:], ALU.mult)
                            qf = K.arotbuf("qf", [128], BF16, n=2)
                            qb = K.arotbuf("qb", [128], BF16, n=2)
                            K.tt(qf[:], qTh[:, cs], Wqf[:], ALU.mult, eng="pool")
                            K.tt(qb[:], qTh[:, cs], Wqb[:], ALU.mult, eng="pool")
                            oc = pbo[:, (n - n0) * 128:(n - n0 + 1) * 128]
                            K.mm(oc, kvtok[:, ti, 128:256], MT[:], start=True, stop=False)
                            K.mm(oc, Sfs[:, ti, :], qf[:], start=False, stop=False)
                            K.mm(oc, Sbs[:, ti, :], qb[:], start=False, stop=True)
                        K.copy(o_sb[:, (tb + n0) * 128:(tb + n0 + cnt) * 128], pbo[:, 0:cnt * 128], eng="act")
                for t0 in range(0, T, 512):
                    ob = o_sb[:, t0:t0 + 512]
                    o2 = K.arotbuf("gn_o2", [512], F32, n=1)
                    K.tt(o2[:], ob, ob, ALU.mult)
                    pm = K.psum()
                    pq = K.psum()
                    for src_, pdst in ((ob, pm), (o2[:], pq)):
                        hi = K.arotbuf("gn_hi", [512], BF16, n=2)
                        lo = K.arotbuf("gn_lo", [512], BF16, n=2)
                        K.copy(hi[:], src_, eng="act")
                        K.tt(lo[:], src_, hi[:], ALU.subtract)
                        K.mm(pdst[:], ones128b[:], hi[:], start=True, stop=False)
                        K.mm(pdst[:], ones128b[:], lo[:], start=False, stop=True)
                    mean = K.arotbuf("gn_mean", [512], F32, n=1)
                    K.copy(mean[:], pm[:], eng="act")
                    msq = K.arotbuf("gn_msq", [512], F32, n=1)
                    K.tt(msq[:], mean[:], mean[:], ALU.mult, eng="pool")
                    K.tt(msq[:], pq[:], msq[:], ALU.subtract)
                    K.act(msq[:], msq[:], AF.Sqrt, bias=epsT[:, 0:1])
                    K.recip(msq[:], msq[:])
                    K.tt(o2[:], ob, mean[:], ALU.subtract, eng="pool")
                    K.tt(o2[:], o2[:], msq[:], ALU.mult)
                    K.stt(yT[:, h, t0:t0 + 512], o2[:], gnT[:, j * 4 + h:j * 4 + h + 1], sgTh[:, t0:t0 + 512], ALU.mult, ALU.mult)
            K.reset_to(m0)
            S.tag = "L%d:e_mlaproj" % layer
            cqn = K.af([2, T], BF16)
            ckvK = K.af([3072], BF16)
            krK = K.af([3072], BF16)
            ropeC = K.af([2048], F32)
            ropeS = K.af([2048], F32)
            K.dma(ropeC[64:96, :], ropeC_d)
            K.dma(ropeS[64:96, :], ropeS_d)
            m1 = K.mark()
            kvn_bc = K.af([256], F32)
            K.dma(kvn_bc[:], kvn_bc_d)
            wm = K.af([8, 416], BF16)
            K.dma(wm[:], w_in_v[:, :, 2048:2464], q="pool")
            wkr = K.af([8, 96], BF16)
            wkrs = K.af([8, 96], BF16)
            K.memset(wkr[:], 0.0, eng="pool")
            K.memset(wkrs[:], 0.0, eng="pool")
            K.dma(wkr[:, :, 64:96], w_in_v[:, :, 2432:2464], q="pool")
            K.dma(wkrs[:, :, 64:80], w_in_v[:, :, 2448:2464], q="pool")
            K.dma(wkrs[:, :, 80:96], w_in_v[:, :, 2432:2448], q="pool")
            for t0 in range(0, T, 512):
                bl = slice(t0, t0 + 512)
                pbs = [K.psum(), K.psum()]
                for cc in range(2):
                    for k in range(8):
                        K.mm(pbs[cc][:], wm[:, k, cc * 128:(cc + 1) * 128], hT[:, k, bl], start=(k == 0), stop=(k == 7))
                sq = K.arotbuf("m_sq", [2, 512], BF16, n=1)
                for cc in range(2):
                    K.act(sq[:, cc, :], pbs[cc][:], AF.Square)
                pr = K.psum()
                for cc in range(2):
                    K.mm(pr[:], ones256[:], sq[:, cc, :], start=(cc == 0), stop=(cc == 1))
                rstd = K.arotbuf("m_rstd", [512], F32, n=1)
                K.act(rstd[:], pr[:], AF.Sqrt, bias=epsT[:, 0:1])
                K.recip(rstd[:], rstd[:])
                for cc in range(2):
                    K.stt(cqn[:, cc, bl], pbs[cc][:], qnT[:, j * 2 + cc:j * 2 + cc + 1], rstd[:], ALU.mult, ALU.mult)
                pc = K.psum()
                for k in range(8):
                    K.mm(pc[:], wm[:, k, 256:384], hT[:, k, bl], start=(k == 0), stop=(k == 7))
                K.act(sq[:, 0, :], pc[:], AF.Square)
                pr2 = K.psum()
                K.mm(pr2[:], ones128b[:], sq[:, 0, :])
                rstd2 = K.arotbuf("m_rstd2", [512], F32, n=1)
                K.act(rstd2[:], pr2[:], AF.Sqrt, bias=epsT[:, 0:1])
                K.recip(rstd2[:], rstd2[:])
                K.stt(ckvK[:, bl], pc[:], kvnT[:, j:j + 1], rstd2[:], ALU.mult, ALU.mult)
                pk = K.psum()
                for k in range(8):
                    K.mm(pk[0:96, :], wkr[:, k, :], hT[:, k, bl], start=(k == 0), stop=(k == 7))
                if t0 < NPT:
                    K.copy(krK[64:96, bl], pk[64:96, :], eng="dve")
                else:
                    pks = K.psum()
                    for k in range(8):
                        K.mm(pks[0:96, :], wkrs[:, k, :], hT[:, k, bl], start=(k == 0), stop=(k == 7))
                    st0 = t0 - NPT
                    t1 = K.arotbuf("m_t1", [512], F32, n=1)
                    t2 = K.arotbuf("m_t2", [512], F32, n=1)
                    K.tt(t1[64:96, :], pk[64:96, :], ropeC[64:96, st0:st0 + 512], ALU.mult)
                    K.tt(t2[64:96, :], pks[64:96, :], ropeS[64:96, st0:st0 + 512], ALU.mult)
                    K.tt(krK[64:96, bl], t1[64:96, :], t2[64:96, :], ALU.add, eng="pool")
            for i in range(4):
                stg = K.arotbuf("m_stg", [128], F32, n=2)
                K.dma(stg[:], ckv_in[j, i * 128:(i + 1) * 128, :])
                pb = K.psum()
                K.tr(pb[:, 0:128], stg[:], ident[:])
                K.copy(ckvK[:, 2560 + i * 128:2560 + (i + 1) * 128], pb[:, 0:128], eng="dve")
                stg2 = K.arotbuf("m_stg2", [96], F32, n=2)
                K.memset(stg2[:, 0:64], 0.0, eng="pool")
                K.dma(stg2[:, 64:96], kr_in[j, i * 128:(i + 1) * 128, :])
                pb2 = K.psum()
                K.tr(pb2[0:96, 0:128], stg2[:], ident[:])
                K.copy(krK[64:96, 2560 + i * 128:2560 + (i + 1) * 128], pb2[64:96, 0:128], eng="dve")
            for ti in range(4):
                pt = K.psum()
                for k in range(8):
                    K.mm(pt[:, 0:160], hT[:, k, ti * 128:(ti + 1) * 128], wm[:, k, 256:416], start=(k == 0), stop=(k == 7))
                junk = K.arotbuf("m_junk", [128], F32, n=1)
                ssq = K.arotbuf("m_ssq", [1], F32, n=2)
                K.memset(ssq[:], 0.0, eng="pool")
                K.act(junk[:], pt[:, 0:128], AF.Square, accum_out=ssq[:, 0:1])
                K.act(ssq[:], ssq[:], AF.Sqrt, bias=epsT[:, 0:1], scale=1.0 / 128.0)
                K.recip(ssq[:], ssq[:])
                ock = K.arotbuf("m_ock", [128], F32, n=2)
                K.stt(ock[:], pt[:, 0:128], ssq[:, 0:1], kvn_bc[:, j * 128:(j + 1) * 128], ALU.mult, ALU.mult)
                sq_i, r0 = ti // 2, (ti % 2) * 128
                finals.append(K.dma(o_ckv[sq_i, j, r0:r0 + 128, :], ock[:]))
                okr = K.arotbuf("m_okr", [32], F32, n=2)
                K.copy(okr[:], pt[:, 128:160], eng="act")
                finals.append(K.dma(o_kr[sq_i, j, r0:r0 + 128, :], okr[:]))
            K.reset_to(m1)
            S.tag = "L%d:e_mlaattn" % layer
            wuq = K.af([2, 768], BF16)
            K.dma(wuq[:], mla_w_uq[j].rearrange("(k p) o -> p k o", p=128), q="pool")
            wuqs = K.af([2, 8, 96], BF16)
            K.memset(wuqs[:], 0.0, eng="pool")
            uqv = mla_w_uq[j].rearrange("(k p) (h x) -> p k h x", p=128, h=8)
            for k_ in range(2):
                K.dma(wuqs[:, k_, :, 64:80], uqv[:, k_, :, 80:96], q="pool")
                K.dma(wuqs[:, k_, :, 80:96], uqv[:, k_, :, 64:80], q="pool")
            wukv = K.af([1024], BF16)
            K.dma(wukv[:], mla_w_ukv[j], q="pool")
            wukv_v = wukv[:].rearrange("p (h x) -> p h x", h=8)
            v_all = K.af([20, 512], BF16)
            for (c0, L, smp) in SEQS:
                attn_flush()
                nk = L + (512 if smp else 0)
                nkt = nk // 128
                for kt in range(nkt):
                    pb = K.psum()
                    K.mm(pb[:].rearrange("p (h x) -> p h x", h=8), ckvK[:, c0 + kt * 128:c0 + (kt + 1) * 128], wukv_v[:, :, 64:128])
                    K.copy(v_all[:, kt, :], pb[:], eng="act")
                for h in range(8):
                    kh = K.arotbuf("kh", [2560], BF16, n=1)
                    for kb in range(0, nk, 512):
                        w = min(512, nk - kb)
                        pb = K.psum()
                        K.mm(pb[0:64, 0:w], wukv[:, h * 128:h * 128 + 64], ckvK[:, c0 + kb:c0 + kb + w])
                        K.copy(kh[0:64, kb:kb + w], pb[0:64, 0:w], eng="dve")
                    K.copy(kh[64:96, 0:nk], krK[64:96, c0:c0 + nk], eng="pool")
                    qh = K.arotbuf("qh", [2048], BF16, n=1)
                    for qb0 in range(0, L, 512):
                        w = min(512, L - qb0)
                        pb = K.psum()
                        for k in range(2):
                            K.mm(pb[0:96, 0:w], wuq[:, k, h * 96:(h + 1) * 96], cqn[:, k, c0 + qb0:c0 + qb0 + w], start=(k == 0), stop=(k == 1))
                        if smp:
                            pb2 = K.psum()
                            for k in range(2):
                                K.mm(pb2[0:96, 0:w], wuqs[:, k, h, :], cqn[:, k, c0 + qb0:c0 + qb0 + w], start=(k == 0), stop=(k == 1))
                            t1 = K.arotbuf("q_t1", [512], F32, n=1)
                            t2 = K.arotbuf("q_t2", [512], F32, n=1)
                            K.tt(t1[64:96, 0:w], pb[64:96, 0:w], ropeC[64:96, qb0:qb0 + w], ALU.mult)
                            K.tt(t2[64:96, 0:w], pb2[64:96, 0:w], ropeS[64:96, qb0:qb0 + w], ALU.mult)
                            K.tt(t1[64:96, 0:w], t1[64:96, 0:w], t2[64:96, 0:w], ALU.add, eng="pool")
                            K.act(qh[0:64, qb0:qb0 + w], pb[0:64, 0:w], AF.Identity, scale=MLA_SCALE)
                            K.act(qh[64:96, qb0:qb0 + w], t1[64:96, 0:w], AF.Identity, scale=MLA_SCALE)
                        else:
                            K.act(qh[0:96, qb0:qb0 + w], pb[0:96, 0:w], AF.Identity, scale=MLA_SCALE)
                    half = h % 2
                    hs = slice(half * 64, half * 64 + 64)
                    kbanks = [(kh[0:96, kb:min(kb + 512, nk)], None) for kb in range(0, nk, 512)]
                    ktiles = [(kt * 128, 128, v_all[:, kt, (h // 2) * 128:(h // 2) * 128 + 128]) for kt in range(nkt)]
                    for qt in range(L // 128):
                        attn_unit(qh[0:96, qt * 128:(qt + 1) * 128], kbanks, ktiles,
                                  hs, yT[hs, 4 + h // 2, c0 + qt * 128:c0 + (qt + 1) * 128], "m")
            attn_flush()
            K.reset_to(m0)
            S.tag = "L%d:e_out" % layer
            out_proj_post(even_w_out[j], yT, gg)

        NA_SCALE = 0.125
        NEGV = -30000.0
        TWO_PI = 2.0 * math.pi
        CW1 = 6.28125
        CW2 = TWO_PI - 6.28125
        SIN_S = 1.0 - 2e-6
        GELU_C = 1.5957691216057308

        def sincos(ang, kiI, kf, sin_out, cos_out):
            K.ts(kiI, ang, 1.0 / TWO_PI, None, ALU.mult)
            K.copy(kf, kiI)
            K.stt(ang, kf, -CW1, ang, ALU.mult, ALU.add)
            K.stt(ang, kf, -CW2, ang, ALU.mult, ALU.add)
            K.ts(ang, ang, -3.1415925, 3.1415925, ALU.max, ALU.min)
            K.act(sin_out, ang, AF.Sin, scale=SIN_S)
            K.act(kf, ang, AF.Abs)
            K.act(cos_out, kf, AF.Sin, scale=-SIN_S, bias=halfpi[:, 0:1])

        def scan(out, d0, d1, init):
            rd = [d0, d1] + ([] if isinstance(init, float) else [init])
            return S.op("dve", lambda e: e.tensor_tensor_scan(out=out, data0=d0, data1=d1, initial=init, op0=ALU.mult, op1=ALU.add),
                        reads=rd, writes=[out])

        def odd_layer(layer, j, mod):
            gs, shift, gg = mod_vectors(layer, mod, 0)
            K.phase()
            yT = K.af([8, T], BF16)
            m0 = K.mark()
            w_in_v = odd_w_in[j].rearrange("(k p) o -> p k o", p=128)

            def make_hT():
                hT = K.af([8, T], BF16)
                mk = K.mark()
                for t0 in range(0, T, 512):
                    pre_block(t0, 512, 0 if t0 < NPT else 1, gs, shift, hT[:, :, t0:t0 + 512])
                K.reset_to(mk)
                return hT

            S.tag = "L%d:o_s5" % layer
            uT = K.af([4, T], BF16)
            mS = K.mark()
            hT = make_hT()
            wu = K.af([8, 512], BF16)
            K.dma(wu[:], w_in_v[:, :, 0:512], q="pool")
            for t0 in range(0, T, 512):
                for c in range(4):
                    pb = K.psum()
                    for k in range(8):
                        K.mm(pb[:], wu[:, k, c * 128:(c + 1) * 128], hT[:, k, t0:t0 + 512], start=(k == 0), stop=(k == 7))
                    K.copy(uT[:, c, t0:t0 + 512], pb[:], eng=("act" if c % 2 else "dve"))
            K.reset_to(mS)
            ygT = yT[:, 4:8, :]
            iotaF = K.af([2048], F32)
            K.dma(iotaF[:], iota_d)
            mT = K.mark()
            y_acc = K.af([T], F32)
            e_tb = K.ptr // 4
            tb = K.af([3, 2048], F32)
            ang = tb[:, 0, :]
            kiI = K.arI[:, e_tb + 2048:e_tb + 4096]
            kf = tb[:, 2, :]
            gbuf = [tb[:, 1, :], tb[:, 2, :]]
            sinT = K.af([2048], F32)
            cosT = K.af([2048], F32)
            rtile = tb[:, 0, :]
            bp = K.af([2, 2048], F32)
            Hb = K.af([2, 2048], BF16)
            prm = [K.af([1104], F32) for _ in range(2)]
            der = K.af([2, 14, 16], F32)
            bbt = K.af([2, 2, 16, 16], F32)
            cng = K.af([2, 16, 16], F32)
            tmp3 = K.af([16, 16], F32)
            hfin = K.af([2, 2, 2, 16], F32)
            e_s = K.ptr // 4
            sml = K.af([3, 16], F32)
            smlI = K.arI[:, e_s + 16:e_s + 32]
            for d in range(2):
                P = prm[d]
                K.dma(P[:], s5p[j, d])
                lre, lim, lst = P[:, 0:16], P[:, 16:32], P[:, 32:48]
                bre3 = P[:, 80:336].rearrange("p (s c) -> p s c", s=16)
                bim3 = P[:, 336:592].rearrange("p (s c) -> p s c", s=16)
                cim3 = P[:, 848:1104].rearrange("p (s c) -> p s c", s=16)
                Dv = lambda i, d=d: der[:, d, i, :]
                K.act(Dv(0), lst, AF.Exp)
                K.tt(Dv(1), lre, Dv(0), ALU.mult)
                K.act(Dv(1), Dv(1), AF.Exp)
                K.tt(Dv(2), lim, Dv(0), ALU.mult)
                K.copy(sml[:, 0, :], Dv(2))
                sincos(sml[:, 0, :], smlI, sml[:, 2, :], Dv(3), Dv(4))
                K.tt(Dv(5), Dv(1), Dv(4), ALU.mult)
                K.tt(Dv(6), Dv(1), Dv(3), ALU.mult)
                K.ts(Dv(5), Dv(5), -1.0, None, ALU.add)
                K.tt(Dv(7), lre, lre, ALU.mult)
                K.tt(Dv(8), lim, lim, ALU.mult)
                K.tt(Dv(7), Dv(7), Dv(8), ALU.add)
                K.recip(Dv(7), Dv(7))
                K.tt(Dv(8), Dv(5), lre, ALU.mult)
                K.tt(Dv(9), Dv(6), lim, ALU.mult)
                K.tt(Dv(8), Dv(8), Dv(9), ALU.add)
                K.tt(Dv(8), Dv(8), Dv(7), ALU.mult)
                K.tt(Dv(9), Dv(6), lre, ALU.mult)
                K.tt(Dv(10), Dv(5), lim, ALU.mult)
                K.tt(Dv(9), Dv(9), Dv(10), ALU.subtract)
                K.tt(Dv(9), Dv(9), Dv(7), ALU.mult)
                K.ts(Dv(13), Dv(2), 1.0 / TWO_PI, None, ALU.mult)
                K.ts(Dv(11), Dv(13), -1.0, None, ALU.mult)
                K.ts(Dv(12), Dv(13), 2049.0, None, ALU.mult)
                zre_b = Dv(8).unsqueeze(2).to_broadcast([128, 16, 16])
                zim_b = Dv(9).unsqueeze(2).to_broadcast([128, 16, 16])
                K.tt(bbt[:, d, 0], bre3, zre_b, ALU.mult)
                K.tt(tmp3[:], bim3, zim_b, ALU.mult)
                K.tt(bbt[:, d, 0], bbt[:, d, 0], tmp3[:], ALU.subtract)
                K.tt(bbt[:, d, 1], bim3, zre_b, ALU.mult)
                K.tt(tmp3[:], bre3, zim_b, ALU.mult)
                K.tt(bbt[:, d, 1], bbt[:, d, 1], tmp3[:], ALU.add)
                K.ts(cng[:, d], cim3, -1.0, None, ALU.mult)
            for c in range(4):
                first = True
                for d in range(2):
                    P = prm[d]
                    cre3 = P[:, 592:848].rearrange("p (s c) -> p s c", s=16)
                    for s_ in range(4 * c, 4 * c + 4):
                        base = (s_ % 4) * 32
                        if d == 0:
                            K.ts(ang, iotaF[:], der[:, d, 13, s_:s_ + 1], None, ALU.mult, eng="pool")
                        else:
                            K.ts(ang, iotaF[:], der[:, d, 11, s_:s_ + 1], der[:, d, 12, s_:s_ + 1], ALU.mult, ALU.add, eng="pool")
                        K.ts(kiI, ang, 1.0, None, ALU.mult)
                        K.copy(kf, kiI)
                        K.tt(ang, ang, kf, ALU.subtract)
                        K.act(sinT[:], ang, AF.Sin, scale=TWO_PI * SIN_S)
                        K.act(kf, ang, AF.Abs)
                        K.act(cosT[:], kf, AF.Sin, scale=-TWO_PI * SIN_S, bias=halfpi[:, 0:1])
                        K.ts(rtile[:], iotaF[:], 0.0, der[:, d, 1, s_:s_ + 1], ALU.mult, ALU.add, eng="pool")
                        BBT = []
                        for ri in range(2):
                            Wp = K.arotbuf("s5_Wp", [128], F32, n=2)
                            K.memset(Wp[:], 0.0, eng="pool")
                            K.copy(Wp[0:64, base:base + 16], bbt[0:64, d, ri, s_, :], eng="pool")
                            K.copy(Wp[64:128, base + 16:base + 32], bbt[64:128, d, ri, s_, :], eng="pool")
                            pb = K.psum()
                            K.tr(pb[:, 0:128], Wp[:], ident[:])
                            bt = K.arotbuf("s5_BBT%d" % ri, [128], BF16, n=2)
                            K.copy(bt[:], pb[:, 0:128], eng="act")
                            BBT.append(bt)
                        CP = []
                        for ri, csrc in ((0, cre3), (1, cng[:, d])):
                            cp = K.arotbuf("s5_Cp%d" % ri, [128], BF16, n=2)
                            K.memset(cp[:], 0.0, eng="pool")
                            K.copy(cp[0:64, base:base + 16], csrc[0:64, s_, :], eng="pool")
                            K.copy(cp[64:128, base + 16:base + 32], csrc[64:128, s_, :], eng="pool")
                            CP.append(cp)
                        for seg in range(2):
                            if seg == 0:
                                c0, L, nseq = 0, 256, 2
                            else:
                                c0, L, nseq = 512, 2048, 1
                            tab0 = 0 if d == 0 else 2048 - L
                            Wt = L * nseq
                            if seg == 0:
                                v3 = lambda ap: ap.rearrange("p (q t) -> p q t", q=2)
                                tabv = lambda tb_, b0, w: tb_[:, tab0:tab0 + 256].unsqueeze(1).to_broadcast([128, 2, 256])
                            else:
                                v3 = lambda ap: ap
                                tabv = lambda tb_, b0, w: tb_[:, tab0 + b0:tab0 + b0 + w]
                            for b0 in range(0, Wt, 512):
                                w = min(512, Wt - b0)
                                pr = K.psum()
                                pi_ = K.psum()
                                K.mm(pr[:, 0:w], BBT[0][:], uT[:, c, c0 + b0:c0 + b0 + w])
                                K.mm(pi_[:, 0:w], BBT[1][:], uT[:, c, c0 + b0:c0 + b0 + w])
                                cosv = tabv(cosT, b0, w)
                                sinv = tabv(sinT, b0, w)
                                t1 = K.arotbuf("s5_t1", [512], F32, n=2)
                                t2 = K.arotbuf("s5_t2", [512], F32, n=2)
                                t3 = K.arotbuf("s5_t3", [512], F32, n=2)
                                t4 = K.arotbuf("s5_t4", [512], F32, n=2)
                                K.tt(v3(t1[:, 0:w]), v3(pr[:, 0:w]), cosv, ALU.mult)
                                K.tt(v3(t2[:, 0:w]), v3(pi_[:, 0:w]), sinv, ALU.mult)
                                K.tt(bp[:, 0, b0:b0 + w], t1[:, 0:w], t2[:, 0:w], ALU.add, eng="pool")
                                K.tt(v3(t3[:, 0:w]), v3(pi_[:, 0:w]), cosv, ALU.mult)
                                K.tt(v3(t4[:, 0:w]), v3(pr[:, 0:w]), sinv, ALU.mult)
                                K.tt(bp[:, 1, b0:b0 + w], t3[:, 0:w], t4[:, 0:w], ALU.subtract, eng="pool")
                            for q_ in range(nseq):
                                qs = slice(q_ * L, (q_ + 1) * L)
                                for ri in range(2):
                                    init = P[:, 48 + ri * 16 + s_:48 + ri * 16 + s_ + 1] if seg == 1 else 0.0
                                    if d == 0:
                                        scan(gbuf[ri][:, qs], rtile[:, 0:L], bp[:, ri, qs], init)
                                    else:
                                        scan(gbuf[ri][:, qs][:, ::-1], rtile[:, 0:L][:, ::-1], bp[:, ri, qs][:, ::-1], init)
                            for b0 in range(0, Wt, 512):
                                w = min(512, Wt - b0)
                                cosv = tabv(cosT, b0, w)
                                sinv = tabv(sinT, b0, w)
                                t1 = K.arotbuf("s5_t1", [512], F32, n=2)
                                t2 = K.arotbuf("s5_t2", [512], F32, n=2)
                                t3 = K.arotbuf("s5_t3", [512], F32, n=2)
                                t4 = K.arotbuf("s5_t4", [512], F32, n=2)
                                K.tt(v3(t1[:, 0:w]), v3(gbuf[0][:, b0:b0 + w]), cosv, ALU.mult, eng="pool")
                                K.tt(v3(t2[:, 0:w]), v3(gbuf[1][:, b0:b0 + w]), sinv, ALU.mult, eng="pool")
                                K.tt(Hb[:, 0, b0:b0 + w], t1[:, 0:w], t2[:, 0:w], ALU.subtract)
                                K.tt(v3(t3[:, 0:w]), v3(gbuf[0][:, b0:b0 + w]), sinv, ALU.mult)
                                K.tt(v3(t4[:, 0:w]), v3(gbuf[1][:, b0:b0 + w]), cosv, ALU.mult)
                                K.tt(Hb[:, 1, b0:b0 + w], t3[:, 0:w], t4[:, 0:w], ALU.add)
                                py = K.psum()
                                K.mm(py[:, 0:w], CP[0][:], Hb[:, 0, b0:b0 + w], start=True, stop=False)
                                K.mm(py[:, 0:w], CP[1][:], Hb[:, 1, b0:b0 + w], start=False, stop=True)
                                ya = y_acc[:, c0 + b0:c0 + b0 + w]
                                if first:
                                    K.copy(ya, py[:, 0:w], eng="act")
                                else:
                                    K.tt(ya, ya, py[:, 0:w], ALU.add)
                            if seg == 0:
                                col = L - 1 if d == 0 else 0
                                tc_ = tab0 + col
                                g0c = gbuf[0][:, 0:512].rearrange("p (q t) -> p q t", q=2)[:, :, col]
                                g1c = gbuf[1][:, 0:512].rearrange("p (q t) -> p q t", q=2)[:, :, col]
                                cc_ = cosT[:, tc_:tc_ + 1].to_broadcast([128, 2])
                                sc_ = sinT[:, tc_:tc_ + 1].to_broadcast([128, 2])
                                f1 = K.arotbuf("s5_f1", [2], F32, n=2)
                                f2 = K.arotbuf("s5_f2", [2], F32, n=2)
                                K.tt(f1[:], g0c, cc_, ALU.mult, eng="pool")
                                K.tt(f2[:], g1c, sc_, ALU.mult, eng="pool")
                                K.tt(hfin[:, :, d, 0, s_], f1[:], f2[:], ALU.subtract, eng="pool")
                                f1 = K.arotbuf("s5_f1", [2], F32, n=2)
                                f2 = K.arotbuf("s5_f2", [2], F32, n=2)
                                K.tt(f1[:], g0c, sc_, ALU.mult, eng="pool")
                                K.tt(f2[:], g1c, cc_, ALU.mult, eng="pool")
                                K.tt(hfin[:, :, d, 1, s_], f1[:], f2[:], ALU.add, eng="pool")
                        first = False
                for t0 in range(0, T, 512):
                    yb = y_acc[:, t0:t0 + 512]
                    K.stt(yb, uT[:, c, t0:t0 + 512], s5dT[:, j * 4 + c:j * 4 + c + 1], yb, ALU.mult, ALU.add)
                    y2 = K.arotbuf("s5_y2", [512], F32, n=1)
                    K.tt(y2[:], yb, yb, ALU.mult, eng="pool")
                    K.ts(y2[:], y2[:], 0.044715, 1.0, ALU.mult, ALU.add)
                    K.tt(y2[:], y2[:], yb, ALU.mult, eng="pool")
                    K.act(y2[:], y2[:], AF.Sigmoid, scale=GELU_C)
                    K.tt(ygT[:, c, t0:t0 + 512], yb, y2[:], ALU.mult)
            for si in range(2):
                for d in range(2):
                    finals.append(K.dma(o_s5re[si, j, d], hfin[:, si, d, 0, :]))
                    finals.append(K.dma(o_s5im[si, j, d], hfin[:, si, d, 1, :]))
            S.tag = "L%d:o_glu" % layer
            K.reset_to(mT)
            wg = K.af([4, 512], BF16)
            K.dma(wg[:], s5_glu_w[j].rearrange("(k p) o -> p k o", p=128), q="pool")
            for t0 in range(0, T, 512):
                for c2 in range(4):
                    pb = K.psum()
                    for k in range(4):
                        K.mm(pb[:], wg[:, k, c2 * 128:(c2 + 1) * 128], ygT[:, k, t0:t0 + 512], start=(k == 0), stop=(k == 3))
                    sg = K.arotbuf("s5_sg", [512], F32, n=2)
                    K.act(sg[:], pb[:], AF.Sigmoid, bias=glubT[:, j * 4 + c2:j * 4 + c2 + 1])
                    K.tt(yT[:, c2, t0:t0 + 512], ygT[:, c2, t0:t0 + 512], sg[:], ALU.mult)

            K.reset_to(m0)
            S.tag = "L%d:o_naproj" % layer
            qT = K.af([4, T], BF16)
            kT = K.af([4, 3072], BF16)
            vtok = K.af([24, 512], BF16)
            mN = K.mark()
            hT = make_hT()
            wq = K.af([8, 512], BF16)
            wk = K.af([8, 512], BF16)
            wv = K.af([8, 512], BF16)
            K.dma(wq[:], w_in_v[:, :, 512:1024], q="pool")
            K.dma(wk[:], w_in_v[:, :, 1024:1536], q="pool")
            K.dma(wv[:], w_in_v[:, :, 1536:2048], q="pool")
            for t0 in range(0, T, 512):
                for c in range(4):
                    pb = K.psum()
                    for k in range(8):
                        K.mm(pb[:], wq[:, k, c * 128:(c + 1) * 128], hT[:, k, t0:t0 + 512], start=(k == 0), stop=(k == 7))
                    K.act(qT[:, c, t0:t0 + 512], pb[:], AF.Identity, scale=NA_SCALE)
                    pb = K.psum()
                    for k in range(8):
                        K.mm(pb[:], wk[:, k, c * 128:(c + 1) * 128], hT[:, k, t0:t0 + 512], start=(k == 0), stop=(k == 7))
                    K.copy(kT[:, c, t0:t0 + 512], pb[:], eng="dve")
            for ti in range(20):
                pb = K.psum()
                for k in range(8):
                    K.mm(pb[:], hT[:, k, ti * 128:(ti + 1) * 128], wv[:, k, :], start=(k == 0), stop=(k == 7))
                K.copy(vtok[:, ti, :], pb[:], eng="act")
                if ti < 4:
                    sq_i, r0 = ti // 2, (ti % 2) * 128
                    ov = K.arotbuf("na_ov", [512], F32, n=2)
                    K.copy(ov[:], pb[:], eng="dve")
                    finals.append(K.dma(o_nav[sq_i, j, r0:r0 + 128, :], ov[:]))
                    pb2 = K.psum()
                    for k in range(8):
                        K.mm(pb2[:], hT[:, k, ti * 128:(ti + 1) * 128], wk[:, k, :], start=(k == 0), stop=(k == 7))
                    ok_ = K.arotbuf("na_ok", [512], F32, n=2)
                    K.copy(ok_[:], pb2[:], eng="act")
                    finals.append(K.dma(o_nak[sq_i, j, r0:r0 + 128, :], ok_[:]))
            for i in range(4):
                stg = K.arotbuf("na_stg", [512], F32, n=2)
                K.dma(stg[:], nak_in[j, i * 128:(i + 1) * 128, :])
                pb = K.psum()
                for c in range(4):
                    K.tr(pb[:, c * 128:(c + 1) * 128], stg[:, c * 128:(c + 1) * 128], ident[:])
                K.copy(kT[:, :, 2560 + i * 128:2560 + (i + 1) * 128], pb[:].rearrange("p (c t) -> p c t", c=4), eng="dve")
                K.dma(vtok[:, 20 + i, :], nav_in[j, i * 128:(i + 1) * 128, :], q="pool")
            K.reset_to(mN)
            Ap = K.af([64, 8, 15], F32)
            rp = K.af([120], F32)
            K.memset(rp[0:32, :], NEGV)
            K.dma(rp[0:31, :], na_rpb[j].rearrange("h r c -> c (h r)"), allow_slow_non_contiguous=True)
            rhi = K.af([120], BF16)
            rlo = K.af([120], BF16)
            K.copy(rhi[0:32, :], rp[0:32, :], eng="act")
            K.tt(rlo[0:32, :], rp[0:32, :], rhi[0:32, :], ALU.subtract)
            for q4 in range(4):
                oh = K.arotbuf("na_oh", [16, 128], BF16, n=2)
                K.dma(oh[0:32, :, :], oh_d[:, q4 * 16:(q4 + 1) * 16, :], q="pool")
                for g4 in range(4):
                    pb = K.psum()
                    for i in range(4):
                        oc = pb[:, i * 120:(i + 1) * 120]
                        K.mm(oc, oh[0:32, g4 * 4 + i, :], rhi[0:32, :], start=True, stop=False)
                        K.mm(oc, oh[0:32, g4 * 4 + i, :], rlo[0:32, :], start=False, stop=True)
                    kc0 = q4 * 16 + g4 * 4
                    K.copy(Ap[:, kc0:kc0 + 4, :, :].rearrange("p k h r -> p k (h r)"), pb[:, 0:480].rearrange("p (k x) -> p k x", k=4),
                           eng=("act" if g4 % 2 else "dve"))
            S.tag = "L%d:o_naattn" % layer
            for (c0, L, smp) in SEQS[0:2]:
                for h in range(8):
                    ch, half = h // 2, h % 2
                    hsl = slice(half * 64, half * 64 + 64)
                    for qt in range(2):
                        qc = slice(c0 + qt * 128, c0 + (qt + 1) * 128)
                        attn_unit(qT[hsl, ch, qc], [(kT[hsl, ch, c0:c0 + 256], None)],
                                  [(t_ * 128, 128, vtok[:, c0 // 128 + t_, ch * 128:(ch + 1) * 128]) for t_ in range(2)],
                                  hsl, yT[hsl, 4 + ch, qc], "n")
            for i in range(16):
                if i < 2:
                    kr0, nr = 0, 8
                elif i > 13:
                    kr0, nr = 24, 8
                else:
                    kr0, nr = 2 * i - 4, 9
                kcol0 = 512 + kr0 * 64
                vt0 = 4 + kr0 // 2
                for h in range(8):
                    ch, half = h // 2, h % 2
                    hsl = slice(half * 64, half * 64 + 64)

                    def biasA(bank, w, i=i, h=h, kr0=kr0):
                        Ssb = K.arotbuf("na_S", [512], F32, n=2)
                        for ql in range(2):
                            qr = 2 * i + ql
                            rs_ = min(max(qr - 4, 0), 24)
                            mlo = max(0, rs_ - kr0)
                            mhi = min(8, rs_ - kr0 + 8)
                            ps_ = slice(ql * 64, ql * 64 + 64)
                            if mlo > 0:
                                K.memset(Ssb[ps_, 0:mlo * 64], NEGV, eng="pool")
                            if mhi < 8:
                                K.memset(Ssb[ps_, mhi * 64:512], NEGV, eng="pool")
                            dr0 = kr0 + mlo - qr + 7
                            nm_ = mhi - mlo
                            K.tt(Ssb[ps_, mlo * 64:mhi * 64].rearrange("p (m k) -> p m k", m=nm_),
                                 bank[ps_, mlo * 64:mhi * 64].rearrange("p (m k) -> p m k", m=nm_),
                                 Ap[ps_, :, h, dr0:dr0 + nm_].rearrange("p k r -> p r k"), ALU.add)
                        return Ssb[:, 0:512]

                    def biasB(bank, w, i=i, h=h, kr0=kr0):
                        Ssb = K.arotbuf("na_SB", [64], F32, n=2)
                        for ql in range(2):
                            qr = 2 * i + ql
                            rs_ = min(max(qr - 4, 0), 24)
                            ps_ = slice(ql * 64, ql * 64 + 64)
                            if rs_ <= kr0 + 8 < rs_ + 8:
                                dr = kr0 + 8 - qr + 7
                                K.tt(Ssb[ps_, 0:64], bank[ps_, 0:64], Ap[ps_, :, h, dr], ALU.add)
                            else:
                                K.memset(Ssb[ps_, 0:64], NEGV, eng="pool")
                        return Ssb[:, 0:64]

                    kbanks = [(kT[hsl, ch, 2560:3072], None), (kT[hsl, ch, kcol0:kcol0 + 512], biasA)]
                    ktiles = [(t_ * 128, 128, vtok[:, 20 + t_, ch * 128:(ch + 1) * 128]) for t_ in range(4)]
                    ktiles += [(512 + t_ * 128, 128, vtok[:, vt0 + t_, ch * 128:(ch + 1) * 128]) for t_ in range(4)]
                    if nr == 9:
                        kbanks.append((kT[hsl, ch, kcol0 + 512:kcol0 + 576], biasB))
                        ktiles.append((1024, 64, vtok[0:64, vt0 + 4, ch * 128:(ch + 1) * 128]))
                    qc = slice(512 + i * 128, 512 + (i + 1) * 128)
                    attn_unit(qT[hsl, ch, qc], kbanks, ktiles, hsl, yT[hsl, 4 + ch, qc], "n")
            attn_flush()
            K.reset_to(m0)
            S.tag = "L%d:o_out" % layer
            out_proj_post(odd_w_out[j], yT, gg)

        for layer in range(DEPTH):
            mod = ada(layer)
            if layer % 2 == 0 and ENABLE_EVEN:
                even_layer(layer, layer // 2, mod)
            if layer % 2 == 1 and ENABLE_ODD:
                odd_layer(layer, layer // 2, mod)
            gs, shift, gg = mod_vectors(layer, mod, 1)
            ffn(layer, gs, shift, gg)
            if dbg_stop == layer:
                break

        S.tag = "final"
        finals = []

        K.phase()

        def xT_to_out(dst, ntile, col0):
            for i in range(ntile):
                xb = K.arotbuf("xinT", [8, 128], F32)
                K.dma(xb[:], xT_v[:, :, col0 + i * 128: col0 + (i + 1) * 128])
                xo = K.arotbuf("xin", [D], F32)
                for g in range(2):
                    pb = K.psum()
                    for c in range(4):
                        K.tr(pb[:, c * 128:(c + 1) * 128], xb[:, g * 4 + c, :], ident[:])
                    K.copy(xo[:, g * 512:(g + 1) * 512], pb[:], eng=("act" if g else "dve"))
                finals.append(K.dma(dst[i * 128:(i + 1) * 128, :], xo[:]))

        xT_to_out(yp, NPT // 128, 0)
        xT_to_out(ys, LS // 128, NPT)
        S.finish(finals)
        global LAST_TAGS
        LAST_TAGS = S.tags
        S.emit()
    return nc


def _make_consts():
    i = np.arange(128, dtype=np.float32)
    jj = i[:, None]
    ii = i[None, :]
    rc = np.zeros((128, 772), np.float32)
    rc[:, 0:128] = np.maximum(ii - jj, 0)
    rc[:, 128:256] = (ii >= jj)
    rc[:, 256:384] = np.maximum(jj - ii, 0)
    rc[:, 384:512] = (jj >= ii)
    rc[:, 512:640] = np.broadcast_to(ii + 1.0, (128, 128))
    rc[:, 640:768] = np.broadcast_to(128.0 - ii, (128, 128))
    rc[:, 768] = 127.0 - i
    rc[:, 769] = i
    rc[:, 770] = 128.0
    inv = (10000.0 ** (-np.arange(8, dtype=np.float32) / np.float32(8))).astype(np.float32)
    t = np.arange(2048)
    row = (t // 64).astype(np.float32)
    col = (t % 64).astype(np.float32)
    ang = np.concatenate([row[:, None] * inv, col[:, None] * inv], axis=-1).astype(np.float32)
    cos, sin = np.cos(ang).T, np.sin(ang).T
    ropeC = np.concatenate([cos, cos], 0).astype(np.float32)
    ropeS = np.concatenate([-sin, sin], 0).astype(np.float32)
    oh = np.zeros((32, 64, 128), np.float32)
    qc = np.arange(64)
    cs = np.clip(qc - 8, 0, 48)
    for kc in range(64):
        jidx = np.clip(kc - qc + 15, 0, 30)
        for ql in range(2):
            oh[jidx, kc, ql * 64 + qc] = 1.0
            oh[31, kc, ql * 64 + qc] = ((kc < cs) | (kc >= cs + 16)).astype(np.float32)
    iota = np.broadcast_to(np.arange(1, 2049, dtype=np.float32)[None, :], (128, 2048))
    return {"rconst": rc, "ropeC": np.ascontiguousarray(ropeC), "ropeS": np.ascontiguousarray(ropeS),
            "oh": oh, "iota": np.ascontiguousarray(iota)}


def _s5_lay(a):
    a = np.asarray(a, np.float32)
    lead = a.shape[:-2] if a.ndim >= 2 else ()
    return a


def _s5_pack(inputs, s):
    def gp(a):
        a = np.asarray(a, np.float32).reshape(2, 2, 16, 2, 64)
        return a.transpose(0, 1, 3, 4, 2).reshape(2, 2, 128, 16)
    lre = gp(inputs["s5_lambda_re"])
    lim = gp(inputs["s5_lambda_im"])
    lst = gp(np.broadcast_to(np.asarray(inputs["s5_log_step"], np.float32)[..., None], (2, 2, 32, 64)))
    h0re = gp(np.asarray(inputs["state_s5_re"], np.float32)[s])
    h0im = gp(np.asarray(inputs["state_s5_im"], np.float32)[s])
    def gps(a):
        a = np.asarray(a, np.float32).reshape(2, 2, 16, 2, 64, 16)
        return a.transpose(0, 1, 3, 4, 2, 5).reshape(2, 2, 128, 256)
    def gsp(a):
        a = np.asarray(a, np.float32).reshape(2, 2, 16, 2, 16, 64)
        return a.transpose(0, 1, 3, 5, 2, 4).reshape(2, 2, 128, 256)
    pk = np.concatenate([lre, lim, lst, h0re, h0im, gps(inputs["s5_b_re"]), gps(inputs["s5_b_im"]),
                         gsp(inputs["s5_c_re"]), gsp(inputs["s5_c_im"])], axis=-1)
    return np.ascontiguousarray(pk.astype(np.float32))


CONSTS = _make_consts()


def make_in_maps(inputs):
    f = lambda a: np.ascontiguousarray(np.asarray(a, dtype=np.float32))
    maps = []
    ident = np.eye(128, dtype=np.float32)
    for core in range(8):
        s = core // 4
        m = {}
        m["xp"] = f(inputs["x_prompt"][2 * core:2 * core + 2]).reshape(NPT, D)
        m["xs"] = f(inputs["x_sample"][s])
        m["cv"] = f(np.stack([inputs["c_ctx"], inputs["c"][s]], 0))
        m["ident_in"] = ident
        m["ada_w"] = f(inputs["ada_w"])
        m["ada_b"] = f(inputs["ada_b"]).reshape(DEPTH * 48, 128)
        for nm in ("mix_pre_g", "mix_post_g", "ffn_pre_g", "ffn_post_g"):
            m[nm] = f(inputs[nm]).reshape(DEPTH * 8, 128)
        m["ffn_w_up"] = f(inputs["ffn_w_up"])
        m["ffn_conv_w"] = f(inputs["ffn_conv_w"]).reshape(DEPTH * 3 * 44, 128)
        m["ffn_conv_b"] = f(inputs["ffn_conv_b"]).reshape(DEPTH * 44, 128)
        m["ffn_w_down"] = f(inputs["ffn_w_down"])
        m["even_w_in"] = f(inputs["even_w_in"])
        m["even_w_out"] = f(inputs["even_w_out"])
        m["ret_logit_b"] = f(np.broadcast_to(np.asarray(inputs["ret_logit"], np.float32).reshape(2, 1, 8), (2, 128, 8)))
        m["ret_gn"] = f(inputs["ret_gn"]).reshape(8, 128)
        m["mla_q_norm"] = f(inputs["mla_q_norm"]).reshape(4, 128)
        m["mla_w_uq"] = f(inputs["mla_w_uq"])
        m["mla_kv_norm"] = f(inputs["mla_kv_norm"])
        m["kvn_bc"] = f(np.broadcast_to(np.asarray(inputs["mla_kv_norm"], np.float32).reshape(1, 256), (128, 256)))
        m["mla_w_ukv"] = f(inputs["mla_w_ukv"])
        m["state_ret_in"] = f(inputs["state_ret"][s])
        m["ckv_in"] = f(inputs["cache_mla_ckv"][s])
        m["kr_in"] = f(inputs["cache_mla_krope"][s])
        m["odd_w_in"] = f(inputs["odd_w_in"])
        m["odd_w_out"] = f(inputs["odd_w_out"])
        m["s5p"] = _s5_pack(inputs, s)
        m["s5_d"] = f(inputs["s5_d"]).reshape(8, 128)
        m["s5_glu_b"] = f(inputs["s5_glu_b"]).reshape(8, 128)
        m["s5_glu_w"] = f(inputs["s5_glu_w"])
        m["na_rpb"] = f(inputs["na_rpb"])
        m["nak_in"] = f(inputs["cache_na_k"][s]).reshape(2, 512, 512)
        m["nav_in"] = f(inputs["cache_na_v"][s]).reshape(2, 512, 512)
        m["oh_in"] = CONSTS["oh"]
        m["iota_in"] = CONSTS["iota"]
        m["rconst_in"] = CONSTS["rconst"]
        m["ropeC_in"] = CONSTS["ropeC"]
        m["ropeS_in"] = CONSTS["ropeS"]
        maps.append(m)
    return maps


def kernel(**inputs):
    nc = build_program()
    maps = make_in_maps(inputs)
    res = run_bass_kernel_spmd(nc, maps, core_ids=list(range(8)))
    R = res.results
    y_prompt = np.concatenate([R[c]["yp"].reshape(2, 256, D) for c in range(8)], 0)
    y_sample = np.stack([R[0]["ys"], R[4]["ys"]], 0)
    st_ret = np.concatenate([R[c]["o_ret"] for c in range(8)], 0)
    ck_ckv = np.concatenate([R[c]["o_ckv"] for c in range(8)], 0)
    ck_kr = np.concatenate([R[c]["o_kr"] for c in range(8)], 0)
    def s5o(name):
        o = np.concatenate([R[c][name] for c in range(8)], 0)
        o = o.reshape(16, 2, 2, 2, 64, 16).transpose(0, 1, 2, 5, 3, 4)
        return np.ascontiguousarray(o.reshape(16, 2, 2, 32, 64))
    st_re, st_im = s5o("o_s5re"), s5o("o_s5im")
    nak = np.concatenate([R[c]["o_nak"] for c in range(8)], 0).reshape(16, 2, 256, 8, 64)
    nav = np.concatenate([R[c]["o_nav"] for c in range(8)], 0).reshape(16, 2, 256, 8, 64)
    return (y_prompt, y_sample, st_ret, ck_ckv, ck_kr, st_re, st_im, nak, nav)
```

```python
import math
import numpy as np
import concourse.bass as bass
import concourse.mybir as mybir
from concourse.bass_utils import run_bass_kernel_spmd

F32 = mybir.dt.float32
BF16 = mybir.dt.bfloat16
I32 = mybir.dt.int32
ALU = mybir.AluOpType
AF = mybir.ActivationFunctionType
AX = mybir.AxisListType

ENGS = ("pe", "act", "dve", "pool", "sp")
STRICT_SAME_ENGINE = True
N_DMA_SEMS = 12


def _box(ap):
    t = ap.tensor
    es = 2 if ap.dtype == BF16 else 4
    pat = [(a * es, b) for a, b in ap.ap]
    off = int(ap.offset) * es
    sp = str(ap.space)
    if "DRAM" in sp.upper() or "Dram" in sp or "dram" in sp:
        lo = off
        hi = off
        for st, cnt in pat:
            if st >= 0:
                hi += st * (cnt - 1)
            else:
                lo += st * (cnt - 1)
        return (t.name, 0, 1, lo, hi + 1)
    pstride = pat[0][0]
    npart = pat[0][1]
    if pstride == 0:
        pstride = 1 << 30
    p0 = off // pstride
    f0 = off % pstride
    lo = f0
    hi = f0
    for st, cnt in pat[1:]:
        if st >= 0:
            hi += st * (cnt - 1)
        else:
            lo += st * (cnt - 1)
    if t.name.startswith("psb"):
        return (t.name, 0, 128, 0, 2048)
    return (t.name, p0, p0 + npart, lo, hi + 1)


def _ovl(a, b):
    return a[1] < b[2] and b[1] < a[2] and a[3] < b[4] and b[3] < a[4]


def _contains(a, b):
    return a[1] <= b[1] and b[2] <= a[2] and a[3] <= b[3] and b[4] <= a[4]


class Sched:
    def __init__(self, nc):
        self.nc = nc
        self.ops = {e: [] for e in ENGS}
        self.track = {}
        self.known = {e: {} for e in ENGS}
        self.ndma = {e: 0 for e in ENGS}
        self.ncomp = {e: 0 for e in ENGS}
        self.dma_last = {e: {} for e in ENGS}
        self.notrack = set()
        self.final_events = []
        self.tag = ""
        self.tags = {e: [] for e in ENGS}

    def _deps(self, reads, writes, eng=None):
        deps = set()
        for ap in reads:
            b = _box(ap)
            if b[0] in self.notrack:
                continue
            tr = self.track.get(b[0])
            if tr is None:
                continue
            for wb, ev in tr["w"]:
                if _ovl(wb, b):
                    deps.add(ev)
            if b[0].startswith("psb"):
                for rb, evs in tr["r"].items():
                    if _ovl(rb, b):
                        deps.update(ev for e_, ev in evs.items() if e_ != eng)
        for ap in writes:
            b = _box(ap)
            tr = self.track.get(b[0])
            if tr is None:
                continue
            for wb, ev in tr["w"]:
                if _ovl(wb, b):
                    deps.add(ev)
            for rb, evs in tr["r"].items():
                if _ovl(rb, b):
                    deps.update(evs.values())
        return deps

    def _record(self, reads, writes, ev, eng):
        for ap in writes:
            b = _box(ap)
            tr = self.track.setdefault(b[0], {"w": [], "r": {}})
            tr["w"] = [(wb, e) for wb, e in tr["w"] if not _contains(b, wb)]
            tr["w"].append((b, ev))
            tr["r"] = {rb: evs for rb, evs in tr["r"].items() if not _contains(b, rb)}
        for ap in reads:
            b = _box(ap)
            if b[0] in self.notrack:
                continue
            tr = self.track.setdefault(b[0], {"w": [], "r": {}})
            tr["r"].setdefault(b, {})[eng] = ev

    def _waits(self, eng, deps, idx, is_dma=False):
        waits = {}
        for semkey, val in deps:
            if semkey == ("c", eng) and not is_dma:
                if eng == "pe":
                    continue
                if (not STRICT_SAME_ENGINE) and val < idx - 1:
                    continue
            if self.known[eng].get(semkey, 0) >= val:
                continue
            if waits.get(semkey, 0) < val:
                waits[semkey] = val
        for k, v in waits.items():
            self.known[eng][k] = v
        return list(waits.items())

    def op(self, eng, fn, reads=(), writes=()):
        reads = [r for r in reads if r is not None and not isinstance(r, (int, float))]
        idx = self.ncomp[eng]
        self.ncomp[eng] = idx + 1
        deps = self._deps(reads, writes, eng)
        waits = self._waits(eng, deps, idx)
        ev = (("c", eng), idx + 1)
        self.tags[eng].append(self.tag)
        self.ops[eng].append((fn, waits, ev, "c"))
        self._record(reads, writes, ev, eng)
        return ev

    def dma(self, eng, out, in_, **kw):
        n = self.ndma[eng]
        self.ndma[eng] = n + 1
        k = n % N_DMA_SEMS
        val = (n // N_DMA_SEMS + 1) * 16
        semkey = ("d", eng, k)
        deps = self._deps([in_], [out])
        if val > 16:
            deps.add((semkey, val - 16))
        idx = self.ncomp[eng]
        waits = self._waits(eng, deps, idx, True)
        ev = (semkey, val)

        def fn(e, out=out, in_=in_, kw=kw):
            return e.dma_start(out=out, in_=in_, **kw)

        self.ops[eng].append((fn, waits, ev, "d"))
        self._record([in_], [out], ev, eng)
        return ev

    def finish(self, final_events):
        self.final_events = list(final_events)

    def emit(self):
        nc = self.nc
        import contextlib

        with contextlib.ExitStack() as st:
            sems = {}
            for e in ENGS:
                sems[("c", e)] = st.enter_context(nc.semaphore("c_" + e))
            for e in ENGS:
                for k in range(min(N_DMA_SEMS, self.ndma[e])):
                    sems[("d", e, k)] = st.enter_context(nc.semaphore("d_%s_%d" % (e, k)))
            block = st.enter_context(nc.Block())

            waited = {e: set() for e in ENGS}
            for e in ENGS:
                for fn, waits, ev, kind in self.ops[e]:
                    for semkey, val in waits:
                        if semkey[0] == "c":
                            waited[semkey[1]].add(val)
            for semkey, val in self.final_events:
                if semkey[0] == "c":
                    waited[semkey[1]].add(val)
            rank = {e: {v: i + 1 for i, v in enumerate(sorted(waited[e]))} for e in ENGS}

            def semval(semkey, val):
                return rank[semkey[1]][val] if semkey[0] == "c" else val

            def run(engname, eobj):
                for fn, waits, ev, kind in self.ops[engname]:
                    for semkey, val in waits:
                        eobj.wait_ge(sems[semkey], semval(semkey, val))
                    ins = fn(eobj)
                    if kind == "c":
                        if ev[1] in waited[engname]:
                            ins.then_inc(sems[ev[0]], 1)
                    else:
                        ins.then_inc(sems[ev[0]], 16)
                if engname == "sp":
                    mx = {}
                    for semkey, val in self.final_events:
                        v = semval(semkey, val)
                        mx[semkey] = max(mx.get(semkey, 0), v)
                    for semkey, val in mx.items():
                        eobj.wait_ge(sems[semkey], val)

            @block.tensor
            def _(e):
                run("pe", e)

            @block.scalar
            def _(e):
                run("act", e)

            @block.vector
            def _(e):
                run("dve", e)

            @block.gpsimd
            def _(e):
                run("pool", e)

            @block.sync
            def _(e):
                run("sp", e)

import contextlib

ENABLE_EVEN = True
ENABLE_ODD = True

D = 1024
NPT = 512
LS = 2048
T = NPT + LS
DFF = 2816
NJ = DFF // 128
EPS = 1e-6
DEPTH = 4


class KB:
    def __init__(self, nc, st):
        self.nc = nc
        self.st = st
        self.S = Sched(nc)
        self.psn = 0
        self.rot = {}

    def sb(self, name, shape, dt=F32):
        return self.st.enter_context(self.nc.sbuf_tensor(name, list(shape), dt))

    def dram(self, name, shape, dt=F32, kind="Internal"):
        return self.nc.dram_tensor(name, list(shape), dt, kind=kind).ap()

    def rotbuf(self, name, shape, dt, n=2):
        key = name
        if key not in self.rot:
            self.rot[key] = [[self.sb("%s_%d" % (name, i), shape, dt) for i in range(n)], 0]
        r = self.rot[key]
        b = r[0][r[1] % len(r[0])]
        r[1] += 1
        return b

    def arena_init(self, nf):
        self.arF = self.sb("arena", [128, nf], F32)
        self.arB = self.arF.bitcast(BF16)
        self.arI = self.arF.bitcast(I32)
        self.nbytes = nf * 4
        self.ptr = 0
        self.arot = {}

    def phase(self):
        self.ptr = 0
        self.arot = {}

    def mark(self):
        return self.ptr

    def reset_to(self, m):
        self.ptr = m
        self.arot = {}

    def af(self, shape, dt=F32):
        n = int(np.prod(shape))
        es = 4 if dt == F32 else 2
        self.ptr = (self.ptr + 3) // 4 * 4
        e0 = self.ptr // es
        a = (self.arF if dt == F32 else self.arB)[:, e0:e0 + n]
        self.ptr += n * es
        assert self.ptr <= self.nbytes, ("arena overflow", self.ptr, self.nbytes)
        if len(shape) == 2:
            a = a.rearrange("p (a b) -> p a b", a=shape[0])
        elif len(shape) == 3:
            a = a.rearrange("p (a b c) -> p a b c", a=shape[0], b=shape[1])
        elif len(shape) == 4:
            a = a.rearrange("p (a b c d) -> p a b c d", a=shape[0], b=shape[1], c=shape[2])
        return a

    def arotbuf(self, name, shape, dt, n=2):
        if name not in self.arot:
            self.arot[name] = [[self.af(shape, dt) for _ in range(n)], 0]
        r = self.arot[name]
        b = r[0][r[1] % len(r[0])]
        r[1] += 1
        return b

    def psum(self):
        b = self.psb[self.psn % len(self.psb)]
        self.psn += 1
        return b

    def mm(self, out, lhsT, rhs, start=True, stop=True):
        return self.S.op("pe", lambda e: e.matmul(out, lhsT=lhsT, rhs=rhs, start=start, stop=stop),
                         reads=[lhsT, rhs], writes=[out])

    def tr(self, out, in_, ident):
        return self.S.op("pe", lambda e: e.transpose(out, in_, ident), reads=[in_, ident], writes=[out])

    def act(self, out, in_, func, bias=None, scale=1.0, accum_out=None):
        kw = {}
        rd = [in_]
        wr = [out]
        if bias is not None:
            kw["bias"] = bias
            rd.append(bias)
        if accum_out is not None:
            kw["accum_out"] = accum_out
            wr.append(accum_out)
        if not isinstance(scale, (int, float)):
            rd.append(scale)
        return self.S.op("act", lambda e: e.activation(out=out, in_=in_, func=func, scale=scale, **kw),
                         reads=rd, writes=wr)

    def tt(self, out, in0, in1, op, eng="dve"):
        return self.S.op(eng, lambda e: e.tensor_tensor(out=out, in0=in0, in1=in1, op=op),
                         reads=[in0, in1], writes=[out])

    def ts(self, out, in0, s1, s2, op0, op1=None, eng="dve", accum_out=None):
        kw = {}
        wr = [out]
        if op1 is not None:
            kw["op1"] = op1
        if accum_out is not None:
            kw["accum_out"] = accum_out
            wr.append(accum_out)
        return self.S.op(eng, lambda e: e.tensor_scalar(out=out, in0=in0, scalar1=s1, scalar2=s2, op0=op0, **kw),
                         reads=[in0, s1, s2], writes=wr)

    def stt(self, out, in0, scalar, in1, op0, op1):
        return self.S.op("dve", lambda e: e.scalar_tensor_tensor(out=out, in0=in0, scalar=scalar, in1=in1, op0=op0, op1=op1),
                         reads=[in0, scalar, in1], writes=[out])

    def copy(self, out, in_, eng="dve"):
        if eng == "act":
            return self.S.op("act", lambda e: e.copy(out=out, in_=in_), reads=[in_], writes=[out])
        return self.S.op(eng, lambda e: e.tensor_copy(out=out, in_=in_), reads=[in_], writes=[out])

    def memset(self, ap, val, eng="dve"):
        return self.S.op(eng, lambda e: e.memset(ap, val), reads=[], writes=[ap])

    def red(self, out, in_, op, axis=AX.X):
        return self.S.op("dve", lambda e: e.tensor_reduce(out=out, in_=in_, axis=axis, op=op), reads=[in_], writes=[out])

    def recip(self, out, in_):
        return self.S.op("dve", lambda e: e.reciprocal(out=out, in_=in_), reads=[in_], writes=[out])

    def dma(self, out, in_, q="sp", **kw):
        return self.S.dma(q, out, in_, **kw)


def build_program(dbg_stop=None):
    nc = bass.Bass("TRN2", target_bir_lowering=False)
    st = contextlib.ExitStack()
    with st:
        K = KB(nc, st)
        S = K.S
        IN = {}

        def inp(name, shape, dt=F32):
            IN[name] = nc.dram_tensor(name, list(shape), dt, kind="ExternalInput").ap()
            S.notrack.add(name)
            return IN[name]

        def outp(name, shape):
            return nc.dram_tensor(name, list(shape), F32, kind="ExternalOutput").ap()

        xp = inp("xp", [NPT, D])
        xs = inp("xs", [LS, D])
        cv = inp("cv", [2, D])
        ident_d = inp("ident_in", [128, 128])
        ada_w = inp("ada_w", [DEPTH, D, 6 * D])
        ada_b = inp("ada_b", [DEPTH * 48, 128])
        gvec = {}
        for nm in ("mix_pre_g", "mix_post_g", "ffn_pre_g", "ffn_post_g"):
            gvec[nm] = inp(nm, [DEPTH * 8, 128])
        ffn_w_up = inp("ffn_w_up", [DEPTH, D, 2 * DFF])
        ffn_conv_w = inp("ffn_conv_w", [DEPTH * 3 * 44, 128])
        ffn_conv_b = inp("ffn_conv_b", [DEPTH * 44, 128])
        ffn_w_down = inp("ffn_w_down", [DEPTH, DFF, D])

        even_w_in = inp("even_w_in", [2, D, 2464])
        even_w_out = inp("even_w_out", [2, D, D])
        ret_logit_b = inp("ret_logit_b", [2, 128, 8])
        ret_gn = inp("ret_gn", [8, 128])
        mla_q_norm = inp("mla_q_norm", [4, 128])
        mla_w_uq = inp("mla_w_uq", [2, 256, 768])
        mla_kv_norm = inp("mla_kv_norm", [2, 128])
        kvn_bc_d = inp("kvn_bc", [128, 256])
        mla_w_ukv = inp("mla_w_ukv", [2, 128, 1024])
        state_ret_in = inp("state_ret_in", [2, 2, 4, 128, 128])
        ckv_in = inp("ckv_in", [2, 512, 128])
        kr_in = inp("kr_in", [2, 512, 32])
        rconst_d = inp("rconst_in", [128, 772])
        ropeC_d = inp("ropeC_in", [32, 2048])
        ropeS_d = inp("ropeS_in", [32, 2048])
        odd_w_in = inp("odd_w_in", [2, D, 2048])
        odd_w_out = inp("odd_w_out", [2, D, D])
        s5p = inp("s5p", [2, 2, 128, 1104])
        s5_d = inp("s5_d", [8, 128])
        s5_glu_b = inp("s5_glu_b", [8, 128])
        s5_glu_w = inp("s5_glu_w", [2, 512, 512])
        na_rpb = inp("na_rpb", [2, 8, 15, 31])
        nak_in = inp("nak_in", [2, 512, 512])
        nav_in = inp("nav_in", [2, 512, 512])
        oh_d = inp("oh_in", [32, 64, 128])
        iota_d = inp("iota_in", [128, 2048])
        o_s5re = outp("o_s5re", [2, 2, 2, 128, 16])
        o_s5im = outp("o_s5im", [2, 2, 2, 128, 16])
        o_nak = outp("o_nak", [2, 2, 256, 512])
        o_nav = outp("o_nav", [2, 2, 256, 512])
        o_ret = outp("o_ret", [2, 2, 2, 4, 128, 128])
        o_ckv = outp("o_ckv", [2, 2, 256, 128])
        o_kr = outp("o_kr", [2, 2, 256, 32])
        finals = []
        yp = outp("yp", [NPT, D])
        ys = outp("ys", [LS, D])

        xT_d = K.dram("xT_d", [D, T])
        xT_v = xT_d.rearrange("(c p) t -> p c t", p=128)

        K.psb = [st.enter_context(nc.psum_tensor("psb%d" % i, [128, 512], F32)) for i in range(8)]

        K.arena_init(50200)
        ident = K.sb("ident", [128, 128], F32)
        K.dma(ident[:], ident_d)
        identb = K.sb("identb", [128, 128], BF16)
        K.copy(identb[:], ident[:])
        epsT = K.sb("epsT", [128, 1], F32)
        K.memset(epsT[:], EPS)
        halo_stash = K.sb("halo_stash", [128, 8, 2], BF16)
        ones1024 = K.sb("ones1024", [128, 128], BF16)
        K.memset(ones1024[:], 1.0 / 1024.0)
        ones256 = K.sb("ones256", [128, 128], BF16)
        K.memset(ones256[:], 1.0 / 256.0)
        ones128b = K.sb("ones128b", [128, 128], BF16)
        K.memset(ones128b[:], 1.0 / 128.0)
        onesb = K.sb("onesb", [128, 128], BF16)
        K.memset(onesb[:], 1.0)
        oneT = K.sb("oneT", [128, 1], F32)
        K.memset(oneT[:], 1.0)
        halfpi = K.sb("halfpi", [128, 1], F32)
        K.memset(halfpi[:], math.pi / 2.0)


        def load_T(dram2d, R, name):
            dst = K.sb(name, [128, R], F32)
            r0 = 0
            while r0 < R:
                r = min(128, R - r0)
                stg = K.rotbuf("ldT_stg", [128, 128], F32)
                K.dma(stg[0:r, :], dram2d[r0:r0 + r, :])
                pb = K.psum()
                K.tr(pb[:, 0:r], stg[0:r, :], ident[0:r, 0:r])
                K.copy(dst[:, r0:r0 + r], pb[:, 0:r])
                r0 += r
            return dst

        gT = {nm: load_T(gvec[nm], DEPTH * 8, "gT_" + nm) for nm in gvec}
        adabT = load_T(ada_b, DEPTH * 48, "adabT")
        convwT = load_T(ffn_conv_w, DEPTH * 3 * 44, "convwT")
        convbT = load_T(ffn_conv_b, DEPTH * 44, "convbT")
        nconvwT = K.sb("nconvwT", [128, DEPTH * 3 * 44], F32)
        K.ts(nconvwT[:], convwT[:], -1.0, None, ALU.mult)
        s5dT = load_T(s5_d, 8, "s5dT")
        glubT = load_T(s5_glu_b, 8, "glubT")
        gnT = load_T(ret_gn, 8, "gnT")
        qnT = load_T(mla_q_norm, 4, "qnT")
        kvnT = load_T(mla_kv_norm, 2, "kvnT")
        cT = load_T(cv.rearrange("v (c p) -> (v c) p", p=128), 16, "cT")
        scT = K.sb("scT", [128, 8, 2], BF16)
        K.act(scT[:].rearrange("p c v -> p v c"), cT[:].rearrange("p (v c) -> p v c", v=2), AF.Silu)

        def x_to_xT(src, ntile, col0):
            for i in range(ntile):
                xt = K.arotbuf("xin", [D], F32)
                K.dma(xt[:], src[i * 128:(i + 1) * 128, :])
                xo = K.arotbuf("xinT", [8, 128], F32)
                for g in range(2):
                    pb = K.psum()
                    for c in range(4):
                        K.tr(pb[:, c * 128:(c + 1) * 128], xt[:, (g * 4 + c) * 128:(g * 4 + c + 1) * 128], ident[:])
                    K.copy(xo[:, g * 4:(g + 1) * 4, :], pb[:].rearrange("p (c t) -> p c t", c=4), eng=("act" if g else "dve"))
                K.dma(xT_v[:, :, col0 + i * 128: col0 + (i + 1) * 128], xo[:])

        K.phase()
        x_to_xT(xp, NPT // 128, 0)
        x_to_xT(xs, LS // 128, NPT)

        def ada(layer):
            K.phase()
            S.tag = "L%d:ada" % layer
            modp = K.psum()
            mview = modp[:, 0:96].rearrange("p (c v) -> p c v", v=2)
            for blk in range(12):
                wb = K.arotbuf("adaw", [8, 512], BF16)
                K.dma(wb[:], ada_w[layer].rearrange("(k p) o -> p k o", p=128)[:, :, blk * 512:(blk + 1) * 512], q="pool")
                for cc in range(4):
                    ch = blk * 4 + cc
                    for k in range(8):
                        K.mm(mview[:, ch, :], wb[:, k, cc * 128:(cc + 1) * 128], scT[:, k, :], start=(k == 0), stop=(k == 7))
            mod = K.rotbuf("mod", [128, 48, 2], F32)
            for v in range(2):
                K.tt(mod[:, :, v], mview[:, :, v], adabT[:, layer * 48:(layer + 1) * 48], ALU.add)
            return mod

        def mod_vectors(layer, mod, sub):
            base = sub * 24
            pre_g = gT["mix_pre_g" if sub == 0 else "ffn_pre_g"]
            post_g = gT["mix_post_g" if sub == 0 else "ffn_post_g"]
            gs = K.rotbuf("gs", [128, 8, 2], F32)
            gg = K.rotbuf("gg", [128, 8, 2], F32)
            for v in range(2):
                K.stt(gs[:, :, v], mod[:, base + 8:base + 16, v], 1.0, pre_g[:, layer * 8:(layer + 1) * 8], ALU.add, ALU.mult)
                K.tt(gg[:, :, v], mod[:, base + 16:base + 24, v], post_g[:, layer * 8:(layer + 1) * 8], ALU.mult)
            return gs, mod[:, base:base + 8, :], gg

        def rstd_from(src3, n, ones):
            C = src3.shape[1]
            sq = K.arotbuf("sq", [8, 512], BF16, n=1)
            K.act(sq[:, 0:C, 0:n], src3, AF.Square)
            pb = K.psum()
            for c in range(C):
                K.mm(pb[:, 0:n], ones[:], sq[:, c, 0:n], start=(c == 0), stop=(c == C - 1))
            rstd = K.arotbuf("rstd", [512], F32)
            K.act(rstd[:, 0:n], pb[:, 0:n], AF.Sqrt, bias=epsT[:, 0:1])
            K.recip(rstd[:, 0:n], rstd[:, 0:n])
            return rstd

        def pre_block(t0, n, mc, gs, shift, dst):
            xb = K.arotbuf("xb", [8, 512], F32)
            if n == 1:
                K.dma(xb[:, :, 0:n], xT_v[:, :, t0:t0 + n], allow_slow_non_contiguous=True)
            else:
                K.dma(xb[:, :, 0:n], xT_v[:, :, t0:t0 + n])
            rstd = rstd_from(xb[:, :, 0:n], n, ones1024)
            for c in range(8):
                tmp = K.arotbuf("tmpn", [512], F32)
                K.stt(tmp[:, 0:n], xb[:, c, 0:n], gs[:, c, mc:mc + 1], rstd[:, 0:n], ALU.mult, ALU.mult)
                K.act(dst[:, c, :], tmp[:, 0:n], AF.Identity, bias=shift[:, c, mc:mc + 1])

        def post_block(t0, n, mc, gg, yf):
            rstd = rstd_from(yf, n, ones1024)
            xb = K.arotbuf("xb", [8, 512], F32)
            K.dma(xb[:, :, 0:n], xT_v[:, :, t0:t0 + n])
            for c in range(8):
                tmp = K.arotbuf("tmpn", [512], F32)
                K.stt(tmp[:, 0:n], yf[:, c, :], gg[:, c, mc:mc + 1], rstd[:, 0:n], ALU.mult, ALU.mult)
                K.tt(xb[:, c, 0:n], xb[:, c, 0:n], tmp[:, 0:n], ALU.add, eng="pool")
            K.dma(xT_v[:, :, t0:t0 + n], xb[:, :, 0:n])

        FFN_GROUPS = [
            (0, 1024, 0, 1, [(0, 256), (256, 512), (512, 1025)]),
            (1024, 2048, 1, 1, [(1023, 2049)]),
            (2048, 2560, 1, 0, [(2047, 2560)]),
        ]

        def ffn(layer, gs, shift, gg):
            wup_v = ffn_w_up[layer].rearrange("(k p) o -> p k o", p=128)
            wdn_v = ffn_w_down[layer].rearrange("(j p) o -> p j o", p=128)
            for (c0, c1, hl, hr, segs) in FFN_GROUPS:
                K.phase()
                S.tag = "L%d:ffn_up" % layer
                b0 = c0 - hl
                W = c1 + hr - b0
                hTg = K.af([8, 1026], BF16)
                cols = []
                if hl:
                    cols.append((b0, 1))
                for t0 in range(c0, c1, 512):
                    cols.append((t0, 512))
                if hr:
                    cols.append((c1, 1))
                for (t0, n) in cols:
                    mc = 0 if t0 < NPT else 1
                    if hl and t0 == b0:
                        K.copy(hTg[:, :, 0:1], halo_stash[:, :, 0:1], eng="pool")
                        continue
                    pre_block(t0, n, mc, gs, shift, hTg[:, :, t0 - b0:t0 - b0 + n])
                if c1 < T:
                    K.copy(halo_stash[:, :, 0:1], hTg[:, :, c1 - 1 - b0:c1 - b0], eng="pool")
                npart = (W + 511) // 512
                psz = (W + npart - 1) // npart
                parts = [(i * psz, min(psz, W - i * psz)) for i in range(npart)]
                mT = K.af([NJ, 1024], BF16)

                def load_up(j):
                    wa = K.arotbuf("wupa", [8, 128], BF16, n=2)
                    wg = K.arotbuf("wupg", [8, 128], BF16, n=2)
                    K.dma(wa[:], wup_v[:, :, j * 128:(j + 1) * 128], q="pool")
                    K.dma(wg[:], wup_v[:, :, DFF + j * 128:DFF + (j + 1) * 128], q="pool")
                    return wa, wg

                bounds = [s0_ for (s0_, s1_) in segs[1:]]
                lo = 1 + hl
                n_own = c1 - c0

                def ffn_X(j, wts):
                    wa, wg = wts
                    us, os_ = [], []
                    for wi, wmat in enumerate((wa, wg)):
                        u = K.arotbuf("u%d" % wi, [1028], F32, n=2)
                        for (p0, pn) in parts:
                            pb = K.psum()
                            for k in range(8):
                                K.mm(pb[:, 0:pn], wmat[:, k, :], hTg[:, k, p0:p0 + pn], start=(k == 0), stop=(k == 7))
                            K.copy(u[:, 1 + p0:1 + p0 + pn], pb[:, 0:pn], eng="act")
                        blk = wi * NJ + j
                        w1 = convwT[:, (layer * 3 + 1) * 44 + blk:(layer * 3 + 1) * 44 + blk + 1]
                        bb = convbT[:, layer * 44 + blk:layer * 44 + blk + 1]
                        o = K.arotbuf("o%d" % wi, [1028], F32, n=2)
                        K.act(o[:, 1:1 + W], u[:, 1:1 + W], AF.Identity, bias=bb, scale=w1)
                        us.append(u)
                        os_.append(o)
                    return us, os_

                def ffn_Y(j, us, os_):
                    for wi in range(2):
                        u, o = us[wi], os_[wi]
                        blk = wi * NJ + j
                        c0w = (layer * 3 + 0) * 44 + blk
                        c2w = (layer * 3 + 2) * 44 + blk
                        w0, w2 = convwT[:, c0w:c0w + 1], convwT[:, c2w:c2w + 1]
                        nw0, nw2 = nconvwT[:, c0w:c0w + 1], nconvwT[:, c2w:c2w + 1]
                        K.stt(o[:, 2:1 + W], u[:, 1:W], w0, o[:, 2:1 + W], ALU.mult, ALU.add)
                        K.stt(o[:, 1:W], u[:, 2:1 + W], w2, o[:, 1:W], ALU.mult, ALU.add)
                        for B in bounds:
                            iB = B - b0 + 1
                            K.stt(o[:, iB:iB + 1], u[:, iB - 1:iB], nw0, o[:, iB:iB + 1], ALU.mult, ALU.add)
                            K.stt(o[:, iB - 1:iB], u[:, iB:iB + 1], nw2, o[:, iB - 1:iB], ALU.mult, ALU.add)
                    oa, og = os_
                    sg = K.arotbuf("sg", [1024], F32, n=1)
                    K.act(sg[:, 0:n_own], og[:, lo:lo + n_own], AF.Silu)
                    K.tt(mT[:, j, 0:n_own], oa[:, lo:lo + n_own], sg[:, 0:n_own], ALU.mult)

                wts = load_up(0)
                wts_next = load_up(1)
                stX = ffn_X(0, wts)
                for j in range(NJ):
                    if j + 1 < NJ:
                        wts = wts_next
                        nxX = ffn_X(j + 1, wts)
                        if j + 2 < NJ:
                            wts_next = load_up(j + 2)
                    ffn_Y(j, *stX)
                    if j + 1 < NJ:
                        stX = nxX
                S.tag = "L%d:ffn_down" % layer
                n_own = c1 - c0
                yfa = K.af([8, 1024], F32)

                def load_dn(c):
                    wd = K.arotbuf("wdn", [NJ, 128], BF16, n=2)
                    K.dma(wd[:], wdn_v[:, :, c * 128:(c + 1) * 128], q="pool")
                    return wd

                nxt = load_dn(0)
                for c in range(8):
                    wd = nxt
                    if c + 1 < 8:
                        nxt = load_dn(c + 1)
                    for t0 in range(0, n_own, 512):
                        pb = K.psum()
                        for j in range(NJ):
                            K.mm(pb[:, :], wd[:, j, :], mT[:, j, t0:t0 + 512], start=(j == 0), stop=(j == NJ - 1))
                        K.copy(yfa[:, c, t0:t0 + 512], pb[:, :], eng="act")
                for t0 in range(0, n_own, 512):
                    mc = 0 if (c0 + t0) < NPT else 1
                    post_block(c0 + t0, 512, mc, gg, yfa[:, :, t0:t0 + 512])

        RS = 128.0 ** -0.5
        MLA_SCALE = 96.0 ** -0.5
        SEQS = [(0, 256, 0), (256, 256, 0), (512, 2048, 1)]

        PW = {"m": 2560, "n": 1152}
        NTT = {"m": 20, "n": 9}

        ATT = {"prev": None, "tb": 0}

        def attn_A(qT_ap, kbanks):
            for b, (kT_ap, bias_fn) in enumerate(kbanks):
                w = kT_ap.shape[1]
                K.mm(K.psb[b][:, 0:w], qT_ap, kT_ap)

        def attn_C(kbanks, tag):
            nb = len(kbanks)
            mx = K.arotbuf("a_mx", [8], F32)
            srcs = []
            col = 0
            for b, (kT_ap, bias_fn) in enumerate(kbanks):
                w = kT_ap.shape[1]
                src = K.psb[b][:, 0:w]
                if bias_fn is not None:
                    src = bias_fn(K.psb[b], w)
                srcs.append((src, col, w))
                K.red(mx[:, b:b + 1], src, ALU.max)
                col += w
            nm = K.arotbuf("a_nm", [1], F32)
            K.red(nm[:, 0:1], mx[:, 0:nb], ALU.max)
            K.ts(nm[:, 0:1], nm[:, 0:1], -1.0, None, ALU.mult)
            P = K.arotbuf("a_P" + tag, [PW[tag]], BF16)
            for (src, c0_, w) in srcs:
                K.act(P[:, c0_:c0_ + w], src, AF.Exp, bias=nm[:, 0:1])
            return P

        def attn_B(st_):
            P, ktiles, hs, dst, tag = st_
            PT = K.arotbuf("a_PT" + tag, [NTT[tag], 128], BF16)
            nkt = len(ktiles)
            for g in range(0, nkt, 8):
                cnt = min(8, nkt - g)
                pbT = K.psb[5 + ATT["tb"] % 2].bitcast(BF16)
                ATT["tb"] += 1
                for i_ in range(cnt):
                    pc0, n_, _ = ktiles[g + i_]
                    K.tr(pbT[0:n_, i_ * 128:(i_ + 1) * 128], P[:, pc0:pc0 + n_], identb[:])
                full = all(ktiles[g + i_][1] == 128 for i_ in range(cnt))
                ce = "dve" if ATT["tb"] % 2 == 0 else "act"
                if full:
                    K.copy(PT[:, g:g + cnt, :], pbT[:, 0:cnt * 128].rearrange("p (a b) -> p a b", a=cnt), eng=ce)
                else:
                    for i_ in range(cnt):
                        n_ = ktiles[g + i_][1]
                        K.copy(PT[0:n_, g + i_, :], pbT[0:n_, i_ * 128:(i_ + 1) * 128], eng=ce)
            po = K.psb[7][:, 0:128]
            pd = K.psb[7][:, 128:256]
            for i_, (pc0, n_, vl) in enumerate(ktiles):
                K.mm(po, vl, PT[0:n_, i_, :], start=(i_ == 0), stop=(i_ == nkt - 1))
            for i_, (pc0, n_, vl) in enumerate(ktiles):
                K.mm(pd, onesb[0:n_, :], PT[0:n_, i_, :], start=(i_ == 0), stop=(i_ == nkt - 1))

        def attn_D(st_):
            P, ktiles, hs, dst, tag = st_
            po = K.psb[7][:, 0:128]
            pd = K.psb[7][:, 128:256]
            rd = K.arotbuf("a_rd", [128], F32)
            K.recip(rd[hs, :], pd[hs, :])
            K.tt(dst, po[hs, :], rd[hs, :], ALU.mult)

        def attn_unit(qT_ap, kbanks, ktiles, hs, dst, tag):
            prev = ATT["prev"]
            attn_A(qT_ap, kbanks)
            if prev is not None:
                attn_B(prev)
            P = attn_C(kbanks, tag)
            if prev is not None:
                attn_D(prev)
            ATT["prev"] = (P, ktiles, hs, dst, tag)

        def attn_flush():
            if ATT["prev"] is not None:
                attn_B(ATT["prev"])
                attn_D(ATT["prev"])
                ATT["prev"] = None

        def out_proj_post(w_out_l, yT, gg):
            wo = K.af([8, 1024], BF16)
            K.dma(wo[:], w_out_l.rearrange("(k p) o -> p k o", p=128), q="pool")
            for t0 in range(0, T, 512):
                yf = K.arotbuf("yf", [8, 512], F32, n=1)
                for c in range(8):
                    pb = K.psum()
                    for k in range(8):
                        K.mm(pb[:], wo[:, k, c * 128:(c + 1) * 128], yT[:, k, t0:t0 + 512], start=(k == 0), stop=(k == 7))
                    K.copy(yf[:, c, :], pb[:], eng="act")
                post_block(t0, 512, 0 if t0 < NPT else 1, gg, yf[:])

        def even_layer(layer, j, mod):
            gs, shift, gg = mod_vectors(layer, mod, 0)
            K.phase()
            hT = K.af([8, T], BF16)
            yT = K.af([8, T], BF16)
            m0 = K.mark()
            S.tag = "L%d:e_pre" % layer
            for t0 in range(0, T, 512):
                pre_block(t0, 512, 0 if t0 < NPT else 1, gs, shift, hT[:, :, t0:t0 + 512])
            K.reset_to(m0)
            S.tag = "L%d:e_ret" % layer
            w_in_v = even_w_in[j].rearrange("(k p) o -> p k o", p=128)
            rconst = K.af([772], F32)
            K.dma(rconst[:], rconst_d)
            RC = lambda i: rconst[:, i * 128:(i + 1) * 128]
            lg = K.af([8], F32)
            K.dma(lg[:], ret_logit_b[j])
            K.act(lg[:], lg[:], AF.Exp, scale=-1.0)
            K.act(lg[:], lg[:], AF.Ln, bias=oneT[:, 0:1])
            K.ts(lg[:], lg[:], -1.0, None, ALU.mult)
            qTh = K.af([T], BF16)
            kTh = K.af([T], BF16)
            sgTh = K.af([T], BF16)
            kvtok = K.af([20, 256], BF16)
            kwfT = K.af([20, 128], BF16)
            kwbT = K.af([20, 128], BF16)
            Sfs = K.af([20, 128], BF16)
            Sbs = K.af([20, 128], BF16)
            o_sb = K.af([T], F32)
            Dm = K.af([128], F32)
            e2 = K.af([128], F32)
            Wqf = K.af([128], F32)
            Wqb = K.af([128], F32)
            kw = K.af([4], F32)
            for h in range(4):
                lf = lg[:, h:h + 1]
                lb = lg[:, 4 + h:5 + h]
                K.act(Dm[:], RC(0), AF.Exp, scale=lf)
                K.tt(Dm[:], Dm[:], RC(1), ALU.mult)
                K.act(e2[:], RC(2), AF.Exp, scale=lb)
                K.tt(e2[:], e2[:], RC(3), ALU.mult)
                K.tt(Dm[:], Dm[:], e2[:], ALU.add)
                K.act(Wqf[:], RC(4), AF.Exp, scale=lf)
                K.act(Wqb[:], RC(5), AF.Exp, scale=lb)
                K.act(kw[:, 0:1], rconst[:, 768:769], AF.Exp, scale=lf)
                K.act(kw[:, 1:2], rconst[:, 769:770], AF.Exp, scale=lb)
                K.act(kw[:, 2:3], rconst[:, 770:771], AF.Exp, scale=lf)
                K.act(kw[:, 3:4], rconst[:, 770:771], AF.Exp, scale=lb)
                wh = K.arotbuf("wh", [8, 4, 128], BF16, n=2)
                for si_ in range(4):
                    K.dma(wh[:, :, si_, :], w_in_v[:, :, si_ * 512 + h * 128:si_ * 512 + (h + 1) * 128], q="pool")
                for t0 in range(0, T, 512):
                    for si, dst in ((0, qTh), (1, kTh), (3, sgTh)):
                        pb = K.psum()
                        for k in range(8):
                            K.mm(pb[:], wh[:, k, si, :], hT[:, k, t0:t0 + 512], start=(k == 0), stop=(k == 7))
                        if si == 0:
                            K.act(dst[:, t0:t0 + 512], pb[:], AF.Identity, scale=RS)
                        elif si == 1:
                            K.copy(dst[:, t0:t0 + 512], pb[:], eng="dve")
                        else:
                            K.act(dst[:, t0:t0 + 512], pb[:], AF.Silu)
                for ti in range(20):
                    pb = K.psum()
                    for k in range(8):
                        K.mm(pb[:, 0:256].rearrange("p (a b) -> p a b", a=2), hT[:, k, ti * 128:(ti + 1) * 128], wh[:, k, 1:3, :], start=(k == 0), stop=(k == 7))
                    K.copy(kvtok[:, ti, :], pb[:, 0:256], eng="act")
                    K.ts(kwfT[:, ti, :], pb[:, 0:128], kw[:, 0:1], None, ALU.mult)
                    K.ts(kwbT[:, ti, :], pb[:, 0:128], kw[:, 1:2], None, ALU.mult)
                for (c0, L, smp) in SEQS:
                    nch = L // 128
                    tb = c0 // 128
                    for d in range(2):
                        Sx = K.arotbuf("Sx", [128], F32, n=2)
                        if smp:
                            K.dma(Sx[:], state_ret_in[j, d, h])
                        else:
                            K.memset(Sx[:], 0.0, eng="pool")
                        store = Sfs if d == 0 else Sbs
                        kwT = kwfT if d == 0 else kwbT
                        order = range(nch) if d == 0 else range(nch - 1, -1, -1)
                        for n in order:
                            ti = tb + n
                            K.copy(store[:, ti, :], Sx[:], eng="pool")
                            pb = K.psum()
                            K.mm(pb[:, 0:128], kwT[:, ti, :], kvtok[:, ti, 128:256])
                            K.stt(Sx[:], Sx[:], kw[:, 2 + d:3 + d], pb[:, 0:128], ALU.mult, ALU.add)
                        if not smp:
                            finals.append(K.dma(o_ret[c0 // 256, j, d, h], Sx[:]))
                    for n0 in range(0, nch, 4):
                        pbo = K.psum()
                        cnt = min(4, nch - n0)
                        for n in range(n0, n0 + cnt):
                            ti = tb + n
                            cs = slice(ti * 128, (ti + 1) * 128)
                            pbs = K.psum()
                            K.mm(pbs[:, 0:128], kTh[:, cs], qTh[:, cs])
                            MT = K.arotbuf("MT", [128], BF16, n=2)
                            K.tt(MT[:], pbs[:, 0:128], Dm[:], ALU.mult)
                            qf = K.arotbuf("qf", [128], BF16, n=2)
                            qb = K.arotbuf("qb", [128], BF16, n=2)
                            K.tt(qf[:], qTh[:, cs], Wqf[:], ALU.mult, eng="pool")
                            K.tt(qb[:], qTh[:, cs], Wqb[:], ALU.mult, eng="pool")
                            oc = pbo[:, (n - n0) * 128:(n - n0 + 1) * 128]
                            K.mm(oc, kvtok[:, ti, 128:256], MT[:], start=True, stop=False)
                            K.mm(oc, Sfs[:, ti, :], qf[:], start=False, stop=False)
                            K.mm(oc, Sbs[:, ti, :], qb[:], start=False, stop=True)
                        K.copy(o_sb[:, (tb + n0) * 128:(tb + n0 + cnt) * 128], pbo[:, 0:cnt * 128], eng="act")
                for t0 in range(0, T, 512):
                    ob = o_sb[:, t0:t0 + 512]
                    o2 = K.arotbuf("gn_o2", [512], F32, n=1)
                    K.tt(o2[:], ob, ob, ALU.mult)
                    pm = K.psum()
                    pq = K.psum()
                    for src_, pdst in ((ob, pm), (o2[:], pq)):
                        hi = K.arotbuf("gn_hi", [512], BF16, n=2)
                        lo = K.arotbuf("gn_lo", [512], BF16, n=2)
                        K.copy(hi[:], src_, eng="act")
                        K.tt(lo[:], src_, hi[:], ALU.subtract)
                        K.mm(pdst[:], ones128b[:], hi[:], start=True, stop=False)
                        K.mm(pdst[:], ones128b[:], lo[:], start=False, stop=True)
                    mean = K.arotbuf("gn_mean", [512], F32, n=1)
                    K.copy(mean[:], pm[:], eng="act")
                    msq = K.arotbuf("gn_msq", [512], F32, n=1)
                    K.tt(msq[:], mean[:], mean[:], ALU.mult, eng="pool")
                    K.tt(msq[:], pq[:], msq[:], ALU.subtract)
                    K.act(msq[:], msq[:], AF.Sqrt, bias=epsT[:, 0:1])
                    K.recip(msq[:], msq[:])
                    K.tt(o2[:], ob, mean[:], ALU.subtract, eng="pool")
                    K.tt(o2[:], o2[:], msq[:], ALU.mult)
                    K.stt(yT[:, h, t0:t0 + 512], o2[:], gnT[:, j * 4 + h:j * 4 + h + 1], sgTh[:, t0:t0 + 512], ALU.mult, ALU.mult)
            K.reset_to(m0)
            S.tag = "L%d:e_mlaproj" % layer
            cqn = K.af([2, T], BF16)
            ckvK = K.af([3072], BF16)
            krK = K.af([3072], BF16)
            ropeC = K.af([2048], F32)
            ropeS = K.af([2048], F32)
            K.dma(ropeC[64:96, :], ropeC_d)
            K.dma(ropeS[64:96, :], ropeS_d)
            m1 = K.mark()
            kvn_bc = K.af([256], F32)
            K.dma(kvn_bc[:], kvn_bc_d)
            wm = K.af([8, 416], BF16)
            K.dma(wm[:], w_in_v[:, :, 2048:2464], q="pool")
            wkr = K.af([8, 96], BF16)
            wkrs = K.af([8, 96], BF16)
            K.memset(wkr[:], 0.0, eng="pool")
            K.memset(wkrs[:], 0.0, eng="pool")
            K.dma(wkr[:, :, 64:96], w_in_v[:, :, 2432:2464], q="pool")
            K.dma(wkrs[:, :, 64:80], w_in_v[:, :, 2448:2464], q="pool")
            K.dma(wkrs[:, :, 80:96], w_in_v[:, :, 2432:2448], q="pool")
            for t0 in range(0, T, 512):
                bl = slice(t0, t0 + 512)
                pbs = [K.psum(), K.psum()]
                for cc in range(2):
                    for k in range(8):
                        K.mm(pbs[cc][:], wm[:, k, cc * 128:(cc + 1) * 128], hT[:, k, bl], start=(k == 0), stop=(k == 7))
                sq = K.arotbuf("m_sq", [2, 512], BF16, n=1)
                for cc in range(2):
                    K.act(sq[:, cc, :], pbs[cc][:], AF.Square)
                pr = K.psum()
                for cc in range(2):
                    K.mm(pr[:], ones256[:], sq[:, cc, :], start=(cc == 0), stop=(cc == 1))
                rstd = K.arotbuf("m_rstd", [512], F32, n=1)
                K.act(rstd[:], pr[:], AF.Sqrt, bias=epsT[:, 0:1])
                K.recip(rstd[:], rstd[:])
                for cc in range(2):
                    K.stt(cqn[:, cc, bl], pbs[cc][:], qnT[:, j * 2 + cc:j * 2 + cc + 1], rstd[:], ALU.mult, ALU.mult)
                pc = K.psum()
                for k in range(8):
                    K.mm(pc[:], wm[:, k, 256:384], hT[:, k, bl], start=(k == 0), stop=(k == 7))
                K.act(sq[:, 0, :], pc[:], AF.Square)
                pr2 = K.psum()
                K.mm(pr2[:], ones128b[:], sq[:, 0, :])
                rstd2 = K.arotbuf("m_rstd2", [512], F32, n=1)
                K.act(rstd2[:], pr2[:], AF.Sqrt, bias=epsT[:, 0:1])
                K.recip(rstd2[:], rstd2[:])
                K.stt(ckvK[:, bl], pc[:], kvnT[:, j:j + 1], rstd2[:], ALU.mult, ALU.mult)
                pk = K.psum()
                for k in range(8):
                    K.mm(pk[0:96, :], wkr[:, k, :], hT[:, k, bl], start=(k == 0), stop=(k == 7))
                if t0 < NPT:
                    K.copy(krK[64:96, bl], pk[64:96, :], eng="dve")
                else:
                    pks = K.psum()
                    for k in range(8):
                        K.mm(pks[0:96, :], wkrs[:, k, :], hT[:, k, bl], start=(k == 0), stop=(k == 7))
                    st0 = t0 - NPT
                    t1 = K.arotbuf("m_t1", [512], F32, n=1)
                    t2 = K.arotbuf("m_t2", [512], F32, n=1)
                    K.tt(t1[64:96, :], pk[64:96, :], ropeC[64:96, st0:st0 + 512], ALU.mult)
                    K.tt(t2[64:96, :], pks[64:96, :], ropeS[64:96, st0:st0 + 512], ALU.mult)
                    K.tt(krK[64:96, bl], t1[64:96, :], t2[64:96, :], ALU.add, eng="pool")
            for i in range(4):
                stg = K.arotbuf("m_stg", [128], F32, n=2)
                K.dma(stg[:], ckv_in[j, i * 128:(i + 1) * 128, :])
                pb = K.psum()
                K.tr(pb[:, 0:128], stg[:], ident[:])
                K.copy(ckvK[:, 2560 + i * 128:2560 + (i + 1) * 128], pb[:, 0:128], eng="dve")
                stg2 = K.arotbuf("m_stg2", [96], F32, n=2)
                K.memset(stg2[:, 0:64], 0.0, eng="pool")
                K.dma(stg2[:, 64:96], kr_in[j, i * 128:(i + 1) * 128, :])
                pb2 = K.psum()
                K.tr(pb2[0:96, 0:128], stg2[:], ident[:])
                K.copy(krK[64:96, 2560 + i * 128:2560 + (i + 1) * 128], pb2[64:96, 0:128], eng="dve")
            for ti in range(4):
                pt = K.psum()
                for k in range(8):
                    K.mm(pt[:, 0:160], hT[:, k, ti * 128:(ti + 1) * 128], wm[:, k, 256:416], start=(k == 0), stop=(k == 7))
                junk = K.arotbuf("m_junk", [128], F32, n=1)
                ssq = K.arotbuf("m_ssq", [1], F32, n=2)
                K.memset(ssq[:], 0.0, eng="pool")
                K.act(junk[:], pt[:, 0:128], AF.Square, accum_out=ssq[:, 0:1])
                K.act(ssq[:], ssq[:], AF.Sqrt, bias=epsT[:, 0:1], scale=1.0 / 128.0)
                K.recip(ssq[:], ssq[:])
                ock = K.arotbuf("m_ock", [128], F32, n=2)
                K.stt(ock[:], pt[:, 0:128], ssq[:, 0:1], kvn_bc[:, j * 128:(j + 1) * 128], ALU.mult, ALU.mult)
                sq_i, r0 = ti // 2, (ti % 2) * 128
                finals.append(K.dma(o_ckv[sq_i, j, r0:r0 + 128, :], ock[:]))
                okr = K.arotbuf("m_okr", [32], F32, n=2)
                K.copy(okr[:], pt[:, 128:160], eng="act")
                finals.append(K.dma(o_kr[sq_i, j, r0:r0 + 128, :], okr[:]))
            K.reset_to(m1)
            S.tag = "L%d:e_mlaattn" % layer
            wuq = K.af([2, 768], BF16)
            K.dma(wuq[:], mla_w_uq[j].rearrange("(k p) o -> p k o", p=128), q="pool")
            wuqs = K.af([2, 8, 96], BF16)
            K.memset(wuqs[:], 0.0, eng="pool")
            uqv = mla_w_uq[j].rearrange("(k p) (h x) -> p k h x", p=128, h=8)
            for k_ in range(2):
                K.dma(wuqs[:, k_, :, 64:80], uqv[:, k_, :, 80:96], q="pool")
                K.dma(wuqs[:, k_, :, 80:96], uqv[:, k_, :, 64:80], q="pool")
            wukv = K.af([1024], BF16)
            K.dma(wukv[:], mla_w_ukv[j], q="pool")
            wukv_v = wukv[:].rearrange("p (h x) -> p h x", h=8)
            v_all = K.af([20, 512], BF16)
            for (c0, L, smp) in SEQS:
                attn_flush()
                nk = L + (512 if smp else 0)
                nkt = nk // 128
                for kt in range(nkt):
                    pb = K.psum()
                    K.mm(pb[:].rearrange("p (h x) -> p h x", h=8), ckvK[:, c0 + kt * 128:c0 + (kt + 1) * 128], wukv_v[:, :, 64:128])
                    K.copy(v_all[:, kt, :], pb[:], eng="act")
                for h in range(8):
                    kh = K.arotbuf("kh", [2560], BF16, n=1)
                    for kb in range(0, nk, 512):
                        w = min(512, nk - kb)
                        pb = K.psum()
                        K.mm(pb[0:64, 0:w], wukv[:, h * 128:h * 128 + 64], ckvK[:, c0 + kb:c0 + kb + w])
                        K.copy(kh[0:64, kb:kb + w], pb[0:64, 0:w], eng="dve")
                    K.copy(kh[64:96, 0:nk], krK[64:96, c0:c0 + nk], eng="pool")
                    qh = K.arotbuf("qh", [2048], BF16, n=1)
                    for qb0 in range(0, L, 512):
                        w = min(512, L - qb0)
                        pb = K.psum()
                        for k in range(2):
                            K.mm(pb[0:96, 0:w], wuq[:, k, h * 96:(h + 1) * 96], cqn[:, k, c0 + qb0:c0 + qb0 + w], start=(k == 0), stop=(k == 1))
                        if smp:
                            pb2 = K.psum()
                            for k in range(2):
                                K.mm(pb2[0:96, 0:w], wuqs[:, k, h, :], cqn[:, k, c0 + qb0:c0 + qb0 + w], start=(k == 0), stop=(k == 1))
                            t1 = K.arotbuf("q_t1", [512], F32, n=1)
                            t2 = K.arotbuf("q_t2", [512], F32, n=1)
                            K.tt(t1[64:96, 0:w], pb[64:96, 0:w], ropeC[64:96, qb0:qb0 + w], ALU.mult)
                            K.tt(t2[64:96, 0:w], pb2[64:96, 0:w], ropeS[64:96, qb0:qb0 + w], ALU.mult)
                            K.tt(t1[64:96, 0:w], t1[64:96, 0:w], t2[64:96, 0:w], ALU.add, eng="pool")
                            K.act(qh[0:64, qb0:qb0 + w], pb[0:64, 0:w], AF.Identity, scale=MLA_SCALE)
                            K.act(qh[64:96, qb0:qb0 + w], t1[64:96, 0:w], AF.Identity, scale=MLA_SCALE)
                        else:
                            K.act(qh[0:96, qb0:qb0 + w], pb[0:96, 0:w], AF.Identity, scale=MLA_SCALE)
                    half = h % 2
                    hs = slice(half * 64, half * 64 + 64)
                    kbanks = [(kh[0:96, kb:min(kb + 512, nk)], None) for kb in range(0, nk, 512)]
                    ktiles = [(kt * 128, 128, v_all[:, kt, (h // 2) * 128:(h // 2) * 128 + 128]) for kt in range(nkt)]
                    for qt in range(L // 128):
                        attn_unit(qh[0:96, qt * 128:(qt + 1) * 128], kbanks, ktiles,
                                  hs, yT[hs, 4 + h // 2, c0 + qt * 128:c0 + (qt + 1) * 128], "m")
            attn_flush()
            K.reset_to(m0)
            S.tag = "L%d:e_out" % layer
            out_proj_post(even_w_out[j], yT, gg)

        NA_SCALE = 0.125
        NEGV = -30000.0
        TWO_PI = 2.0 * math.pi
        CW1 = 6.28125
        CW2 = TWO_PI - 6.28125
        SIN_S = 1.0 - 2e-6
        GELU_C = 1.5957691216057308

        def sincos(ang, kiI, kf, sin_out, cos_out):
            K.ts(kiI, ang, 1.0 / TWO_PI, None, ALU.mult)
            K.copy(kf, kiI)
            K.stt(ang, kf, -CW1, ang, ALU.mult, ALU.add)
            K.stt(ang, kf, -CW2, ang, ALU.mult, ALU.add)
            K.ts(ang, ang, -3.1415925, 3.1415925, ALU.max, ALU.min)
            K.act(sin_out, ang, AF.Sin, scale=SIN_S)
            K.act(kf, ang, AF.Abs)
            K.act(cos_out, kf, AF.Sin, scale=-SIN_S, bias=halfpi[:, 0:1])

        def scan(out, d0, d1, init):
            rd = [d0, d1] + ([] if isinstance(init, float) else [init])
            return S.op("dve", lambda e: e.tensor_tensor_scan(out=out, data0=d0, data1=d1, initial=init, op0=ALU.mult, op1=ALU.add),
                        reads=rd, writes=[out])

        def odd_layer(layer, j, mod):
            gs, shift, gg = mod_vectors(layer, mod, 0)
            K.phase()
            yT = K.af([8, T], BF16)
            m0 = K.mark()
            w_in_v = odd_w_in[j].rearrange("(k p) o -> p k o", p=128)

            def make_hT():
                hT = K.af([8, T], BF16)
                mk = K.mark()
                for t0 in range(0, T, 512):
                    pre_block(t0, 512, 0 if t0 < NPT else 1, gs, shift, hT[:, :, t0:t0 + 512])
                K.reset_to(mk)
                return hT

            S.tag = "L%d:o_s5" % layer
            uT = K.af([4, T], BF16)
            mS = K.mark()
            hT = make_hT()
            wu = K.af([8, 512], BF16)
            K.dma(wu[:], w_in_v[:, :, 0:512], q="pool")
            for t0 in range(0, T, 512):
                for c in range(4):
                    pb = K.psum()
                    for k in range(8):
                        K.mm(pb[:], wu[:, k, c * 128:(c + 1) * 128], hT[:, k, t0:t0 + 512], start=(k == 0), stop=(k == 7))
                    K.copy(uT[:, c, t0:t0 + 512], pb[:], eng=("act" if c % 2 else "dve"))
            K.reset_to(mS)
            ygT = yT[:, 4:8, :]
            iotaF = K.af([2048], F32)
            K.dma(iotaF[:], iota_d)
            mT = K.mark()
            y_acc = K.af([T], F32)
            e_tb = K.ptr // 4
            tb = K.af([3, 2048], F32)
            ang = tb[:, 0, :]
            kiI = K.arI[:, e_tb + 2048:e_tb + 4096]
            kf = tb[:, 2, :]
            gbuf = [tb[:, 1, :], tb[:, 2, :]]
            sinT = K.af([2048], F32)
            cosT = K.af([2048], F32)
            rtile = tb[:, 0, :]
            bp = K.af([2, 2048], F32)
            Hb = K.af([2, 2048], BF16)
            prm = [K.af([1104], F32) for _ in range(2)]
            der = K.af([2, 14, 16], F32)
            bbt = K.af([2, 2, 16, 16], F32)
            cng = K.af([2, 16, 16], F32)
            tmp3 = K.af([16, 16], F32)
            hfin = K.af([2, 2, 2, 16], F32)
            e_s = K.ptr // 4
            sml = K.af([3, 16], F32)
            smlI = K.arI[:, e_s + 16:e_s + 32]
            for d in range(2):
                P = prm[d]
                K.dma(P[:], s5p[j, d])
                lre, lim, lst = P[:, 0:16], P[:, 16:32], P[:, 32:48]
                bre3 = P[:, 80:336].rearrange("p (s c) -> p s c", s=16)
                bim3 = P[:, 336:592].rearrange("p (s c) -> p s c", s=16)
                cim3 = P[:, 848:1104].rearrange("p (s c) -> p s c", s=16)
                Dv = lambda i, d=d: der[:, d, i, :]
                K.act(Dv(0), lst, AF.Exp)
                K.tt(Dv(1), lre, Dv(0), ALU.mult)
                K.act(Dv(1), Dv(1), AF.Exp)
                K.tt(Dv(2), lim, Dv(0), ALU.mult)
                K.copy(sml[:, 0, :], Dv(2))
                sincos(sml[:, 0, :], smlI, sml[:, 2, :], Dv(3), Dv(4))
                K.tt(Dv(5), Dv(1), Dv(4), ALU.mult)
                K.tt(Dv(6), Dv(1), Dv(3), ALU.mult)
                K.ts(Dv(5), Dv(5), -1.0, None, ALU.add)
                K.tt(Dv(7), lre, lre, ALU.mult)
                K.tt(Dv(8), lim, lim, ALU.mult)
                K.tt(Dv(7), Dv(7), Dv(8), ALU.add)
                K.recip(Dv(7), Dv(7))
                K.tt(Dv(8), Dv(5), lre, ALU.mult)
                K.tt(Dv(9), Dv(6), lim, ALU.mult)
                K.tt(Dv(8), Dv(8), Dv(9), ALU.add)
                K.tt(Dv(8), Dv(8), Dv(7), ALU.mult)
                K.tt(Dv(9), Dv(6), lre, ALU.mult)
                K.tt(Dv(10), Dv(5), lim, ALU.mult)
                K.tt(Dv(9), Dv(9), Dv(10), ALU.subtract)
                K.tt(Dv(9), Dv(9), Dv(7), ALU.mult)
                K.ts(Dv(13), Dv(2), 1.0 / TWO_PI, None, ALU.mult)
                K.ts(Dv(11), Dv(13), -1.0, None, ALU.mult)
                K.ts(Dv(12), Dv(13), 2049.0, None, ALU.mult)
                zre_b = Dv(8).unsqueeze(2).to_broadcast([128, 16, 16])
                zim_b = Dv(9).unsqueeze(2).to_broadcast([128, 16, 16])
                K.tt(bbt[:, d, 0], bre3, zre_b, ALU.mult)
                K.tt(tmp3[:], bim3, zim_b, ALU.mult)
                K.tt(bbt[:, d, 0], bbt[:, d, 0], tmp3[:], ALU.subtract)
                K.tt(bbt[:, d, 1], bim3, zre_b, ALU.mult)
                K.tt(tmp3[:], bre3, zim_b, ALU.mult)
                K.tt(bbt[:, d, 1], bbt[:, d, 1], tmp3[:], ALU.add)
                K.ts(cng[:, d], cim3, -1.0, None, ALU.mult)
            for c in range(4):
                first = True
                for d in range(2):
                    P = prm[d]
                    cre3 = P[:, 592:848].rearrange("p (s c) -> p s c", s=16)
                    for s_ in range(4 * c, 4 * c + 4):
                        base = (s_ % 4) * 32
                        if d == 0:
                            K.ts(ang, iotaF[:], der[:, d, 13, s_:s_ + 1], 0.0, ALU.mult, ALU.add, eng="pool")
                        else:
                            K.ts(ang, iotaF[:], der[:, d, 11, s_:s_ + 1], der[:, d, 12, s_:s_ + 1], ALU.mult, ALU.add, eng="pool")
                        K.ts(kiI, ang, 1.0, None, ALU.mult)
                        K.copy(kf, kiI)
                        K.tt(ang, ang, kf, ALU.subtract)
                        K.act(sinT[:], ang, AF.Sin, scale=TWO_PI * SIN_S)
                        K.act(kf, ang, AF.Abs)
                        K.act(cosT[:], kf, AF.Sin, scale=-TWO_PI * SIN_S, bias=halfpi[:, 0:1])
                        K.ts(rtile[:], iotaF[:], 0.0, der[:, d, 1, s_:s_ + 1], ALU.mult, ALU.add, eng="pool")
                        BBT = []
                        for ri in range(2):
                            Wp = K.arotbuf("s5_Wp", [128], F32, n=2)
                            K.memset(Wp[:], 0.0, eng="pool")
                            K.copy(Wp[0:64, base:base + 16], bbt[0:64, d, ri, s_, :], eng="pool")
                            K.copy(Wp[64:128, base + 16:base + 32], bbt[64:128, d, ri, s_, :], eng="pool")
                            pb = K.psum()
                            K.tr(pb[:, 0:128], Wp[:], ident[:])
                            bt = K.arotbuf("s5_BBT%d" % ri, [128], BF16, n=2)
                            K.copy(bt[:], pb[:, 0:128], eng="act")
                            BBT.append(bt)
                        CP = []
                        for ri, csrc in ((0, cre3), (1, cng[:, d])):
                            cp = K.arotbuf("s5_Cp%d" % ri, [128], BF16, n=2)
                            K.memset(cp[:], 0.0, eng="pool")
                            K.copy(cp[0:64, base:base + 16], csrc[0:64, s_, :], eng="pool")
                            K.copy(cp[64:128, base + 16:base + 32], csrc[64:128, s_, :], eng="pool")
                            CP.append(cp)
                        for seg in range(2):
                            if seg == 0:
                                c0, L, nseq = 0, 256, 2
                            else:
                                c0, L, nseq = 512, 2048, 1
                            tab0 = 0 if d == 0 else 2048 - L
                            Wt = L * nseq
                            if seg == 0:
                                v3 = lambda ap: ap.rearrange("p (q t) -> p q t", q=2)
                                tabv = lambda tb_, b0, w: tb_[:, tab0:tab0 + 256].unsqueeze(1).to_broadcast([128, 2, 256])
                            else:
                                v3 = lambda ap: ap
                                tabv = lambda tb_, b0, w: tb_[:, tab0 + b0:tab0 + b0 + w]
                            for b0 in range(0, Wt, 512):
                                w = min(512, Wt - b0)
                                pr = K.psum()
                                pi_ = K.psum()
                                K.mm(pr[:, 0:w], BBT[0][:], uT[:, c, c0 + b0:c0 + b0 + w])
                                K.mm(pi_[:, 0:w], BBT[1][:], uT[:, c, c0 + b0:c0 + b0 + w])
                                cosv = tabv(cosT, b0, w)
                                sinv = tabv(sinT, b0, w)
                                t1 = K.arotbuf("s5_t1", [512], F32, n=2)
                                t2 = K.arotbuf("s5_t2", [512], F32, n=2)
                                t3 = K.arotbuf("s5_t3", [512], F32, n=2)
                                t4 = K.arotbuf("s5_t4", [512], F32, n=2)
                                K.tt(v3(t1[:, 0:w]), v3(pr[:, 0:w]), cosv, ALU.mult)
                                K.tt(v3(t2[:, 0:w]), v3(pi_[:, 0:w]), sinv, ALU.mult)
                                K.tt(bp[:, 0, b0:b0 + w], t1[:, 0:w], t2[:, 0:w], ALU.add, eng="pool")
                                K.tt(v3(t3[:, 0:w]), v3(pi_[:, 0:w]), cosv, ALU.mult)
                                K.tt(v3(t4[:, 0:w]), v3(pr[:, 0:w]), sinv, ALU.mult)
                                K.tt(bp[:, 1, b0:b0 + w], t3[:, 0:w], t4[:, 0:w], ALU.subtract, eng="pool")
                            for q_ in range(nseq):
                                qs = slice(q_ * L, (q_ + 1) * L)
                                for ri in range(2):
                                    init = P[:, 48 + ri * 16 + s_:48 + ri * 16 + s_ + 1] if seg == 1 else 0.0
                                    if d == 0:
                                        scan(gbuf[ri][:, qs], rtile[:, 0:L], bp[:, ri, qs], init)
                                    else:
                                        scan(gbuf[ri][:, qs][:, ::-1], rtile[:, 0:L][:, ::-1], bp[:, ri, qs][:, ::-1], init)
                            for b0 in range(0, Wt, 512):
                                w = min(512, Wt - b0)
                                cosv = tabv(cosT, b0, w)
                                sinv = tabv(sinT, b0, w)
                                t1 = K.arotbuf("s5_t1", [512], F32, n=2)
                                t2 = K.arotbuf("s5_t2", [512], F32, n=2)
                                t3 = K.arotbuf("s5_t3", [512], F32, n=2)
                                t4 = K.arotbuf("s5_t4", [512], F32, n=2)
                                K.tt(v3(t1[:, 0:w]), v3(gbuf[0][:, b0:b0 + w]), cosv, ALU.mult, eng="pool")
                                K.tt(v3(t2[:, 0:w]), v3(gbuf[1][:, b0:b0 + w]), sinv, ALU.mult, eng="pool")
                                K.tt(Hb[:, 0, b0:b0 + w], t1[:, 0:w], t2[:, 0:w], ALU.subtract)
                                K.tt(v3(t3[:, 0:w]), v3(gbuf[0][:, b0:b0 + w]), sinv, ALU.mult)
                                K.tt(v3(t4[:, 0:w]), v3(gbuf[1][:, b0:b0 + w]), cosv, ALU.mult)
                                K.tt(Hb[:, 1, b0:b0 + w], t3[:, 0:w], t4[:, 0:w], ALU.add)
                                py = K.psum()
                                K.mm(py[:, 0:w], CP[0][:], Hb[:, 0, b0:b0 + w], start=True, stop=False)
                                K.mm(py[:, 0:w], CP[1][:], Hb[:, 1, b0:b0 + w], start=False, stop=True)
                                ya = y_acc[:, c0 + b0:c0 + b0 + w]
                                if first:
                                    K.copy(ya, py[:, 0:w], eng="act")
                                else:
                                    K.tt(ya, ya, py[:, 0:w], ALU.add)
                            if seg == 0:
                                col = L - 1 if d == 0 else 0
                                tc_ = tab0 + col
                                g0c = gbuf[0][:, 0:512].rearrange("p (q t) -> p q t", q=2)[:, :, col]
                                g1c = gbuf[1][:, 0:512].rearrange("p (q t) -> p q t", q=2)[:, :, col]
                                cc_ = cosT[:, tc_:tc_ + 1].to_broadcast([128, 2])
                                sc_ = sinT[:, tc_:tc_ + 1].to_broadcast([128, 2])
                                f1 = K.arotbuf("s5_f1", [2], F32, n=2)
                                f2 = K.arotbuf("s5_f2", [2], F32, n=2)
                                K.tt(f1[:], g0c, cc_, ALU.mult, eng="pool")
                                K.tt(f2[:], g1c, sc_, ALU.mult, eng="pool")
                                K.tt(hfin[:, :, d, 0, s_], f1[:], f2[:], ALU.subtract, eng="pool")
                                f1 = K.arotbuf("s5_f1", [2], F32, n=2)
                                f2 = K.arotbuf("s5_f2", [2], F32, n=2)
                                K.tt(f1[:], g0c, sc_, ALU.mult, eng="pool")
                                K.tt(f2[:], g1c, cc_, ALU.mult, eng="pool")
                                K.tt(hfin[:, :, d, 1, s_], f1[:], f2[:], ALU.add, eng="pool")
                        first = False
                for t0 in range(0, T, 512):
                    yb = y_acc[:, t0:t0 + 512]
                    K.stt(yb, uT[:, c, t0:t0 + 512], s5dT[:, j * 4 + c:j * 4 + c + 1], yb, ALU.mult, ALU.add)
                    y2 = K.arotbuf("s5_y2", [512], F32, n=1)
                    K.tt(y2[:], yb, yb, ALU.mult, eng="pool")
                    K.ts(y2[:], y2[:], 0.044715, 1.0, ALU.mult, ALU.add)
                    K.tt(y2[:], y2[:], yb, ALU.mult, eng="pool")
                    K.act(y2[:], y2[:], AF.Sigmoid, scale=GELU_C)
                    K.tt(ygT[:, c, t0:t0 + 512], yb, y2[:], ALU.mult)
            for si in range(2):
                for d in range(2):
                    finals.append(K.dma(o_s5re[si, j, d], hfin[:, si, d, 0, :]))
                    finals.append(K.dma(o_s5im[si, j, d], hfin[:, si, d, 1, :]))
            S.tag = "L%d:o_glu" % layer
            K.reset_to(mT)
            wg = K.af([4, 512], BF16)
            K.dma(wg[:], s5_glu_w[j].rearrange("(k p) o -> p k o", p=128), q="pool")
            for t0 in range(0, T, 512):
                for c2 in range(4):
                    pb = K.psum()
                    for k in range(4):
                        K.mm(pb[:], wg[:, k, c2 * 128:(c2 + 1) * 128], ygT[:, k, t0:t0 + 512], start=(k == 0), stop=(k == 3))
                    sg = K.arotbuf("s5_sg", [512], F32, n=2)
                    K.act(sg[:], pb[:], AF.Sigmoid, bias=glubT[:, j * 4 + c2:j * 4 + c2 + 1])
                    K.tt(yT[:, c2, t0:t0 + 512], ygT[:, c2, t0:t0 + 512], sg[:], ALU.mult)

            K.reset_to(m0)
            S.tag = "L%d:o_naproj" % layer
            qT = K.af([4, T], BF16)
            kT = K.af([4, 3072], BF16)
            vtok = K.af([24, 512], BF16)
            mN = K.mark()
            hT = make_hT()
            wq = K.af([8, 512], BF16)
            wk = K.af([8, 512], BF16)
            wv = K.af([8, 512], BF16)
            K.dma(wq[:], w_in_v[:, :, 512:1024], q="pool")
            K.dma(wk[:], w_in_v[:, :, 1024:1536], q="pool")
            K.dma(wv[:], w_in_v[:, :, 1536:2048], q="pool")
            for t0 in range(0, T, 512):
                for c in range(4):
                    pb = K.psum()
                    for k in range(8):
                        K.mm(pb[:], wq[:, k, c * 128:(c + 1) * 128], hT[:, k, t0:t0 + 512], start=(k == 0), stop=(k == 7))
                    K.act(qT[:, c, t0:t0 + 512], pb[:], AF.Identity, scale=NA_SCALE)
                    pb = K.psum()
                    for k in range(8):
                        K.mm(pb[:], wk[:, k, c * 128:(c + 1) * 128], hT[:, k, t0:t0 + 512], start=(k == 0), stop=(k == 7))
                    K.copy(kT[:, c, t0:t0 + 512], pb[:], eng="dve")
            for ti in range(20):
                pb = K.psum()
                for k in range(8):
                    K.mm(pb[:], hT[:, k, ti * 128:(ti + 1) * 128], wv[:, k, :], start=(k == 0), stop=(k == 7))
                K.copy(vtok[:, ti, :], pb[:], eng="act")
                if ti < 4:
                    sq_i, r0 = ti // 2, (ti % 2) * 128
                    ov = K.arotbuf("na_ov", [512], F32, n=2)
                    K.copy(ov[:], pb[:], eng="dve")
                    finals.append(K.dma(o_nav[sq_i, j, r0:r0 + 128, :], ov[:]))
                    pb2 = K.psum()
                    for k in range(8):
                        K.mm(pb2[:], hT[:, k, ti * 128:(ti + 1) * 128], wk[:, k, :], start=(k == 0), stop=(k == 7))
                    ok_ = K.arotbuf("na_ok", [512], F32, n=2)
                    K.copy(ok_[:], pb2[:], eng="act")
                    finals.append(K.dma(o_nak[sq_i, j, r0:r0 + 128, :], ok_[:]))
            for i in range(4):
                stg = K.arotbuf("na_stg", [512], F32, n=2)
                K.dma(stg[:], nak_in[j, i * 128:(i + 1) * 128, :])
                pb = K.psum()
                for c in range(4):
                    K.tr(pb[:, c * 128:(c + 1) * 128], stg[:, c * 128:(c + 1) * 128], ident[:])
                K.copy(kT[:, :, 2560 + i * 128:2560 + (i + 1) * 128], pb[:].rearrange("p (c t) -> p c t", c=4), eng="dve")
                K.dma(vtok[:, 20 + i, :], nav_in[j, i * 128:(i + 1) * 128, :], q="pool")
            K.reset_to(mN)
            Ap = K.af([64, 8, 15], F32)
            rp = K.af([120], F32)
            K.memset(rp[0:32, :], NEGV)
            K.dma(rp[0:31, :], na_rpb[j].rearrange("h r c -> c (h r)"), allow_slow_non_contiguous=True)
            rhi = K.af([120], BF16)
            rlo = K.af([120], BF16)
            K.copy(rhi[0:32, :], rp[0:32, :], eng="act")
            K.tt(rlo[0:32, :], rp[0:32, :], rhi[0:32, :], ALU.subtract)
            for q4 in range(4):
                oh = K.arotbuf("na_oh", [16, 128], BF16, n=2)
                K.dma(oh[0:32, :, :], oh_d[:, q4 * 16:(q4 + 1) * 16, :], q="pool")
                for g4 in range(4):
                    pb = K.psum()
                    for i in range(4):
                        oc = pb[:, i * 120:(i + 1) * 120]
                        K.mm(oc, oh[0:32, g4 * 4 + i, :], rhi[0:32, :], start=True, stop=False)
                        K.mm(oc, oh[0:32, g4 * 4 + i, :], rlo[0:32, :], start=False, stop=True)
                    kc0 = q4 * 16 + g4 * 4
                    K.copy(Ap[:, kc0:kc0 + 4, :, :].rearrange("p k h r -> p k (h r)"), pb[:, 0:480].rearrange("p (k x) -> p k x", k=4),
                           eng=("act" if g4 % 2 else "dve"))
            S.tag = "L%d:o_naattn" % layer
            for (c0, L, smp) in SEQS[0:2]:
                for h in range(8):
                    ch, half = h // 2, h % 2
                    hsl = slice(half * 64, half * 64 + 64)
                    for qt in range(2):
                        qc = slice(c0 + qt * 128, c0 + (qt + 1) * 128)
                        attn_unit(qT[hsl, ch, qc], [(kT[hsl, ch, c0:c0 + 256], None)],
                                  [(t_ * 128, 128, vtok[:, c0 // 128 + t_, ch * 128:(ch + 1) * 128]) for t_ in range(2)],
                                  hsl, yT[hsl, 4 + ch, qc], "n")
            for i in range(16):
                if i < 2:
                    kr0, nr = 0, 8
                elif i > 13:
                    kr0, nr = 24, 8
                else:
                    kr0, nr = 2 * i - 4, 9
                kcol0 = 512 + kr0 * 64
                vt0 = 4 + kr0 // 2
                for h in range(8):
                    ch, half = h // 2, h % 2
                    hsl = slice(half * 64, half * 64 + 64)

                    def biasA(bank, w, i=i, h=h, kr0=kr0):
                        Ssb = K.arotbuf("na_S", [512], F32, n=2)
                        for ql in range(2):
                            qr = 2 * i + ql
                            rs_ = min(max(qr - 4, 0), 24)
                            mlo = max(0, rs_ - kr0)
                            mhi = min(8, rs_ - kr0 + 8)
                            ps_ = slice(ql * 64, ql * 64 + 64)
                            if mlo > 0:
                                K.memset(Ssb[ps_, 0:mlo * 64], NEGV, eng="pool")
                            if mhi < 8:
                                K.memset(Ssb[ps_, mhi * 64:512], NEGV, eng="pool")
                            dr0 = kr0 + mlo - qr + 7
                            nm_ = mhi - mlo
                            K.tt(Ssb[ps_, mlo * 64:mhi * 64].rearrange("p (m k) -> p m k", m=nm_),
                                 bank[ps_, mlo * 64:mhi * 64].rearrange("p (m k) -> p m k", m=nm_),
                                 Ap[ps_, :, h, dr0:dr0 + nm_].rearrange("p k r -> p r k"), ALU.add)
                        return Ssb[:, 0:512]

                    def biasB(bank, w, i=i, h=h, kr0=kr0):
                        Ssb = K.arotbuf("na_SB", [64], F32, n=2)
                        for ql in range(2):
                            qr = 2 * i + ql
                            rs_ = min(max(qr - 4, 0), 24)
                            ps_ = slice(ql * 64, ql * 64 + 64)
                            if rs_ <= kr0 + 8 < rs_ + 8:
                                dr = kr0 + 8 - qr + 7
                                K.tt(Ssb[ps_, 0:64], bank[ps_, 0:64], Ap[ps_, :, h, dr], ALU.add)
                            else:
                                K.memset(Ssb[ps_, 0:64], NEGV, eng="pool")
                        return Ssb[:, 0:64]

                    kbanks = [(kT[hsl, ch, 2560:3072], None), (kT[hsl, ch, kcol0:kcol0 + 512], biasA)]
                    ktiles = [(t_ * 128, 128, vtok[:, 20 + t_, ch * 128:(ch + 1) * 128]) for t_ in range(4)]
                    ktiles += [(512 + t_ * 128, 128, vtok[:, vt0 + t_, ch * 128:(ch + 1) * 128]) for t_ in range(4)]
                    if nr == 9:
                        kbanks.append((kT[hsl, ch, kcol0 + 512:kcol0 + 576], biasB))
                        ktiles.append((1024, 64, vtok[0:64, vt0 + 4, ch * 128:(ch + 1) * 128]))
                    qc = slice(512 + i * 128, 512 + (i + 1) * 128)
                    attn_unit(qT[hsl, ch, qc], kbanks, ktiles, hsl, yT[hsl, 4 + ch, qc], "n")
            attn_flush()
            K.reset_to(m0)
            S.tag = "L%d:o_out" % layer
            out_proj_post(odd_w_out[j], yT, gg)

        for layer in range(DEPTH):
            mod = ada(layer)
            if layer % 2 == 0 and ENABLE_EVEN:
                even_layer(layer, layer // 2, mod)
            if layer % 2 == 1 and ENABLE_ODD:
                odd_layer(layer, layer // 2, mod)
            gs, shift, gg = mod_vectors(layer, mod, 1)
            ffn(layer, gs, shift, gg)
            if dbg_stop == layer:
                break

        S.tag = "final"
        finals = []

        K.phase()

        def xT_to_out(dst, ntile, col0):
            for i in range(ntile):
                xb = K.arotbuf("xinT", [8, 128], F32)
                K.dma(xb[:], xT_v[:, :, col0 + i * 128: col0 + (i + 1) * 128])
                xo = K.arotbuf("xin", [D], F32)
                for g in range(2):
                    pb = K.psum()
                    for c in range(4):
                        K.tr(pb[:, c * 128:(c + 1) * 128], xb[:, g * 4 + c, :], ident[:])
                    K.copy(xo[:, g * 512:(g + 1) * 512], pb[:], eng=("act" if g else "dve"))
                finals.append(K.dma(dst[i * 128:(i + 1) * 128, :], xo[:]))

        xT_to_out(yp, NPT // 128, 0)
        xT_to_out(ys, LS // 128, NPT)
        S.finish(finals)
        global LAST_TAGS
        LAST_TAGS = S.tags
        S.emit()
    return nc


def _make_consts():
    i = np.arange(128, dtype=np.float32)
    jj = i[:, None]
    ii = i[None, :]
    rc = np.zeros((128, 772), np.float32)
    rc[:, 0:128] = np.maximum(ii - jj, 0)
    rc[:, 128:256] = (ii >= jj)
    rc[:, 256:384] = np.maximum(jj - ii, 0)
    rc[:, 384:512] = (jj >= ii)
    rc[:, 512:640] = np.broadcast_to(ii + 1.0, (128, 128))
    rc[:, 640:768] = np.broadcast_to(128.0 - ii, (128, 128))
    rc[:, 768] = 127.0 - i
    rc[:, 769] = i
    rc[:, 770] = 128.0
    inv = (10000.0 ** (-np.arange(8, dtype=np.float32) / np.float32(8))).astype(np.float32)
    t = np.arange(2048)
    row = (t // 64).astype(np.float32)
    col = (t % 64).astype(np.float32)
    ang = np.concatenate([row[:, None] * inv, col[:, None] * inv], axis=-1).astype(np.float32)
    cos, sin = np.cos(ang).T, np.sin(ang).T
    ropeC = np.concatenate([cos, cos], 0).astype(np.float32)
    ropeS = np.concatenate([-sin, sin], 0).astype(np.float32)
    oh = np.zeros((32, 64, 128), np.float32)
    qc = np.arange(64)
    cs = np.clip(qc - 8, 0, 48)
    for kc in range(64):
        jidx = np.clip(kc - qc + 15, 0, 30)
        for ql in range(2):
            oh[jidx, kc, ql * 64 + qc] = 1.0
            oh[31, kc, ql * 64 + qc] = ((kc < cs) | (kc >= cs + 16)).astype(np.float32)
    iota = np.broadcast_to(np.arange(1, 2049, dtype=np.float32)[None, :], (128, 2048))
    return {"rconst": rc, "ropeC": np.ascontiguousarray(ropeC), "ropeS": np.ascontiguousarray(ropeS),
            "oh": oh, "iota": np.ascontiguousarray(iota)}


def _s5_lay(a):
    a = np.asarray(a, np.float32)
    lead = a.shape[:-2] if a.ndim >= 2 else ()
    return a


def _s5_pack(inputs, s):
    def gp(a):
        a = np.asarray(a, np.float32).reshape(2, 2, 16, 2, 64)
        return a.transpose(0, 1, 3, 4, 2).reshape(2, 2, 128, 16)
    lre = gp(inputs["s5_lambda_re"])
    lim = gp(inputs["s5_lambda_im"])
    lst = gp(np.broadcast_to(np.asarray(inputs["s5_log_step"], np.float32)[..., None], (2, 2, 32, 64)))
    h0re = gp(np.asarray(inputs["state_s5_re"], np.float32)[s])
    h0im = gp(np.asarray(inputs["state_s5_im"], np.float32)[s])
    def gps(a):
        a = np.asarray(a, np.float32).reshape(2, 2, 16, 2, 64, 16)
        return a.transpose(0, 1, 3, 4, 2, 5).reshape(2, 2, 128, 256)
    def gsp(a):
        a = np.asarray(a, np.float32).reshape(2, 2, 16, 2, 16, 64)
        return a.transpose(0, 1, 3, 5, 2, 4).reshape(2, 2, 128, 256)
    pk = np.concatenate([lre, lim, lst, h0re, h0im, gps(inputs["s5_b_re"]), gps(inputs["s5_b_im"]),
                         gsp(inputs["s5_c_re"]), gsp(inputs["s5_c_im"])], axis=-1)
    return np.ascontiguousarray(pk.astype(np.float32))


CONSTS = _make_consts()


def make_in_maps(inputs):
    f = lambda a: np.ascontiguousarray(np.asarray(a, dtype=np.float32))
    maps = []
    ident = np.eye(128, dtype=np.float32)
    for core in range(8):
        s = core // 4
        m = {}
        m["xp"] = f(inputs["x_prompt"][2 * core:2 * core + 2]).reshape(NPT, D)
        m["xs"] = f(inputs["x_sample"][s])
        m["cv"] = f(np.stack([inputs["c_ctx"], inputs["c"][s]], 0))
        m["ident_in"] = ident
        m["ada_w"] = f(inputs["ada_w"])
        m["ada_b"] = f(inputs["ada_b"]).reshape(DEPTH * 48, 128)
        for nm in ("mix_pre_g", "mix_post_g", "ffn_pre_g", "ffn_post_g"):
            m[nm] = f(inputs[nm]).reshape(DEPTH * 8, 128)
        m["ffn_w_up"] = f(inputs["ffn_w_up"])
        m["ffn_conv_w"] = f(inputs["ffn_conv_w"]).reshape(DEPTH * 3 * 44, 128)
        m["ffn_conv_b"] = f(inputs["ffn_conv_b"]).reshape(DEPTH * 44, 128)
        m["ffn_w_down"] = f(inputs["ffn_w_down"])
        m["even_w_in"] = f(inputs["even_w_in"])
        m["even_w_out"] = f(inputs["even_w_out"])
        m["ret_logit_b"] = f(np.broadcast_to(np.asarray(inputs["ret_logit"], np.float32).reshape(2, 1, 8), (2, 128, 8)))
        m["ret_gn"] = f(inputs["ret_gn"]).reshape(8, 128)
        m["mla_q_norm"] = f(inputs["mla_q_norm"]).reshape(4, 128)
        m["mla_w_uq"] = f(inputs["mla_w_uq"])
        m["mla_kv_norm"] = f(inputs["mla_kv_norm"])
        m["kvn_bc"] = f(np.broadcast_to(np.asarray(inputs["mla_kv_norm"], np.float32).reshape(1, 256), (128, 256)))
        m["mla_w_ukv"] = f(inputs["mla_w_ukv"])
        m["state_ret_in"] = f(inputs["state_ret"][s])
        m["ckv_in"] = f(inputs["cache_mla_ckv"][s])
        m["kr_in"] = f(inputs["cache_mla_krope"][s])
        m["odd_w_in"] = f(inputs["odd_w_in"])
        m["odd_w_out"] = f(inputs["odd_w_out"])
        m["s5p"] = _s5_pack(inputs, s)
        m["s5_d"] = f(inputs["s5_d"]).reshape(8, 128)
        m["s5_glu_b"] = f(inputs["s5_glu_b"]).reshape(8, 128)
        m["s5_glu_w"] = f(inputs["s5_glu_w"])
        m["na_rpb"] = f(inputs["na_rpb"])
        m["nak_in"] = f(inputs["cache_na_k"][s]).reshape(2, 512, 512)
        m["nav_in"] = f(inputs["cache_na_v"][s]).reshape(2, 512, 512)
        m["oh_in"] = CONSTS["oh"]
        m["iota_in"] = CONSTS["iota"]
        m["rconst_in"] = CONSTS["rconst"]
        m["ropeC_in"] = CONSTS["ropeC"]
        m["ropeS_in"] = CONSTS["ropeS"]
        maps.append(m)
    return maps


def kernel(**inputs):
    nc = build_program()
    maps = make_in_maps(inputs)
    res = run_bass_kernel_spmd(nc, maps, core_ids=list(range(8)))
    R = res.results
    y_prompt = np.concatenate([R[c]["yp"].reshape(2, 256, D) for c in range(8)], 0)
    y_sample = np.stack([R[0]["ys"], R[4]["ys"]], 0)
    st_ret = np.concatenate([R[c]["o_ret"] for c in range(8)], 0)
    ck_ckv = np.concatenate([R[c]["o_ckv"] for c in range(8)], 0)
    ck_kr = np.concatenate([R[c]["o_kr"] for c in range(8)], 0)
    def s5o(name):
        o = np.concatenate([R[c][name] for c in range(8)], 0)
        o = o.reshape(16, 2, 2, 2, 64, 16).transpose(0, 1, 2, 5, 3, 4)
        return np.ascontiguousarray(o.reshape(16, 2, 2, 32, 64))
    st_re, st_im = s5o("o_s5re"), s5o("o_s5im")
    nak = np.concatenate([R[c]["o_nak"] for c in range(8)], 0).reshape(16, 2, 256, 8, 64)
    nav = np.concatenate([R[c]["o_nav"] for c in range(8)], 0).reshape(16, 2, 256, 8, 64)
    return (y_prompt, y_sample, st_ret, ck_ckv, ck_kr, st_re, st_im, nak, nav)
```

```python
import math
import numpy as np
import concourse.bass as bass
import concourse.mybir as mybir
from concourse.bass_utils import run_bass_kernel_spmd

F32 = mybir.dt.float32
BF16 = mybir.dt.bfloat16
I32 = mybir.dt.int32
ALU = mybir.AluOpType
AF = mybir.ActivationFunctionType
AX = mybir.AxisListType

ENGS = ("pe", "act", "dve", "pool", "sp")
STRICT_SAME_ENGINE = True
N_DMA_SEMS = 12


def _box(ap):
    t = ap.tensor
    es = 2 if ap.dtype == BF16 else 4
    pat = [(a * es, b) for a, b in ap.ap]
    off = int(ap.offset) * es
    sp = str(ap.space)
    if "DRAM" in sp.upper() or "Dram" in sp or "dram" in sp:
        lo = off
        hi = off
        for st, cnt in pat:
            if st >= 0:
                hi += st * (cnt - 1)
            else:
                lo += st * (cnt - 1)
        return (t.name, 0, 1, lo, hi + 1)
    pstride = pat[0][0]
    npart = pat[0][1]
    if pstride == 0:
        pstride = 1 << 30
    p0 = off // pstride
    f0 = off % pstride
    lo = f0
    hi = f0
    for st, cnt in pat[1:]:
        if st >= 0:
            hi += st * (cnt - 1)
        else:
            lo += st * (cnt - 1)
    if t.name.startswith("psb"):
        return (t.name, 0, 128, 0, 2048)
    return (t.name, p0, p0 + npart, lo, hi + 1)


def _ovl(a, b):
    return a[1] < b[2] and b[1] < a[2] and a[3] < b[4] and b[3] < a[4]


def _contains(a, b):
    return a[1] <= b[1] and b[2] <= a[2] and a[3] <= b[3] and b[4] <= a[4]


class Sched:
    def __init__(self, nc):
        self.nc = nc
        self.ops = {e: [] for e in ENGS}
        self.track = {}
        self.known = {e: {} for e in ENGS}
        self.ndma = {e: 0 for e in ENGS}
        self.ncomp = {e: 0 for e in ENGS}
        self.dma_last = {e: {} for e in ENGS}
        self.notrack = set()
        self.final_events = []
        self.tag = ""
        self.tags = {e: [] for e in ENGS}

    def _deps(self, reads, writes, eng=None):
        deps = set()
        for ap in reads:
            b = _box(ap)
            if b[0] in self.notrack:
                continue
            tr = self.track.get(b[0])
            if tr is None:
                continue
            for wb, ev in tr["w"]:
                if _ovl(wb, b):
                    deps.add(ev)
            if b[0].startswith("psb"):
                for rb, evs in tr["r"].items():
                    if _ovl(rb, b):
                        deps.update(ev for e_, ev in evs.items() if e_ != eng)
        for ap in writes:
            b = _box(ap)
            tr = self.track.get(b[0])
            if tr is None:
                continue
            for wb, ev in tr["w"]:
                if _ovl(wb, b):
                    deps.add(ev)
            for rb, evs in tr["r"].items():
                if _ovl(rb, b):
                    deps.update(evs.values())
        return deps

    def _record(self, reads, writes, ev, eng):
        for ap in writes:
            b = _box(ap)
            tr = self.track.setdefault(b[0], {"w": [], "r": {}})
            tr["w"] = [(wb, e) for wb, e in tr["w"] if not _contains(b, wb)]
            tr["w"].append((b, ev))
            tr["r"] = {rb: evs for rb, evs in tr["r"].items() if not _contains(b, rb)}
        for ap in reads:
            b = _box(ap)
            if b[0] in self.notrack:
                continue
            tr = self.track.setdefault(b[0], {"w": [], "r": {}})
            tr["r"].setdefault(b, {})[eng] = ev

    def _waits(self, eng, deps, idx, is_dma=False):
        waits = {}
        for semkey, val in deps:
            if semkey == ("c", eng) and not is_dma:
                if eng == "pe":
                    continue
                if (not STRICT_SAME_ENGINE) and val < idx - 1:
                    continue
            if self.known[eng].get(semkey, 0) >= val:
                continue
            if waits.get(semkey, 0) < val:
                waits[semkey] = val
        for k, v in waits.items():
            self.known[eng][k] = v
        return list(waits.items())

    def op(self, eng, fn, reads=(), writes=()):
        reads = [r for r in reads if r is not None and not isinstance(r, (int, float))]
        idx = self.ncomp[eng]
        self.ncomp[eng] = idx + 1
        deps = self._deps(reads, writes, eng)
        waits = self._waits(eng, deps, idx)
        ev = (("c", eng), idx + 1)
        self.tags[eng].append(self.tag)
        self.ops[eng].append((fn, waits, ev, "c"))
        self._record(reads, writes, ev, eng)
        return ev

    def dma(self, eng, out, in_, **kw):
        n = self.ndma[eng]
        self.ndma[eng] = n + 1
        k = n % N_DMA_SEMS
        val = (n // N_DMA_SEMS + 1) * 16
        semkey = ("d", eng, k)
        deps = self._deps([in_], [out])
        if val > 16:
            deps.add((semkey, val - 16))
        idx = self.ncomp[eng]
        waits = self._waits(eng, deps, idx, True)
        ev = (semkey, val)

        def fn(e, out=out, in_=in_, kw=kw):
            return e.dma_start(out=out, in_=in_, **kw)

        self.ops[eng].append((fn, waits, ev, "d"))
        self._record([in_], [out], ev, eng)
        return ev

    def finish(self, final_events):
        self.final_events = list(final_events)

    def emit(self):
        nc = self.nc
        import contextlib

        with contextlib.ExitStack() as st:
            sems = {}
            for e in ENGS:
                sems[("c", e)] = st.enter_context(nc.semaphore("c_" + e))
            for e in ENGS:
                for k in range(min(N_DMA_SEMS, self.ndma[e])):
                    sems[("d", e, k)] = st.enter_context(nc.semaphore("d_%s_%d" % (e, k)))
            block = st.enter_context(nc.Block())

            waited = {e: set() for e in ENGS}
            for e in ENGS:
                for fn, waits, ev, kind in self.ops[e]:
                    for semkey, val in waits:
                        if semkey[0] == "c":
                            waited[semkey[1]].add(val)
            for semkey, val in self.final_events:
                if semkey[0] == "c":
                    waited[semkey[1]].add(val)
            rank = {e: {v: i + 1 for i, v in enumerate(sorted(waited[e]))} for e in ENGS}

            def semval(semkey, val):
                return rank[semkey[1]][val] if semkey[0] == "c" else val

            def run(engname, eobj):
                for fn, waits, ev, kind in self.ops[engname]:
                    for semkey, val in waits:
                        eobj.wait_ge(sems[semkey], semval(semkey, val))
                    ins = fn(eobj)
                    if kind == "c":
                        if ev[1] in waited[engname]:
                            ins.then_inc(sems[ev[0]], 1)
                    else:
                        ins.then_inc(sems[ev[0]], 16)
                if engname == "sp":
                    mx = {}
                    for semkey, val in self.final_events:
                        v = semval(semkey, val)
                        mx[semkey] = max(mx.get(semkey, 0), v)
                    for semkey, val in mx.items():
                        eobj.wait_ge(sems[semkey], val)

            @block.tensor
            def _(e):
                run("pe", e)

            @block.scalar
            def _(e):
                run("act", e)

            @block.vector
            def _(e):
                run("dve", e)

            @block.gpsimd
            def _(e):
                run("pool", e)

            @block.sync
            def _(e):
                run("sp", e)

import contextlib

ENABLE_EVEN = True
ENABLE_ODD = True

D = 1024
NPT = 512
LS = 2048
T = NPT + LS
DFF = 2816
NJ = DFF // 128
EPS = 1e-6
DEPTH = 4


class KB:
    def __init__(self, nc, st):
        self.nc = nc
        self.st = st
        self.S = Sched(nc)
        self.psn = 0
        self.rot = {}

    def sb(self, name, shape, dt=F32):
        return self.st.enter_context(self.nc.sbuf_tensor(name, list(shape), dt))

    def dram(self, name, shape, dt=F32, kind="Internal"):
        return self.nc.dram_tensor(name, list(shape), dt, kind=kind).ap()

    def rotbuf(self, name, shape, dt, n=2):
        key = name
        if key not in self.rot:
            self.rot[key] = [[self.sb("%s_%d" % (name, i), shape, dt) for i in range(n)], 0]
        r = self.rot[key]
        b = r[0][r[1] % len(r[0])]
        r[1] += 1
        return b

    def arena_init(self, nf):
        self.arF = self.sb("arena", [128, nf], F32)
        self.arB = self.arF.bitcast(BF16)
        self.arI = self.arF.bitcast(I32)
        self.nbytes = nf * 4
        self.ptr = 0
        self.arot = {}

    def phase(self):
        self.ptr = 0
        self.arot = {}

    def mark(self):
        return self.ptr

    def reset_to(self, m):
        self.ptr = m
        self.arot = {}

    def af(self, shape, dt=F32):
        n = int(np.prod(shape))
        es = 4 if dt == F32 else 2
        self.ptr = (self.ptr + 3) // 4 * 4
        e0 = self.ptr // es
        a = (self.arF if dt == F32 else self.arB)[:, e0:e0 + n]
        self.ptr += n * es
        assert self.ptr <= self.nbytes, ("arena overflow", self.ptr, self.nbytes)
        if len(shape) == 2:
            a = a.rearrange("p (a b) -> p a b", a=shape[0])
        elif len(shape) == 3:
            a = a.rearrange("p (a b c) -> p a b c", a=shape[0], b=shape[1])
        elif len(shape) == 4:
            a = a.rearrange("p (a b c d) -> p a b c d", a=shape[0], b=shape[1], c=shape[2])
        return a

    def arotbuf(self, name, shape, dt, n=2):
        if name not in self.arot:
            self.arot[name] = [[self.af(shape, dt) for _ in range(n)], 0]
        r = self.arot[name]
        b = r[0][r[1] % len(r[0])]
        r[1] += 1
        return b

    def psum(self):
        b = self.psb[self.psn % len(self.psb)]
        self.psn += 1
        return b

    def mm(self, out, lhsT, rhs, start=True, stop=True):
        return self.S.op("pe", lambda e: e.matmul(out, lhsT=lhsT, rhs=rhs, start=start, stop=stop),
                         reads=[lhsT, rhs], writes=[out])

    def tr(self, out, in_, ident):
        return self.S.op("pe", lambda e: e.transpose(out, in_, ident), reads=[in_, ident], writes=[out])

    def act(self, out, in_, func, bias=None, scale=1.0, accum_out=None):
        kw = {}
        rd = [in_]
        wr = [out]
        if bias is not None:
            kw["bias"] = bias
            rd.append(bias)
        if accum_out is not None:
            kw["accum_out"] = accum_out
            wr.append(accum_out)
        if not isinstance(scale, (int, float)):
            rd.append(scale)
        return self.S.op("act", lambda e: e.activation(out=out, in_=in_, func=func, scale=scale, **kw),
                         reads=rd, writes=wr)

    def tt(self, out, in0, in1, op, eng="dve"):
        return self.S.op(eng, lambda e: e.tensor_tensor(out=out, in0=in0, in1=in1, op=op),
                         reads=[in0, in1], writes=[out])

    def ts(self, out, in0, s1, s2, op0, op1=None, eng="dve", accum_out=None):
        kw = {}
        wr = [out]
        if op1 is not None:
            kw["op1"] = op1
        if accum_out is not None:
            kw["accum_out"] = accum_out
            wr.append(accum_out)
        return self.S.op(eng, lambda e: e.tensor_scalar(out=out, in0=in0, scalar1=s1, scalar2=s2, op0=op0, **kw),
                         reads=[in0, s1, s2], writes=wr)

    def stt(self, out, in0, scalar, in1, op0, op1):
        return self.S.op("dve", lambda e: e.scalar_tensor_tensor(out=out, in0=in0, scalar=scalar, in1=in1, op0=op0, op1=op1),
                         reads=[in0, scalar, in1], writes=[out])

    def copy(self, out, in_, eng="dve"):
        if eng == "act":
            return self.S.op("act", lambda e: e.copy(out=out, in_=in_), reads=[in_], writes=[out])
        return self.S.op(eng, lambda e: e.tensor_copy(out=out, in_=in_), reads=[in_], writes=[out])

    def memset(self, ap, val, eng="dve"):
        return self.S.op(eng, lambda e: e.memset(ap, val), reads=[], writes=[ap])

    def red(self, out, in_, op, axis=AX.X):
        return self.S.op("dve", lambda e: e.tensor_reduce(out=out, in_=in_, axis=axis, op=op), reads=[in_], writes=[out])

    def recip(self, out, in_):
        return self.S.op("dve", lambda e: e.reciprocal(out=out, in_=in_), reads=[in_], writes=[out])

    def dma(self, out, in_, q="sp", **kw):
        return self.S.dma(q, out, in_, **kw)


def build_program(dbg_stop=None):
    nc = bass.Bass("TRN2", target_bir_lowering=False)
    st = contextlib.ExitStack()
    with st:
        K = KB(nc, st)
        S = K.S
        IN = {}

        def inp(name, shape, dt=F32):
            IN[name] = nc.dram_tensor(name, list(shape), dt, kind="ExternalInput").ap()
            S.notrack.add(name)
            return IN[name]

        def outp(name, shape):
            return nc.dram_tensor(name, list(shape), F32, kind="ExternalOutput").ap()

        xp = inp("xp", [NPT, D])
        xs = inp("xs", [LS, D])
        cv = inp("cv", [2, D])
        ident_d = inp("ident_in", [128, 128])
        ada_w = inp("ada_w", [DEPTH, D, 6 * D])
        ada_b = inp("ada_b", [DEPTH * 48, 128])
        gvec = {}
        for nm in ("mix_pre_g", "mix_post_g", "ffn_pre_g", "ffn_post_g"):
            gvec[nm] = inp(nm, [DEPTH * 8, 128])
        ffn_w_up = inp("ffn_w_up", [DEPTH, D, 2 * DFF])
        ffn_conv_w = inp("ffn_conv_w", [DEPTH * 3 * 44, 128])
        ffn_conv_b = inp("ffn_conv_b", [DEPTH * 44, 128])
        ffn_w_down = inp("ffn_w_down", [DEPTH, DFF, D])

        even_w_in = inp("even_w_in", [2, D, 2464])
        even_w_out = inp("even_w_out", [2, D, D])
        ret_logit_b = inp("ret_logit_b", [2, 128, 8])
        ret_gn = inp("ret_gn", [8, 128])
        mla_q_norm = inp("mla_q_norm", [4, 128])
        mla_w_uq = inp("mla_w_uq", [2, 256, 768])
        mla_kv_norm = inp("mla_kv_norm", [2, 128])
        kvn_bc_d = inp("kvn_bc", [128, 256])
        mla_w_ukv = inp("mla_w_ukv", [2, 128, 1024])
        state_ret_in = inp("state_ret_in", [2, 2, 4, 128, 128])
        ckv_in = inp("ckv_in", [2, 512, 128])
        kr_in = inp("kr_in", [2, 512, 32])
        rconst_d = inp("rconst_in", [128, 772])
        ropeC_d = inp("ropeC_in", [32, 2048])
        ropeS_d = inp("ropeS_in", [32, 2048])
        odd_w_in = inp("odd_w_in", [2, D, 2048])
        odd_w_out = inp("odd_w_out", [2, D, D])
        s5p = inp("s5p", [2, 2, 128, 1104])
        s5_d = inp("s5_d", [8, 128])
        s5_glu_b = inp("s5_glu_b", [8, 128])
        s5_glu_w = inp("s5_glu_w", [2, 512, 512])
        na_rpb = inp("na_rpb", [2, 8, 15, 31])
        nak_in = inp("nak_in", [2, 512, 512])
        nav_in = inp("nav_in", [2, 512, 512])
        oh_d = inp("oh_in", [32, 64, 128])
        iota_d = inp("iota_in", [128, 2048])
        o_s5re = outp("o_s5re", [2, 2, 2, 128, 16])
        o_s5im = outp("o_s5im", [2, 2, 2, 128, 16])
        o_nak = outp("o_nak", [2, 2, 256, 512])
        o_nav = outp("o_nav", [2, 2, 256, 512])
        o_ret = outp("o_ret", [2, 2, 2, 4, 128, 128])
        o_ckv = outp("o_ckv", [2, 2, 256, 128])
        o_kr = outp("o_kr", [2, 2, 256, 32])
        finals = []
        yp = outp("yp", [NPT, D])
        ys = outp("ys", [LS, D])

        xT_d = K.dram("xT_d", [D, T])
        xT_v = xT_d.rearrange("(c p) t -> p c t", p=128)

        K.psb = [st.enter_context(nc.psum_tensor("psb%d" % i, [128, 512], F32)) for i in range(8)]

        K.arena_init(50200)
        ident = K.sb("ident", [128, 128], F32)
        K.dma(ident[:], ident_d)
        identb = K.sb("identb", [128, 128], BF16)
        K.copy(identb[:], ident[:])
        epsT = K.sb("epsT", [128, 1], F32)
        K.memset(epsT[:], EPS)
        halo_stash = K.sb("halo_stash", [128, 8, 2], BF16)
        ones1024 = K.sb("ones1024", [128, 128], BF16)
        K.memset(ones1024[:], 1.0 / 1024.0)
        ones256 = K.sb("ones256", [128, 128], BF16)
        K.memset(ones256[:], 1.0 / 256.0)
        ones128b = K.sb("ones128b", [128, 128], BF16)
        K.memset(ones128b[:], 1.0 / 128.0)
        onesb = K.sb("onesb", [128, 128], BF16)
        K.memset(onesb[:], 1.0)
        oneT = K.sb("oneT", [128, 1], F32)
        K.memset(oneT[:], 1.0)
        halfpi = K.sb("halfpi", [128, 1], F32)
        K.memset(halfpi[:], math.pi / 2.0)


        def load_T(dram2d, R, name):
            dst = K.sb(name, [128, R], F32)
            r0 = 0
            while r0 < R:
                r = min(128, R - r0)
                stg = K.rotbuf("ldT_stg", [128, 128], F32)
                K.dma(stg[0:r, :], dram2d[r0:r0 + r, :])
                pb = K.psum()
                K.tr(pb[:, 0:r], stg[0:r, :], ident[0:r, 0:r])
                K.copy(dst[:, r0:r0 + r], pb[:, 0:r])
                r0 += r
            return dst

        gT = {nm: load_T(gvec[nm], DEPTH * 8, "gT_" + nm) for nm in gvec}
        adabT = load_T(ada_b, DEPTH * 48, "adabT")
        convwT = load_T(ffn_conv_w, DEPTH * 3 * 44, "convwT")
        convbT = load_T(ffn_conv_b, DEPTH * 44, "convbT")
        nconvwT = K.sb("nconvwT", [128, DEPTH * 3 * 44], F32)
        K.ts(nconvwT[:], convwT[:], -1.0, None, ALU.mult)
        s5dT = load_T(s5_d, 8, "s5dT")
        glubT = load_T(s5_glu_b, 8, "glubT")
        gnT = load_T(ret_gn, 8, "gnT")
        qnT = load_T(mla_q_norm, 4, "qnT")
        kvnT = load_T(mla_kv_norm, 2, "kvnT")
        cT = load_T(cv.rearrange("v (c p) -> (v c) p", p=128), 16, "cT")
        scT = K.sb("scT", [128, 8, 2], BF16)
        K.act(scT[:].rearrange("p c v -> p v c"), cT[:].rearrange("p (v c) -> p v c", v=2), AF.Silu)

        def x_to_xT(src, ntile, col0):
            for i in range(ntile):
                xt = K.arotbuf("xin", [D], F32)
                K.dma(xt[:], src[i * 128:(i + 1) * 128, :])
                xo = K.arotbuf("xinT", [8, 128], F32)
                for g in range(2):
                    pb = K.psum()
                    for c in range(4):
                        K.tr(pb[:, c * 128:(c + 1) * 128], xt[:, (g * 4 + c) * 128:(g * 4 + c + 1) * 128], ident[:])
                    K.copy(xo[:, g * 4:(g + 1) * 4, :], pb[:].rearrange("p (c t) -> p c t", c=4), eng=("act" if g else "dve"))
                K.dma(xT_v[:, :, col0 + i * 128: col0 + (i + 1) * 128], xo[:])

        K.phase()
        x_to_xT(xp, NPT // 128, 0)
        x_to_xT(xs, LS // 128, NPT)

        def ada(layer):
            K.phase()
            S.tag = "L%d:ada" % layer
            modp = K.psum()
            mview = modp[:, 0:96].rearrange("p (c v) -> p c v", v=2)
            for blk in range(12):
                wb = K.arotbuf("adaw", [8, 512], BF16)
                K.dma(wb[:], ada_w[layer].rearrange("(k p) o -> p k o", p=128)[:, :, blk * 512:(blk + 1) * 512], q="pool")
                for cc in range(4):
                    ch = blk * 4 + cc
                    for k in range(8):
                        K.mm(mview[:, ch, :], wb[:, k, cc * 128:(cc + 1) * 128], scT[:, k, :], start=(k == 0), stop=(k == 7))
            mod = K.rotbuf("mod", [128, 48, 2], F32)
            for v in range(2):
                K.tt(mod[:, :, v], mview[:, :, v], adabT[:, layer * 48:(layer + 1) * 48], ALU.add)
            return mod

        def mod_vectors(layer, mod, sub):
            base = sub * 24
            pre_g = gT["mix_pre_g" if sub == 0 else "ffn_pre_g"]
            post_g = gT["mix_post_g" if sub == 0 else "ffn_post_g"]
            gs = K.rotbuf("gs", [128, 8, 2], F32)
            gg = K.rotbuf("gg", [128, 8, 2], F32)
            for v in range(2):
                K.stt(gs[:, :, v], mod[:, base + 8:base + 16, v], 1.0, pre_g[:, layer * 8:(layer + 1) * 8], ALU.add, ALU.mult)
                K.tt(gg[:, :, v], mod[:, base + 16:base + 24, v], post_g[:, layer * 8:(layer + 1) * 8], ALU.mult)
            return gs, mod[:, base:base + 8, :], gg

        def rstd_from(src3, n, ones):
            C = src3.shape[1]
            sq = K.arotbuf("sq", [8, 512], BF16, n=1)
            K.act(sq[:, 0:C, 0:n], src3, AF.Square)
            pb = K.psum()
            for c in range(C):
                K.mm(pb[:, 0:n], ones[:], sq[:, c, 0:n], start=(c == 0), stop=(c == C - 1))
            rstd = K.arotbuf("rstd", [512], F32)
            K.act(rstd[:, 0:n], pb[:, 0:n], AF.Ln, bias=epsT[:, 0:1])
            K.act(rstd[:, 0:n], rstd[:, 0:n], AF.Exp, scale=-0.5)
            return rstd

        def pre_block(t0, n, mc, gs, shift, dst):
            xb = K.arotbuf("xb", [8, 512], F32)
            if n == 1:
                K.dma(xb[:, :, 0:n], xT_v[:, :, t0:t0 + n], allow_slow_non_contiguous=True)
            else:
                K.dma(xb[:, :, 0:n], xT_v[:, :, t0:t0 + n])
            rstd = rstd_from(xb[:, :, 0:n], n, ones1024)
            for c in range(8):
                tmp = K.arotbuf("tmpn", [512], F32)
                K.stt(tmp[:, 0:n], xb[:, c, 0:n], gs[:, c, mc:mc + 1], rstd[:, 0:n], ALU.mult, ALU.mult)
                K.act(dst[:, c, :], tmp[:, 0:n], AF.Identity, bias=shift[:, c, mc:mc + 1])

        def post_block(t0, n, mc, gg, yf):
            rstd = rstd_from(yf, n, ones1024)
            xb = K.arotbuf("xb", [8, 512], F32)
            K.dma(xb[:, :, 0:n], xT_v[:, :, t0:t0 + n])
            for c in range(8):
                tmp = K.arotbuf("tmpn", [512], F32)
                K.stt(tmp[:, 0:n], yf[:, c, :], gg[:, c, mc:mc + 1], rstd[:, 0:n], ALU.mult, ALU.mult)
                K.tt(xb[:, c, 0:n], xb[:, c, 0:n], tmp[:, 0:n], ALU.add, eng="pool")
            K.dma(xT_v[:, :, t0:t0 + n], xb[:, :, 0:n])

        FFN_GROUPS = [
            (0, 1024, 0, 1, [(0, 256), (256, 512), (512, 1025)]),
            (1024, 2048, 1, 1, [(1023, 2049)]),
            (2048, 2560, 1, 0, [(2047, 2560)]),
        ]

        def ffn(layer, gs, shift, gg):
            wup_v = ffn_w_up[layer].rearrange("(k p) o -> p k o", p=128)
            wdn_v = ffn_w_down[layer].rearrange("(j p) o -> p j o", p=128)
            for (c0, c1, hl, hr, segs) in FFN_GROUPS:
                K.phase()
                S.tag = "L%d:ffn_up" % layer
                b0 = c0 - hl
                W = c1 + hr - b0
                hTg = K.af([8, 1026], BF16)
                cols = []
                if hl:
                    cols.append((b0, 1))
                for t0 in range(c0, c1, 512):
                    cols.append((t0, 512))
                if hr:
                    cols.append((c1, 1))
                for (t0, n) in cols:
                    mc = 0 if t0 < NPT else 1
                    if hl and t0 == b0:
                        K.copy(hTg[:, :, 0:1], halo_stash[:, :, 0:1], eng="pool")
                        continue
                    pre_block(t0, n, mc, gs, shift, hTg[:, :, t0 - b0:t0 - b0 + n])
                if c1 < T:
                    K.copy(halo_stash[:, :, 0:1], hTg[:, :, c1 - 1 - b0:c1 - b0], eng="pool")
                npart = (W + 511) // 512
                psz = (W + npart - 1) // npart
                parts = [(i * psz, min(psz, W - i * psz)) for i in range(npart)]
                mT = K.af([NJ, 1024], BF16)

                def load_up(j):
                    wa = K.arotbuf("wupa", [8, 128], BF16, n=2)
                    wg = K.arotbuf("wupg", [8, 128], BF16, n=2)
                    K.dma(wa[:], wup_v[:, :, j * 128:(j + 1) * 128], q="pool")
                    K.dma(wg[:], wup_v[:, :, DFF + j * 128:DFF + (j + 1) * 128], q="pool")
                    return wa, wg

                bounds = [s0_ for (s0_, s1_) in segs[1:]]
                lo = 1 + hl
                n_own = c1 - c0

                def ffn_X(j, wts):
                    wa, wg = wts
                    us, os_ = [], []
                    for wi, wmat in enumerate((wa, wg)):
                        u = K.arotbuf("u%d" % wi, [1028], F32, n=2)
                        for (p0, pn) in parts:
                            pb = K.psum()
                            for k in range(8):
                                K.mm(pb[:, 0:pn], wmat[:, k, :], hTg[:, k, p0:p0 + pn], start=(k == 0), stop=(k == 7))
                            K.copy(u[:, 1 + p0:1 + p0 + pn], pb[:, 0:pn], eng="act")
                        blk = wi * NJ + j
                        w1 = convwT[:, (layer * 3 + 1) * 44 + blk:(layer * 3 + 1) * 44 + blk + 1]
                        bb = convbT[:, layer * 44 + blk:layer * 44 + blk + 1]
                        o = K.arotbuf("o%d" % wi, [1028], F32, n=2)
                        K.act(o[:, 1:1 + W], u[:, 1:1 + W], AF.Identity, bias=bb, scale=w1)
                        us.append(u)
                        os_.append(o)
                    return us, os_

                def ffn_Y(j, us, os_):
                    for wi in range(2):
                        u, o = us[wi], os_[wi]
                        blk = wi * NJ + j
                        c0w = (layer * 3 + 0) * 44 + blk
                        c2w = (layer * 3 + 2) * 44 + blk
                        w0, w2 = convwT[:, c0w:c0w + 1], convwT[:, c2w:c2w + 1]
                        nw0, nw2 = nconvwT[:, c0w:c0w + 1], nconvwT[:, c2w:c2w + 1]
                        K.stt(o[:, 2:1 + W], u[:, 1:W], w0, o[:, 2:1 + W], ALU.mult, ALU.add)
                        K.stt(o[:, 1:W], u[:, 2:1 + W], w2, o[:, 1:W], ALU.mult, ALU.add)
                        for B in bounds:
                            iB = B - b0 + 1
                            K.stt(o[:, iB:iB + 1], u[:, iB - 1:iB], nw0, o[:, iB:iB + 1], ALU.mult, ALU.add)
                            K.stt(o[:, iB - 1:iB], u[:, iB:iB + 1], nw2, o[:, iB - 1:iB], ALU.mult, ALU.add)
                    oa, og = os_
                    sg = K.arotbuf("sg", [1024], F32, n=1)
                    K.act(sg[:, 0:n_own], og[:, lo:lo + n_own], AF.Silu)
                    K.tt(mT[:, j, 0:n_own], oa[:, lo:lo + n_own], sg[:, 0:n_own], ALU.mult)

                wts = load_up(0)
                wts_next = load_up(1)
                stX = ffn_X(0, wts)
                for j in range(NJ):
                    if j + 1 < NJ:
                        wts = wts_next
                        nxX = ffn_X(j + 1, wts)
                        if j + 2 < NJ:
                            wts_next = load_up(j + 2)
                    ffn_Y(j, *stX)
                    if j + 1 < NJ:
                        stX = nxX
                S.tag = "L%d:ffn_down" % layer
                n_own = c1 - c0
                yfa = K.af([8, 1024], F32)

                def load_dn(c):
                    wd = K.arotbuf("wdn", [NJ, 128], BF16, n=2)
                    K.dma(wd[:], wdn_v[:, :, c * 128:(c + 1) * 128], q="pool")
                    return wd

                nxt = load_dn(0)
                for c in range(8):
                    wd = nxt
                    if c + 1 < 8:
                        nxt = load_dn(c + 1)
                    for t0 in range(0, n_own, 512):
                        pb = K.psum()
                        for j in range(NJ):
                            K.mm(pb[:, :], wd[:, j, :], mT[:, j, t0:t0 + 512], start=(j == 0), stop=(j == NJ - 1))
                        K.copy(yfa[:, c, t0:t0 + 512], pb[:, :], eng="act")
                for t0 in range(0, n_own, 512):
                    mc = 0 if (c0 + t0) < NPT else 1
                    post_block(c0 + t0, 512, mc, gg, yfa[:, :, t0:t0 + 512])

        RS = 128.0 ** -0.5
        MLA_SCALE = 96.0 ** -0.5
        SEQS = [(0, 256, 0), (256, 256, 0), (512, 2048, 1)]

        PW = {"m": 2560, "n": 1152}
        NTT = {"m": 20, "n": 9}

        ATT = {"prev": None, "tb": 0}

        def attn_A(qT_ap, kbanks):
            for b, (kT_ap, bias_fn) in enumerate(kbanks):
                w = kT_ap.shape[1]
                K.mm(K.psb[b][:, 0:w], qT_ap, kT_ap)

        def attn_C(kbanks, tag):
            nb = len(kbanks)
            mx = K.arotbuf("a_mx", [8], F32)
            srcs = []
            col = 0
            for b, (kT_ap, bias_fn) in enumerate(kbanks):
                w = kT_ap.shape[1]
                src = K.psb[b][:, 0:w]
                if bias_fn is not None:
                    src = bias_fn(K.psb[b], w)
                srcs.append((src, col, w))
                K.red(mx[:, b:b + 1], src, ALU.max)
                col += w
            nm = K.arotbuf("a_nm", [1], F32)
            K.red(nm[:, 0:1], mx[:, 0:nb], ALU.max)
            K.ts(nm[:, 0:1], nm[:, 0:1], -1.0, None, ALU.mult)
            P = K.arotbuf("a_P" + tag, [PW[tag]], BF16)
            for (src, c0_, w) in srcs:
                K.act(P[:, c0_:c0_ + w], src, AF.Exp, bias=nm[:, 0:1])
            return P

        def attn_B(st_):
            P, ktiles, hs, dst, tag = st_
            PT = K.arotbuf("a_PT" + tag, [NTT[tag], 128], BF16)
            nkt = len(ktiles)
            for g in range(0, nkt, 8):
                cnt = min(8, nkt - g)
                pbT = K.psb[5 + ATT["tb"] % 2].bitcast(BF16)
                ATT["tb"] += 1
                for i_ in range(cnt):
                    pc0, n_, _ = ktiles[g + i_]
                    K.tr(pbT[0:n_, i_ * 128:(i_ + 1) * 128], P[:, pc0:pc0 + n_], identb[:])
                full = all(ktiles[g + i_][1] == 128 for i_ in range(cnt))
                ce = "dve" if ATT["tb"] % 2 == 0 else "act"
                if full:
                    K.copy(PT[:, g:g + cnt, :], pbT[:, 0:cnt * 128].rearrange("p (a b) -> p a b", a=cnt), eng=ce)
                else:
                    for i_ in range(cnt):
                        n_ = ktiles[g + i_][1]
                        K.copy(PT[0:n_, g + i_, :], pbT[0:n_, i_ * 128:(i_ + 1) * 128], eng=ce)
            po = K.psb[7][:, 0:128]
            pd = K.psb[7][:, 128:256]
            for i_, (pc0, n_, vl) in enumerate(ktiles):
                K.mm(po, vl, PT[0:n_, i_, :], start=(i_ == 0), stop=(i_ == nkt - 1))
            for i_, (pc0, n_, vl) in enumerate(ktiles):
                K.mm(pd, onesb[0:n_, :], PT[0:n_, i_, :], start=(i_ == 0), stop=(i_ == nkt - 1))

        def attn_D(st_):
            P, ktiles, hs, dst, tag = st_
            po = K.psb[7][:, 0:128]
            pd = K.psb[7][:, 128:256]
            rd = K.arotbuf("a_rd", [128], F32)
            K.recip(rd[hs, :], pd[hs, :])
            K.tt(dst, po[hs, :], rd[hs, :], ALU.mult)

        def attn_unit(qT_ap, kbanks, ktiles, hs, dst, tag):
            prev = ATT["prev"]
            attn_A(qT_ap, kbanks)
            if prev is not None:
                attn_B(prev)
            P = attn_C(kbanks, tag)
            if prev is not None:
                attn_D(prev)
            ATT["prev"] = (P, ktiles, hs, dst, tag)

        def attn_flush():
            if ATT["prev"] is not None:
                attn_B(ATT["prev"])
                attn_D(ATT["prev"])
                ATT["prev"] = None

        def out_proj_post(w_out_l, yT, gg):
            wo = K.af([8, 1024], BF16)
            K.dma(wo[:], w_out_l.rearrange("(k p) o -> p k o", p=128), q="pool")
            for t0 in range(0, T, 512):
                yf = K.arotbuf("yf", [8, 512], F32, n=1)
                for c in range(8):
                    pb = K.psum()
                    for k in range(8):
                        K.mm(pb[:], wo[:, k, c * 128:(c + 1) * 128], yT[:, k, t0:t0 + 512], start=(k == 0), stop=(k == 7))
                    K.copy(yf[:, c, :], pb[:], eng="act")
                post_block(t0, 512, 0 if t0 < NPT else 1, gg, yf[:])

        def even_layer(layer, j, mod):
            gs, shift, gg = mod_vectors(layer, mod, 0)
            K.phase()
            hT = K.af([8, T], BF16)
            yT = K.af([8, T], BF16)
            m0 = K.mark()
            S.tag = "L%d:e_pre" % layer
            for t0 in range(0, T, 512):
                pre_block(t0, 512, 0 if t0 < NPT else 1, gs, shift, hT[:, :, t0:t0 + 512])
            K.reset_to(m0)
            S.tag = "L%d:e_ret" % layer
            w_in_v = even_w_in[j].rearrange("(k p) o -> p k o", p=128)
            rconst = K.af([772], F32)
            K.dma(rconst[:], rconst_d)
            RC = lambda i: rconst[:, i * 128:(i + 1) * 128]
            lg = K.af([8], F32)
            K.dma(lg[:], ret_logit_b[j])
            K.act(lg[:], lg[:], AF.Exp, scale=-1.0)
            K.act(lg[:], lg[:], AF.Ln, bias=oneT[:, 0:1])
            K.ts(lg[:], lg[:], -1.0, None, ALU.mult)
            qTh = K.af([T], BF16)
            kTh = K.af([T], BF16)
            sgTh = K.af([T], BF16)
            kvtok = K.af([20, 256], BF16)
            kwfT = K.af([20, 128], BF16)
            kwbT = K.af([20, 128], BF16)
            Sfs = K.af([20, 128], BF16)
            Sbs = K.af([20, 128], BF16)
            o_sb = K.af([T], F32)
            Dm = K.af([128], F32)
            e2 = K.af([128], F32)
            Wqf = K.af([128], F32)
            Wqb = K.af([128], F32)
            kw = K.af([4], F32)
            for h in range(4):
                lf = lg[:, h:h + 1]
                lb = lg[:, 4 + h:5 + h]
                K.act(Dm[:], RC(0), AF.Exp, scale=lf)
                K.tt(Dm[:], Dm[:], RC(1), ALU.mult)
                K.act(e2[:], RC(2), AF.Exp, scale=lb)
                K.tt(e2[:], e2[:], RC(3), ALU.mult)
                K.tt(Dm[:], Dm[:], e2[:], ALU.add)
                K.act(Wqf[:], RC(4), AF.Exp, scale=lf)
                K.act(Wqb[:], RC(5), AF.Exp, scale=lb)
                K.act(kw[:, 0:1], rconst[:, 768:769], AF.Exp, scale=lf)
                K.act(kw[:, 1:2], rconst[:, 769:770], AF.Exp, scale=lb)
                K.act(kw[:, 2:3], rconst[:, 770:771], AF.Exp, scale=lf)
                K.act(kw[:, 3:4], rconst[:, 770:771], AF.Exp, scale=lb)
                wh = K.arotbuf("wh", [8, 4, 128], BF16, n=2)
                for si_ in range(4):
                    K.dma(wh[:, :, si_, :], w_in_v[:, :, si_ * 512 + h * 128:si_ * 512 + (h + 1) * 128], q="pool")
                for t0 in range(0, T, 512):
                    for si, dst in ((0, qTh), (1, kTh), (3, sgTh)):
                        pb = K.psum()
                        for k in range(8):
                            K.mm(pb[:], wh[:, k, si, :], hT[:, k, t0:t0 + 512], start=(k == 0), stop=(k == 7))
                        if si == 0:
                            K.act(dst[:, t0:t0 + 512], pb[:], AF.Identity, scale=RS)
                        elif si == 1:
                            K.copy(dst[:, t0:t0 + 512], pb[:], eng="dve")
                        else:
                            K.act(dst[:, t0:t0 + 512], pb[:], AF.Silu)
                for ti in range(20):
                    pb = K.psum()
                    for k in range(8):
                        K.mm(pb[:, 0:256].rearrange("p (a b) -> p a b", a=2), hT[:, k, ti * 128:(ti + 1) * 128], wh[:, k, 1:3, :], start=(k == 0), stop=(k == 7))
                    K.copy(kvtok[:, ti, :], pb[:, 0:256], eng="act")
                    K.ts(kwfT[:, ti, :], pb[:, 0:128], kw[:, 0:1], None, ALU.mult)
                    K.ts(kwbT[:, ti, :], pb[:, 0:128], kw[:, 1:2], None, ALU.mult)
                for (c0, L, smp) in SEQS:
                    nch = L // 128
                    tb = c0 // 128
                    for d in range(2):
                        Sx = K.arotbuf("Sx", [128], F32, n=2)
                        if smp:
                            K.dma(Sx[:], state_ret_in[j, d, h])
                        else:
                            K.memset(Sx[:], 0.0, eng="pool")
                        store = Sfs if d == 0 else Sbs
                        kwT = kwfT if d == 0 else kwbT
                        order = range(nch) if d == 0 else range(nch - 1, -1, -1)
                        for n in order:
                            ti = tb + n
                            K.copy(store[:, ti, :], Sx[:], eng="pool")
                            pb = K.psum()
                            K.mm(pb[:, 0:128], kwT[:, ti, :], kvtok[:, ti, 128:256])
                            K.stt(Sx[:], Sx[:], kw[:, 2 + d:3 + d], pb[:, 0:128], ALU.mult, ALU.add)
                        if not smp:
                            finals.append(K.dma(o_ret[c0 // 256, j, d, h], Sx[:]))
                    for n0 in range(0, nch, 4):
                        pbo = K.psum()
                        cnt = min(4, nch - n0)
                        for n in range(n0, n0 + cnt):
                            ti = tb + n
                            cs = slice(ti * 128, (ti + 1) * 128)
                            pbs = K.psum()
                            K.mm(pbs[:, 0:128], kTh[:, cs], qTh[:, cs])
                            MT = K.arotbuf("MT", [128], BF16, n=2)
                            K.tt(MT[:], pbs[:, 0:128], Dm[:], ALU.mult)
                            qf = K.arotbuf("qf", [128], BF16, n=2)
                            qb = K.arotbuf("qb", [128], BF16, n=2)
                            K.tt(qf[:], qTh[:, cs], Wqf[:], ALU.mult, eng="pool")
                            K.tt(qb[:], qTh[:, cs], Wqb[:], ALU.mult, eng="pool")
                            oc = pbo[:, (n - n0) * 128:(n - n0 + 1) * 128]
                            K.mm(oc, kvtok[:, ti, 128:256], MT[:], start=True, stop=False)
                            K.mm(oc, Sfs[:, ti, :], qf[:], start=False, stop=False)
                            K.mm(oc, Sbs[:, ti, :], qb[:], start=False, stop=True)
                        K.copy(o_sb[:, (tb + n0) * 128:(tb + n0 + cnt) * 128], pbo[:, 0:cnt * 128], eng="act")
                for t0 in range(0, T, 512):
                    ob = o_sb[:, t0:t0 + 512]
                    o2 = K.arotbuf("gn_o2", [512], F32, n=1)
                    K.tt(o2[:], ob, ob, ALU.mult)
                    pm = K.psum()
                    pq = K.psum()
                    for src_, pdst in ((ob, pm), (o2[:], pq)):
                        hi = K.arotbuf("gn_hi", [512], BF16, n=2)
                        lo = K.arotbuf("gn_lo", [512], BF16, n=2)
                        K.copy(hi[:], src_, eng="act")
                        K.tt(lo[:], src_, hi[:], ALU.subtract)
                        K.mm(pdst[:], ones128b[:], hi[:], start=True, stop=False)
                        K.mm(pdst[:], ones128b[:], lo[:], start=False, stop=True)
                    mean = K.arotbuf("gn_mean", [512], F32, n=1)
                    K.copy(mean[:], pm[:], eng="act")
                    msq = K.arotbuf("gn_msq", [512], F32, n=1)
                    K.tt(msq[:], mean[:], mean[:], ALU.mult, eng="pool")
                    K.tt(msq[:], pq[:], msq[:], ALU.subtract)
                    K.act(msq[:], msq[:], AF.Sqrt, bias=epsT[:, 0:1])
                    K.recip(msq[:], msq[:])
                    K.tt(o2[:], ob, mean[:], ALU.subtract, eng="pool")
                    K.tt(o2[:], o2[:], msq[:], ALU.mult)
                    K.stt(yT[:, h, t0:t0 + 512], o2[:], gnT[:, j * 4 + h:j * 4 + h + 1], sgTh[:, t0:t0 + 512], ALU.mult, ALU.mult)
            K.reset_to(m0)
            S.tag = "L%d:e_mlaproj" % layer
            cqn = K.af([2, T], BF16)
            ckvK = K.af([3072], BF16)
            krK = K.af([3072], BF16)
            ropeC = K.af([2048], F32)
            ropeS = K.af([2048], F32)
            K.dma(ropeC[64:96, :], ropeC_d)
            K.dma(ropeS[64:96, :], ropeS_d)
            m1 = K.mark()
            kvn_bc = K.af([256], F32)
            K.dma(kvn_bc[:], kvn_bc_d)
            wm = K.af([8, 416], BF16)
            K.dma(wm[:], w_in_v[:, :, 2048:2464], q="pool")
            wkr = K.af([8, 96], BF16)
            wkrs = K.af([8, 96], BF16)
            K.memset(wkr[:], 0.0, eng="pool")
            K.memset(wkrs[:], 0.0, eng="pool")
            K.dma(wkr[:, :, 64:96], w_in_v[:, :, 2432:2464], q="pool")
            K.dma(wkrs[:, :, 64:80], w_in_v[:, :, 2448:2464], q="pool")
            K.dma(wkrs[:, :, 80:96], w_in_v[:, :, 2432:2448], q="pool")
            for t0 in range(0, T, 512):
                bl = slice(t0, t0 + 512)
                pbs = [K.psum(), K.psum()]
                for cc in range(2):
                    for k in range(8):
                        K.mm(pbs[cc][:], wm[:, k, cc * 128:(cc + 1) * 128], hT[:, k, bl], start=(k == 0), stop=(k == 7))
                sq = K.arotbuf("m_sq", [2, 512], BF16, n=1)
                for cc in range(2):
                    K.act(sq[:, cc, :], pbs[cc][:], AF.Square)
                pr = K.psum()
                for cc in range(2):
                    K.mm(pr[:], ones256[:], sq[:, cc, :], start=(cc == 0), stop=(cc == 1))
                rstd = K.arotbuf("m_rstd", [512], F32, n=1)
                K.act(rstd[:], pr[:], AF.Sqrt, bias=epsT[:, 0:1])
                K.recip(rstd[:], rstd[:])
                for cc in range(2):
                    K.stt(cqn[:, cc, bl], pbs[cc][:], qnT[:, j * 2 + cc:j * 2 + cc + 1], rstd[:], ALU.mult, ALU.mult)
                pc = K.psum()
                for k in range(8):
                    K.mm(pc[:], wm[:, k, 256:384], hT[:, k, bl], start=(k == 0), stop=(k == 7))
                K.act(sq[:, 0, :], pc[:], AF.Square)
                pr2 = K.psum()
                K.mm(pr2[:], ones128b[:], sq[:, 0, :])
                rstd2 = K.arotbuf("m_rstd2", [512], F32, n=1)
                K.act(rstd2[:], pr2[:], AF.Sqrt, bias=epsT[:, 0:1])
                K.recip(rstd2[:], rstd2[:])
                K.stt(ckvK[:, bl], pc[:], kvnT[:, j:j + 1], rstd2[:], ALU.mult, ALU.mult)
                pk = K.psum()
                for k in range(8):
                    K.mm(pk[0:96, :], wkr[:, k, :], hT[:, k, bl], start=(k == 0), stop=(k == 7))
                if t0 < NPT:
                    K.copy(krK[64:96, bl], pk[64:96, :], eng="dve")
                else:
                    pks = K.psum()
                    for k in range(8):
                        K.mm(pks[0:96, :], wkrs[:, k, :], hT[:, k, bl], start=(k == 0), stop=(k == 7))
                    st0 = t0 - NPT
                    t1 = K.arotbuf("m_t1", [512], F32, n=1)
                    t2 = K.arotbuf("m_t2", [512], F32, n=1)
                    K.tt(t1[64:96, :], pk[64:96, :], ropeC[64:96, st0:st0 + 512], ALU.mult)
                    K.tt(t2[64:96, :], pks[64:96, :], ropeS[64:96, st0:st0 + 512], ALU.mult)
                    K.tt(krK[64:96, bl], t1[64:96, :], t2[64:96, :], ALU.add, eng="pool")
            for i in range(4):
                stg = K.arotbuf("m_stg", [128], F32, n=2)
                K.dma(stg[:], ckv_in[j, i * 128:(i + 1) * 128, :])
                pb = K.psum()
                K.tr(pb[:, 0:128], stg[:], ident[:])
                K.copy(ckvK[:, 2560 + i * 128:2560 + (i + 1) * 128], pb[:, 0:128], eng="dve")
                stg2 = K.arotbuf("m_stg2", [96], F32, n=2)
                K.memset(stg2[:, 0:64], 0.0, eng="pool")
                K.dma(stg2[:, 64:96], kr_in[j, i * 128:(i + 1) * 128, :])
                pb2 = K.psum()
                K.tr(pb2[0:96, 0:128], stg2[:], ident[:])
                K.copy(krK[64:96, 2560 + i * 128:2560 + (i + 1) * 128], pb2[64:96, 0:128], eng="dve")
            for ti in range(4):
                pt = K.psum()
                for k in range(8):
                    K.mm(pt[:, 0:160], hT[:, k, ti * 128:(ti + 1) * 128], wm[:, k, 256:416], start=(k == 0), stop=(k == 7))
                junk = K.arotbuf("m_junk", [128], F32, n=1)
                ssq = K.arotbuf("m_ssq", [1], F32, n=2)
                K.memset(ssq[:], 0.0, eng="pool")
                K.act(junk[:], pt[:, 0:128], AF.Square, accum_out=ssq[:, 0:1])
                K.act(ssq[:], ssq[:], AF.Sqrt, bias=epsT[:, 0:1], scale=1.0 / 128.0)
                K.recip(ssq[:], ssq[:])
                ock = K.arotbuf("m_ock", [128], F32, n=2)
                K.stt(ock[:], pt[:, 0:128], ssq[:, 0:1], kvn_bc[:, j * 128:(j + 1) * 128], ALU.mult, ALU.mult)
                sq_i, r0 = ti // 2, (ti % 2) * 128
                finals.append(K.dma(o_ckv[sq_i, j, r0:r0 + 128, :], ock[:]))
                okr = K.arotbuf("m_okr", [32], F32, n=2)
                K.copy(okr[:], pt[:, 128:160], eng="act")
                finals.append(K.dma(o_kr[sq_i, j, r0:r0 + 128, :], okr[:]))
            K.reset_to(m1)
            S.tag = "L%d:e_mlaattn" % layer
            wuq = K.af([2, 768], BF16)
            K.dma(wuq[:], mla_w_uq[j].rearrange("(k p) o -> p k o", p=128), q="pool")
            wuqs = K.af([2, 8, 96], BF16)
            K.memset(wuqs[:], 0.0, eng="pool")
            uqv = mla_w_uq[j].rearrange("(k p) (h x) -> p k h x", p=128, h=8)
            for k_ in range(2):
                K.dma(wuqs[:, k_, :, 64:80], uqv[:, k_, :, 80:96], q="pool")
                K.dma(wuqs[:, k_, :, 80:96], uqv[:, k_, :, 64:80], q="pool")
            wukv = K.af([1024], BF16)
            K.dma(wukv[:], mla_w_ukv[j], q="pool")
            wukv_v = wukv[:].rearrange("p (h x) -> p h x", h=8)
            v_all = K.af([20, 512], BF16)
            for (c0, L, smp) in SEQS:
                attn_flush()
                nk = L + (512 if smp else 0)
                nkt = nk // 128
                for kt in range(nkt):
                    pb = K.psum()
                    K.mm(pb[:].rearrange("p (h x) -> p h x", h=8), ckvK[:, c0 + kt * 128:c0 + (kt + 1) * 128], wukv_v[:, :, 64:128])
                    K.copy(v_all[:, kt, :], pb[:], eng="act")
                for h in range(8):
                    kh = K.arotbuf("kh", [2560], BF16, n=1)
                    for kb in range(0, nk, 512):
                        w = min(512, nk - kb)
                        pb = K.psum()
                        K.mm(pb[0:64, 0:w], wukv[:, h * 128:h * 128 + 64], ckvK[:, c0 + kb:c0 + kb + w])
                        K.copy(kh[0:64, kb:kb + w], pb[0:64, 0:w], eng="dve")
                    K.copy(kh[64:96, 0:nk], krK[64:96, c0:c0 + nk], eng="pool")
                    qh = K.arotbuf("qh", [2048], BF16, n=1)
                    for qb0 in range(0, L, 512):
                        w = min(512, L - qb0)
                        pb = K.psum()
                        for k in range(2):
                            K.mm(pb[0:96, 0:w], wuq[:, k, h * 96:(h + 1) * 96], cqn[:, k, c0 + qb0:c0 + qb0 + w], start=(k == 0), stop=(k == 1))
                        if smp:
                            pb2 = K.psum()
                            for k in range(2):
                                K.mm(pb2[0:96, 0:w], wuqs[:, k, h, :], cqn[:, k, c0 + qb0:c0 + qb0 + w], start=(k == 0), stop=(k == 1))
                            t1 = K.arotbuf("q_t1", [512], F32, n=1)
                            t2 = K.arotbuf("q_t2", [512], F32, n=1)
                            K.tt(t1[64:96, 0:w], pb[64:96, 0:w], ropeC[64:96, qb0:qb0 + w], ALU.mult)
                            K.tt(t2[64:96, 0:w], pb2[64:96, 0:w], ropeS[64:96, qb0:qb0 + w], ALU.mult)
                            K.tt(t1[64:96, 0:w], t1[64:96, 0:w], t2[64:96, 0:w], ALU.add, eng="pool")
                            K.act(qh[0:64, qb0:qb0 + w], pb[0:64, 0:w], AF.Identity, scale=MLA_SCALE)
                            K.act(qh[64:96, qb0:qb0 + w], t1[64:96, 0:w], AF.Identity, scale=MLA_SCALE)
                        else:
                            K.act(qh[0:96, qb0:qb0 + w], pb[0:96, 0:w], AF.Identity, scale=MLA_SCALE)
                    half = h % 2
                    hs = slice(half * 64, half * 64 + 64)
                    kbanks = [(kh[0:96, kb:min(kb + 512, nk)], None) for kb in range(0, nk, 512)]
                    ktiles = [(kt * 128, 128, v_all[:, kt, (h // 2) * 128:(h // 2) * 128 + 128]) for kt in range(nkt)]
                    for qt in range(L // 128):
                        attn_unit(qh[0:96, qt * 128:(qt + 1) * 128], kbanks, ktiles,
                                  hs, yT[hs, 4 + h // 2, c0 + qt * 128:c0 + (qt + 1) * 128], "m")
            attn_flush()
            K.reset_to(m0)
            S.tag = "L%d:e_out" % layer
            out_proj_post(even_w_out[j], yT, gg)

        NA_SCALE = 0.125
        NEGV = -30000.0
        TWO_PI = 2.0 * math.pi
        CW1 = 6.28125
        CW2 = TWO_PI - 6.28125
        SIN_S = 1.0 - 2e-6
        GELU_C = 1.5957691216057308

        def sincos(ang, kiI, kf, sin_out, cos_out):
            K.ts(kiI, ang, 1.0 / TWO_PI, None, ALU.mult)
            K.copy(kf, kiI)
            K.stt(ang, kf, -CW1, ang, ALU.mult, ALU.add)
            K.stt(ang, kf, -CW2, ang, ALU.mult, ALU.add)
            K.ts(ang, ang, -3.1415925, 3.1415925, ALU.max, ALU.min)
            K.act(sin_out, ang, AF.Sin, scale=SIN_S)
            K.act(kf, ang, AF.Abs)
            K.act(cos_out, kf, AF.Sin, scale=-SIN_S, bias=halfpi[:, 0:1])

        def scan(out, d0, d1, init):
            rd = [d0, d1] + ([] if isinstance(init, float) else [init])
            return S.op("dve", lambda e: e.tensor_tensor_scan(out=out, data0=d0, data1=d1, initial=init, op0=ALU.mult, op1=ALU.add),
                        reads=rd, writes=[out])

        def odd_layer(layer, j, mod):
            gs, shift, gg = mod_vectors(layer, mod, 0)
            K.phase()
            yT = K.af([8, T], BF16)
            m0 = K.mark()
            w_in_v = odd_w_in[j].rearrange("(k p) o -> p k o", p=128)

            def make_hT():
                hT = K.af([8, T], BF16)
                mk = K.mark()
                for t0 in range(0, T, 512):
                    pre_block(t0, 512, 0 if t0 < NPT else 1, gs, shift, hT[:, :, t0:t0 + 512])
                K.reset_to(mk)
                return hT

            S.tag = "L%d:o_s5" % layer
            uT = K.af([4, T], BF16)
            mS = K.mark()
            hT = make_hT()
            wu = K.af([8, 512], BF16)
            K.dma(wu[:], w_in_v[:, :, 0:512], q="pool")
            for t0 in range(0, T, 512):
                for c in range(4):
                    pb = K.psum()
                    for k in range(8):
                        K.mm(pb[:], wu[:, k, c * 128:(c + 1) * 128], hT[:, k, t0:t0 + 512], start=(k == 0), stop=(k == 7))
                    K.copy(uT[:, c, t0:t0 + 512], pb[:], eng=("act" if c % 2 else "dve"))
            K.reset_to(mS)
            ygT = yT[:, 4:8, :]
            iotaF = K.af([2048], F32)
            K.dma(iotaF[:], iota_d)
            mT = K.mark()
            y_acc = K.af([T], F32)
            e_tb = K.ptr // 4
            tb = K.af([3, 2048], F32)
            ang = tb[:, 0, :]
            kiI = K.arI[:, e_tb + 2048:e_tb + 4096]
            kf = tb[:, 2, :]
            gbuf = [tb[:, 1, :], tb[:, 2, :]]
            sinT = K.af([2048], F32)
            cosT = K.af([2048], F32)
            rtile = tb[:, 0, :]
            bp = K.af([2, 2048], F32)
            Hb = K.af([2, 2048], BF16)
            prm = [K.af([1104], F32) for _ in range(2)]
            der = K.af([2, 14, 16], F32)
            bbt = K.af([2, 2, 16, 16], F32)
            cng = K.af([2, 16, 16], F32)
            tmp3 = K.af([16, 16], F32)
            hfin = K.af([2, 2, 2, 16], F32)
            e_s = K.ptr // 4
            sml = K.af([3, 16], F32)
            smlI = K.arI[:, e_s + 16:e_s + 32]
            for d in range(2):
                P = prm[d]
                K.dma(P[:], s5p[j, d])
                lre, lim, lst = P[:, 0:16], P[:, 16:32], P[:, 32:48]
                bre3 = P[:, 80:336].rearrange("p (s c) -> p s c", s=16)
                bim3 = P[:, 336:592].rearrange("p (s c) -> p s c", s=16)
                cim3 = P[:, 848:1104].rearrange("p (s c) -> p s c", s=16)
                Dv = lambda i, d=d: der[:, d, i, :]
                K.act(Dv(0), lst, AF.Exp)
                K.tt(Dv(1), lre, Dv(0), ALU.mult)
                K.act(Dv(1), Dv(1), AF.Exp)
                K.tt(Dv(2), lim, Dv(0), ALU.mult)
                K.copy(sml[:, 0, :], Dv(2))
                sincos(sml[:, 0, :], smlI, sml[:, 2, :], Dv(3), Dv(4))
                K.tt(Dv(5), Dv(1), Dv(4), ALU.mult)
                K.tt(Dv(6), Dv(1), Dv(3), ALU.mult)
                K.ts(Dv(5), Dv(5), -1.0, None, ALU.add)
                K.tt(Dv(7), lre, lre, ALU.mult)
                K.tt(Dv(8), lim, lim, ALU.mult)
                K.tt(Dv(7), Dv(7), Dv(8), ALU.add)
                K.recip(Dv(7), Dv(7))
                K.tt(Dv(8), Dv(5), lre, ALU.mult)
                K.tt(Dv(9), Dv(6), lim, ALU.mult)
                K.tt(Dv(8), Dv(8), Dv(9), ALU.add)
                K.tt(Dv(8), Dv(8), Dv(7), ALU.mult)
                K.tt(Dv(9), Dv(6), lre, ALU.mult)
                K.tt(Dv(10), Dv(5), lim, ALU.mult)
                K.tt(Dv(9), Dv(9), Dv(10), ALU.subtract)
                K.tt(Dv(9), Dv(9), Dv(7), ALU.mult)
                K.ts(Dv(13), Dv(2), 1.0 / TWO_PI, None, ALU.mult)
                K.ts(Dv(11), Dv(13), -1.0, None, ALU.mult)
                K.ts(Dv(12), Dv(13), 2049.0, None, ALU.mult)
                zre_b = Dv(8).unsqueeze(2).to_broadcast([128, 16, 16])
                zim_b = Dv(9).unsqueeze(2).to_broadcast([128, 16, 16])
                K.tt(bbt[:, d, 0], bre3, zre_b, ALU.mult)
                K.tt(tmp3[:], bim3, zim_b, ALU.mult)
                K.tt(bbt[:, d, 0], bbt[:, d, 0], tmp3[:], ALU.subtract)
                K.tt(bbt[:, d, 1], bim3, zre_b, ALU.mult)
                K.tt(tmp3[:], bre3, zim_b, ALU.mult)
                K.tt(bbt[:, d, 1], bbt[:, d, 1], tmp3[:], ALU.add)
                K.ts(cng[:, d], cim3, -1.0, None, ALU.mult)
            for c in range(4):
                first = True
                for d in range(2):
                    P = prm[d]
                    cre3 = P[:, 592:848].rearrange("p (s c) -> p s c", s=16)
                    for s_ in range(4 * c, 4 * c + 4):
                        base = (s_ % 4) * 32
                        if d == 0:
                            K.ts(ang, iotaF[:], der[:, d, 13, s_:s_ + 1], 0.0, ALU.mult, ALU.add, eng="pool")
                        else:
                            K.ts(ang, iotaF[:], der[:, d, 11, s_:s_ + 1], der[:, d, 12, s_:s_ + 1], ALU.mult, ALU.add, eng="pool")
                        K.ts(kiI, ang, 1.0, None, ALU.mult)
                        K.copy(kf, kiI)
                        K.tt(ang, ang, kf, ALU.subtract)
                        K.act(sinT[:], ang, AF.Sin, scale=TWO_PI * SIN_S)
                        K.act(kf, ang, AF.Abs)
                        K.act(cosT[:], kf, AF.Sin, scale=-TWO_PI * SIN_S, bias=halfpi[:, 0:1])
                        K.ts(rtile[:], iotaF[:], 0.0, der[:, d, 1, s_:s_ + 1], ALU.mult, ALU.add, eng="pool")
                        BBT = []
                        for ri in range(2):
                            Wp = K.arotbuf("s5_Wp", [128], F32, n=2)
                            K.memset(Wp[:], 0.0, eng="pool")
                            K.copy(Wp[0:64, base:base + 16], bbt[0:64, d, ri, s_, :], eng="pool")
                            K.copy(Wp[64:128, base + 16:base + 32], bbt[64:128, d, ri, s_, :], eng="pool")
                            pb = K.psum()
                            K.tr(pb[:, 0:128], Wp[:], ident[:])
                            bt = K.arotbuf("s5_BBT%d" % ri, [128], BF16, n=2)
                            K.copy(bt[:], pb[:, 0:128], eng="act")
                            BBT.append(bt)
                        CP = []
                        for ri, csrc in ((0, cre3), (1, cng[:, d])):
                            cp = K.arotbuf("s5_Cp%d" % ri, [128], BF16, n=2)
                            K.memset(cp[:], 0.0, eng="pool")
                            K.copy(cp[0:64, base:base + 16], csrc[0:64, s_, :], eng="pool")
                            K.copy(cp[64:128, base + 16:base + 32], csrc[64:128, s_, :], eng="pool")
                            CP.append(cp)
                        for seg in range(2):
                            if seg == 0:
                                c0, L, nseq = 0, 256, 2
                            else:
                                c0, L, nseq = 512, 2048, 1
                            tab0 = 0 if d == 0 else 2048 - L
                            Wt = L * nseq
                            if seg == 0:
                                v3 = lambda ap: ap.rearrange("p (q t) -> p q t", q=2)
                                tabv = lambda tb_, b0, w: tb_[:, tab0:tab0 + 256].unsqueeze(1).to_broadcast([128, 2, 256])
                            else:
                                v3 = lambda ap: ap
                                tabv = lambda tb_, b0, w: tb_[:, tab0 + b0:tab0 + b0 + w]
                            for b0 in range(0, Wt, 512):
                                w = min(512, Wt - b0)
                                pr = K.psum()
                                pi_ = K.psum()
                                K.mm(pr[:, 0:w], BBT[0][:], uT[:, c, c0 + b0:c0 + b0 + w])
                                K.mm(pi_[:, 0:w], BBT[1][:], uT[:, c, c0 + b0:c0 + b0 + w])
                                cosv = tabv(cosT, b0, w)
                                sinv = tabv(sinT, b0, w)
                                t1 = K.arotbuf("s5_t1", [512], F32, n=2)
                                t2 = K.arotbuf("s5_t2", [512], F32, n=2)
                                t3 = K.arotbuf("s5_t3", [512], F32, n=2)
                                t4 = K.arotbuf("s5_t4", [512], F32, n=2)
                                K.tt(v3(t1[:, 0:w]), v3(pr[:, 0:w]), cosv, ALU.mult)
                                K.tt(v3(t2[:, 0:w]), v3(pi_[:, 0:w]), sinv, ALU.mult)
                                K.tt(bp[:, 0, b0:b0 + w], t1[:, 0:w], t2[:, 0:w], ALU.add, eng="pool")
                                K.tt(v3(t3[:, 0:w]), v3(pi_[:, 0:w]), cosv, ALU.mult)
                                K.tt(v3(t4[:, 0:w]), v3(pr[:, 0:w]), sinv, ALU.mult)
                                K.tt(bp[:, 1, b0:b0 + w], t3[:, 0:w], t4[:, 0:w], ALU.subtract, eng="pool")
                            for q_ in range(nseq):
                                qs = slice(q_ * L, (q_ + 1) * L)
                                for ri in range(2):
                                    init = P[:, 48 + ri * 16 + s_:48 + ri * 16 + s_ + 1] if seg == 1 else 0.0
                                    if d == 0:
                                        scan(gbuf[ri][:, qs], rtile[:, 0:L], bp[:, ri, qs], init)
                                    else:
                                        scan(gbuf[ri][:, qs][:, ::-1], rtile[:, 0:L][:, ::-1], bp[:, ri, qs][:, ::-1], init)
                            for b0 in range(0, Wt, 512):
                                w = min(512, Wt - b0)
                                cosv = tabv(cosT, b0, w)
                                sinv = tabv(sinT, b0, w)
                                t1 = K.arotbuf("s5_t1", [512], F32, n=2)
                                t2 = K.arotbuf("s5_t2", [512], F32, n=2)
                                t3 = K.arotbuf("s5_t3", [512], F32, n=2)
                                t4 = K.arotbuf("s5_t4", [512], F32, n=2)
                                K.tt(v3(t1[:, 0:w]), v3(gbuf[0][:, b0:b0 + w]), cosv, ALU.mult, eng="pool")
                                K.tt(v3(t2[:, 0:w]), v3(gbuf[1][:, b0:b0 + w]), sinv, ALU.mult, eng="pool")
                                K.tt(Hb[:, 0, b0:b0 + w], t1[:, 0:w], t2[:, 0:w], ALU.subtract)
                                K.tt(v3(t3[:, 0:w]), v3(gbuf[0][:, b0:b0 + w]), sinv, ALU.mult)
                                K.tt(v3(t4[:, 0:w]), v3(gbuf[1][:, b0:b0 + w]), cosv, ALU.mult)
                                K.tt(Hb[:, 1, b0:b0 + w], t3[:, 0:w], t4[:, 0:w], ALU.add)
                                py = K.psum()
                                K.mm(py[:, 0:w], CP[0][:], Hb[:, 0, b0:b0 + w], start=True, stop=False)
                                K.mm(py[:, 0:w], CP[1][:], Hb[:, 1, b0:b0 + w], start=False, stop=True)
                                ya = y_acc[:, c0 + b0:c0 + b0 + w]
                                if first:
                                    K.copy(ya, py[:, 0:w], eng="act")
                                else:
                                    K.tt(ya, ya, py[:, 0:w], ALU.add)
                            if seg == 0:
                                col = L - 1 if d == 0 else 0
                                tc_ = tab0 + col
                                g0c = gbuf[0][:, 0:512].rearrange("p (q t) -> p q t", q=2)[:, :, col]
                                g1c = gbuf[1][:, 0:512].rearrange("p (q t) -> p q t", q=2)[:, :, col]
                                cc_ = cosT[:, tc_:tc_ + 1].to_broadcast([128, 2])
                                sc_ = sinT[:, tc_:tc_ + 1].to_broadcast([128, 2])
                                f1 = K.arotbuf("s5_f1", [2], F32, n=2)
                                f2 = K.arotbuf("s5_f2", [2], F32, n=2)
                                K.tt(f1[:], g0c, cc_, ALU.mult, eng="pool")
                                K.tt(f2[:], g1c, sc_, ALU.mult, eng="pool")
                                K.tt(hfin[:, :, d, 0, s_], f1[:], f2[:], ALU.subtract, eng="pool")
                                f1 = K.arotbuf("s5_f1", [2], F32, n=2)
                                f2 = K.arotbuf("s5_f2", [2], F32, n=2)
                                K.tt(f1[:], g0c, sc_, ALU.mult, eng="pool")
                                K.tt(f2[:], g1c, cc_, ALU.mult, eng="pool")
                                K.tt(hfin[:, :, d, 1, s_], f1[:], f2[:], ALU.add, eng="pool")
                        first = False
                for t0 in range(0, T, 512):
                    yb = y_acc[:, t0:t0 + 512]
                    K.stt(yb, uT[:, c, t0:t0 + 512], s5dT[:, j * 4 + c:j * 4 + c + 1], yb, ALU.mult, ALU.add)
                    y2 = K.arotbuf("s5_y2", [512], F32, n=1)
                    K.tt(y2[:], yb, yb, ALU.mult, eng="pool")
                    K.ts(y2[:], y2[:], 0.044715, 1.0, ALU.mult, ALU.add)
                    K.tt(y2[:], y2[:], yb, ALU.mult, eng="pool")
                    K.act(y2[:], y2[:], AF.Sigmoid, scale=GELU_C)
                    K.tt(ygT[:, c, t0:t0 + 512], yb, y2[:], ALU.mult)
            for si in range(2):
                for d in range(2):
                    finals.append(K.dma(o_s5re[si, j, d], hfin[:, si, d, 0, :]))
                    finals.append(K.dma(o_s5im[si, j, d], hfin[:, si, d, 1, :]))
            S.tag = "L%d:o_glu" % layer
            K.reset_to(mT)
            wg = K.af([4, 512], BF16)
            K.dma(wg[:], s5_glu_w[j].rearrange("(k p) o -> p k o", p=128), q="pool")
            for t0 in range(0, T, 512):
                for c2 in range(4):
                    pb = K.psum()
                    for k in range(4):
                        K.mm(pb[:], wg[:, k, c2 * 128:(c2 + 1) * 128], ygT[:, k, t0:t0 + 512], start=(k == 0), stop=(k == 3))
                    sg = K.arotbuf("s5_sg", [512], F32, n=2)
                    K.act(sg[:], pb[:], AF.Sigmoid, bias=glubT[:, j * 4 + c2:j * 4 + c2 + 1])
                    K.tt(yT[:, c2, t0:t0 + 512], ygT[:, c2, t0:t0 + 512], sg[:], ALU.mult)

            K.reset_to(m0)
            S.tag = "L%d:o_naproj" % layer
            qT = K.af([4, T], BF16)
            kT = K.af([4, 3072], BF16)
            vtok = K.af([24, 512], BF16)
            mN = K.mark()
            hT = make_hT()
            wq = K.af([8, 512], BF16)
            wk = K.af([8, 512], BF16)
            wv = K.af([8, 512], BF16)
            K.dma(wq[:], w_in_v[:, :, 512:1024], q="pool")
            K.dma(wk[:], w_in_v[:, :, 1024:1536], q="pool")
            K.dma(wv[:], w_in_v[:, :, 1536:2048], q="pool")
            for t0 in range(0, T, 512):
                for c in range(4):
                    pb = K.psum()
                    for k in range(8):
                        K.mm(pb[:], wq[:, k, c * 128:(c + 1) * 128], hT[:, k, t0:t0 + 512], start=(k == 0), stop=(k == 7))
                    K.act(qT[:, c, t0:t0 + 512], pb[:], AF.Identity, scale=NA_SCALE)
                    pb = K.psum()
                    for k in range(8):
                        K.mm(pb[:], wk[:, k, c * 128:(c + 1) * 128], hT[:, k, t0:t0 + 512], start=(k == 0), stop=(k == 7))
                    K.copy(kT[:, c, t0:t0 + 512], pb[:], eng="dve")
            for ti in range(20):
                pb = K.psum()
                for k in range(8):
                    K.mm(pb[:], hT[:, k, ti * 128:(ti + 1) * 128], wv[:, k, :], start=(k == 0), stop=(k == 7))
                K.copy(vtok[:, ti, :], pb[:], eng="act")
                if ti < 4:
                    sq_i, r0 = ti // 2, (ti % 2) * 128
                    ov = K.arotbuf("na_ov", [512], F32, n=2)
                    K.copy(ov[:], pb[:], eng="dve")
                    finals.append(K.dma(o_nav[sq_i, j, r0:r0 + 128, :], ov[:]))
                    pb2 = K.psum()
                    for k in range(8):
                        K.mm(pb2[:], hT[:, k, ti * 128:(ti + 1) * 128], wk[:, k, :], start=(k == 0), stop=(k == 7))
                    ok_ = K.arotbuf("na_ok", [512], F32, n=2)
                    K.copy(ok_[:], pb2[:], eng="act")
                    finals.append(K.dma(o_nak[sq_i, j, r0:r0 + 128, :], ok_[:]))
            for i in range(4):
                stg = K.arotbuf("na_stg", [512], F32, n=2)
                K.dma(stg[:], nak_in[j, i * 128:(i + 1) * 128, :])
                pb = K.psum()
                for c in range(4):
                    K.tr(pb[:, c * 128:(c + 1) * 128], stg[:, c * 128:(c + 1) * 128], ident[:])
                K.copy(kT[:, :, 2560 + i * 128:2560 + (i + 1) * 128], pb[:].rearrange("p (c t) -> p c t", c=4), eng="dve")
                K.dma(vtok[:, 20 + i, :], nav_in[j, i * 128:(i + 1) * 128, :], q="pool")
            K.reset_to(mN)
            Ap = K.af([64, 8, 15], F32)
            rp = K.af([120], F32)
            K.memset(rp[0:32, :], NEGV)
            K.dma(rp[0:31, :], na_rpb[j].rearrange("h r c -> c (h r)"), allow_slow_non_contiguous=True)
            rhi = K.af([120], BF16)
            rlo = K.af([120], BF16)
            K.copy(rhi[0:32, :], rp[0:32, :], eng="act")
            K.tt(rlo[0:32, :], rp[0:32, :], rhi[0:32, :], ALU.subtract)
            for q4 in range(4):
                oh = K.arotbuf("na_oh", [16, 128], BF16, n=2)
                K.dma(oh[0:32, :, :], oh_d[:, q4 * 16:(q4 + 1) * 16, :], q="pool")
                for g4 in range(4):
                    pb = K.psum()
                    for i in range(4):
                        oc = pb[:, i * 120:(i + 1) * 120]
                        K.mm(oc, oh[0:32, g4 * 4 + i, :], rhi[0:32, :], start=True, stop=False)
                        K.mm(oc, oh[0:32, g4 * 4 + i, :], rlo[0:32, :], start=False, stop=True)
                    kc0 = q4 * 16 + g4 * 4
                    K.copy(Ap[:, kc0:kc0 + 4, :, :].rearrange("p k h r -> p k (h r)"), pb[:, 0:480].rearrange("p (k x) -> p k x", k=4),
                           eng=("act" if g4 % 2 else "dve"))
            S.tag = "L%d:o_naattn" % layer
            for (c0, L, smp) in SEQS[0:2]:
                for h in range(8):
                    ch, half = h // 2, h % 2
                    hsl = slice(half * 64, half * 64 + 64)
                    for qt in range(2):
                        qc = slice(c0 + qt * 128, c0 + (qt + 1) * 128)
                        attn_unit(qT[hsl, ch, qc], [(kT[hsl, ch, c0:c0 + 256], None)],
                                  [(t_ * 128, 128, vtok[:, c0 // 128 + t_, ch * 128:(ch + 1) * 128]) for t_ in range(2)],
                                  hsl, yT[hsl, 4 + ch, qc], "n")
            for i in range(16):
                if i < 2:
                    kr0, nr = 0, 8
                elif i > 13:
                    kr0, nr = 24, 8
                else:
                    kr0, nr = 2 * i - 4, 9
                kcol0 = 512 + kr0 * 64
                vt0 = 4 + kr0 // 2
                for h in range(8):
                    ch, half = h // 2, h % 2
                    hsl = slice(half * 64, half * 64 + 64)

                    def biasA(bank, w, i=i, h=h, kr0=kr0):
                        Ssb = K.arotbuf("na_S", [512], F32, n=2)
                        for ql in range(2):
                            qr = 2 * i + ql
                            rs_ = min(max(qr - 4, 0), 24)
                            mlo = max(0, rs_ - kr0)
                            mhi = min(8, rs_ - kr0 + 8)
                            ps_ = slice(ql * 64, ql * 64 + 64)
                            if mlo > 0:
                                K.memset(Ssb[ps_, 0:mlo * 64], NEGV, eng="pool")
                            if mhi < 8:
                                K.memset(Ssb[ps_, mhi * 64:512], NEGV, eng="pool")
                            dr0 = kr0 + mlo - qr + 7
                            nm_ = mhi - mlo
                            K.tt(Ssb[ps_, mlo * 64:mhi * 64].rearrange("p (m k) -> p m k", m=nm_),
                                 bank[ps_, mlo * 64:mhi * 64].rearrange("p (m k) -> p m k", m=nm_),
                                 Ap[ps_, :, h, dr0:dr0 + nm_].rearrange("p k r -> p r k"), ALU.add)
                        return Ssb[:, 0:512]

                    def biasB(bank, w, i=i, h=h, kr0=kr0):
                        Ssb = K.arotbuf("na_SB", [64], F32, n=2)
                        for ql in range(2):
                            qr = 2 * i + ql
                            rs_ = min(max(qr - 4, 0), 24)
                            ps_ = slice(ql * 64, ql * 64 + 64)
                            if rs_ <= kr0 + 8 < rs_ + 8:
                                dr = kr0 + 8 - qr + 7
                                K.tt(Ssb[ps_, 0:64], bank[ps_, 0:64], Ap[ps_, :, h, dr], ALU.add)
                            else:
                                K.memset(Ssb[ps_, 0:64], NEGV, eng="pool")
                        return Ssb[:, 0:64]

                    kbanks = [(kT[hsl, ch, 2560:3072], None), (kT[hsl, ch, kcol0:kcol0 + 512], biasA)]
                    ktiles = [(t_ * 128, 128, vtok[:, 20 + t_, ch * 128:(ch + 1) * 128]) for t_ in range(4)]
                    ktiles += [(512 + t_ * 128, 128, vtok[:, vt0 + t_, ch * 128:(ch + 1) * 128]) for t_ in range(4)]
                    if nr == 9:
                        kbanks.append((kT[hsl, ch, kcol0 + 512:kcol0 + 576], biasB))
                        ktiles.append((1024, 64, vtok[0:64, vt0 + 4, ch * 128:(ch + 1) * 128]))
                    qc = slice(512 + i * 128, 512 + (i + 1) * 128)
                    attn_unit(qT[hsl, ch, qc], kbanks, ktiles, hsl, yT[hsl, 4 + ch, qc], "n")
            attn_flush()
            K.reset_to(m0)
            S.tag = "L%d:o_out" % layer
            out_proj_post(odd_w_out[j], yT, gg)

        for layer in range(DEPTH):
            mod = ada(layer)
            if layer % 2 == 0 and ENABLE_EVEN:
                even_layer(layer, layer // 2, mod)
            if layer % 2 == 1 and ENABLE_ODD:
                odd_layer(layer, layer // 2, mod)
            gs, shift, gg = mod_vectors(layer, mod, 1)
            ffn(layer, gs, shift, gg)
            if dbg_stop == layer:
                break

        S.tag = "final"
        finals = []

        K.phase()

        def xT_to_out(dst, ntile, col0):
            for i in range(ntile):
                xb = K.arotbuf("xinT", [8, 128], F32)
                K.dma(xb[:], xT_v[:, :, col0 + i * 128: col0 + (i + 1) * 128])
                xo = K.arotbuf("xin", [D], F32)
                for g in range(2):
                    pb = K.psum()
                    for c in range(4):
                        K.tr(pb[:, c * 128:(c + 1) * 128], xb[:, g * 4 + c, :], ident[:])
                    K.copy(xo[:, g * 512:(g + 1) * 512], pb[:], eng=("act" if g else "dve"))
                finals.append(K.dma(dst[i * 128:(i + 1) * 128, :], xo[:]))

        xT_to_out(yp, NPT // 128, 0)
        xT_to_out(ys, LS // 128, NPT)
        S.finish(finals)
        global LAST_TAGS
        LAST_TAGS = S.tags
        S.emit()
    return nc


def _make_consts():
    i = np.arange(128, dtype=np.float32)
    jj = i[:, None]
    ii = i[None, :]
    rc = np.zeros((128, 772), np.float32)
    rc[:, 0:128] = np.maximum(ii - jj, 0)
    rc[:, 128:256] = (ii >= jj)
    rc[:, 256:384] = np.maximum(jj - ii, 0)
    rc[:, 384:512] = (jj >= ii)
    rc[:, 512:640] = np.broadcast_to(ii + 1.0, (128, 128))
    rc[:, 640:768] = np.broadcast_to(128.0 - ii, (128, 128))
    rc[:, 768] = 127.0 - i
    rc[:, 769] = i
    rc[:, 770] = 128.0
    inv = (10000.0 ** (-np.arange(8, dtype=np.float32) / np.float32(8))).astype(np.float32)
    t = np.arange(2048)
    row = (t // 64).astype(np.float32)
    col = (t % 64).astype(np.float32)
    ang = np.concatenate([row[:, None] * inv, col[:, None] * inv], axis=-1).astype(np.float32)
    cos, sin = np.cos(ang).T, np.sin(ang).T
    ropeC = np.concatenate([cos, cos], 0).astype(np.float32)
    ropeS = np.concatenate([-sin, sin], 0).astype(np.float32)
    oh = np.zeros((32, 64, 128), np.float32)
    qc = np.arange(64)
    cs = np.clip(qc - 8, 0, 48)
    for kc in range(64):
        jidx = np.clip(kc - qc + 15, 0, 30)
        for ql in range(2):
            oh[jidx, kc, ql * 64 + qc] = 1.0
            oh[31, kc, ql * 64 + qc] = ((kc < cs) | (kc >= cs + 16)).astype(np.float32)
    iota = np.broadcast_to(np.arange(1, 2049, dtype=np.float32)[None, :], (128, 2048))
    return {"rconst": rc, "ropeC": np.ascontiguousarray(ropeC), "ropeS": np.ascontiguousarray(ropeS),
            "oh": oh, "iota": np.ascontiguousarray(iota)}


def _s5_lay(a):
    a = np.asarray(a, np.float32)
    lead = a.shape[:-2] if a.ndim >= 2 else ()
    return a


def _s5_pack(inputs, s):
    def gp(a):
        a = np.asarray(a, np.float32).reshape(2, 2, 16, 2, 64)
        return a.transpose(0, 1, 3, 4, 2).reshape(2, 2, 128, 16)
    lre = gp(inputs["s5_lambda_re"])
    lim = gp(inputs["s5_lambda_im"])
    lst = gp(np.broadcast_to(np.asarray(inputs["s5_log_step"], np.float32)[..., None], (2, 2, 32, 64)))
    h0re = gp(np.asarray(inputs["state_s5_re"], np.float32)[s])
    h0im = gp(np.asarray(inputs["state_s5_im"], np.float32)[s])
    def gps(a):
        a = np.asarray(a, np.float32).reshape(2, 2, 16, 2, 64, 16)
        return a.transpose(0, 1, 3, 4, 2, 5).reshape(2, 2, 128, 256)
    def gsp(a):
        a = np.asarray(a, np.float32).reshape(2, 2, 16, 2, 16, 64)
        return a.transpose(0, 1, 3, 5, 2, 4).reshape(2, 2, 128, 256)
    pk = np.concatenate([lre, lim, lst, h0re, h0im, gps(inputs["s5_b_re"]), gps(inputs["s5_b_im"]),
                         gsp(inputs["s5_c_re"]), gsp(inputs["s5_c_im"])], axis=-1)
    return np.ascontiguousarray(pk.astype(np.float32))


CONSTS = _make_consts()


def make_in_maps(inputs):
    f = lambda a: np.ascontiguousarray(np.asarray(a, dtype=np.float32))
    maps = []
    ident = np.eye(128, dtype=np.float32)
    for core in range(8):
        s = core // 4
        m = {}
        m["xp"] = f(inputs["x_prompt"][2 * core:2 * core + 2]).reshape(NPT, D)
        m["xs"] = f(inputs["x_sample"][s])
        m["cv"] = f(np.stack([inputs["c_ctx"], inputs["c"][s]], 0))
        m["ident_in"] = ident
        m["ada_w"] = f(inputs["ada_w"])
        m["ada_b"] = f(inputs["ada_b"]).reshape(DEPTH * 48, 128)
        for nm in ("mix_pre_g", "mix_post_g", "ffn_pre_g", "ffn_post_g"):
            m[nm] = f(inputs[nm]).reshape(DEPTH * 8, 128)
        m["ffn_w_up"] = f(inputs["ffn_w_up"])
        m["ffn_conv_w"] = f(inputs["ffn_conv_w"]).reshape(DEPTH * 3 * 44, 128)
        m["ffn_conv_b"] = f(inputs["ffn_conv_b"]).reshape(DEPTH * 44, 128)
        m["ffn_w_down"] = f(inputs["ffn_w_down"])
        m["even_w_in"] = f(inputs["even_w_in"])
        m["even_w_out"] = f(inputs["even_w_out"])
        m["ret_logit_b"] = f(np.broadcast_to(np.asarray(inputs["ret_logit"], np.float32).reshape(2, 1, 8), (2, 128, 8)))
        m["ret_gn"] = f(inputs["ret_gn"]).reshape(8, 128)
        m["mla_q_norm"] = f(inputs["mla_q_norm"]).reshape(4, 128)
        m["mla_w_uq"] = f(inputs["mla_w_uq"])
        m["mla_kv_norm"] = f(inputs["mla_kv_norm"])
        m["kvn_bc"] = f(np.broadcast_to(np.asarray(inputs["mla_kv_norm"], np.float32).reshape(1, 256), (128, 256)))
        m["mla_w_ukv"] = f(inputs["mla_w_ukv"])
        m["state_ret_in"] = f(inputs["state_ret"][s])
        m["ckv_in"] = f(inputs["cache_mla_ckv"][s])
        m["kr_in"] = f(inputs["cache_mla_krope"][s])
        m["odd_w_in"] = f(inputs["odd_w_in"])
        m["odd_w_out"] = f(inputs["odd_w_out"])
        m["s5p"] = _s5_pack(inputs, s)
        m["s5_d"] = f(inputs["s5_d"]).reshape(8, 128)
        m["s5_glu_b"] = f(inputs["s5_glu_b"]).reshape(8, 128)
        m["s5_glu_w"] = f(inputs["s5_glu_w"])
        m["na_rpb"] = f(inputs["na_rpb"])
        m["nak_in"] = f(inputs["cache_na_k"][s]).reshape(2, 512, 512)
        m["nav_in"] = f(inputs["cache_na_v"][s]).reshape(2, 512, 512)
        m["oh_in"] = CONSTS["oh"]
        m["iota_in"] = CONSTS["iota"]
        m["rconst_in"] = CONSTS["rconst"]
        m["ropeC_in"] = CONSTS["ropeC"]
        m["ropeS_in"] = CONSTS["ropeS"]
        maps.append(m)
    return maps


def kernel(**inputs):
    nc = build_program()
    maps = make_in_maps(inputs)
    res = run_bass_kernel_spmd(nc, maps, core_ids=list(range(8)))
    R = res.results
    y_prompt = np.concatenate([R[c]["yp"].reshape(2, 256, D) for c in range(8)], 0)
    y_sample = np.stack([R[0]["ys"], R[4]["ys"]], 0)
    st_ret = np.concatenate([R[c]["o_ret"] for c in range(8)], 0)
    ck_ckv = np.concatenate([R[c]["o_ckv"] for c in range(8)], 0)
    ck_kr = np.concatenate([R[c]["o_kr"] for c in range(8)], 0)
    def s5o(name):
        o = np.concatenate([R[c][name] for c in range(8)], 0)
        o = o.reshape(16, 2, 2, 2, 64, 16).transpose(0, 1, 2, 5, 3, 4)
        return np.ascontiguousarray(o.reshape(16, 2, 2, 32, 64))
    st_re, st_im = s5o("o_s5re"), s5o("o_s5im")
    nak = np.concatenate([R[c]["o_nak"] for c in range(8)], 0).reshape(16, 2, 256, 8, 64)
    nav = np.concatenate([R[c]["o_nav"] for c in range(8)], 0).reshape(16, 2, 256, 8, 64)
    return (y_prompt, y_sample, st_ret, ck_ckv, ck_kr, st_re, st_im, nak, nav)
```
